# Optimizing a Trainium2 kernel written in Bass

```python
import math
import jax, jax.numpy as jnp
from jax import lax
import numpy as np

D_MODEL = 1024
BATCH = 8
SEQ = 4096
DEPTH = 4

N_MIXERS = 3
EXPAND = 2
D_INNER = EXPAND * D_MODEL
LN_EPS = 1e-5
DEEPNORM_ALPHA = (2.0 * DEPTH) ** 0.25
DEEPNORM_BETA = (8.0 * DEPTH) ** -0.25

HY_SHORT_CONV = 3
HY_EMB_DIM = 33
HY_FILTER_HIDDEN = 64
HY_FAST_DECAY_PCT = 0.3
HY_SLOW_DECAY_PCT = 1.5
HY_DECAY_TARGET = 1e-2
HY_IN_DIM = 4 * D_INNER

SSD_HEADDIM = 64
SSD_HEADS = D_INNER // SSD_HEADDIM
SSD_STATE = 128
SSD_GROUPS = 4
SSD_CONV = 5
SSD_CHUNK = 128
SSD_CONV_DIM = D_INNER + 2 * SSD_GROUPS * SSD_STATE
SSD_IN_DIM = D_INNER + SSD_CONV_DIM + 2 * SSD_HEADS
SSD_DT_MIN = 1e-3
SSD_DT_MAX = 1e-1

HG_HEADS = 16
HG_KEY_DIM = 128
HG_VAL_DIM = D_INNER // HG_HEADS
HG_FDIM = HG_HEADS * HG_KEY_DIM
HG_CHUNK = 64
HG_IN_DIM = 3 * HG_FDIM + 2 * D_INNER

kernel_name = "hybrid_hyena_ssd_hgrn2_deepnorm_encoder"


def layer_norm(x, g, b):
    xf = x.astype(jnp.float32)
    mu = jnp.mean(xf, axis=-1, keepdims=True)
    var = jnp.mean(jnp.square(xf - mu), axis=-1, keepdims=True)
    return (xf - mu) * lax.rsqrt(var + LN_EPS) * g + b


def rms_norm_groups(x, g, n_groups):
    shp = x.shape
    xf = x.astype(jnp.float32).reshape(shp[:-1] + (n_groups, shp[-1] // n_groups))
    xf = xf * lax.rsqrt(jnp.mean(jnp.square(xf), axis=-1, keepdims=True) + LN_EPS)
    return xf.reshape(shp) * g


def centred_depthwise_conv(x, w, b):
    k = w.shape[0]
    pad = k // 2
    L = x.shape[1]
    xp = jnp.pad(x, ((0, 0), (pad, pad), (0, 0)))
    return sum(xp[:, j:j + L] * w[j] for j in range(k)) + b


def hyena_filters(L, w1, b1, w2, b2, w3, b3, freq, w_out):
    t = jnp.linspace(0.0, 1.0, L, dtype=jnp.float32)[:, None]
    bands = (HY_EMB_DIM - 1) // 2
    w = 2.0 * math.pi * jnp.arange(L, dtype=jnp.float32)[:, None] / L
    f = jnp.linspace(1e-4, bands - 1, bands, dtype=jnp.float32)[None, :]
    z = jnp.concatenate([t, jnp.cos(f * w), -jnp.sin(f * w)], axis=-1)
    fr = freq.astype(jnp.float32)
    h = jnp.sin(fr * (z @ w1.astype(jnp.float32) + b1))
    h = jnp.sin(fr * (h @ w2.astype(jnp.float32) + b2))
    h = jnp.sin(fr * (h @ w3.astype(jnp.float32) + b3))
    h = h @ w_out.astype(jnp.float32)
    min_d = math.log(HY_DECAY_TARGET) / HY_FAST_DECAY_PCT
    max_d = math.log(HY_DECAY_TARGET) / HY_SLOW_DECAY_PCT
    deltas = jnp.abs(jnp.linspace(min_d, max_d, D_INNER, dtype=jnp.float32))
    h = h * jnp.exp(-t * jnp.concatenate([deltas, deltas]))
    return h[:, :D_INNER], h[:, D_INNER:]


def bidirectional_fftconv(u, h_fwd, h_bwd, skip):
    L = u.shape[1]
    n = 2 * L
    kfull = jnp.concatenate([h_fwd, jnp.zeros((1, h_fwd.shape[1]), jnp.float32), h_bwd[:0:-1]], axis=0)
    k_f = jnp.fft.rfft(kfull, n=n, axis=0)
    uf32 = u.astype(jnp.float32)
    u_f = jnp.fft.rfft(uf32, n=n, axis=1)
    y = jnp.fft.irfft(u_f * k_f[None], n=n, axis=1)[:, :L]
    return y + uf32 * skip.astype(jnp.float32)


def hyena_mixer(x, w_in, conv_w, conv_b, filt_w1, filt_b1, filt_w2, filt_b2, filt_w3, filt_b3,
                filt_freq, filt_w_out, skip, w_out):
    L = x.shape[1]
    proj = x @ w_in
    u, z = proj[..., :3 * D_INNER], proj[..., 3 * D_INNER:]
    u = centred_depthwise_conv(u, conv_w, conv_b)
    x0, x1, v = jnp.split(u.astype(jnp.float32), 3, axis=-1)
    h_f, h_b = hyena_filters(L, filt_w1, filt_b1, filt_w2, filt_b2, filt_w3, filt_b3, filt_freq, filt_w_out)
    y = x0 * bidirectional_fftconv(x1 * v, h_f, h_b, skip)
    y = y * jax.nn.silu(z.astype(jnp.float32))
    return (y.astype(x.dtype) @ w_out).astype(x.dtype)


def segsum(a):
    T = a.shape[-1]
    cs = jnp.cumsum(a, axis=-1)
    diff = cs[..., :, None] - cs[..., None, :]
    return jnp.where(jnp.tril(jnp.ones((T, T), dtype=bool)), diff, -jnp.inf)


def ssd_chunked(xh, dt, a, bm, cm):
    b, L, h, p = xh.shape
    g, n = bm.shape[2], bm.shape[3]
    hg = h // g
    T = SSD_CHUNK
    nc = L // T
    xdt = (xh * dt[..., None]).reshape(b, nc, T, g, hg, p)
    adt = jnp.moveaxis((dt * a).reshape(b, nc, T, g, hg), 2, -1)
    a_cs = jnp.cumsum(adt, axis=-1)
    bc = bm.reshape(b, nc, T, g, n)
    cc = cm.reshape(b, nc, T, g, n)
    lmat = jnp.exp(segsum(adt))
    cb = jnp.einsum('bctgn,bcsgn->bcgts', cc, bc)
    y_diag = jnp.einsum('bcgts,bcghts,bcsghp->bctghp', cb, lmat, xdt)
    decay_states = jnp.exp(a_cs[..., -1:] - a_cs)
    states = jnp.einsum('bcsgn,bcghs,bcsghp->bcghpn', bc, decay_states, xdt)
    chunk_decay = jnp.pad(jnp.moveaxis(a_cs[..., -1], 1, -1), ((0, 0), (0, 0), (0, 0), (1, 0)))
    dchunk = jnp.exp(segsum(chunk_decay))
    states0 = jnp.concatenate([jnp.zeros_like(states[:, :1]), states], axis=1)
    new_states = jnp.einsum('bghzc,bcghpn->bzghpn', dchunk, states0)
    prev_states = new_states[:, :-1]
    y_off = jnp.einsum('bctgn,bcghpn,bcght->bctghp', cc, prev_states, jnp.exp(a_cs))
    return (y_diag + y_off).reshape(b, L, h, p)


def ssd_mixer(x, w_in, conv_w, conv_b, dt_bias, a_log, d_skip, norm_g, w_out):
    b, L, _ = x.shape
    gn = SSD_GROUPS * SSD_STATE
    proj = x @ w_in
    z = proj[..., :D_INNER].astype(jnp.float32)
    xbc = proj[..., D_INNER:D_INNER + SSD_CONV_DIM]
    dt_raw = proj[..., D_INNER + SSD_CONV_DIM:].astype(jnp.float32).reshape(b, L, 2, SSD_HEADS)
    xbc = jax.nn.silu(centred_depthwise_conv(xbc, conv_w, conv_b).astype(jnp.float32))
    xs = xbc[..., :D_INNER].reshape(b, L, SSD_HEADS, SSD_HEADDIM)
    bm = xbc[..., D_INNER:D_INNER + gn].reshape(b, L, SSD_GROUPS, SSD_STATE)
    cm = xbc[..., D_INNER + gn:].reshape(b, L, SSD_GROUPS, SSD_STATE)
    dt = jax.nn.softplus(dt_raw + dt_bias.astype(jnp.float32))
    a = -jnp.exp(a_log.astype(jnp.float32))
    y_f = ssd_chunked(xs, dt[:, :, 0], a[0], bm, cm)
    y_b = ssd_chunked(xs[:, ::-1], dt[:, ::-1, 1], a[1], bm[:, ::-1], cm[:, ::-1])[:, ::-1]
    y = y_f + y_b + xs * d_skip.astype(jnp.float32)[:, None]
    y = rms_norm_groups(y.reshape(b, L, D_INNER) * jax.nn.silu(z), norm_g, SSD_GROUPS)
    return (y.astype(x.dtype) @ w_out).astype(x.dtype)


def hgrn2_chunked(q, k, log_f, v):
    b, L, h, dk = q.shape
    dv = v.shape[-1]
    T = HG_CHUNK
    nc = L // T

    def to_chunks(a):
        return jnp.moveaxis(a.reshape(b, nc, T, h, a.shape[-1]), 1, 0)

    causal = jnp.tril(jnp.ones((T, T), dtype=bool))[None, :, :, None, None]

    def step(S, inp):
        qt, kt, gt, vt = inp
        G = jnp.cumsum(gt, axis=1)
        o_inter = jnp.einsum('bthk,bhkv->bthv', qt * jnp.exp(G), S)
        diff = G[:, :, None] - G[:, None, :]
        decay = jnp.exp(jnp.where(causal, diff, -jnp.inf))
        att = jnp.einsum('bthk,bshk,btshk->bhts', qt, kt, decay)
        o_intra = jnp.einsum('bhts,bshv->bthv', att, vt)
        g_last = G[:, -1]
        k_dec = kt * jnp.exp(g_last[:, None] - G)
        S_new = jnp.exp(g_last)[..., None] * S + jnp.einsum('bshk,bshv->bhkv', k_dec, vt)
        return S_new, o_inter + o_intra

    S0 = jnp.zeros((b, h, dk, dv), jnp.float32)
    _, o = lax.scan(step, S0, (to_chunks(q), to_chunks(k), to_chunks(log_f), to_chunks(v)))
    return jnp.moveaxis(o, 0, 1).reshape(b, L, h, dv)


def hgrn2_mixer(x, w_in, lb, norm_g, w_out):
    b, L, _ = x.shape
    F = HG_FDIM
    proj = (x @ w_in).astype(jnp.float32)
    q = proj[..., :F].reshape(b, L, HG_HEADS, HG_KEY_DIM)
    f_raw = proj[..., F:3 * F].reshape(b, L, 2, HG_HEADS, HG_KEY_DIM)
    v = proj[..., 3 * F:3 * F + D_INNER].reshape(b, L, HG_HEADS, HG_VAL_DIM)
    z = proj[..., 3 * F + D_INNER:]
    lb = lb.astype(jnp.float32).reshape(2, HG_HEADS, HG_KEY_DIM)
    log_f = jnp.logaddexp(jnp.log(lb), jnp.log1p(-lb) + jax.nn.log_sigmoid(f_raw))
    k = (1.0 - lb) * jax.nn.sigmoid(-f_raw)
    o_f = hgrn2_chunked(q, k[:, :, 0], log_f[:, :, 0], v)
    o_b = hgrn2_chunked(q[:, ::-1], k[:, ::-1, 1], log_f[:, ::-1, 1], v[:, ::-1])[:, ::-1]
    o = rms_norm_groups((o_f + o_b).reshape(b, L, D_INNER), norm_g, HG_HEADS) * jax.nn.silu(z)
    return (o.astype(x.dtype) @ w_out).astype(x.dtype)


def _normal(key, shape, scale):
    return jax.random.normal(key, shape, jnp.float32) * scale


def _ln_params(p, ks):
    return {p + "ln_g": 1.0 + _normal(next(ks), (D_MODEL,), 0.02),
            p + "ln_b": _normal(next(ks), (D_MODEL,), 0.02)}


def _hyena_params(p, ks):
    H = HY_FILTER_HIDDEN
    d = {
        p + "w_in": _normal(next(ks), (D_MODEL, HY_IN_DIM), D_MODEL ** -0.5),
        p + "conv_w": _normal(next(ks), (HY_SHORT_CONV, 3 * D_INNER), HY_SHORT_CONV ** -0.5),
        p + "conv_b": _normal(next(ks), (3 * D_INNER,), 0.02),
        p + "filt_w1": _normal(next(ks), (HY_EMB_DIM, H), HY_EMB_DIM ** -0.5),
        p + "filt_b1": _normal(next(ks), (H,), 0.02),
        p + "filt_w2": _normal(next(ks), (H, H), H ** -0.5),
        p + "filt_b2": _normal(next(ks), (H,), 0.02),
        p + "filt_w3": _normal(next(ks), (H, H), H ** -0.5),
        p + "filt_b3": _normal(next(ks), (H,), 0.02),
        p + "filt_freq": 1.0 + _normal(next(ks), (H,), 0.02),
        p + "filt_w_out": _normal(next(ks), (H, 2 * D_INNER), H ** -0.5),
        p + "skip": _normal(next(ks), (D_INNER,), 1.0),
        p + "w_out": _normal(next(ks), (D_INNER, D_MODEL), D_INNER ** -0.5 * DEEPNORM_BETA),
    }
    d.update(_ln_params(p, ks))
    return d


def _ssd_params(p, ks):
    log_dt = jax.random.uniform(next(ks), (2, SSD_HEADS), jnp.float32,
                                math.log(SSD_DT_MIN), math.log(SSD_DT_MAX))
    dt = jnp.exp(log_dt)
    d = {
        p + "w_in": _normal(next(ks), (D_MODEL, SSD_IN_DIM), D_MODEL ** -0.5),
        p + "conv_w": _normal(next(ks), (SSD_CONV, SSD_CONV_DIM), SSD_CONV ** -0.5),
        p + "conv_b": _normal(next(ks), (SSD_CONV_DIM,), 0.02),
        p + "dt_bias": dt + jnp.log(-jnp.expm1(-dt)),
        p + "a_log": jnp.log(jax.random.uniform(next(ks), (2, SSD_HEADS), jnp.float32, 1.0, 16.0)),
        p + "d_skip": 1.0 + _normal(next(ks), (SSD_HEADS,), 0.1),
        p + "norm_g": 1.0 + _normal(next(ks), (D_INNER,), 0.02),
        p + "w_out": _normal(next(ks), (D_INNER, D_MODEL), D_INNER ** -0.5 * DEEPNORM_BETA),
    }
    d.update(_ln_params(p, ks))
    return d


def _hgrn2_params(p, ks):
    d = {
        p + "w_in": _normal(next(ks), (D_MODEL, HG_IN_DIM), D_MODEL ** -0.5),
        p + "norm_g": 1.0 + _normal(next(ks), (D_INNER,), 0.02),
        p + "w_out": _normal(next(ks), (D_INNER, D_MODEL), D_INNER ** -0.5 * DEEPNORM_BETA),
    }
    d.update(_ln_params(p, ks))
    return d


def setup_inputs(seed: int = 0) -> dict:
    key = jax.random.key(seed)
    ks = iter(jax.random.split(key, 64))
    inputs = {
        "x": _normal(next(ks), (BATCH, SEQ, D_MODEL), 1.0),
        "hgrn_lower_bounds": _normal(next(ks), (DEPTH, 2 * HG_FDIM), 0.1),
    }
    inputs.update(_hyena_params("l0_", ks))
    inputs.update(_ssd_params("l1_", ks))
    inputs.update(_hgrn2_params("l2_", ks))
    inputs.update(_hyena_params("l3_", ks))
    return inputs


def reference(x, hgrn_lower_bounds,
              l0_w_in, l0_conv_w, l0_conv_b, l0_filt_w1, l0_filt_b1, l0_filt_w2, l0_filt_b2,
              l0_filt_w3, l0_filt_b3, l0_filt_freq, l0_filt_w_out, l0_skip, l0_w_out, l0_ln_g, l0_ln_b,
              l1_w_in, l1_conv_w, l1_conv_b, l1_dt_bias, l1_a_log, l1_d_skip, l1_norm_g, l1_w_out,
              l1_ln_g, l1_ln_b,
              l2_w_in, l2_norm_g, l2_w_out, l2_ln_g, l2_ln_b,
              l3_w_in, l3_conv_w, l3_conv_b, l3_filt_w1, l3_filt_b1, l3_filt_w2, l3_filt_b2,
              l3_filt_w3, l3_filt_b3, l3_filt_freq, l3_filt_w_out, l3_skip, l3_w_out, l3_ln_g, l3_ln_b):
    layer_params = [
        (l0_w_in, l0_conv_w, l0_conv_b, l0_filt_w1, l0_filt_b1, l0_filt_w2, l0_filt_b2,
         l0_filt_w3, l0_filt_b3, l0_filt_freq, l0_filt_w_out, l0_skip, l0_w_out),
        (l1_w_in, l1_conv_w, l1_conv_b, l1_dt_bias, l1_a_log, l1_d_skip, l1_norm_g, l1_w_out),
        (l2_w_in, l2_norm_g, l2_w_out),
        (l3_w_in, l3_conv_w, l3_conv_b, l3_filt_w1, l3_filt_b1, l3_filt_w2, l3_filt_b2,
         l3_filt_w3, l3_filt_b3, l3_filt_freq, l3_filt_w_out, l3_skip, l3_w_out),
    ]
    ln_params = [(l0_ln_g, l0_ln_b), (l1_ln_g, l1_ln_b), (l2_ln_g, l2_ln_b), (l3_ln_g, l3_ln_b)]
    lb_all = jnp.cumsum(jax.nn.softmax(hgrn_lower_bounds.astype(jnp.float32), axis=0), axis=0)
    lb_all = lb_all - lb_all[0]
    h = x
    for i in range(DEPTH):
        kind = i % N_MIXERS
        if kind == 0:
            y = hyena_mixer(h, *layer_params[i])
        elif kind == 1:
            y = ssd_mixer(h, *layer_params[i])
        else:
            w_in, norm_g, w_out = layer_params[i]
            y = hgrn2_mixer(h, w_in, lb_all[i], norm_g, w_out)
        h = layer_norm(DEEPNORM_ALPHA * h + y, *ln_params[i]).astype(x.dtype)
    return h
```

```python
import math
from contextlib import ExitStack
import numpy as np
import concourse.bass as bass
import concourse.mybir as mybir
from concourse.bass_utils import run_bass_kernel_spmd

F32 = mybir.dt.float32
BF16 = mybir.dt.bfloat16
I32 = mybir.dt.int32
AF = mybir.ActivationFunctionType
ALU = mybir.AluOpType
AX = mybir.AxisListType

D = 1024
L = 4096
DI = 2048
DEPTH = 4
ALPHA = (2.0 * DEPTH) ** 0.25
LN_EPS = 1e-5
PAD = 2
NT = L // 128
SAME_ENG_SYNC = True
NDS = 48
DEBUG_SCR = False
DBG_STOP = ""
DBG_LEVEL = 99
DBG_SKIPA = False
DBG_DIRS = 2
DBG_SUB = 99
DBG_NT = 2


class Buf:
    __slots__ = ("t", "w", "r", "name")

    def __init__(self, t, name=""):
        self.t = t
        self.w = None
        self.r = {}
        self.name = name

    def __getitem__(self, idx):
        return V(self, self.t[idx])

    @property
    def v(self):
        return V(self, self.t)


class V:
    __slots__ = ("b", "ap")

    def __init__(self, b, ap):
        self.b = b
        self.ap = ap

    def __getitem__(self, idx):
        return V(self.b, self.ap[idx])

    def re(self, s, **kw):
        return V(self.b, self.ap.rearrange(s, **kw))

    def bc(self, shape):
        return V(self.b, self.ap.broadcast_to(shape))

    def tb(self, shape):
        return V(self.b, self.ap.to_broadcast(shape))

    def pb(self, n):
        return V(self.b, self.ap.partition_broadcast(n))

    def cast(self, dt):
        return V(self.b, self.ap.bitcast(dt))

    def us(self, ax):
        return V(self.b, self.ap.unsqueeze(ax))


class Eng:
    def __init__(self, name, eng, sem):
        self.name = name
        self.eng = eng
        self.sem = sem
        self.n = 0
        self.seen = {}


class Prog:
    def __init__(self):
        nc = bass.Bass("TRN2", target_bir_lowering=False)
        self.nc = nc
        self.es = ExitStack()
        self.E = {}
        for name, e in [("pe", nc.tensor), ("dve", nc.vector), ("act", nc.scalar),
                        ("pool", nc.gpsimd), ("sp", nc.sync)]:
            sem = self.es.enter_context(nc.semaphore("s_" + name))
            self.E[name] = Eng(name, e, sem)
        self.dsem = []
        for i in range(NDS):
            self.dsem.append([self.es.enter_context(nc.semaphore("d%d" % i)), 0])
        self.dnext = 0
        self.phase = None
        self.uid = 0
        self.ninst = 0

    def begin_phase(self):
        self.phase = ExitStack()

    def end_phase(self):
        self.barrier()
        self.phase.close()
        self.phase = None

    def _nm(self, name):
        self.uid += 1
        return "%s_%d" % (name, self.uid)

    def sb(self, name, shape, dt, perm=False):
        st = self.es if perm else self.phase
        t = st.enter_context(self.nc.sbuf_tensor(self._nm(name), list(shape), dt))
        return Buf(t[:], name)

    def ps(self, name, shape, dt=F32, perm=False):
        st = self.es if perm else self.phase
        esz = 4 if dt == F32 else 2
        n = 1
        for x in shape[1:]:
            n *= x
        per_bank = 2048 // esz
        npad = ((n + per_bank - 1) // per_bank) * per_bank
        t = st.enter_context(self.nc.psum_tensor(self._nm(name), [shape[0], npad], dt))
        ap = t[:][:, 0:n]
        if len(shape) == 3:
            ap = ap.rearrange("p (a b) -> p a b", a=shape[1])
        return Buf(ap, name)

    def dram(self, name, shape, dt, kind="Internal"):
        if DEBUG_SCR and kind == "Internal" and name.startswith("s_"):
            kind = "ExternalOutput"
        t = self.nc.dram_tensor(name, list(shape), dt, kind=kind)
        return Buf(t.ap(), name)

    def _wait(self, E, toks):
        need = {}
        for (s, v) in toks:
            if s is E.sem and (E.name == "pe" or not SAME_ENG_SYNC):
                continue
            k = id(s)
            if k not in need or need[k][1] < v:
                need[k] = (s, v)
        for k, (s, v) in need.items():
            if E.seen.get(k, 0) >= v:
                continue
            E.eng.wait_ge(s, v)
            self.ninst += 1
            E.seen[k] = v

    @staticmethod
    def _deps(outs, ins):
        toks = []
        for v in ins:
            if v.b.w is not None:
                toks.append(v.b.w)
        for v in outs:
            if v.b.w is not None:
                toks.append(v.b.w)
            toks.extend(v.b.r.values())
        return toks

    @staticmethod
    def _mark(tok, outs, ins):
        k = id(tok[0])
        for v in ins:
            r = v.b.r
            if k not in r or r[k][1] < tok[1]:
                r[k] = tok
        for v in outs:
            v.b.w = tok
            v.b.r = {}

    def op(self, ename, fn, outs, ins):
        E = self.E[ename]
        self._wait(E, self._deps(outs, ins))
        E.n += 1
        fn(E.eng).then_inc(E.sem, 1)
        self.ninst += 1
        self._mark((E.sem, E.n), outs, ins)

    def dma(self, out, in_, q="sp", slow=False):
        E = self.E[q]
        toks = self._deps([out], [in_])
        i = self.dnext
        self.dnext = (i + 1) % NDS
        sem, cnt = self.dsem[i]
        if cnt > 0:
            toks.append((sem, cnt))
        self._wait(E, toks)
        if slow:
            E.eng.dma_start(out=out.ap, in_=in_.ap, allow_slow_non_contiguous=True).then_inc(sem, 16)
        else:
            E.eng.dma_start(out=out.ap, in_=in_.ap).then_inc(sem, 16)
        self.ninst += 1
        self.dsem[i][1] = cnt + 16
        self._mark((sem, cnt + 16), [out], [in_])

    def barrier(self):
        toks = [(e.sem, e.n) for e in self.E.values() if e.n > 0]
        toks += [(s, c) for s, c in self.dsem if c > 0]
        for E in self.E.values():
            self._wait(E, [t for t in toks if t[0] is not E.sem])

    def mm(self, o, lhsT, rhs, start=True, stop=True):
        self.op("pe", lambda e: e.matmul(o.ap, lhsT=lhsT.ap, rhs=rhs.ap, start=start, stop=stop),
                [o], [lhsT, rhs])

    def tr(self, o, a, ident):
        self.op("pe", lambda e: e.transpose(o.ap, a.ap, ident.ap), [o], [a, ident])

    def act(self, o, a, func, scale=1.0, bias=0.0, accum=None, eng="act"):
        ins = [a]
        outs = [o]
        kw = {}
        if isinstance(scale, V):
            ins.append(scale)
            kw["scale"] = scale.ap
        else:
            kw["scale"] = float(scale)
        if isinstance(bias, V):
            ins.append(bias)
            kw["bias"] = bias.ap
        else:
            kw["bias"] = float(bias)
        if accum is not None:
            outs.append(accum)
            kw["accum_out"] = accum.ap
        self.op("act", lambda e: e.activation(out=o.ap, in_=a.ap, func=func, **kw), outs, ins)

    def tt(self, o, a, b, op, eng="dve"):
        self.op(eng, lambda e: e.tensor_tensor(out=o.ap, in0=a.ap, in1=b.ap, op=op), [o], [a, b])

    def ts(self, o, a, s1, op0, s2=None, op1=None, eng="dve", accum=None):
        ins = [a]
        outs = [o]
        a1 = s1.ap if isinstance(s1, V) else float(s1)
        if isinstance(s1, V):
            ins.append(s1)
        a2 = None
        if s2 is not None:
            a2 = s2.ap if isinstance(s2, V) else float(s2)
            if isinstance(s2, V):
                ins.append(s2)
        kw = {}
        if op1 is not None:
            kw["op1"] = op1
        if accum is not None:
            kw["accum_out"] = accum.ap
            outs.append(accum)
        self.op(eng, lambda e: e.tensor_scalar(out=o.ap, in0=a.ap, scalar1=a1, scalar2=a2, op0=op0, **kw),
                outs, ins)

    def stt(self, o, a, s, b, op0, op1, eng="dve"):
        ins = [a, b]
        sv = s.ap if isinstance(s, V) else float(s)
        if isinstance(s, V):
            ins.append(s)
        self.op(eng, lambda e: e.scalar_tensor_tensor(out=o.ap, in0=a.ap, scalar=sv, in1=b.ap, op0=op0, op1=op1),
                [o], ins)

    def cp(self, o, a, eng="dve"):
        if eng == "act":
            self.op("act", lambda e: e.copy(out=o.ap, in_=a.ap), [o], [a])
        else:
            self.op(eng, lambda e: e.tensor_copy(out=o.ap, in_=a.ap), [o], [a])

    def memset(self, o, val, eng="pool"):
        self.op(eng, lambda e: e.memset(o.ap, val), [o], [])


def _ident_bf16():
    import ml_dtypes
    return np.eye(128, dtype=np.float32).astype(ml_dtypes.bfloat16)


class Ctx:
    pass


def setup_common(P, C):
    C.hT = P.sb("hT", [128, D // 128, L + 2 * PAD], BF16, perm=True)
    C.ident = P.sb("ident", [128, 128], BF16, perm=True)
    C.identf = P.sb("identf", [128, 128], F32, perm=True)
    C.ones_bf = P.sb("ones_bf", [128, 128], BF16, perm=True)
    C.ones_f = P.sb("ones_f", [128, 128], F32, perm=True)
    P.dma(C.ident.v, C.d_ident.v)
    P.dma(C.identf.v, C.d_identf.v)
    P.memset(C.ones_bf.v, 1.0)
    P.memset(C.ones_f.v, 1.0)
    P.memset(C.hT.v, 0.0)
    C.eps_col = P.sb("eps_col", [128, 1], F32, perm=True)
    P.memset(C.eps_col.v, LN_EPS)


def load_x_to_hT(P, C):
    P.begin_phase()
    xt = [P.sb("xt%d" % i, [128, D], F32) for i in range(2)]
    xb = [P.sb("xb%d" % i, [128, D], BF16) for i in range(2)]
    pt = [P.ps("ptr%d" % i, [128, D], BF16) for i in range(2)]
    for tt in range(NT):
        s = tt % 2
        P.dma(xt[s].v, C.x[tt * 128:(tt + 1) * 128, :])
        P.dma(C.h[tt * 128:(tt + 1) * 128, :], xt[s].v)
        P.cp(xb[s].v, xt[s].v, eng="act")
        for kc in range(D // 128):
            P.tr(pt[s][:, kc * 128:(kc + 1) * 128], xb[s][:, kc * 128:(kc + 1) * 128], C.ident.v)
        P.cp(C.hT[:, :, PAD + tt * 128: PAD + (tt + 1) * 128],
             pt[s].v.re("p (k t) -> p k t", k=D // 128))
    P.end_phase()


def row_bcast(P, dst, src_buf, sl):
    P.dma(dst, V(src_buf, src_buf.t[sl].partition_broadcast(128)))


def prep_w_chunk(P, Wt, w_d, c0, n, stage, convw_d=None, cc0=0, cwb=None, ntaps=0):
    nk = w_d.t.shape[0] // 128
    P.dma(stage[:, :nk, :n], w_d[:, c0:c0 + n].re("(k p) c -> p k c", p=128))
    if ntaps == 0:
        P.cp(Wt[0][:, :nk, :n], stage[:, :nk, :n], eng="pool")
        return
    row_bcast(P, cwb[:, :ntaps, :n], convw_d, (slice(None), slice(cc0, cc0 + n)))
    for j in range(ntaps):
        P.tt(Wt[j][:, :nk, :n], stage[:, :nk, :n],
             cwb[:, j:j + 1, :n].bc([128, nk, n]), ALU.mult, eng=("pool" if j % 2 else "dve"))


def mm_proj_tm(P, ps, C, Wts, shifts, tt, n, bias=None):
    nk = D // 128
    tot = len(Wts) * nk + (1 if bias is not None else 0)
    i = 0
    for Wt, sh in zip(Wts, shifts):
        for kc in range(nk):
            t0 = PAD + tt * 128 + sh
            P.mm(ps, C.hT[:, kc, t0:t0 + 128], Wt[:, kc, :n], start=(i == 0), stop=(i == tot - 1))
            i += 1
    if bias is not None:
        P.mm(ps, C.ones_bf[0:1, :], bias, start=False, stop=True)


class OutProj:
    def __init__(self, P, C, wout_d, lng_d, lnb_d, last=False):
        self.P, self.C, self.last = P, C, last
        self.w = P.sb("wout", [128, DI // 128, D], BF16)
        st = P.sb("wost", [128, 4, D], F32)
        for q in range(4):
            P.dma(st.v, wout_d[q * 512:(q + 1) * 512, :].re("(k p) c -> p k c", p=128))
            P.cp(self.w[:, q * 4:(q + 1) * 4, :], st.v, eng=("pool" if q % 2 else "dve"))
        self.g = P.sb("lng", [128, D], F32)
        self.b = P.sb("lnb", [128, D], F32)
        row_bcast(P, self.g.v, lng_d, slice(None))
        row_bcast(P, self.b.v, lnb_d, slice(None))
        self.yT = P.sb("yT", [128, DI // 128, 128], BF16)
        self.ho = P.sb("ho", [128, D], F32)
        self.r = P.sb("r", [128, D], F32)
        self.hn = P.sb("hn", [128, D], F32)
        self.hb = P.sb("hb", [128, D], BF16)
        self.st6 = P.sb("st6", [128, 2, 6], F32)
        self.mv = P.sb("mv", [128, 2], F32)
        self.rstd = P.sb("rstd", [128, 1], F32)
        self.pT = P.ps("opT", [128, DI], BF16)
        self.po = P.ps("opo", [128, D], F32)
        self.pH = P.ps("opH", [128, D], BF16)

    def tile(self, tt, y):
        P, C = self.P, self.C
        for c in range(DI // 128):
            P.tr(self.pT[:, c * 128:(c + 1) * 128], y[:, c * 128:(c + 1) * 128], C.ident.v)
        P.cp(self.yT.v, self.pT.v.re("p (c t) -> p c t", c=DI // 128), eng="act")
        P.dma(self.ho.v, C.h[tt * 128:(tt + 1) * 128, :])
        for dh in range(2):
            for c in range(DI // 128):
                P.mm(self.po[:, dh * 512:(dh + 1) * 512], self.yT[:, c, :],
                     self.w[:, c, dh * 512:(dh + 1) * 512], start=(c == 0), stop=(c == DI // 128 - 1))
        P.stt(self.r.v, self.ho.v, ALPHA, self.po.v, ALU.mult, ALU.add)
        for q in range(2):
            P.op("dve", lambda e, q=q: e.bn_stats(out=self.st6[:, q, :].ap, in_=self.r[:, q * 512:(q + 1) * 512].ap),
                 [self.st6.v], [self.r.v])
        P.op("dve", lambda e: e.bn_aggr(out=self.mv.v.ap, in_=self.st6.v.re("p a b -> p (a b)").ap),
             [self.mv.v], [self.st6.v])
        P.act(self.rstd.v, self.mv[:, 1:2], AF.Sqrt, scale=1.0, bias=C.eps_col[:, 0:1])
        P.op("dve", lambda e: e.reciprocal(out=self.rstd.v.ap, in_=self.rstd.v.ap), [self.rstd.v], [self.rstd.v])
        P.ts(self.hn.v, self.r.v, self.mv[:, 0:1], ALU.subtract, self.rstd[:, 0:1], ALU.mult)
        P.tt(self.hn.v, self.hn.v, self.g.v, ALU.mult, eng="pool")
        P.tt(self.hn.v, self.hn.v, self.b.v, ALU.add, eng="pool")
        if self.last:
            P.dma(C.out[tt * 128:(tt + 1) * 128, :], self.hn.v)
            return
        P.dma(C.h[tt * 128:(tt + 1) * 128, :], self.hn.v)
        P.cp(self.hb.v, self.hn.v, eng="act")
        for kc in range(D // 128):
            P.tr(self.pH[:, kc * 128:(kc + 1) * 128], self.hb[:, kc * 128:(kc + 1) * 128], C.ident.v)
        P.cp(C.hT[:, :, PAD + tt * 128: PAD + (tt + 1) * 128],
             self.pH.v.re("p (k t) -> p k t", k=D // 128))


def mm_proj_fm(P, ps, C, Wt, c0, t0, n, shift=0, start=True, stop=True):
    nk = D // 128
    for kc in range(nk):
        a = PAD + t0 + shift
        P.mm(ps, Wt[:, kc, c0:c0 + 128], C.hT[:, kc, a:a + n],
             start=(start and kc == 0), stop=(stop and kc == nk - 1))


HG_H = 16


def hgrn2_consts():
    s = np.arange(128)[:, None]
    t = np.arange(128)[None, :]
    same = (s // 64) == (t // 64)
    mf = ((s <= t) & same).astype(np.float32)
    mb = ((s >= t) & same).astype(np.float32)
    rm = np.ones((128, 512), np.float32)
    rm[:, ::64] = 0.0
    return {"hg_mf": mf, "hg_mb": mb, "hg_rm": rm}


def hgrn2_layer(P, C, li, w_in, lb_raw, norm_g, w_out, ln_g, ln_b, last):
    S = C.scr
    qg = [S["qgf"], S["qgb"]]
    kg = [S["kgf"], S["kgb"]]
    P.begin_phase()
    lbr = P.sb("lbr", [32, 4, 128], F32)
    P.dma(lbr.v, lb_raw.v.re("l (g k) -> g l k", k=128))
    lbT = P.sb("lbT", [128, 4, 32], F32)
    pl = P.ps("pl", [128, 4, 32], F32)
    for l in range(4):
        P.tr(pl[:, l, :], lbr[:, l, :], C.identf[0:32, 0:32])
    P.act(lbT.v, pl.v, AF.Exp)
    den = P.sb("den", [128, 32], F32)
    num = P.sb("num", [128, 32], F32)
    P.tt(den.v, lbT[:, 0, :], lbT[:, 1, :], ALU.add)
    P.tt(den.v, den.v, lbT[:, 2, :], ALU.add)
    P.tt(den.v, den.v, lbT[:, 3, :], ALU.add)
    P.memset(num.v, 0.0, eng="dve")
    for j in range(1, li + 1):
        P.tt(num.v, num.v, lbT[:, j, :], ALU.add)
    P.op("dve", lambda e: e.reciprocal(out=den.v.ap, in_=den.v.ap), [den.v], [den.v])
    lb = P.sb("lb", [128, 32], F32)
    ln1mlb = P.sb("ln1mlb", [128, 32], F32)
    P.tt(lb.v, num.v, den.v, ALU.mult)
    P.ts(ln1mlb.v, lb.v, -1.0, ALU.mult, 1.0, ALU.add)
    P.act(ln1mlb.v, ln1mlb.v, AF.Ln)
    etot = C.hg_etot
    rm = P.sb("rm", [128, 512], F32)
    P.dma(rm.v, C.consts["hg_rm"].v)
    stage = P.sb("stA", [128, 8, 384], F32)
    Wt = [P.sb("WtA%d" % i, [128, 8, 384], BF16) for i in range(2)]
    pq = [P.ps("pq%d" % i, [128, 512], F32) for i in range(2)]
    pf = [[P.ps("pf%d_%d" % (d, i), [128, 512], F32) for i in range(2)] for d in range(2)]
    nb = 2
    e_ = [P.sb("e%d" % i, [128, 512], F32) for i in range(nb)]
    l1 = [P.sb("l1%d" % i, [128, 512], F32) for i in range(nb)]
    l2 = [P.sb("l2%d" % i, [128, 512], F32) for i in range(nb)]
    lf = [P.sb("lf%d" % i, [128, 512], F32) for i in range(nb)]
    pp = [P.sb("pp%d" % i, [128, 512], F32) for i in range(nb)]
    lk = [P.sb("lk%d" % i, [128, 512], F32) for i in range(nb)]
    eg = [P.sb("eg%d" % i, [128, 512], F32) for i in range(nb)]
    qo = [P.sb("qo%d" % i, [128, 512], BF16) for i in range(nb)]
    ko = [P.sb("ko%d" % i, [128, 512], BF16) for i in range(nb)]
    tot8 = [P.sb("tot8%d" % i, [128, 8], F32) for i in range(nb)]
    it = 0
    for hd in range(HG_H):
        W = Wt[hd % 2]
        for j, c0 in enumerate([hd * 128, 2048 + hd * 128, 4096 + hd * 128]):
            P.dma(stage[:, :, j * 128:(j + 1) * 128], w_in[:, c0:c0 + 128].re("(k p) c -> p k c", p=128))
        P.cp(W.v, stage.v, eng="pool")
        for tc in range(L // 512):
            t0 = tc * 512
            s = tc % 2
            mm_proj_fm(P, pq[s].v, C, W, 0, t0, 512)
            for d in range(2):
                mm_proj_fm(P, pf[d][s].v, C, W, 128 * (1 + d), t0, 512)
            for d in range(2):
                b = it % nb
                it += 1
                col = d * HG_H + hd
                ff = pf[d][s]
                P.act(e_[b].v, ff.v, AF.Exp, scale=-1.0)
                P.act(l1[b].v, e_[b].v, AF.Ln, scale=lb[:, col:col + 1], bias=1.0)
                P.act(l2[b].v, e_[b].v, AF.Ln, scale=1.0, bias=1.0)
                P.tt(lf[b].v, l1[b].v, l2[b].v, ALU.subtract)
                P.op("dve", lambda e, b=b: e.tensor_tensor_scan(out=pp[b].v.ap, data0=rm.v.ap, data1=lf[b].v.ap,
                                                               initial=0.0, op0=ALU.mult, op1=ALU.add),
                     [pp[b].v], [rm.v, lf[b].v])
                P.stt(lk[b].v, ff.v, -1.0, l2[b].v, ALU.mult, ALU.subtract)
                if d == 1:
                    P.tt(lf[b].v, lf[b].v, pp[b].v, ALU.subtract, eng="pool")
                    P.cp(tot8[b].v, pp[b].v.re("p (c t) -> p c t", t=64)[:, :, 63], eng="pool")
                    P.tt(pp[b].v.re("p (c t) -> p c t", t=64), lf[b].v.re("p (c t) -> p c t", t=64),
                         tot8[b].v.us(2).bc([128, 8, 64]), ALU.add)
                G = pp[b]
                P.act(eg[b].v, G.v, AF.Exp)
                ev = eg[b].v.re("p (c t) -> p c t", t=64)
                P.cp(etot[:, d, hd, tc * 8:(tc + 1) * 8], (ev[:, :, 63] if d == 0 else ev[:, :, 0]), eng="pool")
                P.tt(qo[b].v, pq[s].v, eg[b].v, ALU.mult)
                P.tt(lk[b].v, lk[b].v, G.v, ALU.subtract, eng="pool")
                P.act(ko[b].v, lk[b].v, AF.Exp, bias=ln1mlb[:, col:col + 1])
                P.dma(qg[d][hd, :, t0:t0 + 512], qo[b].v)
                P.dma(kg[d][hd, :, t0:t0 + 512], ko[b].v)
    P.end_phase()
    P.begin_phase()
    stage = P.sb("stB", [128, 8, 512], F32)
    WtB = [P.sb("WtB%d" % i, [128, 8, 512], BF16) for i in range(2)]
    pv = [P.ps("pv%d" % i, [128, 512], F32) for i in range(2)]
    vo = [P.sb("vo%d" % i, [128, 512], BF16) for i in range(2)]
    zo = [P.sb("zo%d" % i, [128, 512], F32) for i in range(2)]
    it = 0
    for isz in range(2):
        for c in range(4):
            W = WtB[c % 2]
            prep_w_chunk(P, [W], w_in, 6144 + isz * 2048 + c * 512, 512, stage)
            for tt in range(NT):
                s = it % 2
                it += 1
                mm_proj_tm(P, pv[s].v, C, [W], [0], tt, 512)
                if isz == 0:
                    P.cp(vo[s].v, pv[s].v, eng="act")
                    P.dma(S["vtm"][tt * 128:(tt + 1) * 128, c * 512:(c + 1) * 512], vo[s].v)
                else:
                    P.act(zo[s].v, pv[s].v, AF.Silu)
                    P.dma(S["zg"][tt * 128:(tt + 1) * 128, c * 512:(c + 1) * 512], zo[s].v)
    P.end_phase()
    P.begin_phase()
    mask = [P.sb("mk%d" % i, [128, 128], F32) for i in range(2)]
    P.dma(mask[0].v, C.consts["hg_mf"].v)
    P.dma(mask[1].v, C.consts["hg_mb"].v)
    gb = P.sb("ngb", [128, DI], F32)
    row_bcast(P, gb.v, norm_g, slice(None))
    Sf = P.sb("Sf", [128, HG_H, 128], F32)
    Sb = P.sb("Sb", [128, HG_H, 128], BF16)
    qT = [P.sb("qT%d" % i, [128, HG_H, 128], BF16) for i in range(2)]
    kT = [P.sb("kT%d" % i, [128, HG_H, 128], BF16) for i in range(2)]
    vt = [P.sb("vt%d" % i, [128, DI], BF16) for i in range(2)]
    ot = [P.sb("ot%d" % i, [128, DI], F32) for i in range(2)]
    of_t = [P.sb("oft%d" % i, [128, DI], F32) for i in range(2)]
    zt = [P.sb("zt%d" % i, [128, DI], F32) for i in range(2)]
    sq = P.sb("sq", [128, DI], F32)
    ss = P.sb("ss", [128, HG_H], F32)
    yb = [P.sb("yb%d" % i, [128, DI], BF16) for i in range(2)]
    GH = 4
    pA = [P.ps("pA%d" % i, [128, GH, 128], F32) for i in range(2)]
    pT = [P.ps("pT%d" % i, [128, GH, 128], BF16) for i in range(2)]
    pO = [P.ps("pO%d" % i, [128, GH, 128], F32) for i in range(2)]
    pU = [P.ps("pU%d" % i, [128, GH, 128], F32) for i in range(2)]
    am = [P.sb("am%d" % i, [128, GH, 128], BF16) for i in range(2)]
    ktm = [P.sb("ktm%d" % i, [128, GH, 128], BF16) for i in range(2)]
    gi = 0
    for d in range(2):
        P.memset(Sf.v, 0.0, eng="dve")
        P.memset(Sb.v, 0.0, eng="pool")
        order = list(range(NT)) if d == 0 else list(range(NT - 1, -1, -1))
        for n_, tt in enumerate(order):
            s = n_ % 2
            tsl = slice(tt * 128, (tt + 1) * 128)
            P.dma(qT[s].v, qg[d][:, :, tsl].re("h k t -> k h t"))
            P.dma(kT[s].v, kg[d][:, :, tsl].re("h k t -> k h t"))
            P.dma(vt[s].v, S["vtm"][tsl, :])
            if d == 1:
                P.dma(of_t[s].v, S["of"][tsl, :])
                P.dma(zt[s].v, S["zg"][tsl, :])
            chunks = [0, 1] if d == 0 else [1, 0]
            for g in range(HG_H // GH):
                p = gi % 2
                gi += 1
                hs = range(g * GH, (g + 1) * GH)
                for i, hd in enumerate(hs):
                    P.mm(pA[p][:, i, :], kT[s][:, hd, :], qT[s][:, hd, :])
                    P.tr(pT[p][:, i, :], kT[s][:, hd, :], C.ident.v)
                P.tt(am[p].v, pA[p].v, mask[d].v.us(1).bc([128, GH, 128]), ALU.mult)
                P.cp(ktm[p].v, pT[p].v, eng="act")
                for ci, ch in enumerate(chunks):
                    rs = slice(ch * 64, (ch + 1) * 64)
                    cidx = tt * 2 + ch
                    for i, hd in enumerate(hs):
                        vs = vt[s][:, hd * 128:(hd + 1) * 128]
                        P.mm(pO[p][rs, i, :], am[p][:, i, rs], vs, start=True, stop=False)
                        P.mm(pO[p][rs, i, :], qT[s][:, hd, rs], Sb[:, hd, :], start=False, stop=True)
                        P.mm(pU[p][:, i, :], ktm[p][rs, i, :], vt[s][rs, hd * 128:(hd + 1) * 128])
                    for i, hd in enumerate(hs):
                        ec = etot[:, d, hd, cidx:cidx + 1]
                        P.ts(Sf[:, hd, :], Sf[:, hd, :], ec, ALU.mult)
                        P.stt(Sf[:, hd, :], pU[p][:, i, :], ec, Sf[:, hd, :], ALU.mult, ALU.add)
                        P.cp(Sb[:, hd, :], Sf[:, hd, :], eng="pool")
                osl = ot[s][:, g * GH * 128:(g + 1) * GH * 128]
                if d == 0:
                    P.cp(osl, pO[p].v.re("p g v -> p (g v)"), eng="act")
                else:
                    P.tt(osl, pO[p].v.re("p g v -> p (g v)"), of_t[s][:, g * GH * 128:(g + 1) * GH * 128], ALU.add)
            if d == 0:
                P.dma(S["of"][tsl, :], ot[s].v)
            else:
                o3 = ot[s].v.re("p (h v) -> p h v", v=128)
                P.tt(sq.v, ot[s].v, ot[s].v, ALU.mult, eng="pool")
                P.op("dve", lambda e: e.tensor_reduce(out=ss.v.ap, in_=sq.v.re("p (h v) -> p h v", v=128).ap,
                                                     axis=AX.X, op=ALU.add), [ss.v], [sq.v])
                P.act(ss.v, ss.v, AF.Sqrt, scale=1.0 / 128, bias=C.eps_col[:, 0:1])
                P.op("dve", lambda e: e.reciprocal(out=ss.v.ap, in_=ss.v.ap), [ss.v], [ss.v])
                P.tt(o3, o3, ss.v.us(2).bc([128, HG_H, 128]), ALU.mult)
                P.tt(zt[s].v, zt[s].v, gb.v, ALU.mult, eng="pool")
                P.tt(yb[s].v, ot[s].v, zt[s].v, ALU.mult)
                P.dma(S["y"][tsl, :], yb[s].v)
    P.end_phase()
    outproj_phase(P, C, w_out, ln_g, ln_b, last)


def outproj_phase(P, C, w_out, ln_g, ln_b, last):
    P.begin_phase()
    op = OutProj(P, C, w_out, ln_g, ln_b, last=last)
    yb = [P.sb("ypb%d" % i, [128, DI], BF16) for i in range(2)]
    for tt in range(NT):
        P.dma(yb[tt % 2].v, C.scr["y"][tt * 128:(tt + 1) * 128, :])
        op.tile(tt, yb[tt % 2].v)
    P.end_phase()


LAYER_KIND = ["hyena", "ssd", "hgrn2", "hyena"]
PARAMS = {
    "hyena": ["w_in", "conv_w", "conv_b", "filt_w1", "filt_b1", "filt_w2", "filt_b2", "filt_w3", "filt_b3",
              "filt_freq", "filt_w_out", "skip", "w_out", "ln_g", "ln_b"],
    "ssd": ["w_in", "conv_w", "conv_b", "dt_bias", "a_log", "d_skip", "norm_g", "w_out", "ln_g", "ln_b"],
    "hgrn2": ["w_in", "norm_g", "w_out", "ln_g", "ln_b"],
}
SHAPES = {
    "hyena": {"w_in": [D, 8192], "conv_w": [3, 6144], "conv_b": [6144], "filt_w1": [33, 64], "filt_b1": [64],
              "filt_w2": [64, 64], "filt_b2": [64], "filt_w3": [64, 64], "filt_b3": [64], "filt_freq": [64],
              "filt_w_out": [64, 4096], "skip": [DI], "w_out": [DI, D], "ln_g": [D], "ln_b": [D]},
    "ssd": {"w_in": [D, 5184], "conv_w": [5, 3072], "conv_b": [3072], "dt_bias": [2, 32], "a_log": [2, 32],
            "d_skip": [32], "norm_g": [DI], "w_out": [DI, D], "ln_g": [D], "ln_b": [D]},
    "hgrn2": {"w_in": [D, 10240], "norm_g": [DI], "w_out": [DI, D], "ln_g": [D], "ln_b": [D]},
}


def all_consts():
    c = {"ident": _ident_bf16(), "identf": np.eye(128, dtype=np.float32)}
    c.update(hgrn2_consts())
    c.update(ssd_consts())
    c.update(hyena_consts())
    return c


def build_program(layers):
    P = Prog()
    C = Ctx()
    C.x = P.dram("x", [L, D], F32, kind="ExternalInput")
    C.out = P.dram("out", [L, D], F32, kind="ExternalOutput")
    C.h = P.dram("h_scr", [L, D], F32)
    C.consts = {}
    cst = all_consts()
    for k, v in cst.items():
        dt = BF16 if v.dtype != np.float32 else F32
        C.consts[k] = P.dram("c_" + k, list(v.shape), dt, kind="ExternalInput")
    C.d_ident = C.consts["ident"]
    C.d_identf = C.consts["identf"]
    C.lbraw = P.dram("hgrn_lower_bounds", [4, 4096], F32, kind="ExternalInput")
    C.prm = {}
    for li in layers:
        kind = LAYER_KIND[li]
        for nme in PARAMS[kind]:
            key = "l%d_%s" % (li, nme)
            C.prm[key] = P.dram(key, SHAPES[kind][nme], F32, kind="ExternalInput")
    S = {}
    for nme in ["qgf", "qgb", "kgf", "kgb"]:
        S[nme] = P.dram("s_" + nme, [HG_H, 128, L], BF16)
    S["vtm"] = P.dram("s_vtm", [L, DI], BF16)
    S["zg"] = P.dram("s_zg", [L, DI], F32)
    S["of"] = P.dram("s_of", [L, DI], F32)
    S["y"] = P.dram("s_y", [L, DI], BF16)
    S["xs"] = P.dram("s_xs", [L, DI], F32)
    S["btm"] = P.dram("s_btm", [L, 512], BF16)
    S["bT"] = P.dram("s_bT", [4, 128, L], BF16)
    S["cT"] = P.dram("s_cT", [4, 128, L], BF16)
    S["dt"] = P.dram("s_dt", [L, 64], F32)
    S["Zs"] = P.dram("s_Zs", [128, 64, DI], BF16)
    S["Zv"] = P.dram("s_Zv", [128, 64, DI], BF16)
    S["Kf"] = P.dram("s_Kf", [NK2, 128, DI], F32)
    S["adt"] = P.dram("s_adt", [L, 64], F32)
    C.scr = S
    setup_common(P, C)
    C.hg_etot = P.sb("etot", [128, 2, HG_H, 64], F32, perm=True)
    load_x_to_hT(P, C)
    for n_, li in enumerate(layers):
        kind = LAYER_KIND[li]
        last = (n_ == len(layers) - 1)
        pr = lambda nme: C.prm["l%d_%s" % (li, nme)]
        if kind == "hgrn2":
            hgrn2_layer(P, C, li, pr("w_in"), C.lbraw, pr("norm_g"), pr("w_out"), pr("ln_g"), pr("ln_b"), last)
        elif kind == "ssd":
            ssd_layer(P, C, li, pr("w_in"), pr("conv_w"), pr("conv_b"), pr("dt_bias"), pr("a_log"), pr("d_skip"),
                      pr("norm_g"), pr("w_out"), pr("ln_g"), pr("ln_b"), last)
        else:
            hyena_layer(P, C, li, {nme: pr(nme) for nme in PARAMS["hyena"]}, last)
    P.barrier()
    return P, C


def in_map_for(inputs, b, layers):
    m = {"x": np.ascontiguousarray(inputs["x"][b]), "hgrn_lower_bounds": inputs["hgrn_lower_bounds"]}
    for k, v in all_consts().items():
        m["c_" + k] = v
    for li in layers:
        for nme in PARAMS[LAYER_KIND[li]]:
            key = "l%d_%s" % (li, nme)
            m[key] = inputs[key]
    return m


def ssd_consts():
    r = np.arange(128)[:, None]
    t = np.arange(128)[None, :]
    return {"sd_mf": (r <= t).astype(np.float32), "sd_mb": (r >= t).astype(np.float32)}


def ssd_layer(P, C, li, w_in, conv_w, conv_b, dt_bias, a_log, d_skip, norm_g, w_out, ln_g, ln_b, last):
    S = C.scr
    NTAP = 5
    shifts = [-2, -1, 0, 1, 2]
    P.begin_phase()
    stage = P.sb("sstA", [128, 8, 512], F32)
    cwb = P.sb("cwb", [128, NTAP, 512], F32)
    Wz = [P.sb("Wz%d" % i, [128, 8, 512], BF16) for i in range(2)]
    Wt = [P.sb("Wc%d" % j, [128, 8, 512], BF16) for j in range(NTAP)]
    cbf = P.sb("cbf", [1, 3072], F32)
    cbb = P.sb("cbb", [1, 3072], BF16)
    P.dma(cbf.v, conv_b.v.us(0))
    P.cp(cbb.v, cbf.v)
    pz = [P.ps("pz%d" % i, [128, 512], F32) for i in range(2)]
    zo = [P.sb("szo%d" % i, [128, 512], F32) for i in range(2)]
    bo = [P.sb("sbo%d" % i, [128, 512], BF16) for i in range(2)]
    it = 0
    for c in range(0 if DBG_SKIPA else 4):
        W = Wz[c % 2]
        prep_w_chunk(P, [W], w_in, c * 512, 512, stage)
        for tt in range(NT):
            s = it % 2
            it += 1
            mm_proj_tm(P, pz[s].v, C, [W], [0], tt, 512)
            P.act(zo[s].v, pz[s].v, AF.Silu)
            P.dma(S["zg"][tt * 128:(tt + 1) * 128, c * 512:(c + 1) * 512], zo[s].v)
    for c in range(0 if DBG_SKIPA else 5):
        prep_w_chunk(P, Wt, w_in, 2048 + c * 512, 512, stage, convw_d=conv_w, cc0=c * 512, cwb=cwb, ntaps=NTAP)
        for tt in range(NT):
            s = it % 2
            it += 1
            mm_proj_tm(P, pz[s].v, C, Wt, shifts, tt, 512, bias=cbb[0:1, c * 512:(c + 1) * 512])
            if c < 4:
                P.act(zo[s].v, pz[s].v, AF.Silu)
                P.dma(S["xs"][tt * 128:(tt + 1) * 128, c * 512:(c + 1) * 512], zo[s].v)
            else:
                P.act(bo[s].v, pz[s].v, AF.Silu)
                P.dma(S["btm"][tt * 128:(tt + 1) * 128, :], bo[s].v)
    bcol = P.sb("bcol", [128, 8], F32)
    P.dma(bcol.v, conv_b[2048:3072].re("(g n) -> n g", n=128), slow=True)
    for bc_ in range(0 if DBG_SKIPA else 2):
        prep_w_chunk(P, Wt, w_in, 4096 + bc_ * 512, 512, stage, convw_d=conv_w, cc0=2048 + bc_ * 512, cwb=cwb,
                     ntaps=NTAP)
        dst = S["bT"] if bc_ == 0 else S["cT"]
        for g in range(4):
            for tc in range(L // 512):
                s = it % 2
                it += 1
                for j in range(NTAP):
                    mm_proj_fm(P, pz[s].v, C, Wt[j], g * 128, tc * 512, 512, shift=shifts[j],
                               start=(j == 0), stop=(j == NTAP - 1))
                P.act(bo[s].v, pz[s].v, AF.Silu, bias=bcol[:, bc_ * 4 + g: bc_ * 4 + g + 1])
                P.dma(dst[g, :, tc * 512:(tc + 1) * 512], bo[s].v)
    Wd = Wz[0]
    prep_w_chunk(P, [Wd], w_in, 5120, 64, stage)
    dtb = P.sb("dtb", [128, 64], F32)
    arow = P.sb("arow", [128, 64], F32)
    P.dma(dtb.v, V(dt_bias, dt_bias.t.rearrange("a b -> (a b)").partition_broadcast(128)))
    P.dma(arow.v, V(a_log, a_log.t.rearrange("a b -> (a b)").partition_broadcast(128)))
    P.act(arow.v, arow.v, AF.Exp)
    dto = [P.sb("dto%d" % i, [128, 64], F32) for i in range(2)]
    ado = [P.sb("ado%d" % i, [128, 64], F32) for i in range(2)]
    for tt in range(NT):
        s = tt % 2
        mm_proj_tm(P, pz[s][:, 0:64], C, [Wd], [0], tt, 64)
        P.tt(dto[s].v, pz[s][:, 0:64], dtb.v, ALU.add)
        P.act(dto[s].v, dto[s].v, AF.Exp)
        P.act(dto[s].v, dto[s].v, AF.Ln, bias=1.0)
        P.stt(ado[s].v, dto[s].v, -1.0, arow.v, ALU.mult, ALU.mult)
        P.dma(S["dt"][tt * 128:(tt + 1) * 128, :], dto[s].v)
        P.dma(S["adt"][tt * 128:(tt + 1) * 128, :], ado[s].v)
    P.end_phase()
    if DBG_STOP == "ssdA":
        return
    P.begin_phase()
    M1 = [P.sb("M1%d" % i, [128, 128], F32) for i in range(2)]
    P.dma(M1[0].v, C.consts["sd_mf"].v)
    P.dma(M1[1].v, C.consts["sd_mb"].v)
    gb = P.sb("sgb", [128, DI], F32)
    row_bcast(P, gb.v, norm_g, slice(None))
    dsk = P.sb("dsk", [128, 32], F32)
    row_bcast(P, dsk.v, d_skip, slice(None))
    Sp = P.sb("Sp", [128, 4, 512], F32)
    Spb = P.sb("Spb", [128, 4, 512], BF16)
    xs = [P.sb("xs%d" % i, [128, DI], F32) for i in range(2)]
    dtt = [P.sb("dtt%d" % i, [128, 64], F32) for i in range(2)]
    adt = [P.sb("adt%d" % i, [128, 64], F32) for i in range(2)]
    btm = [P.sb("btm%d" % i, [128, 512], BF16) for i in range(2)]
    bT = [P.sb("bT%d" % i, [128, 4, 128], BF16) for i in range(2)]
    cT = [P.sb("cT%d" % i, [128, 4, 128], BF16) for i in range(2)]
    yft = P.sb("yft", [128, DI], F32)
    zt = P.sb("szt", [128, DI], F32)
    ct = P.sb("ct", [128, 64], F32)
    ncs = P.sb("ncs", [128, 32], F32)
    ecs = P.sb("ecs", [128, 32], F32)
    dst_ = P.sb("dst", [128, 32], F32)
    etot = P.sb("setot", [128, 32], F32)
    xdt = P.sb("xdt", [128, DI], BF16)
    xdtd = P.sb("xdtd", [128, DI], BF16)
    CBm = P.sb("CBm", [128, 128], F32)
    X = P.sb("X", [128, 8, 128], F32)
    X2 = P.sb("X2", [128, 8, 128], F32)
    Mh = P.sb("Mh", [128, 8, 128], BF16)
    tmp = P.sb("stmp", [128, 512], F32)
    yt = P.sb("syt", [128, DI], F32)
    ss = P.sb("sss", [128, 4], F32)
    sq = P.sb("ssq", [128, DI], F32)
    yb = [P.sb("syb%d" % i, [128, DI], BF16) for i in range(2)]
    p_ct = P.ps("p_ct", [128, 64], F32)
    p_cb = P.ps("p_cb", [128, 128], F32)
    p_row = P.ps("p_row", [128, 8, 128], F32)
    p_y = P.ps("p_y", [128, 512], F32)
    p_st = P.ps("p_st", [128, 512], F32)
    p_yo = P.ps("p_yo", [128, 512], F32)
    for d in range(DBG_DIRS):
        P.memset(Sp.v, 0.0, eng="dve")
        P.memset(Spb.v, 0.0, eng="pool")
        order = list(range(NT)) if d == 0 else list(range(NT - 1, -1, -1))
        if DBG_LEVEL < 99:
            order = order[:DBG_NT]
        hc = slice(d * 32, (d + 1) * 32)
        for n_, tt in enumerate(order):
            s = n_ % 2
            tsl = slice(tt * 128, (tt + 1) * 128)
            P.dma(xs[s].v, S["xs"][tsl, :])
            P.dma(dtt[s].v, S["dt"][tsl, :])
            P.dma(adt[s].v, S["adt"][tsl, :])
            P.dma(btm[s].v, S["btm"][tsl, :])
            P.dma(bT[s].v, S["bT"][:, :, tsl].re("g n t -> n g t"))
            P.dma(cT[s].v, S["cT"][:, :, tsl].re("g n t -> n g t"))
            if d == 1:
                P.dma(yft.v, S["of"][tsl, :])
                P.dma(zt.v, S["zg"][tsl, :])
            if DBG_LEVEL < 1:
                continue
            P.mm(p_ct[:, 0:32], M1[d].v, adt[s][:, hc])
            P.mm(p_ct[:, 32:64], C.ones_f.v, adt[s][:, hc])
            P.cp(ct.v, p_ct.v)
            P.ts(ncs.v, ct[:, 0:32], -1.0, ALU.mult)
            P.act(ecs.v, ct[:, 0:32], AF.Exp)
            P.act(etot.v, ct[:, 32:64], AF.Exp)
            P.tt(dst_.v, ct[:, 32:64], ct[:, 0:32], ALU.subtract)
            P.act(dst_.v, dst_.v, AF.Exp)
            P.tt(dst_.v, dst_.v, dtt[s][:, hc], ALU.mult)
            x3 = xs[s].v.re("p (h q) -> p h q", q=64)
            if DBG_LEVEL < 2:
                continue
            P.tt(xdt.v.re("p (h q) -> p h q", q=64), x3, dtt[s][:, hc].us(2).bc([128, 32, 64]), ALU.mult)
            P.tt(xdtd.v.re("p (h q) -> p h q", q=64), x3, dst_.v.us(2).bc([128, 32, 64]), ALU.mult, eng="pool")
            for g in range(4 if DBG_LEVEL >= 3 else 0):
                hg = slice(d * 32 + g * 8, d * 32 + (g + 1) * 8)
                hl = slice(g * 8, (g + 1) * 8)
                P.mm(p_cb.v, bT[s][:, g, :], cT[s][:, g, :])
                P.tt(CBm.v, p_cb.v, M1[d].v, ALU.mult)
                P.tt(X.v, M1[d].v.us(1).bc([128, 8, 128]), adt[s][:, hg].us(2).bc([128, 8, 128]), ALU.mult, eng="pool")
                for q in range(2):
                    P.mm(p_row[:, q * 4:(q + 1) * 4, :], C.ones_f.v, X[:, q * 4:(q + 1) * 4, :])
                if DBG_LEVEL < 4:
                    continue
                P.tt(X2.v, p_row.v, ncs[:, hl].us(2).bc([128, 8, 128]), ALU.add)
                P.ts(X2.v, X2.v, 0.0, ALU.min, eng="pool")
                P.act(X2.v, X2.v, AF.Exp)
                P.tt(Mh.v, X2.v, CBm.v.us(1).bc([128, 8, 128]), ALU.mult)
                if DBG_LEVEL < 5:
                    continue
                for h in range(8):
                    hh = g * 8 + h
                    P.mm(p_y[:, h * 64:(h + 1) * 64], Mh[:, h, :], xdt[:, hh * 64:(hh + 1) * 64])
                P.mm(p_yo.v, cT[s][:, g, :], Spb[:, g, :])
                P.mm(p_st.v, btm[s][:, g * 128:(g + 1) * 128], xdtd[:, g * 512:(g + 1) * 512])
                P.tt(tmp.v.re("p (h q) -> p h q", q=64), p_yo.v.re("p (h q) -> p h q", q=64),
                     ecs[:, hl].us(2).bc([128, 8, 64]), ALU.mult)
                P.tt(yt[:, g * 512:(g + 1) * 512], p_y.v, tmp.v, ALU.add)
                sg = Sp[:, g, :]
                P.tt(sg.re("p (h q) -> p h q", q=64), sg.re("p (h q) -> p h q", q=64),
                     etot[:, hl].us(2).bc([128, 8, 64]), ALU.mult, eng="pool")
                P.tt(sg, sg, p_st.v, ALU.add)
                P.cp(Spb[:, g, :], sg, eng="act")
            if DBG_LEVEL < 6:
                continue
            if d == 0:
                P.dma(S["of"][tsl, :], yt.v)
            else:
                P.tt(yt.v, yt.v, yft.v, ALU.add)
                P.tt(sq.v.re("p (h q) -> p h q", q=64), x3, dsk.v.us(2).bc([128, 32, 64]), ALU.mult, eng="pool")
                P.tt(yt.v, yt.v, sq.v, ALU.add)
                P.tt(yt.v, yt.v, zt.v, ALU.mult)
                P.tt(sq.v, yt.v, yt.v, ALU.mult, eng="pool")
                P.op("dve", lambda e: e.tensor_reduce(out=ss.v.ap, in_=sq.v.re("p (g v) -> p g v", v=512).ap,
                                                     axis=AX.X, op=ALU.add), [ss.v], [sq.v])
                P.act(ss.v, ss.v, AF.Sqrt, scale=1.0 / 512, bias=C.eps_col[:, 0:1])
                P.op("dve", lambda e: e.reciprocal(out=ss.v.ap, in_=ss.v.ap), [ss.v], [ss.v])
                P.tt(yt.v.re("p (g v) -> p g v", v=512), yt.v.re("p (g v) -> p g v", v=512),
                     ss.v.us(2).bc([128, 4, 512]), ALU.mult)
                P.tt(yb[s].v, yt.v, gb.v, ALU.mult)
                P.dma(S["y"][tsl, :], yb[s].v)
    P.end_phase()
    outproj_phase(P, C, w_out, ln_g, ln_b, last)


NF = 8192
NK2 = 65


def hyena_consts():
    import ml_dtypes
    bf = ml_dtypes.bfloat16
    n = NF
    tp = 2.0 * np.pi
    p = np.arange(128)
    k2 = np.arange(65)
    k2s = np.arange(1, 64)
    F1 = np.zeros((128, 128))
    F1[:, :65] = np.cos(tp * ((p[:, None] * k2[None, :]) % 128) / 128)
    F1[:, 65:] = -np.sin(tp * ((p[:, None] * k2s[None, :]) % 128) / 128)
    pp = np.arange(64)
    m = np.full(65, 2.0)
    m[0] = 1.0
    m[64] = 1.0
    Finv = np.zeros((128, 64))
    Finv[:65, :] = (m[:, None] / n) * np.cos(tp * ((k2[:, None] * pp[None, :]) % 128) / 128)
    Finv[65:, :] = -(2.0 / n) * np.sin(tp * ((k2s[:, None] * pp[None, :]) % 128) / 128)
    a = np.arange(64)
    k1 = np.arange(64)
    TA = np.zeros((65, 128, 128))
    TB = np.zeros((65, 128, 128))
    TC = np.zeros((65, 128, 128))
    for q in range(65):
        th = tp * ((a[:, None] * (q + 128 * k1[None, :])) % n) / n
        Tr = np.cos(th)
        Ti = -np.sin(th)
        TA[q, :64, :64] = Tr
        TA[q, :64, 64:] = Ti
        TA[q, 64:, :64] = -Ti
        TA[q, 64:, 64:] = Tr
        TB[q, :64, :64] = -Ti
        TB[q, :64, 64:] = Tr
        TB[q, 64:, :64] = -Tr
        TB[q, 64:, 64:] = -Ti
        Ur = np.cos(th).T
        Ui = np.sin(th).T
        TC[q, :64, :64] = Ur
        TC[q, 64:, :64] = -Ui
        TC[q, :64, 64:] = Ui
        TC[q, 64:, 64:] = Ur
    j = (64 * p[None, :] + a[:, None]).reshape(-1)
    pos = np.where(j < L, j, n - j)
    pos[j == L] = 0
    tl = np.linspace(0.0, 1.0, L)
    t = tl[pos]
    w = tp * pos / L
    f = np.linspace(1e-4, 15.0, 16)
    z = np.concatenate([t[:, None], np.cos(f[None, :] * w[:, None]), -np.sin(f[None, :] * w[:, None])], axis=1)
    tdec = t.copy()
    tdec[j == L] = 1e4
    ntpos = -(tdec.reshape(64, 128).T)
    min_d = math.log(1e-2) / 0.3
    max_d = math.log(1e-2) / 1.5
    deltas = np.abs(np.linspace(min_d, max_d, DI))
    return {
        "hy_F1": F1.astype(np.float32).astype(bf),
        "hy_Finv": Finv.astype(np.float32).astype(bf),
        "hy_TA": np.ascontiguousarray(TA.transpose(1, 0, 2)).astype(np.float32).astype(bf),
        "hy_TB": np.ascontiguousarray(TB.transpose(1, 0, 2)).astype(np.float32).astype(bf),
        "hy_TC": np.ascontiguousarray(TC.transpose(1, 0, 2)).astype(np.float32).astype(bf),
        "hy_zT": np.ascontiguousarray(z.T).astype(np.float32),
        "hy_ntpos": ntpos.astype(np.float32),
        "hy_deltas": deltas.astype(np.float32),
    }


def sin_reduced(P, o, arg, tmpf, tmpi, tmp2):
    tp = 2.0 * math.pi
    P.ts(tmpf, arg, 1.0 / tp, ALU.mult, 64.5, ALU.add)
    P.cp(tmpi, tmpf)
    P.cp(tmpf, tmpi)
    P.ts(tmpf, tmpf, -64.0, ALU.add, -tp, ALU.mult)
    P.tt(tmpf, tmpf, arg, ALU.add)
    P.ts(tmp2, tmpf, math.pi, ALU.is_gt, -tp, ALU.mult)
    P.tt(tmpf, tmpf, tmp2, ALU.add)
    P.ts(tmp2, tmpf, -math.pi, ALU.is_lt, tp, ALU.mult)
    P.tt(tmpf, tmpf, tmp2, ALU.add)
    P.ts(tmpf, tmpf, 3.14159, ALU.min, -3.14159, ALU.max)
    P.act(o, tmpf, AF.Sin)


def hyena_layer(P, C, li, prm, last):
    S = C.scr
    w_in, conv_w, conv_b = prm["w_in"], prm["conv_w"], prm["conv_b"]
    shifts = [-1, 0, 1]
    P.begin_phase()
    stage = P.sb("hstA", [128, 8, 512], F32)
    cwb = P.sb("hcwb", [128, 3, 512], F32)
    W1 = [P.sb("hW1_%d" % j, [128, 8, 512], BF16) for j in range(3)]
    W2 = [P.sb("hW2_%d" % j, [128, 8, 512], BF16) for j in range(3)]
    cbf = P.sb("hcbf", [1, 6144], F32)
    cbb = P.sb("hcbb", [1, 6144], BF16)
    P.dma(cbf.v, conv_b.v.us(0))
    P.cp(cbb.v, cbf.v)
    p1 = [P.ps("hp1_%d" % i, [128, 512], F32) for i in range(2)]
    p2 = [P.ps("hp2_%d" % i, [128, 512], F32) for i in range(2)]
    s1 = [P.sb("hs1_%d" % i, [128, 512], F32) for i in range(2)]
    go = [P.sb("hgo_%d" % i, [128, 512], F32) for i in range(2)]
    it = 0
    for c in range(4):
        prep_w_chunk(P, W1, w_in, 2048 + c * 512, 512, stage, convw_d=conv_w, cc0=2048 + c * 512, cwb=cwb, ntaps=3)
        prep_w_chunk(P, W2, w_in, 4096 + c * 512, 512, stage, convw_d=conv_w, cc0=4096 + c * 512, cwb=cwb, ntaps=3)
        for tt in range(NT):
            s = it % 2
            it += 1
            mm_proj_tm(P, p1[s].v, C, W1, shifts, tt, 512, bias=cbb[0:1, 2048 + c * 512: 2048 + (c + 1) * 512])
            mm_proj_tm(P, p2[s].v, C, W2, shifts, tt, 512, bias=cbb[0:1, 4096 + c * 512: 4096 + (c + 1) * 512])
            P.cp(s1[s].v, p1[s].v, eng="act")
            P.tt(go[s].v, p2[s].v, s1[s].v, ALU.mult)
            P.dma(S["xs"][tt * 128:(tt + 1) * 128, c * 512:(c + 1) * 512], go[s].v)
    for c in range(4):
        prep_w_chunk(P, W1, w_in, c * 512, 512, stage, convw_d=conv_w, cc0=c * 512, cwb=cwb, ntaps=3)
        prep_w_chunk(P, [W2[0]], w_in, 6144 + c * 512, 512, stage)
        for tt in range(NT):
            s = it % 2
            it += 1
            mm_proj_tm(P, p1[s].v, C, W1, shifts, tt, 512, bias=cbb[0:1, c * 512:(c + 1) * 512])
            mm_proj_tm(P, p2[s].v, C, [W2[0]], [0], tt, 512)
            P.act(s1[s].v, p2[s].v, AF.Silu)
            P.tt(go[s].v, p1[s].v, s1[s].v, ALU.mult)
            P.dma(S["zg"][tt * 128:(tt + 1) * 128, c * 512:(c + 1) * 512], go[s].v)
    P.end_phase()
    P.begin_phase()
    F1 = P.sb("F1", [128, 128], BF16)
    P.dma(F1.v, C.consts["hy_F1"].v)
    TA = P.sb("TA", [128, NK2, 128], BF16)
    P.dma(TA.v, C.consts["hy_TA"].v)
    ntp = P.sb("ntp", [128, 64], F32)
    P.dma(ntp.v, C.consts["hy_ntpos"].v)
    dl = P.sb("dl", [128, DI], F32)
    row_bcast(P, dl.v, C.consts["hy_deltas"], slice(None))
    w1 = P.sb("fw1", [33, 64], F32)
    w2 = P.sb("fw2", [64, 64], F32)
    w3 = P.sb("fw3", [64, 64], F32)
    P.dma(w1.v, prm["filt_w1"].v)
    P.dma(w2.v, prm["filt_w2"].v)
    P.dma(w3.v, prm["filt_w3"].v)
    cols = P.sb("fcols", [64, 4], F32)
    for i, nme in enumerate(["filt_freq", "filt_b1", "filt_b2", "filt_b3"]):
        P.dma(cols[:, i:i + 1], prm[nme].v.us(1))
    fb = P.sb("ffb", [64, 3], F32)
    for i in range(3):
        P.tt(fb[:, i:i + 1], cols[:, 0:1], cols[:, i + 1:i + 2], ALU.mult)
    wo_f = P.sb("fwo_f", [64, 4096], F32)
    wo = P.sb("fwo", [64, 4096], BF16)
    P.dma(wo_f.v, prm["filt_w_out"].v)
    P.cp(wo.v, wo_f.v, eng="pool")
    h3T = P.sb("h3T", [64, NF], BF16)
    zc = [P.sb("zc%d" % i, [33, 512], F32) for i in range(2)]
    arg = P.sb("farg", [64, 512], F32)
    tmf = P.sb("ftmf", [64, 512], F32)
    tmi = P.sb("ftmi", [64, 512], I32)
    tm2 = P.sb("ftm2", [64, 512], F32)
    hh = [P.sb("fhh%d" % i, [64, 512], F32) for i in range(2)]
    pm = P.ps("fpm", [64, 512], F32)
    for ch in range(NF // 512):
        s = ch % 2
        P.dma(zc[s].v, C.consts["hy_zT"][:, ch * 512:(ch + 1) * 512])
        src = zc[s].v
        for lyr, wl in enumerate([w1, w2, w3]):
            P.mm(pm.v, wl.v, src)
            P.act(arg.v, pm.v, AF.Identity, scale=cols[:, 0:1], bias=fb[:, lyr:lyr + 1])
            dst = hh[lyr % 2].v if lyr < 2 else h3T[:, ch * 512:(ch + 1) * 512]
            sin_reduced(P, dst, arg.v, tmf.v, tmi.v, tm2.v)
            src = dst
    pk = [P.ps("fpk%d" % i, [128, 512], F32) for i in range(2)]
    pz = [P.ps("fpz%d" % i, [128, 512], F32) for i in range(2)]
    dec = [P.sb("fdec%d" % i, [128, DI], F32) for i in range(2)]
    ka = [P.sb("fka%d" % i, [128, 512], BF16) for i in range(2)]
    zt = [P.sb("fzt%d" % i, [128, DI], BF16) for i in range(2)]
    it = 0
    for a in range(64):
        sa = a % 2
        P.act(dec[sa].v, dl.v, AF.Exp, scale=ntp[:, a:a + 1])
        for c in range(4):
            s = it % 2
            it += 1
            P.mm(pk[s][0:64, :], h3T[:, a * 128: a * 128 + 64], wo[:, c * 512:(c + 1) * 512])
            P.mm(pk[s][64:128, :], h3T[:, a * 128 + 64: a * 128 + 128], wo[:, 2048 + c * 512: 2048 + (c + 1) * 512])
            P.tt(ka[s].v, pk[s].v, dec[sa][:, c * 512:(c + 1) * 512], ALU.mult)
            P.mm(pz[s].v, F1.v, ka[s].v)
            P.cp(zt[sa][:, c * 512:(c + 1) * 512], pz[s].v, eng="act")
        P.dma(S["Zs"][:, a, :], zt[sa].v)
    zin = [P.sb("fzin%d" % i, [128, DI], BF16) for i in range(2)]
    kf = [P.sb("fkf%d" % i, [128, DI], F32) for i in range(2)]
    for q in range(NK2):
        s = q % 2
        P.dma(zin[s][0:64, :], S["Zs"][q, :, :])
        if 1 <= q <= 63:
            P.dma(zin[s][64:128, :], S["Zs"][64 + q, :, :])
        else:
            P.memset(zin[s][64:128, :], 0.0)
        for c in range(4):
            b = (q * 4 + c) % 2
            P.mm(pz[b].v, TA[:, q, :], zin[s][:, c * 512:(c + 1) * 512])
            P.cp(kf[s][:, c * 512:(c + 1) * 512], pz[b].v, eng=("act" if c % 2 else "dve"))
        P.dma(S["Kf"][q, :, :], kf[s].v)
    P.end_phase()
    P.begin_phase()
    F1 = P.sb("F1c", [128, 128], BF16)
    P.dma(F1.v, C.consts["hy_F1"].v)
    gf = [P.sb("cgf%d" % i, [64, DI], F32) for i in range(2)]
    gb_ = [P.sb("cgb%d" % i, [64, DI], BF16) for i in range(2)]
    zt = [P.sb("czt%d" % i, [128, DI], BF16) for i in range(2)]
    pz = [P.ps("cpz%d" % i, [128, 512], F32) for i in range(2)]
    gview = S["xs"].v.re("(p a) c -> a p c", a=64)
    for a in range(64):
        s = a % 2
        P.dma(gf[s].v, gview[a])
        P.cp(gb_[s].v, gf[s].v, eng="pool")
        for c in range(4):
            b = c % 2
            P.mm(pz[b].v, F1[0:64, :], gb_[s][:, c * 512:(c + 1) * 512])
            P.cp(zt[s][:, c * 512:(c + 1) * 512], pz[b].v, eng=("act" if c % 2 else "dve"))
        P.dma(S["Zs"][:, a, :], zt[s].v)
    P.end_phase()
    P.begin_phase()
    TA = P.sb("dTA", [128, NK2, 128], BF16)
    TB = P.sb("dTB", [128, NK2, 128], BF16)
    TC = P.sb("dTC", [128, NK2, 128], BF16)
    P.dma(TA.v, C.consts["hy_TA"].v)
    P.dma(TB.v, C.consts["hy_TB"].v)
    P.dma(TC.v, C.consts["hy_TC"].v)
    zin = [P.sb("dzin%d" % i, [128, DI], BF16) for i in range(2)]
    KA = [P.sb("dKA%d" % i, [128, DI], F32) for i in range(2)]
    KB = [P.sb("dKB%d" % i, [128, DI], F32) for i in range(2)]
    t1 = [P.sb("dt1_%d" % i, [128, 512], F32) for i in range(2)]
    t2 = [P.sb("dt2_%d" % i, [128, 512], F32) for i in range(2)]
    yb = [P.sb("dyb%d" % i, [128, 512], BF16) for i in range(2)]
    vo = [P.sb("dvo%d" % i, [128, DI], BF16) for i in range(2)]
    pA = [P.ps("dpA%d" % i, [128, 512], F32) for i in range(2)]
    pB = [P.ps("dpB%d" % i, [128, 512], F32) for i in range(2)]
    pV = [P.ps("dpV%d" % i, [128, 512], F32) for i in range(2)]
    it = 0
    for q in range(NK2):
        s = q % 2
        P.dma(zin[s][0:64, :], S["Zs"][q, :, :])
        if 1 <= q <= 63:
            P.dma(zin[s][64:128, :], S["Zs"][64 + q, :, :])
        else:
            P.memset(zin[s][64:128, :], 0.0)
        P.dma(KA[s][0:64, :], S["Kf"][q, 0:64, :])
        P.dma(KA[s][64:128, :], S["Kf"][q, 0:64, :])
        P.dma(KB[s][0:64, :], S["Kf"][q, 64:128, :])
        P.dma(KB[s][64:128, :], S["Kf"][q, 64:128, :])
        for c in range(4):
            b = it % 2
            it += 1
            cs_ = slice(c * 512, (c + 1) * 512)
            P.mm(pA[b].v, TA[:, q, :], zin[s][:, cs_])
            P.mm(pB[b].v, TB[:, q, :], zin[s][:, cs_])
            P.tt(t1[b].v, pA[b].v, KA[s][:, cs_], ALU.mult)
            P.tt(t2[b].v, pB[b].v, KB[s][:, cs_], ALU.mult)
            P.tt(yb[b].v, t1[b].v, t2[b].v, ALU.add, eng="pool")
            P.mm(pV[b].v, TC[:, q, :], yb[b].v)
            P.cp(vo[s][:, cs_], pV[b].v, eng="act")
        P.dma(S["Zv"][q, :, :], vo[s][0:64, :])
        if 1 <= q <= 63:
            P.dma(S["Zv"][64 + q, :, :], vo[s][64:128, :])
    P.end_phase()
    P.begin_phase()
    Fi = P.sb("eFi", [128, 64], BF16)
    P.dma(Fi.v, C.consts["hy_Finv"].v)
    skb = P.sb("eskb", [128, DI], F32)
    row_bcast(P, skb.v, prm["skip"], slice(None))
    vin = [P.sb("evin%d" % i, [128, 2, DI], BF16) for i in range(2)]
    gt = [P.sb("egt%d" % i, [128, DI], F32) for i in range(2)]
    zg = [P.sb("ezg%d" % i, [128, DI], F32) for i in range(2)]
    yo = [P.sb("eyo%d" % i, [128, DI], BF16) for i in range(2)]
    py = [P.ps("epy%d" % i, [128, 512], F32) for i in range(2)]
    gview = S["xs"].v.re("(p a) c -> a p c", a=64)
    zview = S["zg"].v.re("(p a) c -> a p c", a=64)
    yview = S["y"].v.re("(p a) c -> a p c", a=64)
    it = 0
    for a2 in range(32):
        s = a2 % 2
        for h in range(2):
            a = a2 * 2 + h
            P.dma(vin[s][:, h, :], S["Zv"][:, a, :])
            P.dma(gt[s][h * 64:(h + 1) * 64, :], gview[a])
            P.dma(zg[s][h * 64:(h + 1) * 64, :], zview[a])
        P.tt(gt[s].v, gt[s].v, skb.v, ALU.mult, eng="pool")
        for c in range(4):
            b = it % 2
            it += 1
            cs_ = slice(c * 512, (c + 1) * 512)
            for h in range(2):
                P.mm(py[b][h * 64:(h + 1) * 64, :], Fi.v, vin[s][:, h, cs_])
            P.tt(gt[s][:, cs_], py[b].v, gt[s][:, cs_], ALU.add)
            P.tt(yo[s][:, cs_], gt[s][:, cs_], zg[s][:, cs_], ALU.mult, eng="pool")
        for h in range(2):
            P.dma(yview[a2 * 2 + h], yo[s][h * 64:(h + 1) * 64, :])
    P.end_phase()
    outproj_phase(P, C, prm["w_out"], prm["ln_g"], prm["ln_b"], last)


_PROG_CACHE = {}


def kernel(**inputs):
    layers = [0, 1, 2, 3]
    if "prog" not in _PROG_CACHE:
        _PROG_CACHE["prog"] = build_program(layers)
    P, C = _PROG_CACHE["prog"]
    inputs = {k: np.asarray(v) for k, v in inputs.items()}
    nb = inputs["x"].shape[0]
    in_maps = [in_map_for(inputs, b, layers) for b in range(nb)]
    res = run_bass_kernel_spmd(P.nc, in_maps, core_ids=list(range(nb)))
    out = np.stack([np.asarray(res.results[b]["out"]) for b in range(nb)], axis=0)
    return out.astype(np.float32)
```

```python
import math
from contextlib import ExitStack
import numpy as np
import concourse.bass as bass
import concourse.mybir as mybir
from concourse.bass_utils import run_bass_kernel_spmd

F32 = mybir.dt.float32
BF16 = mybir.dt.bfloat16
I32 = mybir.dt.int32
AF = mybir.ActivationFunctionType
ALU = mybir.AluOpType
AX = mybir.AxisListType

D = 1024
L = 4096
DI = 2048
DEPTH = 4
ALPHA = (2.0 * DEPTH) ** 0.25
LN_EPS = 1e-5
PAD = 2
NT = L // 128
SAME_ENG_SYNC = True
NDS = 48
SCHED = True
SCHED_WINDOW = 24
DEBUG_SCR = False
DBG_STOP = ""
DBG_LEVEL = 99
DBG_SKIPA = False
DBG_DIRS = 2
DBG_SUB = 99
DBG_NT = 2
DBG_PH = 99
_phc = [0]


class Buf:
    __slots__ = ("t", "w", "r", "name", "wn", "rn")

    def __init__(self, t, name=""):
        self.t = t
        self.w = None
        self.r = {}
        self.name = name
        self.wn = None
        self.rn = []

    def __getitem__(self, idx):
        return V(self, self.t[idx])

    @property
    def v(self):
        return V(self, self.t)


class V:
    __slots__ = ("b", "ap")

    def __init__(self, b, ap):
        self.b = b
        self.ap = ap

    def __getitem__(self, idx):
        return V(self.b, self.ap[idx])

    def re(self, s, **kw):
        return V(self.b, self.ap.rearrange(s, **kw))

    def bc(self, shape):
        return V(self.b, self.ap.broadcast_to(shape))

    def tb(self, shape):
        return V(self.b, self.ap.to_broadcast(shape))

    def pb(self, n):
        return V(self.b, self.ap.partition_broadcast(n))

    def cast(self, dt):
        return V(self.b, self.ap.bitcast(dt))

    def us(self, ax):
        return V(self.b, self.ap.unsqueeze(ax))


class StopBuild(Exception):
    pass


class Eng:
    def __init__(self, name, eng, sem):
        self.name = name
        self.eng = eng
        self.sem = sem
        self.n = 0
        self.seen = {}


class Prog:
    def __init__(self):
        nc = bass.Bass("TRN2", target_bir_lowering=False)
        self.nc = nc
        self.es = ExitStack()
        self.E = {}
        for name, e in [("pe", nc.tensor), ("dve", nc.vector), ("act", nc.scalar),
                        ("pool", nc.gpsimd), ("sp", nc.sync)]:
            sem = self.es.enter_context(nc.semaphore("s_" + name))
            self.E[name] = Eng(name, e, sem)
        self.dsem = []
        for i in range(NDS):
            self.dsem.append([self.es.enter_context(nc.semaphore("d%d" % i)), 0])
        self.dnext = 0
        self.phase = None
        self.uid = 0
        self.ninst = 0
        self.deferred = False
        self.nodes = []
        self.touched = []
        self.sim_time = 0.0
        self.synced = False

    def begin_phase(self):
        self.phase = ExitStack()
        self.deferred = SCHED
        if SCHED and not self.synced:
            self.barrier()
            self.synced = True

    def end_phase(self):
        if self.deferred:
            self._flush()
            self.deferred = False
        self.barrier()
        self.phase.close()
        self.phase = None
        _phc[0] += 1
        if _phc[0] >= DBG_PH:
            raise StopBuild()

    def _nm(self, name):
        self.uid += 1
        return "%s_%d" % (name, self.uid)

    def sb(self, name, shape, dt, perm=False):
        st = self.es if perm else self.phase
        t = st.enter_context(self.nc.sbuf_tensor(self._nm(name), list(shape), dt))
        return Buf(t[:], name)

    def ps(self, name, shape, dt=F32, perm=False):
        st = self.es if perm else self.phase
        esz = 4 if dt == F32 else 2
        n = 1
        for x in shape[1:]:
            n *= x
        per_bank = 2048 // esz
        npad = ((n + per_bank - 1) // per_bank) * per_bank
        t = st.enter_context(self.nc.psum_tensor(self._nm(name), [shape[0], npad], dt))
        ap = t[:][:, 0:n]
        if len(shape) == 3:
            ap = ap.rearrange("p (a b) -> p a b", a=shape[1])
        return Buf(ap, name)

    def dram(self, name, shape, dt, kind="Internal"):
        if DEBUG_SCR and kind == "Internal" and name.startswith("s_"):
            kind = "ExternalOutput"
        t = self.nc.dram_tensor(name, list(shape), dt, kind=kind)
        return Buf(t.ap(), name)

    def _wait(self, E, toks):
        need = {}
        for (s, v) in toks:
            if s is E.sem and (E.name == "pe" or not SAME_ENG_SYNC):
                continue
            k = id(s)
            if k not in need or need[k][1] < v:
                need[k] = (s, v)
        for k, (s, v) in need.items():
            if E.seen.get(k, 0) >= v:
                continue
            E.eng.wait_ge(s, v)
            self.ninst += 1
            E.seen[k] = v

    @staticmethod
    def _deps(outs, ins):
        toks = []
        for v in ins:
            if v.b.w is not None:
                toks.append(v.b.w)
        for v in outs:
            if v.b.w is not None:
                toks.append(v.b.w)
            toks.extend(v.b.r.values())
        return toks

    @staticmethod
    def _mark(tok, outs, ins):
        k = id(tok[0])
        for v in ins:
            r = v.b.r
            if k not in r or r[k][1] < tok[1]:
                r[k] = tok
        for v in outs:
            v.b.w = tok
            v.b.r = {}

    def op(self, ename, fn, outs, ins, cost=0.3):
        if self.deferred:
            self._record(ename, "op", fn, outs, ins, cost, cost)
            return
        E = self.E[ename]
        self._wait(E, self._deps(outs, ins))
        E.n += 1
        fn(E.eng).then_inc(E.sem, 1)
        self.ninst += 1
        self._mark((E.sem, E.n), outs, ins)

    def dma(self, out, in_, q="sp", slow=False):
        if self.deferred:
            nbytes = 1
            for x in out.ap.shape:
                nbytes *= x
            nbytes *= (4 if out.ap.dtype in (F32, I32) else 2)
            self._record(q, "dma", (out, in_, slow), [out], [in_], 0.12, 2.0 + nbytes / 120e3)
            return
        self._emit_dma(self.E[q], out, in_, slow, self._deps([out], [in_]), True)

    def _emit_dma(self, E, out, in_, slow, toks, mark):
        i = self.dnext
        self.dnext = (i + 1) % NDS
        sem, cnt = self.dsem[i]
        if cnt > 0:
            toks = list(toks) + [(sem, cnt)]
        self._wait(E, toks)
        if slow:
            E.eng.dma_start(out=out.ap, in_=in_.ap, allow_slow_non_contiguous=True).then_inc(sem, 16)
        else:
            E.eng.dma_start(out=out.ap, in_=in_.ap).then_inc(sem, 16)
        self.ninst += 1
        self.dsem[i][1] = cnt + 16
        tok = (sem, cnt + 16)
        if mark:
            self._mark(tok, [out], [in_])
        return tok

    def _record(self, ename, kind, payload, outs, ins, busy, lat):
        nid = len(self.nodes)
        deps = set()
        for v in ins:
            b = v.b
            if b.wn is not None:
                deps.add(b.wn)
        for v in outs:
            b = v.b
            if b.wn is not None:
                deps.add(b.wn)
            deps.update(b.rn)
        for v in ins:
            b = v.b
            b.rn.append(nid)
            self.touched.append(b)
        for v in outs:
            b = v.b
            b.wn = nid
            b.rn = []
            self.touched.append(b)
        deps.discard(nid)
        self.nodes.append([ename, kind, payload, busy, lat, deps])

    def _flush(self):
        nodes = self.nodes
        n = len(nodes)
        if n == 0:
            return
        succ = [[] for _ in range(n)]
        ndep = [0] * n
        for i, nd in enumerate(nodes):
            ndep[i] = len(nd[5])
            for d in nd[5]:
                succ[d].append(i)
        ready = [0.0] * n
        fin = [0.0] * n
        queues = {}
        for i, nd in enumerate(nodes):
            queues.setdefault(nd[0], []).append(i)
        head = {e: 0 for e in queues}
        done = [False] * n
        T = {e: 0.0 for e in queues}
        order = {e: [] for e in queues}
        W = SCHED_WINDOW
        left = n
        while left:
            best = None
            for e, q in queues.items():
                h = head[e]
                while h < len(q) and done[q[h]]:
                    h += 1
                head[e] = h
                cnt = 0
                j = h
                te = T[e]
                while j < len(q) and cnt < W:
                    i = q[j]
                    j += 1
                    if done[i]:
                        continue
                    cnt += 1
                    if ndep[i]:
                        continue
                    st = ready[i] if ready[i] > te else te
                    if best is None or st < best[0] - 1e-9 or (st < best[0] + 1e-9 and i < best[1]):
                        best = (st, i, e)
                    if st <= te:
                        break
            st, i, e = best
            nd = nodes[i]
            T[e] = st + nd[3]
            f = st + nd[4]
            fin[i] = f
            done[i] = True
            left -= 1
            order[e].append(i)
            for sidx in succ[i]:
                ndep[sidx] -= 1
                lat = f + (0.12 if nodes[sidx][0] == e else 0.3)
                if lat > ready[sidx]:
                    ready[sidx] = lat
        tok = [None] * n
        for e, lst in order.items():
            E = self.E[e]
            k = E.n
            for i in lst:
                if nodes[i][1] == "op":
                    k += 1
                    tok[i] = (E.sem, k)
        dma_engs = [e for e in order if any(nodes[i][1] == "dma" for i in order[e])]
        pending = {e: list(order[e]) for e in order}
        ptr = {e: 0 for e in order}
        progress = True
        while progress:
            progress = False
            for e in list(pending.keys()):
                lst = pending[e]
                E = self.E[e]
                while ptr[e] < len(lst):
                    i = lst[ptr[e]]
                    nd = nodes[i]
                    if any(tok[d] is None for d in nd[5]):
                        break
                    toks = [tok[d] for d in nd[5]]
                    if nd[1] == "dma":
                        out, in_, slow = nd[2]
                        tok[i] = self._emit_dma(E, out, in_, slow, toks, False)
                    else:
                        self._wait(E, toks)
                        E.n += 1
                        assert tok[i] == (E.sem, E.n)
                        nd[2](E.eng).then_inc(E.sem, 1)
                        self.ninst += 1
                    ptr[e] += 1
                    progress = True
                if ptr[e] >= len(lst):
                    del pending[e]
        assert not pending, "emission deadlock"
        self.sim_time += max(T.values())
        self.nodes = []
        for b in self.touched:
            b.wn = None
            b.rn = []
        self.touched = []

    def barrier(self):
        toks = [(e.sem, e.n) for e in self.E.values() if e.n > 0]
        toks += [(s, c) for s, c in self.dsem if c > 0]
        for E in self.E.values():
            self._wait(E, [t for t in toks if t[0] is not E.sem])

    @staticmethod
    def _fs(v):
        n = 1
        for x in v.ap.shape[1:]:
            n *= x
        return n

    def _cost(self, eng, v, cast=False):
        n = self._fs(v)
        if eng == "act":
            return 0.2 + n / 1400.0
        if eng == "pool":
            return 0.3 + n * (0.0035 if cast else 0.0022)
        return 0.1 + n / 960.0

    def mm(self, o, lhsT, rhs, start=True, stop=True):
        n = self._fs(rhs) * (4 if lhsT.ap.dtype == F32 else 1)
        self.op("pe", lambda e: e.matmul(o.ap, lhsT=lhsT.ap, rhs=rhs.ap, start=start, stop=stop),
                [o], [lhsT, rhs], cost=0.07 + n / 2200.0)

    def tr(self, o, a, ident):
        self.op("pe", lambda e: e.transpose(o.ap, a.ap, ident.ap), [o], [a, ident], cost=0.15)

    def act(self, o, a, func, scale=1.0, bias=0.0, accum=None, eng="act"):
        ins = [a]
        outs = [o]
        kw = {}
        if isinstance(scale, V):
            ins.append(scale)
            kw["scale"] = scale.ap
        else:
            kw["scale"] = float(scale)
        if isinstance(bias, V):
            ins.append(bias)
            kw["bias"] = bias.ap
        else:
            kw["bias"] = float(bias)
        if accum is not None:
            outs.append(accum)
            kw["accum_out"] = accum.ap
        self.op("act", lambda e: e.activation(out=o.ap, in_=a.ap, func=func, **kw), outs, ins,
                cost=self._cost("act", a))

    def tt(self, o, a, b, op, eng="dve"):
        self.op(eng, lambda e: e.tensor_tensor(out=o.ap, in0=a.ap, in1=b.ap, op=op), [o], [a, b],
                cost=self._cost(eng, o))

    def ts(self, o, a, s1, op0, s2=None, op1=None, eng="dve", accum=None):
        ins = [a]
        outs = [o]
        a1 = s1.ap if isinstance(s1, V) else float(s1)
        if isinstance(s1, V):
            ins.append(s1)
        a2 = None
        if s2 is not None:
            a2 = s2.ap if isinstance(s2, V) else float(s2)
            if isinstance(s2, V):
                ins.append(s2)
        kw = {}
        if op1 is not None:
            kw["op1"] = op1
        if accum is not None:
            kw["accum_out"] = accum.ap
            outs.append(accum)
        self.op(eng, lambda e: e.tensor_scalar(out=o.ap, in0=a.ap, scalar1=a1, scalar2=a2, op0=op0, **kw),
                outs, ins, cost=self._cost(eng, o))

    def stt(self, o, a, s, b, op0, op1, eng="dve"):
        ins = [a, b]
        sv = s.ap if isinstance(s, V) else float(s)
        if isinstance(s, V):
            ins.append(s)
        self.op(eng, lambda e: e.scalar_tensor_tensor(out=o.ap, in0=a.ap, scalar=sv, in1=b.ap, op0=op0, op1=op1),
                [o], ins, cost=self._cost(eng, o))

    def cp(self, o, a, eng="dve"):
        if eng == "act":
            self.op("act", lambda e: e.copy(out=o.ap, in_=a.ap), [o], [a], cost=self._cost("act", o))
        else:
            self.op(eng, lambda e: e.tensor_copy(out=o.ap, in_=a.ap), [o], [a], cost=self._cost(eng, o, cast=True))

    def memset(self, o, val, eng="pool"):
        self.op(eng, lambda e: e.memset(o.ap, val), [o], [], cost=self._cost(eng, o))


def _ident_bf16():
    import ml_dtypes
    return np.eye(128, dtype=np.float32).astype(ml_dtypes.bfloat16)


class Ctx:
    pass


def setup_common(P, C):
    C.hT = P.sb("hT", [128, D // 128, L + 2 * PAD], BF16, perm=True)
    C.ident = P.sb("ident", [128, 128], BF16, perm=True)
    C.identf = P.sb("identf", [128, 128], F32, perm=True)
    C.ones_bf = P.sb("ones_bf", [128, 128], BF16, perm=True)
    C.ones_f = P.sb("ones_f", [128, 128], F32, perm=True)
    P.dma(C.ident.v, C.d_ident.v)
    P.dma(C.identf.v, C.d_identf.v)
    P.memset(C.ones_bf.v, 1.0)
    P.memset(C.ones_f.v, 1.0)
    P.memset(C.hT.v, 0.0)
    C.eps_col = P.sb("eps_col", [128, 1], F32, perm=True)
    P.memset(C.eps_col.v, LN_EPS)


def load_x_to_hT(P, C):
    P.begin_phase()
    xt = [P.sb("xt%d" % i, [128, D], F32) for i in range(2)]
    xb = [P.sb("xb%d" % i, [128, D], BF16) for i in range(2)]
    pt = [P.ps("ptr%d" % i, [128, D], BF16) for i in range(2)]
    for tt in range(NT):
        s = tt % 2
        P.dma(xt[s].v, C.x[tt * 128:(tt + 1) * 128, :])
        P.dma(C.h[tt * 128:(tt + 1) * 128, :], xt[s].v)
        P.cp(xb[s].v, xt[s].v, eng="act")
        for kc in range(D // 128):
            P.tr(pt[s][:, kc * 128:(kc + 1) * 128], xb[s][:, kc * 128:(kc + 1) * 128], C.ident.v)
        P.cp(C.hT[:, :, PAD + tt * 128: PAD + (tt + 1) * 128],
             pt[s].v.re("p (k t) -> p k t", k=D // 128))
    P.end_phase()


def row_bcast(P, dst, src_buf, sl):
    P.dma(dst, V(src_buf, src_buf.t[sl].partition_broadcast(128)))


def prep_w_chunk(P, Wt, w_d, c0, n, stage, convw_d=None, cc0=0, cwb=None, ntaps=0):
    nk = w_d.t.shape[0] // 128
    P.dma(stage[:, :nk, :n], w_d[:, c0:c0 + n].re("(k p) c -> p k c", p=128))
    if ntaps == 0:
        P.cp(Wt[0][:, :nk, :n], stage[:, :nk, :n], eng="pool")
        return
    row_bcast(P, cwb[:, :ntaps, :n], convw_d, (slice(None), slice(cc0, cc0 + n)))
    for j in range(ntaps):
        P.tt(Wt[j][:, :nk, :n], stage[:, :nk, :n],
             cwb[:, j:j + 1, :n].bc([128, nk, n]), ALU.mult, eng=("pool" if j % 2 else "dve"))


def mm_proj_tm(P, ps, C, Wts, shifts, tt, n, bias=None):
    nk = D // 128
    tot = len(Wts) * nk + (1 if bias is not None else 0)
    i = 0
    for Wt, sh in zip(Wts, shifts):
        for kc in range(nk):
            t0 = PAD + tt * 128 + sh
            P.mm(ps, C.hT[:, kc, t0:t0 + 128], Wt[:, kc, :n], start=(i == 0), stop=(i == tot - 1))
            i += 1
    if bias is not None:
        P.mm(ps, C.ones_bf[0:1, :], bias, start=False, stop=True)


class OutProj:
    def __init__(self, P, C, wout_d, lng_d, lnb_d, last=False):
        self.P, self.C, self.last = P, C, last
        self.w = P.sb("wout", [128, DI // 128, D], BF16)
        st = P.sb("wost", [128, 4, D], F32)
        for q in range(4):
            P.dma(st.v, wout_d[q * 512:(q + 1) * 512, :].re("(k p) c -> p k c", p=128))
            P.cp(self.w[:, q * 4:(q + 1) * 4, :], st.v, eng=("pool" if q % 2 else "dve"))
        self.g = P.sb("lng", [128, D], F32)
        self.b = P.sb("lnb", [128, D], F32)
        row_bcast(P, self.g.v, lng_d, slice(None))
        row_bcast(P, self.b.v, lnb_d, slice(None))
        self.yT = P.sb("yT", [128, DI // 128, 128], BF16)
        self.ho = P.sb("ho", [128, D], F32)
        self.r = P.sb("r", [128, D], F32)
        self.hn = P.sb("hn", [128, D], F32)
        self.hb = P.sb("hb", [128, D], BF16)
        self.st6 = P.sb("st6", [128, 2, 6], F32)
        self.mv = P.sb("mv", [128, 2], F32)
        self.rstd = P.sb("rstd", [128, 1], F32)
        self.pT = P.ps("opT", [128, DI], BF16)
        self.po = P.ps("opo", [128, D], F32)
        self.pH = P.ps("opH", [128, D], BF16)

    def tile(self, tt, y):
        P, C = self.P, self.C
        for c in range(DI // 128):
            P.tr(self.pT[:, c * 128:(c + 1) * 128], y[:, c * 128:(c + 1) * 128], C.ident.v)
        P.cp(self.yT.v, self.pT.v.re("p (c t) -> p c t", c=DI // 128), eng="act")
        P.dma(self.ho.v, C.h[tt * 128:(tt + 1) * 128, :])
        for dh in range(2):
            for c in range(DI // 128):
                P.mm(self.po[:, dh * 512:(dh + 1) * 512], self.yT[:, c, :],
                     self.w[:, c, dh * 512:(dh + 1) * 512], start=(c == 0), stop=(c == DI // 128 - 1))
        P.stt(self.r.v, self.ho.v, ALPHA, self.po.v, ALU.mult, ALU.add)
        for q in range(2):
            P.op("dve", lambda e, q=q: e.bn_stats(out=self.st6[:, q, :].ap, in_=self.r[:, q * 512:(q + 1) * 512].ap),
                 [self.st6.v], [self.r.v])
        P.op("dve", lambda e: e.bn_aggr(out=self.mv.v.ap, in_=self.st6.v.re("p a b -> p (a b)").ap),
             [self.mv.v], [self.st6.v])
        P.act(self.rstd.v, self.mv[:, 1:2], AF.Sqrt, scale=1.0, bias=C.eps_col[:, 0:1])
        P.op("dve", lambda e: e.reciprocal(out=self.rstd.v.ap, in_=self.rstd.v.ap), [self.rstd.v], [self.rstd.v])
        P.ts(self.hn.v, self.r.v, self.mv[:, 0:1], ALU.subtract, self.rstd[:, 0:1], ALU.mult)
        P.tt(self.hn.v, self.hn.v, self.g.v, ALU.mult, eng="pool")
        P.tt(self.hn.v, self.hn.v, self.b.v, ALU.add, eng="pool")
        if self.last:
            P.dma(C.out[tt * 128:(tt + 1) * 128, :], self.hn.v)
            return
        P.dma(C.h[tt * 128:(tt + 1) * 128, :], self.hn.v)
        P.cp(self.hb.v, self.hn.v, eng="act")
        for kc in range(D // 128):
            P.tr(self.pH[:, kc * 128:(kc + 1) * 128], self.hb[:, kc * 128:(kc + 1) * 128], C.ident.v)
        P.cp(C.hT[:, :, PAD + tt * 128: PAD + (tt + 1) * 128],
             self.pH.v.re("p (k t) -> p k t", k=D // 128))


def mm_proj_fm(P, ps, C, Wt, c0, t0, n, shift=0, start=True, stop=True):
    nk = D // 128
    for kc in range(nk):
        a = PAD + t0 + shift
        P.mm(ps, Wt[:, kc, c0:c0 + 128], C.hT[:, kc, a:a + n],
             start=(start and kc == 0), stop=(stop and kc == nk - 1))


HG_H = 16


def hgrn2_consts():
    s = np.arange(128)[:, None]
    t = np.arange(128)[None, :]
    same = (s // 64) == (t // 64)
    mf = ((s <= t) & same).astype(np.float32)
    mb = ((s >= t) & same).astype(np.float32)
    rm = np.ones((128, 512), np.float32)
    rm[:, ::64] = 0.0
    return {"hg_mf": mf, "hg_mb": mb, "hg_rm": rm}


def hgrn2_layer(P, C, li, w_in, lb_raw, norm_g, w_out, ln_g, ln_b, last):
    S = C.scr
    qg = [S["qgf"], S["qgb"]]
    kg = [S["kgf"], S["kgb"]]
    P.begin_phase()
    lbr = P.sb("lbr", [32, 4, 128], F32)
    P.dma(lbr.v, lb_raw.v.re("l (g k) -> g l k", k=128))
    lbT = P.sb("lbT", [128, 4, 32], F32)
    pl = P.ps("pl", [128, 4, 32], F32)
    for l in range(4):
        P.tr(pl[:, l, :], lbr[:, l, :], C.identf[0:32, 0:32])
    P.act(lbT.v, pl.v, AF.Exp)
    den = P.sb("den", [128, 32], F32)
    num = P.sb("num", [128, 32], F32)
    P.tt(den.v, lbT[:, 0, :], lbT[:, 1, :], ALU.add)
    P.tt(den.v, den.v, lbT[:, 2, :], ALU.add)
    P.tt(den.v, den.v, lbT[:, 3, :], ALU.add)
    P.memset(num.v, 0.0, eng="dve")
    for j in range(1, li + 1):
        P.tt(num.v, num.v, lbT[:, j, :], ALU.add)
    P.op("dve", lambda e: e.reciprocal(out=den.v.ap, in_=den.v.ap), [den.v], [den.v])
    lb = P.sb("lb", [128, 32], F32)
    ln1mlb = P.sb("ln1mlb", [128, 32], F32)
    P.tt(lb.v, num.v, den.v, ALU.mult)
    P.ts(ln1mlb.v, lb.v, -1.0, ALU.mult, 1.0, ALU.add)
    P.act(ln1mlb.v, ln1mlb.v, AF.Ln)
    etot = C.hg_etot
    rm = P.sb("rm", [128, 512], F32)
    P.dma(rm.v, C.consts["hg_rm"].v)
    stage = P.sb("stA", [128, 8, 384], F32)
    Wt = [P.sb("WtA%d" % i, [128, 8, 384], BF16) for i in range(2)]
    pq = [P.ps("pq%d" % i, [128, 512], F32) for i in range(2)]
    pf = [[P.ps("pf%d_%d" % (d, i), [128, 512], F32) for i in range(2)] for d in range(2)]
    nb = 2
    e_ = [P.sb("e%d" % i, [128, 512], F32) for i in range(nb)]
    l1 = [P.sb("l1%d" % i, [128, 512], F32) for i in range(nb)]
    l2 = [P.sb("l2%d" % i, [128, 512], F32) for i in range(nb)]
    lf = [P.sb("lf%d" % i, [128, 512], F32) for i in range(nb)]
    pp = [P.sb("pp%d" % i, [128, 512], F32) for i in range(nb)]
    lk = [P.sb("lk%d" % i, [128, 512], F32) for i in range(nb)]
    eg = [P.sb("eg%d" % i, [128, 512], F32) for i in range(nb)]
    qo = [P.sb("qo%d" % i, [128, 512], BF16) for i in range(nb)]
    ko = [P.sb("ko%d" % i, [128, 512], BF16) for i in range(nb)]
    tot8 = [P.sb("tot8%d" % i, [128, 8], F32) for i in range(nb)]
    it = 0
    for hd in range(HG_H):
        W = Wt[hd % 2]
        for j, c0 in enumerate([hd * 128, 2048 + hd * 128, 4096 + hd * 128]):
            P.dma(stage[:, :, j * 128:(j + 1) * 128], w_in[:, c0:c0 + 128].re("(k p) c -> p k c", p=128))
        P.cp(W.v, stage.v, eng="pool")
        for tc in range(L // 512):
            t0 = tc * 512
            s = tc % 2
            mm_proj_fm(P, pq[s].v, C, W, 0, t0, 512)
            for d in range(2):
                mm_proj_fm(P, pf[d][s].v, C, W, 128 * (1 + d), t0, 512)
            for d in range(2):
                b = it % nb
                it += 1
                col = d * HG_H + hd
                ff = pf[d][s]
                P.act(e_[b].v, ff.v, AF.Exp, scale=-1.0)
                P.act(l1[b].v, e_[b].v, AF.Ln, scale=lb[:, col:col + 1], bias=1.0)
                P.act(l2[b].v, e_[b].v, AF.Ln, scale=1.0, bias=1.0)
                P.tt(lf[b].v, l1[b].v, l2[b].v, ALU.subtract)
                P.op("dve", lambda e, b=b: e.tensor_tensor_scan(out=pp[b].v.ap, data0=rm.v.ap, data1=lf[b].v.ap,
                                                               initial=0.0, op0=ALU.mult, op1=ALU.add),
                     [pp[b].v], [rm.v, lf[b].v])
                P.stt(lk[b].v, ff.v, -1.0, l2[b].v, ALU.mult, ALU.subtract)
                if d == 1:
                    P.tt(lf[b].v, lf[b].v, pp[b].v, ALU.subtract, eng="pool")
                    P.cp(tot8[b].v, pp[b].v.re("p (c t) -> p c t", t=64)[:, :, 63], eng="pool")
                    P.tt(pp[b].v.re("p (c t) -> p c t", t=64), lf[b].v.re("p (c t) -> p c t", t=64),
                         tot8[b].v.us(2).bc([128, 8, 64]), ALU.add)
                G = pp[b]
                P.act(eg[b].v, G.v, AF.Exp)
                ev = eg[b].v.re("p (c t) -> p c t", t=64)
                P.cp(etot[:, d, hd, tc * 8:(tc + 1) * 8], (ev[:, :, 63] if d == 0 else ev[:, :, 0]), eng="pool")
                P.tt(qo[b].v, pq[s].v, eg[b].v, ALU.mult)
                P.tt(lk[b].v, lk[b].v, G.v, ALU.subtract, eng="pool")
                P.act(ko[b].v, lk[b].v, AF.Exp, bias=ln1mlb[:, col:col + 1])
                P.dma(qg[d][hd, :, t0:t0 + 512], qo[b].v)
                P.dma(kg[d][hd, :, t0:t0 + 512], ko[b].v)
    P.end_phase()
    P.begin_phase()
    stage = P.sb("stB", [128, 8, 512], F32)
    WtB = [P.sb("WtB%d" % i, [128, 8, 512], BF16) for i in range(2)]
    pv = [P.ps("pv%d" % i, [128, 512], F32) for i in range(2)]
    vo = [P.sb("vo%d" % i, [128, 512], BF16) for i in range(2)]
    zo = [P.sb("zo%d" % i, [128, 512], F32) for i in range(2)]
    it = 0
    for isz in range(2):
        for c in range(4):
            W = WtB[c % 2]
            prep_w_chunk(P, [W], w_in, 6144 + isz * 2048 + c * 512, 512, stage)
            for tt in range(NT):
                s = it % 2
                it += 1
                mm_proj_tm(P, pv[s].v, C, [W], [0], tt, 512)
                if isz == 0:
                    P.cp(vo[s].v, pv[s].v, eng="act")
                    P.dma(S["vtm"][tt * 128:(tt + 1) * 128, c * 512:(c + 1) * 512], vo[s].v)
                else:
                    P.act(zo[s].v, pv[s].v, AF.Silu)
                    P.dma(S["zg"][tt * 128:(tt + 1) * 128, c * 512:(c + 1) * 512], zo[s].v)
    P.end_phase()
    P.begin_phase()
    mask = [P.sb("mk%d" % i, [128, 128], F32) for i in range(2)]
    P.dma(mask[0].v, C.consts["hg_mf"].v)
    P.dma(mask[1].v, C.consts["hg_mb"].v)
    gb = P.sb("ngb", [128, DI], F32)
    row_bcast(P, gb.v, norm_g, slice(None))
    Sf = P.sb("Sf", [128, HG_H, 128], F32)
    Sb = P.sb("Sb", [128, HG_H, 128], BF16)
    qT = [P.sb("qT%d" % i, [128, HG_H, 128], BF16) for i in range(2)]
    kT = [P.sb("kT%d" % i, [128, HG_H, 128], BF16) for i in range(2)]
    vt = [P.sb("vt%d" % i, [128, DI], BF16) for i in range(2)]
    ot = [P.sb("ot%d" % i, [128, DI], F32) for i in range(2)]
    of_t = [P.sb("oft%d" % i, [128, DI], F32) for i in range(2)]
    zt = [P.sb("zt%d" % i, [128, DI], F32) for i in range(2)]
    sq = P.sb("sq", [128, DI], F32)
    ss = P.sb("ss", [128, HG_H], F32)
    yb = [P.sb("yb%d" % i, [128, DI], BF16) for i in range(2)]
    GH = 4
    pA = [P.ps("pA%d" % i, [128, GH, 128], F32) for i in range(2)]
    pT = [P.ps("pT%d" % i, [128, GH, 128], BF16) for i in range(2)]
    pO = [P.ps("pO%d" % i, [128, GH, 128], F32) for i in range(2)]
    pU = [P.ps("pU%d" % i, [128, GH, 128], F32) for i in range(2)]
    am = [P.sb("am%d" % i, [128, GH, 128], BF16) for i in range(2)]
    ktm = [P.sb("ktm%d" % i, [128, GH, 128], BF16) for i in range(2)]
    gi = 0
    for d in range(2):
        P.memset(Sf.v, 0.0, eng="dve")
        P.memset(Sb.v, 0.0, eng="pool")
        order = list(range(NT)) if d == 0 else list(range(NT - 1, -1, -1))
        for n_, tt in enumerate(order):
            s = n_ % 2
            tsl = slice(tt * 128, (tt + 1) * 128)
            P.dma(qT[s].v, qg[d][:, :, tsl].re("h k t -> k h t"))
            P.dma(kT[s].v, kg[d][:, :, tsl].re("h k t -> k h t"))
            P.dma(vt[s].v, S["vtm"][tsl, :])
            if d == 1:
                P.dma(of_t[s].v, S["of"][tsl, :])
                P.dma(zt[s].v, S["zg"][tsl, :])
            chunks = [0, 1] if d == 0 else [1, 0]
            for g in range(HG_H // GH):
                p = gi % 2
                gi += 1
                hs = range(g * GH, (g + 1) * GH)
                for i, hd in enumerate(hs):
                    P.mm(pA[p][:, i, :], kT[s][:, hd, :], qT[s][:, hd, :])
                    P.tr(pT[p][:, i, :], kT[s][:, hd, :], C.ident.v)
                P.tt(am[p].v, pA[p].v, mask[d].v.us(1).bc([128, GH, 128]), ALU.mult)
                P.cp(ktm[p].v, pT[p].v, eng="act")
                for ci, ch in enumerate(chunks):
                    rs = slice(ch * 64, (ch + 1) * 64)
                    cidx = tt * 2 + ch
                    for i, hd in enumerate(hs):
                        vs = vt[s][:, hd * 128:(hd + 1) * 128]
                        P.mm(pO[p][rs, i, :], am[p][:, i, rs], vs, start=True, stop=False)
                        P.mm(pO[p][rs, i, :], qT[s][:, hd, rs], Sb[:, hd, :], start=False, stop=True)
                        P.mm(pU[p][:, i, :], ktm[p][rs, i, :], vt[s][rs, hd * 128:(hd + 1) * 128])
                    for i, hd in enumerate(hs):
                        ec = etot[:, d, hd, cidx:cidx + 1]
                        P.ts(Sf[:, hd, :], Sf[:, hd, :], ec, ALU.mult)
                        P.stt(Sf[:, hd, :], pU[p][:, i, :], ec, Sf[:, hd, :], ALU.mult, ALU.add)
                        P.cp(Sb[:, hd, :], Sf[:, hd, :], eng="pool")
                osl = ot[s][:, g * GH * 128:(g + 1) * GH * 128]
                if d == 0:
                    P.cp(osl, pO[p].v.re("p g v -> p (g v)"), eng="act")
                else:
                    P.tt(osl, pO[p].v.re("p g v -> p (g v)"), of_t[s][:, g * GH * 128:(g + 1) * GH * 128], ALU.add)
            if d == 0:
                P.dma(S["of"][tsl, :], ot[s].v)
            else:
                o3 = ot[s].v.re("p (h v) -> p h v", v=128)
                P.tt(sq.v, ot[s].v, ot[s].v, ALU.mult, eng="pool")
                P.op("dve", lambda e: e.tensor_reduce(out=ss.v.ap, in_=sq.v.re("p (h v) -> p h v", v=128).ap,
                                                     axis=AX.X, op=ALU.add), [ss.v], [sq.v])
                P.act(ss.v, ss.v, AF.Sqrt, scale=1.0 / 128, bias=C.eps_col[:, 0:1])
                P.op("dve", lambda e: e.reciprocal(out=ss.v.ap, in_=ss.v.ap), [ss.v], [ss.v])
                P.tt(o3, o3, ss.v.us(2).bc([128, HG_H, 128]), ALU.mult)
                P.tt(zt[s].v, zt[s].v, gb.v, ALU.mult, eng="pool")
                P.tt(yb[s].v, ot[s].v, zt[s].v, ALU.mult)
                P.dma(S["y"][tsl, :], yb[s].v)
    P.end_phase()
    outproj_phase(P, C, w_out, ln_g, ln_b, last)


def outproj_phase(P, C, w_out, ln_g, ln_b, last):
    P.begin_phase()
    op = OutProj(P, C, w_out, ln_g, ln_b, last=last)
    yb = [P.sb("ypb%d" % i, [128, DI], BF16) for i in range(2)]
    for tt in range(NT):
        P.dma(yb[tt % 2].v, C.scr["y"][tt * 128:(tt + 1) * 128, :])
        op.tile(tt, yb[tt % 2].v)
    P.end_phase()


LAYER_KIND = ["hyena", "ssd", "hgrn2", "hyena"]
PARAMS = {
    "hyena": ["w_in", "conv_w", "conv_b", "filt_w1", "filt_b1", "filt_w2", "filt_b2", "filt_w3", "filt_b3",
              "filt_freq", "filt_w_out", "skip", "w_out", "ln_g", "ln_b"],
    "ssd": ["w_in", "conv_w", "conv_b", "dt_bias", "a_log", "d_skip", "norm_g", "w_out", "ln_g", "ln_b"],
    "hgrn2": ["w_in", "norm_g", "w_out", "ln_g", "ln_b"],
}
SHAPES = {
    "hyena": {"w_in": [D, 8192], "conv_w": [3, 6144], "conv_b": [6144], "filt_w1": [33, 64], "filt_b1": [64],
              "filt_w2": [64, 64], "filt_b2": [64], "filt_w3": [64, 64], "filt_b3": [64], "filt_freq": [64],
              "filt_w_out": [64, 4096], "skip": [DI], "w_out": [DI, D], "ln_g": [D], "ln_b": [D]},
    "ssd": {"w_in": [D, 5184], "conv_w": [5, 3072], "conv_b": [3072], "dt_bias": [2, 32], "a_log": [2, 32],
            "d_skip": [32], "norm_g": [DI], "w_out": [DI, D], "ln_g": [D], "ln_b": [D]},
    "hgrn2": {"w_in": [D, 10240], "norm_g": [DI], "w_out": [DI, D], "ln_g": [D], "ln_b": [D]},
}


def all_consts():
    c = {"ident": _ident_bf16(), "identf": np.eye(128, dtype=np.float32)}
    c.update(hgrn2_consts())
    c.update(ssd_consts())
    c.update(hyena_consts())
    return c


def build_program(layers):
    P = Prog()
    C = Ctx()
    C.x = P.dram("x", [L, D], F32, kind="ExternalInput")
    C.out = P.dram("out", [L, D], F32, kind="ExternalOutput")
    C.h = P.dram("h_scr", [L, D], F32)
    C.consts = {}
    cst = all_consts()
    for k, v in cst.items():
        dt = BF16 if v.dtype != np.float32 else F32
        C.consts[k] = P.dram("c_" + k, list(v.shape), dt, kind="ExternalInput")
    C.d_ident = C.consts["ident"]
    C.d_identf = C.consts["identf"]
    C.lbraw = P.dram("hgrn_lower_bounds", [4, 4096], F32, kind="ExternalInput")
    C.prm = {}
    for li in layers:
        kind = LAYER_KIND[li]
        for nme in PARAMS[kind]:
            key = "l%d_%s" % (li, nme)
            C.prm[key] = P.dram(key, SHAPES[kind][nme], F32, kind="ExternalInput")
    S = {}
    for nme in ["qgf", "qgb", "kgf", "kgb"]:
        S[nme] = P.dram("s_" + nme, [HG_H, 128, L], BF16)
    S["vtm"] = P.dram("s_vtm", [L, DI], BF16)
    S["zg"] = P.dram("s_zg", [L, DI], F32)
    S["of"] = P.dram("s_of", [L, DI], F32)
    S["y"] = P.dram("s_y", [L, DI], BF16)
    S["xs"] = P.dram("s_xs", [L, DI], F32)
    S["btm"] = P.dram("s_btm", [L, 512], BF16)
    S["bT"] = P.dram("s_bT", [4, 128, L], BF16)
    S["cT"] = P.dram("s_cT", [4, 128, L], BF16)
    S["dt"] = P.dram("s_dt", [L, 64], F32)
    S["Zs"] = P.dram("s_Zs", [128, 64, DI], BF16)
    S["Zv"] = P.dram("s_Zv", [128, 64, DI], BF16)
    S["Kf"] = P.dram("s_Kf", [NK2, 128, DI], F32)
    S["adt"] = P.dram("s_adt", [L, 64], F32)
    C.scr = S
    setup_common(P, C)
    C.hg_etot = P.sb("etot", [128, 2, HG_H, 64], F32, perm=True)
    load_x_to_hT(P, C)
    for n_, li in enumerate(layers):
      try:
          kind = LAYER_KIND[li]
          last = (n_ == len(layers) - 1)
          pr = lambda nme: C.prm["l%d_%s" % (li, nme)]
          if kind == "hgrn2":
              hgrn2_layer(P, C, li, pr("w_in"), C.lbraw, pr("norm_g"), pr("w_out"), pr("ln_g"), pr("ln_b"), last)
          elif kind == "ssd":
              ssd_layer(P, C, li, pr("w_in"), pr("conv_w"), pr("conv_b"), pr("dt_bias"), pr("a_log"), pr("d_skip"),
                        pr("norm_g"), pr("w_out"), pr("ln_g"), pr("ln_b"), last)
          else:
              hyena_layer(P, C, li, {nme: pr(nme) for nme in PARAMS["hyena"]}, last)
      except StopBuild:
        break
    P.barrier()
    return P, C


def in_map_for(inputs, b, layers):
    m = {"x": np.ascontiguousarray(inputs["x"][b]), "hgrn_lower_bounds": inputs["hgrn_lower_bounds"]}
    for k, v in all_consts().items():
        m["c_" + k] = v
    for li in layers:
        for nme in PARAMS[LAYER_KIND[li]]:
            key = "l%d_%s" % (li, nme)
            m[key] = inputs[key]
    return m


def pipeline(n, stages, offs=None):
    ns = len(stages)
    if offs is None:
        offs = list(range(ns))
    for step in range(n + max(offs)):
        for k in range(ns - 1, -1, -1):
            i = step - offs[k]
            if 0 <= i < n:
                stages[k](i)


def ssd_scan_phase(P, C, norm_g, d_skip):
    S = C.scr
    P.begin_phase()
    M1 = [P.sb("M1%d" % i, [128, 128], F32) for i in range(2)]
    P.dma(M1[0].v, C.consts["sd_mf"].v)
    P.dma(M1[1].v, C.consts["sd_mb"].v)
    gb = P.sb("sgb", [128, DI], F32)
    row_bcast(P, gb.v, norm_g, slice(None))
    dsk = P.sb("dsk", [128, 32], F32)
    row_bcast(P, dsk.v, d_skip, slice(None))
    Sp = P.sb("Sp", [128, 4, 512], F32)
    Spb = P.sb("Spb", [128, 4, 512], BF16)
    NL = 2
    xs = [P.sb("xs%d" % i, [128, DI], F32) for i in range(NL)]
    dtt = [P.sb("dtt%d" % i, [128, 64], F32) for i in range(NL)]
    adt = [P.sb("adt%d" % i, [128, 64], F32) for i in range(NL)]
    btm = [P.sb("btm%d" % i, [128, 512], BF16) for i in range(NL)]
    bT = [P.sb("bT%d" % i, [128, 4, 128], BF16) for i in range(NL)]
    cT = [P.sb("cT%d" % i, [128, 4, 128], BF16) for i in range(NL)]
    yft = [P.sb("yft%d" % i, [128, DI], F32) for i in range(2)]
    zt = [P.sb("szt%d" % i, [128, DI], F32) for i in range(2)]
    ct = [P.sb("ct%d" % i, [128, 64], F32) for i in range(2)]
    ncs = [P.sb("ncs%d" % i, [128, 32], F32) for i in range(2)]
    ecs = [P.sb("ecs%d" % i, [128, 32], F32) for i in range(2)]
    dst_ = [P.sb("dst%d" % i, [128, 32], F32) for i in range(2)]
    etot = [P.sb("setot%d" % i, [128, 32], F32) for i in range(2)]
    xdt = [P.sb("xdt%d" % i, [128, DI], BF16) for i in range(2)]
    xdtd = [P.sb("xdtd%d" % i, [128, DI], BF16) for i in range(2)]
    CBm = [P.sb("CBm%d" % i, [128, 128], F32) for i in range(1)] * 2
    X = [P.sb("X%d" % i, [128, 8, 128], F32) for i in range(1)] * 2
    X2 = [P.sb("X2%d" % i, [128, 8, 128], F32) for i in range(1)] * 2
    Mh = [P.sb("Mh%d" % i, [128, 8, 128], BF16) for i in range(2)]
    tmp = [P.sb("stmp%d" % i, [128, 512], F32) for i in range(1)] * 2
    yt = [P.sb("syt%d" % i, [128, DI], F32) for i in range(2)]
    ss = P.sb("sss", [128, 4], F32)
    yb = [P.sb("syb%d" % i, [128, DI], BF16) for i in range(1)] * 2
    p_ct = P.ps("p_ct", [128, 64], F32)
    p_cb = [P.ps("p_cb%d" % i, [128, 128], F32) for i in range(2)]
    p_row = P.ps("p_row", [128, 8, 128], F32)
    p_y = P.ps("p_y", [128, 512], F32)
    p_st = P.ps("p_st", [128, 512], F32)
    p_yo = P.ps("p_yo", [128, 512], F32)
    NTD = NT if DBG_LEVEL >= 99 else DBG_NT
    tiles = []
    for d in range(2):
        order = list(range(NT)) if d == 0 else list(range(NT - 1, -1, -1))
        tiles += [(d, tt) for tt in order[:NTD]]
    NG = len(tiles) * 4

    def info(i):
        Ti = i // 4
        d, tt = tiles[Ti]
        return Ti, d, tt, i % 4

    def st_load(i):
        Ti, d, tt, g = info(i)
        if g != 0:
            return
        s = Ti % NL
        tsl = slice(tt * 128, (tt + 1) * 128)
        P.dma(xs[s].v, S["xs"][tsl, :])
        P.dma(dtt[s].v, S["dt"][tsl, :])
        P.dma(adt[s].v, S["adt"][tsl, :])
        P.dma(btm[s].v, S["btm"][tsl, :])
        P.dma(bT[s].v, S["bT"][:, :, tsl].re("g n t -> n g t"))
        P.dma(cT[s].v, S["cT"][:, :, tsl].re("g n t -> n g t"))
        if d == 1:
            P.dma(zt[Ti % 2].v, S["zg"][tsl, :])

    def st_pro(i):
        Ti, d, tt, g = info(i)
        if g != 0:
            return
        s = Ti % NL
        u = Ti % 2
        hc = slice(d * 32, (d + 1) * 32)
        P.mm(p_ct[:, 0:32], M1[d].v, adt[s][:, hc])
        P.mm(p_ct[:, 32:64], C.ones_f.v, adt[s][:, hc])
        P.cp(ct[u].v, p_ct.v)
        P.ts(ncs[u].v, ct[u][:, 0:32], -1.0, ALU.mult)
        P.act(ecs[u].v, ct[u][:, 0:32], AF.Exp)
        P.act(etot[u].v, ct[u][:, 32:64], AF.Exp)
        P.tt(dst_[u].v, ct[u][:, 32:64], ct[u][:, 0:32], ALU.subtract)
        P.act(dst_[u].v, dst_[u].v, AF.Exp)
        P.tt(dst_[u].v, dst_[u].v, dtt[s][:, hc], ALU.mult)
        x3 = xs[s].v.re("p (h q) -> p h q", q=64)
        P.tt(xdt[u].v.re("p (h q) -> p h q", q=64), x3, dtt[s][:, hc].us(2).bc([128, 32, 64]), ALU.mult)
        P.tt(xdtd[u].v.re("p (h q) -> p h q", q=64), x3, dst_[u].v.us(2).bc([128, 32, 64]), ALU.mult, eng="pool")

    def st1(i):
        Ti, d, tt, g = info(i)
        s = Ti % NL
        k = i % 2
        hg = slice(d * 32 + g * 8, d * 32 + (g + 1) * 8)
        P.mm(p_cb[k].v, bT[s][:, g, :], cT[s][:, g, :])
        P.tt(X[k].v, M1[d].v.us(1).bc([128, 8, 128]), adt[s][:, hg].us(2).bc([128, 8, 128]), ALU.mult, eng="pool")
        for q in range(2):
            P.mm(p_row[:, q * 4:(q + 1) * 4, :], C.ones_f.v, X[k][:, q * 4:(q + 1) * 4, :])

    def st2(i):
        Ti, d, tt, g = info(i)
        u = Ti % 2
        k = i % 2
        hl = slice(g * 8, (g + 1) * 8)
        P.tt(CBm[k].v, p_cb[k].v, M1[d].v, ALU.mult)
        P.tt(X2[k].v, p_row.v, ncs[u][:, hl].us(2).bc([128, 8, 128]), ALU.add)
        P.ts(X2[k].v, X2[k].v, 0.0, ALU.min)
        P.act(X2[k].v, X2[k].v, AF.Exp)
        P.tt(Mh[k].v, X2[k].v, CBm[k].v.us(1).bc([128, 8, 128]), ALU.mult, eng="pool")

    def st3(i):
        Ti, d, tt, g = info(i)
        s = Ti % NL
        u = Ti % 2
        k = i % 2
        hl = slice(g * 8, (g + 1) * 8)
        tsl = slice(tt * 128, (tt + 1) * 128)
        if g == 0 and tt == (0 if d == 0 else NT - 1):
            P.memset(Sp.v, 0.0, eng="dve")
            P.memset(Spb.v, 0.0, eng="pool")
        if g == 0 and d == 1:
            P.dma(yft[u].v, S["of"][tsl, :])
        for h in range(8):
            hh = g * 8 + h
            P.mm(p_y[:, h * 64:(h + 1) * 64], Mh[k][:, h, :], xdt[u][:, hh * 64:(hh + 1) * 64])
        P.mm(p_yo.v, cT[s][:, g, :], Spb[:, g, :])
        P.mm(p_st.v, btm[s][:, g * 128:(g + 1) * 128], xdtd[u][:, g * 512:(g + 1) * 512])
        P.tt(tmp[k].v.re("p (h q) -> p h q", q=64), p_yo.v.re("p (h q) -> p h q", q=64),
             ecs[u][:, hl].us(2).bc([128, 8, 64]), ALU.mult)
        P.tt(yt[u][:, g * 512:(g + 1) * 512], p_y.v, tmp[k].v, ALU.add)
        sg = Sp[:, g, :]
        P.tt(sg.re("p (h q) -> p h q", q=64), sg.re("p (h q) -> p h q", q=64),
             etot[u][:, hl].us(2).bc([128, 8, 64]), ALU.mult, eng="pool")
        P.tt(sg, sg, p_st.v, ALU.add)
        P.cp(Spb[:, g, :], sg, eng="act")
        if g != 3:
            return
        if d == 0:
            P.dma(S["of"][tsl, :], yt[u].v)
            return
        x3 = xs[s].v.re("p (h q) -> p h q", q=64)
        sq = yft[u]
        P.tt(yt[u].v, yt[u].v, yft[u].v, ALU.add)
        P.tt(sq.v.re("p (h q) -> p h q", q=64), x3, dsk.v.us(2).bc([128, 32, 64]), ALU.mult, eng="pool")
        P.tt(yt[u].v, yt[u].v, sq.v, ALU.add)
        P.tt(yt[u].v, yt[u].v, zt[u].v, ALU.mult)
        P.tt(sq.v, yt[u].v, yt[u].v, ALU.mult, eng="pool")
        P.op("dve", lambda e: e.tensor_reduce(out=ss.v.ap, in_=sq.v.re("p (g v) -> p g v", v=512).ap,
                                             axis=AX.X, op=ALU.add), [ss.v], [sq.v])
        P.act(ss.v, ss.v, AF.Sqrt, scale=1.0 / 512, bias=C.eps_col[:, 0:1])
        P.op("dve", lambda e: e.reciprocal(out=ss.v.ap, in_=ss.v.ap), [ss.v], [ss.v])
        P.tt(yt[u].v.re("p (g v) -> p g v", v=512), yt[u].v.re("p (g v) -> p g v", v=512),
             ss.v.us(2).bc([128, 4, 512]), ALU.mult, eng="pool")
        P.tt(yb[u].v, yt[u].v, gb.v, ALU.mult)
        P.dma(S["y"][tsl, :], yb[u].v)

    pipeline(NG, [st_load, st_pro, st1, st2, st3], [2, 4, 5, 6, 7])
    P.end_phase()


def ssd_consts():
    r = np.arange(128)[:, None]
    t = np.arange(128)[None, :]
    return {"sd_mf": (r <= t).astype(np.float32), "sd_mb": (r >= t).astype(np.float32)}


def ssd_layer(P, C, li, w_in, conv_w, conv_b, dt_bias, a_log, d_skip, norm_g, w_out, ln_g, ln_b, last):
    S = C.scr
    NTAP = 5
    shifts = [-2, -1, 0, 1, 2]
    P.begin_phase()
    stage = P.sb("sstA", [128, 8, 512], F32)
    cwb = P.sb("cwb", [128, NTAP, 512], F32)
    Wz = [P.sb("Wz%d" % i, [128, 8, 512], BF16) for i in range(2)]
    Wt = [P.sb("Wc%d" % j, [128, 8, 512], BF16) for j in range(NTAP)]
    cbf = P.sb("cbf", [1, 3072], F32)
    cbb = P.sb("cbb", [1, 3072], BF16)
    P.dma(cbf.v, conv_b.v.us(0))
    P.cp(cbb.v, cbf.v)
    pz = [P.ps("pz%d" % i, [128, 512], F32) for i in range(2)]
    zo = [P.sb("szo%d" % i, [128, 512], F32) for i in range(2)]
    bo = [P.sb("sbo%d" % i, [128, 512], BF16) for i in range(2)]
    it = 0
    for c in range(0 if DBG_SKIPA else 4):
        W = Wz[c % 2]
        prep_w_chunk(P, [W], w_in, c * 512, 512, stage)
        for tt in range(NT):
            s = it % 2
            it += 1
            mm_proj_tm(P, pz[s].v, C, [W], [0], tt, 512)
            P.act(zo[s].v, pz[s].v, AF.Silu)
            P.dma(S["zg"][tt * 128:(tt + 1) * 128, c * 512:(c + 1) * 512], zo[s].v)
    for c in range(0 if DBG_SKIPA else 5):
        prep_w_chunk(P, Wt, w_in, 2048 + c * 512, 512, stage, convw_d=conv_w, cc0=c * 512, cwb=cwb, ntaps=NTAP)
        for tt in range(NT):
            s = it % 2
            it += 1
            mm_proj_tm(P, pz[s].v, C, Wt, shifts, tt, 512, bias=cbb[0:1, c * 512:(c + 1) * 512])
            if c < 4:
                P.act(zo[s].v, pz[s].v, AF.Silu)
                P.dma(S["xs"][tt * 128:(tt + 1) * 128, c * 512:(c + 1) * 512], zo[s].v)
            else:
                P.act(bo[s].v, pz[s].v, AF.Silu)
                P.dma(S["btm"][tt * 128:(tt + 1) * 128, :], bo[s].v)
    bcol = P.sb("bcol", [128, 8], F32)
    P.dma(bcol.v, conv_b[2048:3072].re("(g n) -> n g", n=128), slow=True)
    for bc_ in range(0 if DBG_SKIPA else 2):
        prep_w_chunk(P, Wt, w_in, 4096 + bc_ * 512, 512, stage, convw_d=conv_w, cc0=2048 + bc_ * 512, cwb=cwb,
                     ntaps=NTAP)
        dst = S["bT"] if bc_ == 0 else S["cT"]
        for g in range(4):
            for tc in range(L // 512):
                s = it % 2
                it += 1
                for j in range(NTAP):
                    mm_proj_fm(P, pz[s].v, C, Wt[j], g * 128, tc * 512, 512, shift=shifts[j],
                               start=(j == 0), stop=(j == NTAP - 1))
                P.act(bo[s].v, pz[s].v, AF.Silu, bias=bcol[:, bc_ * 4 + g: bc_ * 4 + g + 1])
                P.dma(dst[g, :, tc * 512:(tc + 1) * 512], bo[s].v)
    Wd = Wz[0]
    prep_w_chunk(P, [Wd], w_in, 5120, 64, stage)
    dtb = P.sb("dtb", [128, 64], F32)
    arow = P.sb("arow", [128, 64], F32)
    P.dma(dtb.v, V(dt_bias, dt_bias.t.rearrange("a b -> (a b)").partition_broadcast(128)))
    P.dma(arow.v, V(a_log, a_log.t.rearrange("a b -> (a b)").partition_broadcast(128)))
    P.act(arow.v, arow.v, AF.Exp)
    dto = [P.sb("dto%d" % i, [128, 64], F32) for i in range(2)]
    ado = [P.sb("ado%d" % i, [128, 64], F32) for i in range(2)]
    for tt in range(NT):
        s = tt % 2
        mm_proj_tm(P, pz[s][:, 0:64], C, [Wd], [0], tt, 64)
        P.tt(dto[s].v, pz[s][:, 0:64], dtb.v, ALU.add)
        P.act(dto[s].v, dto[s].v, AF.Exp)
        P.act(dto[s].v, dto[s].v, AF.Ln, bias=1.0)
        P.stt(ado[s].v, dto[s].v, -1.0, arow.v, ALU.mult, ALU.mult)
        P.dma(S["dt"][tt * 128:(tt + 1) * 128, :], dto[s].v)
        P.dma(S["adt"][tt * 128:(tt + 1) * 128, :], ado[s].v)
    P.end_phase()
    ssd_scan_phase(P, C, norm_g, d_skip)
    outproj_phase(P, C, w_out, ln_g, ln_b, last)


NF = 8192
NK2 = 65


def hyena_consts():
    import ml_dtypes
    bf = ml_dtypes.bfloat16
    n = NF
    tp = 2.0 * np.pi
    p = np.arange(128)
    k2 = np.arange(65)
    k2s = np.arange(1, 64)
    F1 = np.zeros((128, 128))
    F1[:, :65] = np.cos(tp * ((p[:, None] * k2[None, :]) % 128) / 128)
    F1[:, 65:] = -np.sin(tp * ((p[:, None] * k2s[None, :]) % 128) / 128)
    pp = np.arange(64)
    m = np.full(65, 2.0)
    m[0] = 1.0
    m[64] = 1.0
    Finv = np.zeros((128, 64))
    Finv[:65, :] = (m[:, None] / n) * np.cos(tp * ((k2[:, None] * pp[None, :]) % 128) / 128)
    Finv[65:, :] = -(2.0 / n) * np.sin(tp * ((k2s[:, None] * pp[None, :]) % 128) / 128)
    a = np.arange(64)
    k1 = np.arange(64)
    TA = np.zeros((65, 128, 128))
    TB = np.zeros((65, 128, 128))
    TC = np.zeros((65, 128, 128))
    for q in range(65):
        th = tp * ((a[:, None] * (q + 128 * k1[None, :])) % n) / n
        Tr = np.cos(th)
        Ti = -np.sin(th)
        TA[q, :64, :64] = Tr
        TA[q, :64, 64:] = Ti
        TA[q, 64:, :64] = -Ti
        TA[q, 64:, 64:] = Tr
        TB[q, :64, :64] = -Ti
        TB[q, :64, 64:] = Tr
        TB[q, 64:, :64] = -Tr
        TB[q, 64:, 64:] = -Ti
        Ur = np.cos(th).T
        Ui = np.sin(th).T
        TC[q, :64, :64] = Ur
        TC[q, 64:, :64] = -Ui
        TC[q, :64, 64:] = Ui
        TC[q, 64:, 64:] = Ur
    j = (64 * p[None, :] + a[:, None]).reshape(-1)
    pos = np.where(j < L, j, n - j)
    pos[j == L] = 0
    tl = np.linspace(0.0, 1.0, L)
    t = tl[pos]
    w = tp * pos / L
    f = np.linspace(1e-4, 15.0, 16)
    z = np.concatenate([t[:, None], np.cos(f[None, :] * w[:, None]), -np.sin(f[None, :] * w[:, None])], axis=1)
    tdec = t.copy()
    tdec[j == L] = 1e4
    ntpos = -(tdec.reshape(64, 128).T)
    min_d = math.log(1e-2) / 0.3
    max_d = math.log(1e-2) / 1.5
    deltas = np.abs(np.linspace(min_d, max_d, DI))
    return {
        "hy_F1": F1.astype(np.float32).astype(bf),
        "hy_Finv": Finv.astype(np.float32).astype(bf),
        "hy_TA": np.ascontiguousarray(TA.transpose(1, 0, 2)).astype(np.float32).astype(bf),
        "hy_TB": np.ascontiguousarray(TB.transpose(1, 0, 2)).astype(np.float32).astype(bf),
        "hy_TC": np.ascontiguousarray(TC.transpose(1, 0, 2)).astype(np.float32).astype(bf),
        "hy_zT": np.ascontiguousarray(z.T).astype(np.float32),
        "hy_ntpos": ntpos.astype(np.float32),
        "hy_deltas": deltas.astype(np.float32),
    }


def sin_reduced(P, o, arg, tmpf, tmpi, tmp2):
    tp = 2.0 * math.pi
    P.ts(tmpf, arg, 1.0 / tp, ALU.mult, 64.5, ALU.add)
    P.cp(tmpi, tmpf)
    P.cp(tmpf, tmpi)
    P.ts(tmpf, tmpf, -64.0, ALU.add, -tp, ALU.mult)
    P.tt(tmpf, tmpf, arg, ALU.add)
    P.ts(tmp2, tmpf, math.pi, ALU.is_gt, -tp, ALU.mult)
    P.tt(tmpf, tmpf, tmp2, ALU.add)
    P.ts(tmp2, tmpf, -math.pi, ALU.is_lt, tp, ALU.mult)
    P.tt(tmpf, tmpf, tmp2, ALU.add)
    P.ts(tmpf, tmpf, 3.14159, ALU.min, -3.14159, ALU.max)
    P.act(o, tmpf, AF.Sin)


def hyena_layer(P, C, li, prm, last):
    S = C.scr
    w_in, conv_w, conv_b = prm["w_in"], prm["conv_w"], prm["conv_b"]
    shifts = [-1, 0, 1]
    P.begin_phase()
    stage = P.sb("hstA", [128, 8, 512], F32)
    cwb = P.sb("hcwb", [128, 3, 512], F32)
    W1 = [P.sb("hW1_%d" % j, [128, 8, 512], BF16) for j in range(3)]
    W2 = [P.sb("hW2_%d" % j, [128, 8, 512], BF16) for j in range(3)]
    cbf = P.sb("hcbf", [1, 6144], F32)
    cbb = P.sb("hcbb", [1, 6144], BF16)
    P.dma(cbf.v, conv_b.v.us(0))
    P.cp(cbb.v, cbf.v)
    p1 = [P.ps("hp1_%d" % i, [128, 512], F32) for i in range(2)]
    p2 = [P.ps("hp2_%d" % i, [128, 512], F32) for i in range(2)]
    s1 = [P.sb("hs1_%d" % i, [128, 512], F32) for i in range(2)]
    go = [P.sb("hgo_%d" % i, [128, 512], F32) for i in range(2)]
    it = 0
    for c in range(4):
        prep_w_chunk(P, W1, w_in, 2048 + c * 512, 512, stage, convw_d=conv_w, cc0=2048 + c * 512, cwb=cwb, ntaps=3)
        prep_w_chunk(P, W2, w_in, 4096 + c * 512, 512, stage, convw_d=conv_w, cc0=4096 + c * 512, cwb=cwb, ntaps=3)
        for tt in range(NT):
            s = it % 2
            it += 1
            mm_proj_tm(P, p1[s].v, C, W1, shifts, tt, 512, bias=cbb[0:1, 2048 + c * 512: 2048 + (c + 1) * 512])
            mm_proj_tm(P, p2[s].v, C, W2, shifts, tt, 512, bias=cbb[0:1, 4096 + c * 512: 4096 + (c + 1) * 512])
            P.cp(s1[s].v, p1[s].v, eng="act")
            P.tt(go[s].v, p2[s].v, s1[s].v, ALU.mult)
            P.dma(S["xs"][tt * 128:(tt + 1) * 128, c * 512:(c + 1) * 512], go[s].v)
    for c in range(4):
        prep_w_chunk(P, W1, w_in, c * 512, 512, stage, convw_d=conv_w, cc0=c * 512, cwb=cwb, ntaps=3)
        prep_w_chunk(P, [W2[0]], w_in, 6144 + c * 512, 512, stage)
        for tt in range(NT):
            s = it % 2
            it += 1
            mm_proj_tm(P, p1[s].v, C, W1, shifts, tt, 512, bias=cbb[0:1, c * 512:(c + 1) * 512])
            mm_proj_tm(P, p2[s].v, C, [W2[0]], [0], tt, 512)
            P.act(s1[s].v, p2[s].v, AF.Silu)
            P.tt(go[s].v, p1[s].v, s1[s].v, ALU.mult)
            P.dma(S["zg"][tt * 128:(tt + 1) * 128, c * 512:(c + 1) * 512], go[s].v)
    P.end_phase()
    P.begin_phase()
    F1 = P.sb("F1", [128, 128], BF16)
    P.dma(F1.v, C.consts["hy_F1"].v)
    TA = P.sb("TA", [128, NK2, 128], BF16)
    P.dma(TA.v, C.consts["hy_TA"].v)
    ntp = P.sb("ntp", [128, 64], F32)
    P.dma(ntp.v, C.consts["hy_ntpos"].v)
    dl = P.sb("dl", [128, DI], F32)
    row_bcast(P, dl.v, C.consts["hy_deltas"], slice(None))
    w1 = P.sb("fw1", [33, 64], F32)
    w2 = P.sb("fw2", [64, 64], F32)
    w3 = P.sb("fw3", [64, 64], F32)
    P.dma(w1.v, prm["filt_w1"].v)
    P.dma(w2.v, prm["filt_w2"].v)
    P.dma(w3.v, prm["filt_w3"].v)
    cols = P.sb("fcols", [64, 4], F32)
    for i, nme in enumerate(["filt_freq", "filt_b1", "filt_b2", "filt_b3"]):
        P.dma(cols[:, i:i + 1], prm[nme].v.us(1))
    fb = P.sb("ffb", [64, 3], F32)
    for i in range(3):
        P.tt(fb[:, i:i + 1], cols[:, 0:1], cols[:, i + 1:i + 2], ALU.mult)
    wo_f = P.sb("fwo_f", [64, 4096], F32)
    wo = P.sb("fwo", [64, 4096], BF16)
    P.dma(wo_f.v, prm["filt_w_out"].v)
    P.cp(wo.v, wo_f.v, eng="pool")
    h3T = P.sb("h3T", [64, NF], BF16)
    zc = [P.sb("zc%d" % i, [33, 512], F32) for i in range(2)]
    arg = P.sb("farg", [64, 512], F32)
    tmf = P.sb("ftmf", [64, 512], F32)
    tmi = P.sb("ftmi", [64, 512], I32)
    tm2 = P.sb("ftm2", [64, 512], F32)
    hh = [P.sb("fhh%d" % i, [64, 512], F32) for i in range(2)]
    pm = P.ps("fpm", [64, 512], F32)
    for ch in range(NF // 512):
        s = ch % 2
        P.dma(zc[s].v, C.consts["hy_zT"][:, ch * 512:(ch + 1) * 512])
        src = zc[s].v
        for lyr, wl in enumerate([w1, w2, w3]):
            P.mm(pm.v, wl.v, src)
            P.act(arg.v, pm.v, AF.Identity, scale=cols[:, 0:1], bias=fb[:, lyr:lyr + 1])
            dst = hh[lyr % 2].v if lyr < 2 else h3T[:, ch * 512:(ch + 1) * 512]
            sin_reduced(P, dst, arg.v, tmf.v, tmi.v, tm2.v)
            src = dst
    pk = [P.ps("fpk%d" % i, [128, 512], F32) for i in range(2)]
    pz = [P.ps("fpz%d" % i, [128, 512], F32) for i in range(2)]
    dec = [P.sb("fdec%d" % i, [128, DI], F32) for i in range(2)]
    ka = [P.sb("fka%d" % i, [128, 512], BF16) for i in range(2)]
    zt = [P.sb("fzt%d" % i, [128, DI], BF16) for i in range(2)]
    it = 0
    for a in range(64):
        sa = a % 2
        P.act(dec[sa].v, dl.v, AF.Exp, scale=ntp[:, a:a + 1])
        for c in range(4):
            s = it % 2
            it += 1
            P.mm(pk[s][0:64, :], h3T[:, a * 128: a * 128 + 64], wo[:, c * 512:(c + 1) * 512])
            P.mm(pk[s][64:128, :], h3T[:, a * 128 + 64: a * 128 + 128], wo[:, 2048 + c * 512: 2048 + (c + 1) * 512])
            P.tt(ka[s].v, pk[s].v, dec[sa][:, c * 512:(c + 1) * 512], ALU.mult)
            P.mm(pz[s].v, F1.v, ka[s].v)
            P.cp(zt[sa][:, c * 512:(c + 1) * 512], pz[s].v, eng="act")
        P.dma(S["Zs"][:, a, :], zt[sa].v)
    zin = [P.sb("fzin%d" % i, [128, DI], BF16) for i in range(2)]
    kf = [P.sb("fkf%d" % i, [128, DI], F32) for i in range(2)]
    for q in range(NK2):
        s = q % 2
        P.dma(zin[s][0:64, :], S["Zs"][q, :, :])
        if 1 <= q <= 63:
            P.dma(zin[s][64:128, :], S["Zs"][64 + q, :, :])
        else:
            P.memset(zin[s][64:128, :], 0.0)
        for c in range(4):
            b = (q * 4 + c) % 2
            P.mm(pz[b].v, TA[:, q, :], zin[s][:, c * 512:(c + 1) * 512])
            P.cp(kf[s][:, c * 512:(c + 1) * 512], pz[b].v, eng=("act" if c % 2 else "dve"))
        P.dma(S["Kf"][q, :, :], kf[s].v)
    P.end_phase()
    P.begin_phase()
    F1 = P.sb("F1c", [128, 128], BF16)
    P.dma(F1.v, C.consts["hy_F1"].v)
    gf = [P.sb("cgf%d" % i, [64, DI], F32) for i in range(2)]
    gb_ = [P.sb("cgb%d" % i, [64, DI], BF16) for i in range(2)]
    zt = [P.sb("czt%d" % i, [128, DI], BF16) for i in range(2)]
    pz = [P.ps("cpz%d" % i, [128, 512], F32) for i in range(2)]
    gview = S["xs"].v.re("(p a) c -> a p c", a=64)
    for a in range(64):
        s = a % 2
        P.dma(gf[s].v, gview[a])
        P.cp(gb_[s].v, gf[s].v, eng="pool")
        for c in range(4):
            b = c % 2
            P.mm(pz[b].v, F1[0:64, :], gb_[s][:, c * 512:(c + 1) * 512])
            P.cp(zt[s][:, c * 512:(c + 1) * 512], pz[b].v, eng=("act" if c % 2 else "dve"))
        P.dma(S["Zs"][:, a, :], zt[s].v)
    P.end_phase()
    P.begin_phase()
    TA = P.sb("dTA", [128, NK2, 128], BF16)
    TB = P.sb("dTB", [128, NK2, 128], BF16)
    TC = P.sb("dTC", [128, NK2, 128], BF16)
    P.dma(TA.v, C.consts["hy_TA"].v)
    P.dma(TB.v, C.consts["hy_TB"].v)
    P.dma(TC.v, C.consts["hy_TC"].v)
    zin = [P.sb("dzin%d" % i, [128, DI], BF16) for i in range(2)]
    KA = [P.sb("dKA%d" % i, [128, DI], F32) for i in range(2)]
    KB = [P.sb("dKB%d" % i, [128, DI], F32) for i in range(2)]
    t1 = [P.sb("dt1_%d" % i, [128, 512], F32) for i in range(2)]
    t2 = [P.sb("dt2_%d" % i, [128, 512], F32) for i in range(2)]
    yb = [P.sb("dyb%d" % i, [128, 512], BF16) for i in range(2)]
    vo = [P.sb("dvo%d" % i, [128, DI], BF16) for i in range(2)]
    pA = [P.ps("dpA%d" % i, [128, 512], F32) for i in range(2)]
    pB = [P.ps("dpB%d" % i, [128, 512], F32) for i in range(2)]
    pV = [P.ps("dpV%d" % i, [128, 512], F32) for i in range(2)]
    it = 0
    for q in range(NK2):
        s = q % 2
        P.dma(zin[s][0:64, :], S["Zs"][q, :, :])
        if 1 <= q <= 63:
            P.dma(zin[s][64:128, :], S["Zs"][64 + q, :, :])
        else:
            P.memset(zin[s][64:128, :], 0.0)
        P.dma(KA[s][0:64, :], S["Kf"][q, 0:64, :])
        P.dma(KA[s][64:128, :], S["Kf"][q, 0:64, :])
        P.dma(KB[s][0:64, :], S["Kf"][q, 64:128, :])
        P.dma(KB[s][64:128, :], S["Kf"][q, 64:128, :])
        for c in range(4):
            b = it % 2
            it += 1
            cs_ = slice(c * 512, (c + 1) * 512)
            P.mm(pA[b].v, TA[:, q, :], zin[s][:, cs_])
            P.mm(pB[b].v, TB[:, q, :], zin[s][:, cs_])
            P.tt(t1[b].v, pA[b].v, KA[s][:, cs_], ALU.mult)
            P.tt(t2[b].v, pB[b].v, KB[s][:, cs_], ALU.mult)
            P.tt(yb[b].v, t1[b].v, t2[b].v, ALU.add, eng="pool")
            P.mm(pV[b].v, TC[:, q, :], yb[b].v)
            P.cp(vo[s][:, cs_], pV[b].v, eng="act")
        P.dma(S["Zv"][q, :, :], vo[s][0:64, :])
        if 1 <= q <= 63:
            P.dma(S["Zv"][64 + q, :, :], vo[s][64:128, :])
    P.end_phase()
    P.begin_phase()
    Fi = P.sb("eFi", [128, 64], BF16)
    P.dma(Fi.v, C.consts["hy_Finv"].v)
    skb = P.sb("eskb", [128, DI], F32)
    row_bcast(P, skb.v, prm["skip"], slice(None))
    vin = [P.sb("evin%d" % i, [128, 2, DI], BF16) for i in range(2)]
    gt = [P.sb("egt%d" % i, [128, DI], F32) for i in range(2)]
    zg = [P.sb("ezg%d" % i, [128, DI], F32) for i in range(2)]
    yo = [P.sb("eyo%d" % i, [128, DI], BF16) for i in range(2)]
    py = [P.ps("epy%d" % i, [128, 512], F32) for i in range(2)]
    gview = S["xs"].v.re("(p a) c -> a p c", a=64)
    zview = S["zg"].v.re("(p a) c -> a p c", a=64)
    yview = S["y"].v.re("(p a) c -> a p c", a=64)
    it = 0
    for a2 in range(32):
        s = a2 % 2
        for h in range(2):
            a = a2 * 2 + h
            P.dma(vin[s][:, h, :], S["Zv"][:, a, :])
            P.dma(gt[s][h * 64:(h + 1) * 64, :], gview[a])
            P.dma(zg[s][h * 64:(h + 1) * 64, :], zview[a])
        P.tt(gt[s].v, gt[s].v, skb.v, ALU.mult, eng="pool")
        for c in range(4):
            b = it % 2
            it += 1
            cs_ = slice(c * 512, (c + 1) * 512)
            for h in range(2):
                P.mm(py[b][h * 64:(h + 1) * 64, :], Fi.v, vin[s][:, h, cs_])
            P.tt(gt[s][:, cs_], py[b].v, gt[s][:, cs_], ALU.add)
            P.tt(yo[s][:, cs_], gt[s][:, cs_], zg[s][:, cs_], ALU.mult, eng="pool")
        for h in range(2):
            P.dma(yview[a2 * 2 + h], yo[s][h * 64:(h + 1) * 64, :])
    P.end_phase()
    outproj_phase(P, C, prm["w_out"], prm["ln_g"], prm["ln_b"], last)


_PROG_CACHE = {}


def kernel(**inputs):
    layers = [0, 1, 2, 3]
    if "prog" not in _PROG_CACHE:
        _PROG_CACHE["prog"] = build_program(layers)
    P, C = _PROG_CACHE["prog"]
    inputs = {k: np.asarray(v) for k, v in inputs.items()}
    nb = inputs["x"].shape[0]
    in_maps = [in_map_for(inputs, b, layers) for b in range(nb)]
    res = run_bass_kernel_spmd(P.nc, in_maps, core_ids=list(range(nb)))
    out = np.stack([np.asarray(res.results[b]["out"]) for b in range(nb)], axis=0)
    return out.astype(np.float32)
```

```python
import math
from contextlib import ExitStack
import numpy as np
import concourse.bass as bass
import concourse.mybir as mybir
from concourse.bass_utils import run_bass_kernel_spmd

F32 = mybir.dt.float32
BF16 = mybir.dt.bfloat16
I32 = mybir.dt.int32
AF = mybir.ActivationFunctionType
ALU = mybir.AluOpType
AX = mybir.AxisListType

D = 1024
L = 4096
DI = 2048
DEPTH = 4
ALPHA = (2.0 * DEPTH) ** 0.25
LN_EPS = 1e-5
PAD = 2
NT = L // 128
SAME_ENG_SYNC = True
NDS = 48
SCHED = True
SCHED_WINDOW = 24
DEBUG_SCR = False
DBG_STOP = ""
DBG_LEVEL = 99
DBG_SKIPA = False
DBG_DIRS = 2
DBG_SUB = 99
DBG_NT = 2
DBG_PH = 99
_phc = [0]


class Buf:
    __slots__ = ("t", "w", "r", "name", "wn", "rn")

    def __init__(self, t, name=""):
        self.t = t
        self.w = None
        self.r = {}
        self.name = name
        self.wn = None
        self.rn = []

    def __getitem__(self, idx):
        return V(self, self.t[idx])

    @property
    def v(self):
        return V(self, self.t)


class V:
    __slots__ = ("b", "ap")

    def __init__(self, b, ap):
        self.b = b
        self.ap = ap

    def __getitem__(self, idx):
        return V(self.b, self.ap[idx])

    def re(self, s, **kw):
        return V(self.b, self.ap.rearrange(s, **kw))

    def bc(self, shape):
        return V(self.b, self.ap.broadcast_to(shape))

    def tb(self, shape):
        return V(self.b, self.ap.to_broadcast(shape))

    def pb(self, n):
        return V(self.b, self.ap.partition_broadcast(n))

    def cast(self, dt):
        return V(self.b, self.ap.bitcast(dt))

    def us(self, ax):
        return V(self.b, self.ap.unsqueeze(ax))


class StopBuild(Exception):
    pass


class Eng:
    def __init__(self, name, eng, sem):
        self.name = name
        self.eng = eng
        self.sem = sem
        self.n = 0
        self.seen = {}


class Prog:
    def __init__(self):
        nc = bass.Bass("TRN2", target_bir_lowering=False)
        self.nc = nc
        self.es = ExitStack()
        self.E = {}
        for name, e in [("pe", nc.tensor), ("dve", nc.vector), ("act", nc.scalar),
                        ("pool", nc.gpsimd), ("sp", nc.sync)]:
            sem = self.es.enter_context(nc.semaphore("s_" + name))
            self.E[name] = Eng(name, e, sem)
        self.dsem = []
        for i in range(NDS):
            self.dsem.append([self.es.enter_context(nc.semaphore("d%d" % i)), 0])
        self.dnext = 0
        self.phase = None
        self.uid = 0
        self.ninst = 0
        self.deferred = False
        self.nodes = []
        self.touched = []
        self.sim_time = 0.0
        self.synced = False
        self.phase_stats = []

    def begin_phase(self):
        self.phase = ExitStack()
        self.deferred = SCHED
        if SCHED and not self.synced:
            self.barrier()
            self.synced = True

    def end_phase(self):
        if self.deferred:
            self._flush()
            self.deferred = False
        self.barrier()
        self.phase.close()
        self.phase = None
        _phc[0] += 1
        if _phc[0] >= DBG_PH:
            raise StopBuild()

    def _nm(self, name):
        self.uid += 1
        return "%s_%d" % (name, self.uid)

    def sb(self, name, shape, dt, perm=False):
        st = self.es if perm else self.phase
        t = st.enter_context(self.nc.sbuf_tensor(self._nm(name), list(shape), dt))
        return Buf(t[:], name)

    def ps(self, name, shape, dt=F32, perm=False):
        st = self.es if perm else self.phase
        esz = 4 if dt == F32 else 2
        n = 1
        for x in shape[1:]:
            n *= x
        per_bank = 2048 // esz
        npad = ((n + per_bank - 1) // per_bank) * per_bank
        t = st.enter_context(self.nc.psum_tensor(self._nm(name), [shape[0], npad], dt))
        ap = t[:][:, 0:n]
        if len(shape) == 3:
            ap = ap.rearrange("p (a b) -> p a b", a=shape[1])
        return Buf(ap, name)

    def dram(self, name, shape, dt, kind="Internal"):
        if DEBUG_SCR and kind == "Internal" and name.startswith("s_"):
            kind = "ExternalOutput"
        t = self.nc.dram_tensor(name, list(shape), dt, kind=kind)
        return Buf(t.ap(), name)

    def _wait(self, E, toks):
        need = {}
        for (s, v) in toks:
            if s is E.sem and (E.name == "pe" or not SAME_ENG_SYNC):
                continue
            k = id(s)
            if k not in need or need[k][1] < v:
                need[k] = (s, v)
        for k, (s, v) in need.items():
            if E.seen.get(k, 0) >= v:
                continue
            E.eng.wait_ge(s, v)
            self.ninst += 1
            E.seen[k] = v

    @staticmethod
    def _deps(outs, ins):
        toks = []
        for v in ins:
            if v.b.w is not None:
                toks.append(v.b.w)
        for v in outs:
            if v.b.w is not None:
                toks.append(v.b.w)
            toks.extend(v.b.r.values())
        return toks

    @staticmethod
    def _mark(tok, outs, ins):
        k = id(tok[0])
        for v in ins:
            r = v.b.r
            if k not in r or r[k][1] < tok[1]:
                r[k] = tok
        for v in outs:
            v.b.w = tok
            v.b.r = {}

    def op(self, ename, fn, outs, ins, cost=0.3):
        if self.deferred:
            self._record(ename, "op", fn, outs, ins, cost, cost)
            return
        E = self.E[ename]
        self._wait(E, self._deps(outs, ins))
        E.n += 1
        fn(E.eng).then_inc(E.sem, 1)
        self.ninst += 1
        self._mark((E.sem, E.n), outs, ins)

    def dma(self, out, in_, q="sp", slow=False):
        if self.deferred:
            nbytes = 1
            for x in out.ap.shape:
                nbytes *= x
            nbytes *= (4 if out.ap.dtype in (F32, I32) else 2)
            self._record(q, "dma", (out, in_, slow), [out], [in_], 0.12, 2.0 + nbytes / 120e3)
            return
        self._emit_dma(self.E[q], out, in_, slow, self._deps([out], [in_]), True)

    def _emit_dma(self, E, out, in_, slow, toks, mark):
        i = self.dnext
        self.dnext = (i + 1) % NDS
        sem, cnt = self.dsem[i]
        if cnt > 0:
            toks = list(toks) + [(sem, cnt)]
        self._wait(E, toks)
        if slow:
            E.eng.dma_start(out=out.ap, in_=in_.ap, allow_slow_non_contiguous=True).then_inc(sem, 16)
        else:
            E.eng.dma_start(out=out.ap, in_=in_.ap).then_inc(sem, 16)
        self.ninst += 1
        self.dsem[i][1] = cnt + 16
        tok = (sem, cnt + 16)
        if mark:
            self._mark(tok, [out], [in_])
        return tok

    def _record(self, ename, kind, payload, outs, ins, busy, lat):
        nid = len(self.nodes)
        deps = set()
        for v in ins:
            b = v.b
            if b.wn is not None:
                deps.add(b.wn)
        for v in outs:
            b = v.b
            if b.wn is not None:
                deps.add(b.wn)
            deps.update(b.rn)
        for v in ins:
            b = v.b
            b.rn.append(nid)
            self.touched.append(b)
        for v in outs:
            b = v.b
            b.wn = nid
            b.rn = []
            self.touched.append(b)
        deps.discard(nid)
        self.nodes.append([ename, kind, payload, busy, lat, deps])

    def _flush(self):
        nodes = self.nodes
        n = len(nodes)
        if n == 0:
            return
        succ = [[] for _ in range(n)]
        ndep = [0] * n
        for i, nd in enumerate(nodes):
            ndep[i] = len(nd[5])
            for d in nd[5]:
                succ[d].append(i)
        ready = [0.0] * n
        fin = [0.0] * n
        queues = {}
        for i, nd in enumerate(nodes):
            queues.setdefault(nd[0], []).append(i)
        head = {e: 0 for e in queues}
        done = [False] * n
        T = {e: 0.0 for e in queues}
        order = {e: [] for e in queues}
        W = SCHED_WINDOW
        left = n
        while left:
            best = None
            for e, q in queues.items():
                h = head[e]
                while h < len(q) and done[q[h]]:
                    h += 1
                head[e] = h
                cnt = 0
                j = h
                te = T[e]
                while j < len(q) and cnt < W:
                    i = q[j]
                    j += 1
                    if done[i]:
                        continue
                    cnt += 1
                    if ndep[i]:
                        continue
                    st = ready[i] if ready[i] > te else te
                    if best is None or st < best[0] - 1e-9 or (st < best[0] + 1e-9 and i < best[1]):
                        best = (st, i, e)
                    if st <= te:
                        break
            st, i, e = best
            nd = nodes[i]
            T[e] = st + nd[3]
            f = st + nd[4]
            fin[i] = f
            done[i] = True
            left -= 1
            order[e].append(i)
            for sidx in succ[i]:
                ndep[sidx] -= 1
                lat = f + (0.12 if nodes[sidx][0] == e else 0.3)
                if lat > ready[sidx]:
                    ready[sidx] = lat
        tok = [None] * n
        for e, lst in order.items():
            E = self.E[e]
            k = E.n
            for i in lst:
                if nodes[i][1] == "op":
                    k += 1
                    tok[i] = (E.sem, k)
        dma_engs = [e for e in order if any(nodes[i][1] == "dma" for i in order[e])]
        pending = {e: list(order[e]) for e in order}
        ptr = {e: 0 for e in order}
        progress = True
        while progress:
            progress = False
            for e in list(pending.keys()):
                lst = pending[e]
                E = self.E[e]
                while ptr[e] < len(lst):
                    i = lst[ptr[e]]
                    nd = nodes[i]
                    if any(tok[d] is None for d in nd[5]):
                        break
                    toks = [tok[d] for d in nd[5]]
                    if nd[1] == "dma":
                        out, in_, slow = nd[2]
                        tok[i] = self._emit_dma(E, out, in_, slow, toks, False)
                    else:
                        self._wait(E, toks)
                        E.n += 1
                        assert tok[i] == (E.sem, E.n)
                        nd[2](E.eng).then_inc(E.sem, 1)
                        self.ninst += 1
                    ptr[e] += 1
                    progress = True
                if ptr[e] >= len(lst):
                    del pending[e]
        assert not pending, "emission deadlock"
        self.sim_time += max(T.values())
        busy = {}
        for nd in nodes:
            busy[nd[0]] = busy.get(nd[0], 0.0) + nd[3]
        self.phase_stats.append((max(T.values()), {k: round(v) for k, v in busy.items()}, n))
        self.nodes = []
        for b in self.touched:
            b.wn = None
            b.rn = []
        self.touched = []

    def barrier(self):
        toks = [(e.sem, e.n) for e in self.E.values() if e.n > 0]
        toks += [(s, c) for s, c in self.dsem if c > 0]
        for E in self.E.values():
            self._wait(E, [t for t in toks if t[0] is not E.sem])

    @staticmethod
    def _fs(v):
        n = 1
        for x in v.ap.shape[1:]:
            n *= x
        return n

    def _cost(self, eng, v, cast=False):
        n = self._fs(v)
        if eng == "act":
            return 0.2 + n / 1400.0
        if eng == "pool":
            return 0.3 + n * (0.0035 if cast else 0.0022)
        return 0.1 + n / 960.0

    def mm(self, o, lhsT, rhs, start=True, stop=True):
        n = self._fs(rhs) * (4 if lhsT.ap.dtype == F32 else 1)
        self.op("pe", lambda e: e.matmul(o.ap, lhsT=lhsT.ap, rhs=rhs.ap, start=start, stop=stop),
                [o], [lhsT, rhs], cost=0.07 + n / 2200.0)

    def tr(self, o, a, ident):
        self.op("pe", lambda e: e.transpose(o.ap, a.ap, ident.ap), [o], [a, ident], cost=0.15)

    def act(self, o, a, func, scale=1.0, bias=0.0, accum=None, eng="act"):
        ins = [a]
        outs = [o]
        kw = {}
        if isinstance(scale, V):
            ins.append(scale)
            kw["scale"] = scale.ap
        else:
            kw["scale"] = float(scale)
        if isinstance(bias, V):
            ins.append(bias)
            kw["bias"] = bias.ap
        else:
            kw["bias"] = float(bias)
        if accum is not None:
            outs.append(accum)
            kw["accum_out"] = accum.ap
        self.op("act", lambda e: e.activation(out=o.ap, in_=a.ap, func=func, **kw), outs, ins,
                cost=self._cost("act", a))

    def tt(self, o, a, b, op, eng="dve"):
        self.op(eng, lambda e: e.tensor_tensor(out=o.ap, in0=a.ap, in1=b.ap, op=op), [o], [a, b],
                cost=self._cost(eng, o))

    def ts(self, o, a, s1, op0, s2=None, op1=None, eng="dve", accum=None):
        ins = [a]
        outs = [o]
        a1 = s1.ap if isinstance(s1, V) else float(s1)
        if isinstance(s1, V):
            ins.append(s1)
        a2 = None
        if s2 is not None:
            a2 = s2.ap if isinstance(s2, V) else float(s2)
            if isinstance(s2, V):
                ins.append(s2)
        kw = {}
        if op1 is not None:
            kw["op1"] = op1
        if accum is not None:
            kw["accum_out"] = accum.ap
            outs.append(accum)
        self.op(eng, lambda e: e.tensor_scalar(out=o.ap, in0=a.ap, scalar1=a1, scalar2=a2, op0=op0, **kw),
                outs, ins, cost=self._cost(eng, o))

    def stt(self, o, a, s, b, op0, op1, eng="dve"):
        ins = [a, b]
        sv = s.ap if isinstance(s, V) else float(s)
        if isinstance(s, V):
            ins.append(s)
        self.op(eng, lambda e: e.scalar_tensor_tensor(out=o.ap, in0=a.ap, scalar=sv, in1=b.ap, op0=op0, op1=op1),
                [o], ins, cost=self._cost(eng, o))

    def cp(self, o, a, eng="dve"):
        if eng == "act":
            self.op("act", lambda e: e.copy(out=o.ap, in_=a.ap), [o], [a], cost=self._cost("act", o))
        else:
            self.op(eng, lambda e: e.tensor_copy(out=o.ap, in_=a.ap), [o], [a], cost=self._cost(eng, o, cast=True))

    def memset(self, o, val, eng="pool"):
        self.op(eng, lambda e: e.memset(o.ap, val), [o], [], cost=self._cost(eng, o))


def _ident_bf16():
    import ml_dtypes
    return np.eye(128, dtype=np.float32).astype(ml_dtypes.bfloat16)


class Ctx:
    pass


def setup_common(P, C):
    C.hT = P.sb("hT", [128, D // 128, L + 2 * PAD], BF16, perm=True)
    C.ident = P.sb("ident", [128, 128], BF16, perm=True)
    C.identf = P.sb("identf", [128, 128], F32, perm=True)
    C.ones_bf = P.sb("ones_bf", [128, 128], BF16, perm=True)
    C.ones_f = P.sb("ones_f", [128, 128], F32, perm=True)
    P.dma(C.ident.v, C.d_ident.v)
    P.dma(C.identf.v, C.d_identf.v)
    P.memset(C.ones_bf.v, 1.0)
    P.memset(C.ones_f.v, 1.0)
    P.memset(C.hT.v, 0.0)
    C.eps_col = P.sb("eps_col", [128, 1], F32, perm=True)
    P.memset(C.eps_col.v, LN_EPS)


def load_x_to_hT(P, C):
    P.begin_phase()
    xt = [P.sb("xt%d" % i, [128, D], F32) for i in range(2)]
    xb = [P.sb("xb%d" % i, [128, D], BF16) for i in range(2)]
    pt = [P.ps("ptr%d" % i, [128, D], BF16) for i in range(2)]
    for tt in range(NT):
        s = tt % 2
        P.dma(xt[s].v, C.x[tt * 128:(tt + 1) * 128, :])
        P.dma(C.h[tt * 128:(tt + 1) * 128, :], xt[s].v)
        P.cp(xb[s].v, xt[s].v, eng="act")
        for kc in range(D // 128):
            P.tr(pt[s][:, kc * 128:(kc + 1) * 128], xb[s][:, kc * 128:(kc + 1) * 128], C.ident.v)
        P.cp(C.hT[:, :, PAD + tt * 128: PAD + (tt + 1) * 128],
             pt[s].v.re("p (k t) -> p k t", k=D // 128))
    P.end_phase()


def row_bcast(P, dst, src_buf, sl):
    P.dma(dst, V(src_buf, src_buf.t[sl].partition_broadcast(128)))


def prep_w_chunk(P, Wt, w_d, c0, n, stage, convw_d=None, cc0=0, cwb=None, ntaps=0):
    nk = w_d.t.shape[0] // 128
    P.dma(stage[:, :nk, :n], w_d[:, c0:c0 + n].re("(k p) c -> p k c", p=128))
    if ntaps == 0:
        P.cp(Wt[0][:, :nk, :n], stage[:, :nk, :n], eng="pool")
        return
    row_bcast(P, cwb[:, :ntaps, :n], convw_d, (slice(None), slice(cc0, cc0 + n)))
    for j in range(ntaps):
        P.tt(Wt[j][:, :nk, :n], stage[:, :nk, :n],
             cwb[:, j:j + 1, :n].bc([128, nk, n]), ALU.mult, eng=("pool" if j % 2 else "dve"))


def mm_proj_tm(P, ps, C, Wts, shifts, tt, n, bias=None):
    nk = D // 128
    tot = len(Wts) * nk + (1 if bias is not None else 0)
    i = 0
    for Wt, sh in zip(Wts, shifts):
        for kc in range(nk):
            t0 = PAD + tt * 128 + sh
            P.mm(ps, C.hT[:, kc, t0:t0 + 128], Wt[:, kc, :n], start=(i == 0), stop=(i == tot - 1))
            i += 1
    if bias is not None:
        P.mm(ps, C.ones_bf[0:1, :], bias, start=False, stop=True)


class OutProj:
    def __init__(self, P, C, wout_d, lng_d, lnb_d, last=False):
        self.P, self.C, self.last = P, C, last
        self.w = P.sb("wout", [128, DI // 128, D], BF16)
        st = [P.sb("wost%d" % i, [128, 4, D], F32) for i in range(2)]
        for q in range(4):
            P.dma(st[q % 2].v, wout_d[q * 512:(q + 1) * 512, :].re("(k p) c -> p k c", p=128))
            P.cp(self.w[:, q * 4:(q + 1) * 4, :], st[q % 2].v, eng=("act" if q % 2 else "dve"))
        self.g = P.sb("lng", [128, D], F32)
        self.b = P.sb("lnb", [128, D], F32)
        row_bcast(P, self.g.v, lng_d, slice(None))
        row_bcast(P, self.b.v, lnb_d, slice(None))
        nb = 2
        self.yT = [P.sb("yT%d" % i, [128, DI // 128, 128], BF16) for i in range(nb)]
        self.ho = [P.sb("ho%d" % i, [128, D], F32) for i in range(nb)]
        self.r = [P.sb("r%d" % i, [128, D], F32) for i in range(nb)]
        self.hn = [P.sb("hn%d" % i, [128, D], F32) for i in range(nb)]
        self.hb = [P.sb("hb%d" % i, [128, D], BF16) for i in range(nb)]
        self.st6 = [P.sb("st6%d" % i, [128, 2, 6], F32) for i in range(nb)]
        self.mv = [P.sb("mv%d" % i, [128, 2], F32) for i in range(nb)]
        self.rstd = [P.sb("rstd%d" % i, [128, 1], F32) for i in range(nb)]
        self.pT = [P.ps("opT%d" % i, [128, DI], BF16) for i in range(2)]
        self.po = P.ps("opo", [128, D], F32)
        self.pH = P.ps("opH", [128, D], BF16)

    def tile(self, tt, y):
        P, C = self.P, self.C
        k = tt % 2
        yT, ho, r, hn, hb, st6, mv, rstd, pT = (self.yT[k], self.ho[k], self.r[k], self.hn[k], self.hb[k],
                                                  self.st6[k], self.mv[k], self.rstd[k], self.pT[k])
        for c in range(DI // 128):
            P.tr(pT[:, c * 128:(c + 1) * 128], y[:, c * 128:(c + 1) * 128], C.ident.v)
        h8 = DI // 256
        P.cp(yT[:, 0:h8, :], pT[:, 0:DI // 2].re("p (c t) -> p c t", c=h8), eng="act")
        P.cp(yT[:, h8:, :], pT[:, DI // 2:].re("p (c t) -> p c t", c=h8), eng="dve")
        P.dma(ho.v, C.h[tt * 128:(tt + 1) * 128, :])
        for dh in range(2):
            for c in range(DI // 128):
                P.mm(self.po[:, dh * 512:(dh + 1) * 512], yT[:, c, :],
                     self.w[:, c, dh * 512:(dh + 1) * 512], start=(c == 0), stop=(c == DI // 128 - 1))
        P.stt(r.v, ho.v, ALPHA, self.po.v, ALU.mult, ALU.add)
        for q in range(2):
            P.op("dve", lambda e, q=q: e.bn_stats(out=st6[:, q, :].ap, in_=r[:, q * 512:(q + 1) * 512].ap),
                 [st6.v], [r.v], cost=0.65)
        P.op("dve", lambda e: e.bn_aggr(out=mv.v.ap, in_=st6.v.re("p a b -> p (a b)").ap),
             [mv.v], [st6.v], cost=0.15)
        P.act(rstd.v, mv[:, 1:2], AF.Sqrt, scale=1.0, bias=C.eps_col[:, 0:1])
        P.op("dve", lambda e: e.reciprocal(out=rstd.v.ap, in_=rstd.v.ap), [rstd.v], [rstd.v], cost=0.12)
        P.ts(hn.v, r.v, mv[:, 0:1], ALU.subtract, rstd[:, 0:1], ALU.mult)
        P.tt(hn.v, hn.v, self.g.v, ALU.mult, eng="pool")
        P.tt(hn.v, hn.v, self.b.v, ALU.add)
        if self.last:
            P.dma(C.out[tt * 128:(tt + 1) * 128, :], hn.v)
            return
        P.dma(C.h[tt * 128:(tt + 1) * 128, :], hn.v)
        P.cp(hb.v, hn.v, eng="act")
        for kc in range(D // 128):
            P.tr(self.pH[:, kc * 128:(kc + 1) * 128], hb[:, kc * 128:(kc + 1) * 128], C.ident.v)
        P.cp(C.hT[:, :, PAD + tt * 128: PAD + (tt + 1) * 128],
             self.pH.v.re("p (k t) -> p k t", k=D // 128))


def mm_proj_fm(P, ps, C, Wt, c0, t0, n, shift=0, start=True, stop=True):
    nk = D // 128
    for kc in range(nk):
        a = PAD + t0 + shift
        P.mm(ps, Wt[:, kc, c0:c0 + 128], C.hT[:, kc, a:a + n],
             start=(start and kc == 0), stop=(stop and kc == nk - 1))


HG_H = 16


def hgrn2_consts():
    s = np.arange(128)[:, None]
    t = np.arange(128)[None, :]
    same = (s // 64) == (t // 64)
    mf = ((s <= t) & same).astype(np.float32)
    mb = ((s >= t) & same).astype(np.float32)
    rm = np.ones((128, 512), np.float32)
    rm[:, ::64] = 0.0
    return {"hg_mf": mf, "hg_mb": mb, "hg_rm": rm}


def hgrn2_layer(P, C, li, w_in, lb_raw, norm_g, w_out, ln_g, ln_b, last):
    S = C.scr
    qg = [S["qgf"], S["qgb"]]
    kg = [S["kgf"], S["kgb"]]
    P.begin_phase()
    lbr = P.sb("lbr", [32, 4, 128], F32)
    P.dma(lbr.v, lb_raw.v.re("l (g k) -> g l k", k=128))
    lbT = P.sb("lbT", [128, 4, 32], F32)
    pl = P.ps("pl", [128, 4, 32], F32)
    for l in range(4):
        P.tr(pl[:, l, :], lbr[:, l, :], C.identf[0:32, 0:32])
    P.act(lbT.v, pl.v, AF.Exp)
    den = P.sb("den", [128, 32], F32)
    num = P.sb("num", [128, 32], F32)
    P.tt(den.v, lbT[:, 0, :], lbT[:, 1, :], ALU.add)
    P.tt(den.v, den.v, lbT[:, 2, :], ALU.add)
    P.tt(den.v, den.v, lbT[:, 3, :], ALU.add)
    P.memset(num.v, 0.0, eng="dve")
    for j in range(1, li + 1):
        P.tt(num.v, num.v, lbT[:, j, :], ALU.add)
    P.op("dve", lambda e: e.reciprocal(out=den.v.ap, in_=den.v.ap), [den.v], [den.v])
    lb = P.sb("lb", [128, 32], F32)
    ln1mlb = P.sb("ln1mlb", [128, 32], F32)
    P.tt(lb.v, num.v, den.v, ALU.mult)
    P.ts(ln1mlb.v, lb.v, -1.0, ALU.mult, 1.0, ALU.add)
    P.act(ln1mlb.v, ln1mlb.v, AF.Ln)
    etot = P.sb("etot", [128, 2, HG_H, 64], F32)
    rm = P.sb("rm", [128, 512], F32)
    P.dma(rm.v, C.consts["hg_rm"].v)
    stage = P.sb("stA", [128, 8, 384], F32)
    Wt = [P.sb("WtA%d" % i, [128, 8, 384], BF16) for i in range(2)]
    pq = [P.ps("pq%d" % i, [128, 512], F32) for i in range(2)]
    pf = [[P.ps("pf%d_%d" % (d, i), [128, 512], F32) for i in range(2)] for d in range(2)]
    nb = 2
    e_ = [P.sb("e%d" % i, [128, 512], F32) for i in range(nb)]
    l1 = [P.sb("l1%d" % i, [128, 512], F32) for i in range(nb)]
    l2 = [P.sb("l2%d" % i, [128, 512], F32) for i in range(nb)]
    lf = [P.sb("lf%d" % i, [128, 512], F32) for i in range(nb)]
    pp = [P.sb("pp%d" % i, [128, 512], F32) for i in range(nb)]
    lk = [P.sb("lk%d" % i, [128, 512], F32) for i in range(nb)]
    eg = [P.sb("eg%d" % i, [128, 512], F32) for i in range(nb)]
    qo = [P.sb("qo%d" % i, [128, 512], BF16) for i in range(nb)]
    ko = [P.sb("ko%d" % i, [128, 512], BF16) for i in range(nb)]
    tot8 = [P.sb("tot8%d" % i, [128, 8], F32) for i in range(nb)]
    it = 0
    for hd in range(HG_H):
        W = Wt[hd % 2]
        for j, c0 in enumerate([hd * 128, 2048 + hd * 128, 4096 + hd * 128]):
            P.dma(stage[:, :, j * 128:(j + 1) * 128], w_in[:, c0:c0 + 128].re("(k p) c -> p k c", p=128))
        P.cp(W.v, stage.v, eng="pool")
        for tc in range(L // 512):
            t0 = tc * 512
            s = tc % 2
            mm_proj_fm(P, pq[s].v, C, W, 0, t0, 512)
            for d in range(2):
                mm_proj_fm(P, pf[d][s].v, C, W, 128 * (1 + d), t0, 512)
            for d in range(2):
                b = it % nb
                it += 1
                col = d * HG_H + hd
                ff = pf[d][s]
                P.act(e_[b].v, ff.v, AF.Exp, scale=-1.0)
                P.act(l1[b].v, e_[b].v, AF.Ln, scale=lb[:, col:col + 1], bias=1.0)
                P.act(l2[b].v, e_[b].v, AF.Ln, scale=1.0, bias=1.0)
                P.tt(lf[b].v, l1[b].v, l2[b].v, ALU.subtract)
                P.op("dve", lambda e, b=b: e.tensor_tensor_scan(out=pp[b].v.ap, data0=rm.v.ap, data1=lf[b].v.ap,
                                                               initial=0.0, op0=ALU.mult, op1=ALU.add),
                     [pp[b].v], [rm.v, lf[b].v])
                P.stt(lk[b].v, ff.v, -1.0, l2[b].v, ALU.mult, ALU.subtract)
                if d == 1:
                    P.tt(lf[b].v, lf[b].v, pp[b].v, ALU.subtract, eng="pool")
                    P.cp(tot8[b].v, pp[b].v.re("p (c t) -> p c t", t=64)[:, :, 63], eng="pool")
                    P.tt(pp[b].v.re("p (c t) -> p c t", t=64), lf[b].v.re("p (c t) -> p c t", t=64),
                         tot8[b].v.us(2).bc([128, 8, 64]), ALU.add)
                G = pp[b]
                P.act(eg[b].v, G.v, AF.Exp)
                ev = eg[b].v.re("p (c t) -> p c t", t=64)
                P.cp(etot[:, d, hd, tc * 8:(tc + 1) * 8], (ev[:, :, 63] if d == 0 else ev[:, :, 0]), eng="pool")
                P.tt(qo[b].v, pq[s].v, eg[b].v, ALU.mult)
                P.tt(lk[b].v, lk[b].v, G.v, ALU.subtract, eng="pool")
                P.act(ko[b].v, lk[b].v, AF.Exp, bias=ln1mlb[:, col:col + 1])
                P.dma(qg[d][hd, :, t0:t0 + 512], qo[b].v)
                P.dma(kg[d][hd, :, t0:t0 + 512], ko[b].v)
    P.dma(S["etot"].v, etot.v.re("p a h c -> p (a h c)"))
    P.end_phase()
    P.begin_phase()
    stage = P.sb("stB", [128, 8, 512], F32)
    WtB = [P.sb("WtB%d" % i, [128, 8, 512], BF16) for i in range(2)]
    pv = [P.ps("pv%d" % i, [128, 512], F32) for i in range(2)]
    vo = [P.sb("vo%d" % i, [128, 512], BF16) for i in range(2)]
    zo = [P.sb("zo%d" % i, [128, 512], F32) for i in range(2)]
    it = 0
    for isz in range(2):
        for c in range(4):
            W = WtB[c % 2]
            prep_w_chunk(P, [W], w_in, 6144 + isz * 2048 + c * 512, 512, stage)
            for tt in range(NT):
                s = it % 2
                it += 1
                mm_proj_tm(P, pv[s].v, C, [W], [0], tt, 512)
                if isz == 0:
                    P.cp(vo[s].v, pv[s].v, eng="act")
                    P.dma(S["vtm"][tt * 128:(tt + 1) * 128, c * 512:(c + 1) * 512], vo[s].v)
                else:
                    P.act(zo[s].v, pv[s].v, AF.Silu)
                    P.dma(S["zg"][tt * 128:(tt + 1) * 128, c * 512:(c + 1) * 512], zo[s].v)
    P.end_phase()
    P.begin_phase()
    etot = P.sb("etotC", [128, 2, HG_H, 64], F32)
    P.dma(etot.v.re("p a h c -> p (a h c)"), S["etot"].v)
    mask = [P.sb("mk%d" % i, [128, 128], F32) for i in range(2)]
    P.dma(mask[0].v, C.consts["hg_mf"].v)
    P.dma(mask[1].v, C.consts["hg_mb"].v)
    gb = P.sb("ngb", [128, DI], F32)
    row_bcast(P, gb.v, norm_g, slice(None))
    GH = 4
    Sf = [P.sb("Sf%d" % i, [128, GH, 128], F32) for i in range(HG_H // GH)]
    Sb = [P.sb("Sb%d" % i, [128, GH, 128], BF16) for i in range(HG_H // GH)]
    qT = [P.sb("qT%d" % i, [128, HG_H, 128], BF16) for i in range(2)]
    kT = [P.sb("kT%d" % i, [128, HG_H, 128], BF16) for i in range(2)]
    vt = [P.sb("vt%d" % i, [128, DI], BF16) for i in range(2)]
    ot = [P.sb("ot%d" % i, [128, DI], F32) for i in range(2)]
    of_t = [P.sb("oft%d" % i, [128, DI], F32) for i in range(2)]
    zt = [P.sb("zt%d" % i, [128, DI], F32) for i in range(2)]
    sq = P.sb("sq", [128, DI], F32)
    ss = P.sb("ss", [128, HG_H], F32)
    yb = [P.sb("yb%d" % i, [128, DI], BF16) for i in range(2)]
    pA = [P.ps("pA%d" % i, [128, GH, 128], F32) for i in range(2)]
    pT = [P.ps("pT%d" % i, [128, GH, 128], BF16) for i in range(2)]
    pO = [P.ps("pO%d" % i, [128, GH, 128], F32) for i in range(2)]
    pU = [P.ps("pU%d" % i, [128, GH, 128], F32) for i in range(2)]
    am = [P.sb("am%d" % i, [128, GH, 128], BF16) for i in range(2)]
    ktm = [P.sb("ktm%d" % i, [128, GH, 128], BF16) for i in range(2)]
    gi = 0
    for d in range(2):
        for g in range(HG_H // GH):
            P.memset(Sf[g].v, 0.0, eng="dve")
            P.memset(Sb[g].v, 0.0, eng="dve")
        order = list(range(NT)) if d == 0 else list(range(NT - 1, -1, -1))
        for n_, tt in enumerate(order):
            s = n_ % 2
            tsl = slice(tt * 128, (tt + 1) * 128)
            P.dma(qT[s].v, qg[d][:, :, tsl].re("h k t -> k h t"))
            P.dma(kT[s].v, kg[d][:, :, tsl].re("h k t -> k h t"))
            P.dma(vt[s].v, S["vtm"][tsl, :])
            if d == 1:
                P.dma(of_t[s].v, S["of"][tsl, :])
                P.dma(zt[s].v, S["zg"][tsl, :])
            chunks = [0, 1] if d == 0 else [1, 0]
            for g in range(HG_H // GH):
                p = gi % 2
                gi += 1
                hs = range(g * GH, (g + 1) * GH)
                for i, hd in enumerate(hs):
                    P.mm(pA[p][:, i, :], kT[s][:, hd, :], qT[s][:, hd, :])
                    P.tr(pT[p][:, i, :], kT[s][:, hd, :], C.ident.v)
                P.tt(am[p].v, pA[p].v, mask[d].v.us(1).bc([128, GH, 128]), ALU.mult)
                P.cp(ktm[p].v, pT[p].v, eng="act")
                for ci, ch in enumerate(chunks):
                    rs = slice(ch * 64, (ch + 1) * 64)
                    cidx = tt * 2 + ch
                    for i, hd in enumerate(hs):
                        vs = vt[s][:, hd * 128:(hd + 1) * 128]
                        P.mm(pO[p][rs, i, :], am[p][:, i, rs], vs, start=True, stop=False)
                        P.mm(pO[p][rs, i, :], qT[s][:, hd, rs], Sb[g][:, i, :], start=False, stop=True)
                        P.mm(pU[p][:, i, :], ktm[p][rs, i, :], vt[s][rs, hd * 128:(hd + 1) * 128])
                    e_bc = etot[:, d, g * GH:(g + 1) * GH, cidx].us(2).bc([128, GH, 128])
                    P.tt(Sf[g].v, pU[p].v, Sf[g].v, ALU.add)
                    P.tt(Sf[g].v, Sf[g].v, e_bc, ALU.mult)
                    P.cp(Sb[g].v, Sf[g].v, eng="act")
                osl = ot[s][:, g * GH * 128:(g + 1) * GH * 128]
                if d == 0:
                    P.cp(osl, pO[p].v.re("p g v -> p (g v)"), eng="act")
                else:
                    P.tt(osl, pO[p].v.re("p g v -> p (g v)"), of_t[s][:, g * GH * 128:(g + 1) * GH * 128], ALU.add)
            if d == 0:
                P.dma(S["of"][tsl, :], ot[s].v)
            else:
                o3 = ot[s].v.re("p (h v) -> p h v", v=128)
                P.tt(sq.v, ot[s].v, ot[s].v, ALU.mult, eng="pool")
                P.op("dve", lambda e: e.tensor_reduce(out=ss.v.ap, in_=sq.v.re("p (h v) -> p h v", v=128).ap,
                                                     axis=AX.X, op=ALU.add), [ss.v], [sq.v])
                P.act(ss.v, ss.v, AF.Sqrt, scale=1.0 / 128, bias=C.eps_col[:, 0:1])
                P.op("dve", lambda e: e.reciprocal(out=ss.v.ap, in_=ss.v.ap), [ss.v], [ss.v])
                P.tt(o3, o3, ss.v.us(2).bc([128, HG_H, 128]), ALU.mult)
                P.tt(zt[s].v, zt[s].v, gb.v, ALU.mult, eng="pool")
                P.tt(yb[s].v, ot[s].v, zt[s].v, ALU.mult)
                P.dma(S["y"][tsl, :], yb[s].v)
    P.end_phase()
    outproj_phase(P, C, w_out, ln_g, ln_b, last)


def outproj_phase(P, C, w_out, ln_g, ln_b, last):
    P.begin_phase()
    op = OutProj(P, C, w_out, ln_g, ln_b, last=last)
    yb = [P.sb("ypb%d" % i, [128, DI], BF16) for i in range(2)]
    for tt in range(NT):
        P.dma(yb[tt % 2].v, C.scr["y"][tt * 128:(tt + 1) * 128, :])
        op.tile(tt, yb[tt % 2].v)
    P.end_phase()


LAYER_KIND = ["hyena", "ssd", "hgrn2", "hyena"]
PARAMS = {
    "hyena": ["w_in", "conv_w", "conv_b", "filt_w1", "filt_b1", "filt_w2", "filt_b2", "filt_w3", "filt_b3",
              "filt_freq", "filt_w_out", "skip", "w_out", "ln_g", "ln_b"],
    "ssd": ["w_in", "conv_w", "conv_b", "dt_bias", "a_log", "d_skip", "norm_g", "w_out", "ln_g", "ln_b"],
    "hgrn2": ["w_in", "norm_g", "w_out", "ln_g", "ln_b"],
}
SHAPES = {
    "hyena": {"w_in": [D, 8192], "conv_w": [3, 6144], "conv_b": [6144], "filt_w1": [33, 64], "filt_b1": [64],
              "filt_w2": [64, 64], "filt_b2": [64], "filt_w3": [64, 64], "filt_b3": [64], "filt_freq": [64],
              "filt_w_out": [64, 4096], "skip": [DI], "w_out": [DI, D], "ln_g": [D], "ln_b": [D]},
    "ssd": {"w_in": [D, 5184], "conv_w": [5, 3072], "conv_b": [3072], "dt_bias": [2, 32], "a_log": [2, 32],
            "d_skip": [32], "norm_g": [DI], "w_out": [DI, D], "ln_g": [D], "ln_b": [D]},
    "hgrn2": {"w_in": [D, 10240], "norm_g": [DI], "w_out": [DI, D], "ln_g": [D], "ln_b": [D]},
}


def all_consts():
    c = {"ident": _ident_bf16(), "identf": np.eye(128, dtype=np.float32)}
    c.update(hgrn2_consts())
    c.update(ssd_consts())
    c.update(hyena_consts())
    return c


def build_program(layers):
    P = Prog()
    C = Ctx()
    C.x = P.dram("x", [L, D], F32, kind="ExternalInput")
    C.out = P.dram("out", [L, D], F32, kind="ExternalOutput")
    C.h = P.dram("h_scr", [L, D], F32)
    C.consts = {}
    cst = all_consts()
    for k, v in cst.items():
        dt = BF16 if v.dtype != np.float32 else F32
        C.consts[k] = P.dram("c_" + k, list(v.shape), dt, kind="ExternalInput")
    C.d_ident = C.consts["ident"]
    C.d_identf = C.consts["identf"]
    C.lbraw = P.dram("hgrn_lower_bounds", [4, 4096], F32, kind="ExternalInput")
    C.prm = {}
    for li in layers:
        kind = LAYER_KIND[li]
        for nme in PARAMS[kind]:
            key = "l%d_%s" % (li, nme)
            C.prm[key] = P.dram(key, SHAPES[kind][nme], F32, kind="ExternalInput")
    S = {}
    for nme in ["qgf", "qgb", "kgf", "kgb"]:
        S[nme] = P.dram("s_" + nme, [HG_H, 128, L], BF16)
    S["vtm"] = P.dram("s_vtm", [L, DI], BF16)
    S["zg"] = P.dram("s_zg", [L, DI], F32)
    S["of"] = P.dram("s_of", [L, DI], F32)
    S["y"] = P.dram("s_y", [L, DI], BF16)
    S["xs"] = P.dram("s_xs", [L, DI], F32)
    S["btm"] = P.dram("s_btm", [L, 512], BF16)
    S["bT"] = P.dram("s_bT", [4, 128, L], BF16)
    S["cT"] = P.dram("s_cT", [4, 128, L], BF16)
    S["dt"] = P.dram("s_dt", [L, 64], F32)
    S["Zs"] = P.dram("s_Zs", [128, 64, DI], BF16)
    S["Zv"] = P.dram("s_Zv", [128, 64, DI], BF16)
    S["Kf"] = P.dram("s_Kf", [NK2, 128, DI], BF16)
    S["gbf"] = P.dram("s_gbf", [L, DI], BF16)
    S["etot"] = P.dram("s_etot", [128, 2 * HG_H * 64], F32)
    S["gatebf"] = P.dram("s_gatebf", [L, DI], BF16)
    S["adt"] = P.dram("s_adt", [L, 64], F32)
    C.scr = S
    setup_common(P, C)
    load_x_to_hT(P, C)
    for n_, li in enumerate(layers):
      try:
          kind = LAYER_KIND[li]
          last = (n_ == len(layers) - 1)
          pr = lambda nme: C.prm["l%d_%s" % (li, nme)]
          if kind == "hgrn2":
              hgrn2_layer(P, C, li, pr("w_in"), C.lbraw, pr("norm_g"), pr("w_out"), pr("ln_g"), pr("ln_b"), last)
          elif kind == "ssd":
              ssd_layer(P, C, li, pr("w_in"), pr("conv_w"), pr("conv_b"), pr("dt_bias"), pr("a_log"), pr("d_skip"),
                        pr("norm_g"), pr("w_out"), pr("ln_g"), pr("ln_b"), last)
          else:
              hyena_layer(P, C, li, {nme: pr(nme) for nme in PARAMS["hyena"]}, last)
      except StopBuild:
        break
    P.barrier()
    return P, C


def in_map_for(inputs, b, layers):
    m = {"x": np.ascontiguousarray(inputs["x"][b]), "hgrn_lower_bounds": inputs["hgrn_lower_bounds"]}
    for k, v in all_consts().items():
        m["c_" + k] = v
    for li in layers:
        for nme in PARAMS[LAYER_KIND[li]]:
            key = "l%d_%s" % (li, nme)
            m[key] = inputs[key]
    return m


def pipeline(n, stages, offs=None):
    ns = len(stages)
    if offs is None:
        offs = list(range(ns))
    for step in range(n + max(offs)):
        for k in range(ns - 1, -1, -1):
            i = step - offs[k]
            if 0 <= i < n:
                stages[k](i)


def ssd_scan_phase(P, C, norm_g, d_skip):
    S = C.scr
    P.begin_phase()
    M1 = [P.sb("M1%d" % i, [128, 128], F32) for i in range(2)]
    P.dma(M1[0].v, C.consts["sd_mf"].v)
    P.dma(M1[1].v, C.consts["sd_mb"].v)
    gb = P.sb("sgb", [128, DI], F32)
    row_bcast(P, gb.v, norm_g, slice(None))
    dsk = P.sb("dsk", [128, 32], F32)
    row_bcast(P, dsk.v, d_skip, slice(None))
    Sp = [P.sb("Sp%d" % i, [128, 512], F32) for i in range(4)]
    Spb = [P.sb("Spb%d" % i, [128, 512], BF16) for i in range(4)]
    NL = 2
    xs = [P.sb("xs%d" % i, [128, DI], F32) for i in range(NL)]
    dtt = [P.sb("dtt%d" % i, [128, 64], F32) for i in range(NL)]
    adt = [P.sb("adt%d" % i, [128, 64], F32) for i in range(NL)]
    btm = [P.sb("btm%d" % i, [128, 512], BF16) for i in range(NL)]
    bT = [P.sb("bT%d" % i, [128, 4, 128], BF16) for i in range(NL)]
    cT = [P.sb("cT%d" % i, [128, 4, 128], BF16) for i in range(NL)]
    yft = [P.sb("yft%d" % i, [128, DI], F32) for i in range(2)]
    zt = [P.sb("szt%d" % i, [128, DI], F32) for i in range(2)]
    ct = [P.sb("ct%d" % i, [128, 64], F32) for i in range(2)]
    ncs = [P.sb("ncs%d" % i, [128, 32], F32) for i in range(2)]
    ecs = [P.sb("ecs%d" % i, [128, 32], F32) for i in range(2)]
    dst_ = [P.sb("dst%d" % i, [128, 32], F32) for i in range(2)]
    etot = [P.sb("setot%d" % i, [128, 32], F32) for i in range(2)]
    xdt = [P.sb("xdt%d" % i, [128, DI], BF16) for i in range(2)]
    xdtd = [P.sb("xdtd%d" % i, [128, DI], BF16) for i in range(2)]
    CBm = [P.sb("CBm%d" % i, [128, 128], F32) for i in range(1)] * 2
    X = [P.sb("X%d" % i, [128, 8, 128], F32) for i in range(1)] * 2
    X2 = [P.sb("X2%d" % i, [128, 8, 128], F32) for i in range(1)] * 2
    Mh = [P.sb("Mh%d" % i, [128, 8, 128], BF16) for i in range(2)]
    tmp = [P.sb("stmp%d" % i, [128, 512], F32) for i in range(1)] * 2
    yt = [P.sb("syt%d" % i, [128, DI], F32) for i in range(2)]
    ss = P.sb("sss", [128, 4], F32)
    yb = [P.sb("syb%d" % i, [128, DI], BF16) for i in range(1)] * 2
    p_ct = P.ps("p_ct", [128, 64], F32)
    p_cb = [P.ps("p_cb%d" % i, [128, 128], F32) for i in range(2)]
    p_row = P.ps("p_row", [128, 8, 128], F32)
    p_y = P.ps("p_y", [128, 512], F32)
    p_st = P.ps("p_st", [128, 512], F32)
    p_yo = P.ps("p_yo", [128, 512], F32)
    NTD = NT if DBG_LEVEL >= 99 else DBG_NT
    tiles = []
    for d in range(2):
        order = list(range(NT)) if d == 0 else list(range(NT - 1, -1, -1))
        tiles += [(d, tt) for tt in order[:NTD]]
    NG = len(tiles) * 4

    def info(i):
        Ti = i // 4
        d, tt = tiles[Ti]
        return Ti, d, tt, i % 4

    def st_load(i):
        Ti, d, tt, g = info(i)
        if g != 0:
            return
        s = Ti % NL
        tsl = slice(tt * 128, (tt + 1) * 128)
        P.dma(xs[s].v, S["xs"][tsl, :])
        P.dma(dtt[s].v, S["dt"][tsl, :])
        P.dma(adt[s].v, S["adt"][tsl, :])
        P.dma(btm[s].v, S["btm"][tsl, :])
        P.dma(bT[s].v, S["bT"][:, :, tsl].re("g n t -> n g t"))
        P.dma(cT[s].v, S["cT"][:, :, tsl].re("g n t -> n g t"))
        if d == 1:
            P.dma(zt[Ti % 2].v, S["zg"][tsl, :])

    def st_pro(i):
        Ti, d, tt, g = info(i)
        if g != 0:
            return
        s = Ti % NL
        u = Ti % 2
        hc = slice(d * 32, (d + 1) * 32)
        P.mm(p_ct[:, 0:32], M1[d].v, adt[s][:, hc])
        P.mm(p_ct[:, 32:64], C.ones_f.v, adt[s][:, hc])
        P.cp(ct[u].v, p_ct.v)
        P.ts(ncs[u].v, ct[u][:, 0:32], -1.0, ALU.mult)
        P.act(ecs[u].v, ct[u][:, 0:32], AF.Exp)
        P.act(etot[u].v, ct[u][:, 32:64], AF.Exp)
        P.tt(dst_[u].v, ct[u][:, 32:64], ct[u][:, 0:32], ALU.subtract)
        P.act(dst_[u].v, dst_[u].v, AF.Exp)
        P.tt(dst_[u].v, dst_[u].v, dtt[s][:, hc], ALU.mult)
        x3 = xs[s].v.re("p (h q) -> p h q", q=64)
        P.tt(xdt[u].v.re("p (h q) -> p h q", q=64), x3, dtt[s][:, hc].us(2).bc([128, 32, 64]), ALU.mult)
        P.tt(xdtd[u].v.re("p (h q) -> p h q", q=64), x3, dst_[u].v.us(2).bc([128, 32, 64]), ALU.mult, eng="pool")

    def st1(i):
        Ti, d, tt, g = info(i)
        s = Ti % NL
        k = i % 2
        hg = slice(d * 32 + g * 8, d * 32 + (g + 1) * 8)
        P.mm(p_cb[k].v, bT[s][:, g, :], cT[s][:, g, :])
        P.tt(X[k].v, M1[d].v.us(1).bc([128, 8, 128]), adt[s][:, hg].us(2).bc([128, 8, 128]), ALU.mult, eng="pool")
        for q in range(2):
            P.mm(p_row[:, q * 4:(q + 1) * 4, :], C.ones_f.v, X[k][:, q * 4:(q + 1) * 4, :])

    def st2(i):
        Ti, d, tt, g = info(i)
        u = Ti % 2
        k = i % 2
        hl = slice(g * 8, (g + 1) * 8)
        P.tt(CBm[k].v, p_cb[k].v, M1[d].v, ALU.mult)
        P.tt(X2[k].v, p_row.v, ncs[u][:, hl].us(2).bc([128, 8, 128]), ALU.add)
        P.act(X2[k].v, X2[k].v, AF.Relu, scale=-1.0)
        P.act(X2[k].v, X2[k].v, AF.Exp, scale=-1.0)
        P.tt(Mh[k].v, X2[k].v, CBm[k].v.us(1).bc([128, 8, 128]), ALU.mult)

    def st3(i):
        Ti, d, tt, g = info(i)
        s = Ti % NL
        u = Ti % 2
        k = i % 2
        hl = slice(g * 8, (g + 1) * 8)
        tsl = slice(tt * 128, (tt + 1) * 128)
        if tt == (0 if d == 0 else NT - 1):
            P.memset(Sp[g].v, 0.0, eng="dve")
            P.memset(Spb[g].v, 0.0, eng="dve")
        if g == 0 and d == 1:
            P.dma(yft[u].v, S["of"][tsl, :])
        for h in range(8):
            hh = g * 8 + h
            P.mm(p_y[:, h * 64:(h + 1) * 64], Mh[k][:, h, :], xdt[u][:, hh * 64:(hh + 1) * 64])
        P.mm(p_yo.v, cT[s][:, g, :], Spb[g].v)
        P.mm(p_st.v, btm[s][:, g * 128:(g + 1) * 128], xdtd[u][:, g * 512:(g + 1) * 512])
        P.tt(tmp[k].v.re("p (h q) -> p h q", q=64), p_yo.v.re("p (h q) -> p h q", q=64),
             ecs[u][:, hl].us(2).bc([128, 8, 64]), ALU.mult)
        P.tt(yt[u][:, g * 512:(g + 1) * 512], p_y.v, tmp[k].v, ALU.add)
        sg = Sp[g].v
        P.tt(sg.re("p (h q) -> p h q", q=64), sg.re("p (h q) -> p h q", q=64),
             etot[u][:, hl].us(2).bc([128, 8, 64]), ALU.mult, eng="pool")
        P.tt(sg, sg, p_st.v, ALU.add)
        P.cp(Spb[g].v, sg, eng="act")
        if g != 3:
            return
        if d == 0:
            P.dma(S["of"][tsl, :], yt[u].v)
            return
        x3 = xs[s].v.re("p (h q) -> p h q", q=64)
        sq = yft[u]
        P.tt(yt[u].v, yt[u].v, yft[u].v, ALU.add)
        P.tt(sq.v.re("p (h q) -> p h q", q=64), x3, dsk.v.us(2).bc([128, 32, 64]), ALU.mult, eng="pool")
        P.tt(yt[u].v, yt[u].v, sq.v, ALU.add)
        P.tt(yt[u].v, yt[u].v, zt[u].v, ALU.mult)
        for q in range(4):
            P.act(sq[:, q * 512:(q + 1) * 512], yt[u][:, q * 512:(q + 1) * 512], AF.Square, accum=ss[:, q:q + 1])
        P.act(ss.v, ss.v, AF.Sqrt, scale=1.0 / 512, bias=C.eps_col[:, 0:1])
        P.op("dve", lambda e: e.reciprocal(out=ss.v.ap, in_=ss.v.ap), [ss.v], [ss.v], cost=0.12)
        for q in range(4):
            P.act(yt[u][:, q * 512:(q + 1) * 512], yt[u][:, q * 512:(q + 1) * 512], AF.Copy, scale=ss[:, q:q + 1])
        P.tt(yb[u].v, yt[u].v, gb.v, ALU.mult)
        P.dma(S["y"][tsl, :], yb[u].v)

    pipeline(NG, [st_load, st_pro, st1, st2, st3], [2, 4, 5, 6, 7])
    P.end_phase()


def ssd_consts():
    r = np.arange(128)[:, None]
    t = np.arange(128)[None, :]
    return {"sd_mf": (r <= t).astype(np.float32), "sd_mb": (r >= t).astype(np.float32)}


def ssd_layer(P, C, li, w_in, conv_w, conv_b, dt_bias, a_log, d_skip, norm_g, w_out, ln_g, ln_b, last):
    S = C.scr
    NTAP = 5
    shifts = [-2, -1, 0, 1, 2]
    P.begin_phase()
    stage = P.sb("sstA", [128, 8, 512], F32)
    cwb = P.sb("cwb", [128, NTAP, 512], F32)
    Wz = [P.sb("Wz%d" % i, [128, 8, 512], BF16) for i in range(2)]
    Wt = [P.sb("Wc%d" % j, [128, 8, 512], BF16) for j in range(NTAP)]
    cbf = P.sb("cbf", [1, 3072], F32)
    cbb = P.sb("cbb", [1, 3072], BF16)
    P.dma(cbf.v, conv_b.v.us(0))
    P.cp(cbb.v, cbf.v)
    pz = [P.ps("pz%d" % i, [128, 512], F32) for i in range(2)]
    zo = [P.sb("szo%d" % i, [128, 512], F32) for i in range(2)]
    bo = [P.sb("sbo%d" % i, [128, 512], BF16) for i in range(2)]
    it = 0
    for c in range(0 if DBG_SKIPA else 4):
        W = Wz[c % 2]
        prep_w_chunk(P, [W], w_in, c * 512, 512, stage)
        for tt in range(NT):
            s = it % 2
            it += 1
            mm_proj_tm(P, pz[s].v, C, [W], [0], tt, 512)
            P.act(zo[s].v, pz[s].v, AF.Silu)
            P.dma(S["zg"][tt * 128:(tt + 1) * 128, c * 512:(c + 1) * 512], zo[s].v)
    for c in range(0 if DBG_SKIPA else 5):
        prep_w_chunk(P, Wt, w_in, 2048 + c * 512, 512, stage, convw_d=conv_w, cc0=c * 512, cwb=cwb, ntaps=NTAP)
        for tt in range(NT):
            s = it % 2
            it += 1
            mm_proj_tm(P, pz[s].v, C, Wt, shifts, tt, 512, bias=cbb[0:1, c * 512:(c + 1) * 512])
            if c < 4:
                P.act(zo[s].v, pz[s].v, AF.Silu)
                P.dma(S["xs"][tt * 128:(tt + 1) * 128, c * 512:(c + 1) * 512], zo[s].v)
            else:
                P.act(bo[s].v, pz[s].v, AF.Silu)
                P.dma(S["btm"][tt * 128:(tt + 1) * 128, :], bo[s].v)
    bcol = P.sb("bcol", [128, 8], F32)
    P.dma(bcol.v, conv_b[2048:3072].re("(g n) -> n g", n=128), slow=True)
    for bc_ in range(0 if DBG_SKIPA else 2):
        prep_w_chunk(P, Wt, w_in, 4096 + bc_ * 512, 512, stage, convw_d=conv_w, cc0=2048 + bc_ * 512, cwb=cwb,
                     ntaps=NTAP)
        dst = S["bT"] if bc_ == 0 else S["cT"]
        for g in range(4):
            for tc in range(L // 512):
                s = it % 2
                it += 1
                for j in range(NTAP):
                    mm_proj_fm(P, pz[s].v, C, Wt[j], g * 128, tc * 512, 512, shift=shifts[j],
                               start=(j == 0), stop=(j == NTAP - 1))
                P.act(bo[s].v, pz[s].v, AF.Silu, bias=bcol[:, bc_ * 4 + g: bc_ * 4 + g + 1])
                P.dma(dst[g, :, tc * 512:(tc + 1) * 512], bo[s].v)
    Wd = Wz[0]
    prep_w_chunk(P, [Wd], w_in, 5120, 64, stage)
    dtb = P.sb("dtb", [128, 64], F32)
    arow = P.sb("arow", [128, 64], F32)
    P.dma(dtb.v, V(dt_bias, dt_bias.t.rearrange("a b -> (a b)").partition_broadcast(128)))
    P.dma(arow.v, V(a_log, a_log.t.rearrange("a b -> (a b)").partition_broadcast(128)))
    P.act(arow.v, arow.v, AF.Exp)
    dto = [P.sb("dto%d" % i, [128, 64], F32) for i in range(2)]
    ado = [P.sb("ado%d" % i, [128, 64], F32) for i in range(2)]
    for tt in range(NT):
        s = tt % 2
        mm_proj_tm(P, pz[s][:, 0:64], C, [Wd], [0], tt, 64)
        P.tt(dto[s].v, pz[s][:, 0:64], dtb.v, ALU.add)
        P.act(dto[s].v, dto[s].v, AF.Exp)
        P.act(dto[s].v, dto[s].v, AF.Ln, bias=1.0)
        P.stt(ado[s].v, dto[s].v, -1.0, arow.v, ALU.mult, ALU.mult)
        P.dma(S["dt"][tt * 128:(tt + 1) * 128, :], dto[s].v)
        P.dma(S["adt"][tt * 128:(tt + 1) * 128, :], ado[s].v)
    P.end_phase()
    ssd_scan_phase(P, C, norm_g, d_skip)
    outproj_phase(P, C, w_out, ln_g, ln_b, last)


NF = 8192
NK2 = 65


def hyena_consts():
    import ml_dtypes
    bf = ml_dtypes.bfloat16
    n = NF
    tp = 2.0 * np.pi
    p = np.arange(128)
    k2 = np.arange(65)
    k2s = np.arange(1, 64)
    F1 = np.zeros((128, 128))
    F1[:, :65] = np.cos(tp * ((p[:, None] * k2[None, :]) % 128) / 128)
    F1[:, 65:] = -np.sin(tp * ((p[:, None] * k2s[None, :]) % 128) / 128)
    pp = np.arange(64)
    m = np.full(65, 2.0)
    m[0] = 1.0
    m[64] = 1.0
    Finv = np.zeros((128, 64))
    Finv[:65, :] = (m[:, None] / n) * np.cos(tp * ((k2[:, None] * pp[None, :]) % 128) / 128)
    Finv[65:, :] = -(2.0 / n) * np.sin(tp * ((k2s[:, None] * pp[None, :]) % 128) / 128)
    a = np.arange(64)
    k1 = np.arange(64)
    TA = np.zeros((65, 128, 128))
    TB = np.zeros((65, 128, 128))
    TC = np.zeros((65, 128, 128))
    for q in range(65):
        th = tp * ((a[:, None] * (q + 128 * k1[None, :])) % n) / n
        Tr = np.cos(th)
        Ti = -np.sin(th)
        TA[q, :64, :64] = Tr
        TA[q, :64, 64:] = Ti
        TA[q, 64:, :64] = -Ti
        TA[q, 64:, 64:] = Tr
        TB[q, :64, :64] = -Ti
        TB[q, :64, 64:] = Tr
        TB[q, 64:, :64] = -Tr
        TB[q, 64:, 64:] = -Ti
        Ur = np.cos(th).T
        Ui = np.sin(th).T
        TC[q, :64, :64] = Ur
        TC[q, 64:, :64] = -Ui
        TC[q, :64, 64:] = Ui
        TC[q, 64:, 64:] = Ur
    j = (64 * p[None, :] + a[:, None]).reshape(-1)
    pos = np.where(j < L, j, n - j)
    pos[j == L] = 0
    tl = np.linspace(0.0, 1.0, L)
    t = tl[pos]
    w = tp * pos / L
    f = np.linspace(1e-4, 15.0, 16)
    z = np.concatenate([t[:, None], np.cos(f[None, :] * w[:, None]), -np.sin(f[None, :] * w[:, None])], axis=1)
    tdec = t.copy()
    tdec[j == L] = 1e4
    ntpos = -(tdec.reshape(64, 128).T)
    min_d = math.log(1e-2) / 0.3
    max_d = math.log(1e-2) / 1.5
    deltas = np.abs(np.linspace(min_d, max_d, DI))
    return {
        "hy_F1": F1.astype(np.float32).astype(bf),
        "hy_Finv": Finv.astype(np.float32).astype(bf),
        "hy_TA": np.ascontiguousarray(TA.transpose(1, 0, 2)).astype(np.float32).astype(bf),
        "hy_TB": np.ascontiguousarray(TB.transpose(1, 0, 2)).astype(np.float32).astype(bf),
        "hy_TC": np.ascontiguousarray(TC.transpose(1, 0, 2)).astype(np.float32).astype(bf),
        "hy_zT": np.ascontiguousarray(z.T).astype(np.float32),
        "hy_ntpos": ntpos.astype(np.float32),
        "hy_deltas": deltas.astype(np.float32),
    }


def sin_reduced(P, o, arg, tmpf, tmpi, tmp2):
    tp = 2.0 * math.pi
    P.ts(tmpf, arg, 1.0 / tp, ALU.mult, 64.0, ALU.add)
    P.cp(tmpi, tmpf)
    P.cp(tmpf, tmpi)
    P.ts(tmpf, tmpf, -64.0, ALU.add, -tp, ALU.mult)
    P.tt(tmpf, tmpf, arg, ALU.add)
    P.ts(tmp2, tmpf, math.pi, ALU.is_gt, -tp, ALU.mult)
    P.tt(tmpf, tmpf, tmp2, ALU.add)
    P.ts(tmpf, tmpf, 3.14159, ALU.min, -3.14159, ALU.max)
    P.act(o, tmpf, AF.Sin)


def conv_cols(P, C, conv_w, conv_b, ntaps, nch):
    wrow = P.sb("cw_row", [nch, ntaps, 128], F32)
    brow = P.sb("cb_row", [nch, 128], F32)
    P.dma(wrow.v, conv_w.v.re("j (k p) -> k j p", p=128))
    P.dma(brow.v, conv_b.v.re("(k p) -> k p", p=128))
    wcol = P.sb("cw_col", [128, ntaps, nch], F32)
    bcol = P.sb("cb_col", [128, nch], F32)
    pt = P.ps("cw_ps", [128, ntaps + 1, nch], F32)
    for j in range(ntaps):
        P.tr(pt[:, j, :], wrow[:, j, :], C.identf[0:nch, 0:nch])
    P.tr(pt[:, ntaps, :], brow.v, C.identf[0:nch, 0:nch])
    P.cp(wcol.v, pt[:, 0:ntaps, :])
    P.cp(bcol.v, pt[:, ntaps, :])
    return wcol, bcol


def conv_row(P, out, raw, wcol, bcol, k, ntaps, func=None):
    c = ntaps // 2
    H = L // 2
    for hf in range(2):
        o = out[:, hf * H:(hf + 1) * H]
        P.act(o, raw[:, c + hf * H: c + (hf + 1) * H], AF.Identity, scale=wcol[:, c, k:k + 1], bias=bcol[:, k:k + 1])
        js = [j for j in range(ntaps) if j != c]
        for n_, j in enumerate(js):
            P.stt(o, raw[:, j + hf * H: j + (hf + 1) * H], wcol[:, j, k:k + 1], o, ALU.mult, ALU.add)
        if func is not None:
            P.act(o, o, func)


def hyena_layer(P, C, li, prm, last):
    S = C.scr
    w_in, conv_w, conv_b = prm["w_in"], prm["conv_w"], prm["conv_b"]
    shifts = [-1, 0, 1]
    P.begin_phase()
    wcol, bcol = conv_cols(P, C, conv_w, conv_b, 3, 48)
    NG2 = 2
    stage = [P.sb("hstA%d" % i, [128, 8, 2, 128], F32) for i in range(1)] * 2
    Wb = [P.sb("hWb%d" % i, [128, 8, 512], BF16) for i in range(2)]
    raw = [[P.sb("hraw%d_%d" % (i, k), [128, L + 2], BF16) for k in range(3)] for i in range(2)]
    for i in range(2):
        for k in range(3):
            P.memset(raw[i][k][:, 0:1], 0.0, eng="dve")
            P.memset(raw[i][k][:, L + 1:L + 2], 0.0, eng="dve")
    cc = [P.sb("hcc%d" % k, [128, L], BF16) for k in range(2)]
    zs2 = [P.sb("hzs%d" % i, [128, L], BF16) for i in range(2)]
    gT = [P.sb("hgT%d" % i, [128, NG2, L], BF16) for i in range(2)]
    pp = [P.ps("hpp%d" % i, [128, 512], F32) for i in range(4)]
    ptr = [P.ps("hptr%d" % i, [128, 2, NG2 * 128], BF16) for i in range(2)]
    so = [P.sb("hso%d" % i, [128, 2, NG2 * 128], BF16) for i in range(2)]
    col0 = [2048, 4096, 0, 6144]
    cch = [16, 32, 0]
    it = 0
    for i in range(16):
        sb_ = i % 2
        zs = zs2[sb_]
        for hf in range(2):
            for k2_ in range(2):
                k = hf * 2 + k2_
                P.dma(stage[hf][:, :, k2_, :],
                      w_in[:, col0[k] + i * 128: col0[k] + (i + 1) * 128].re("(k p) c -> p k c", p=128))
            P.cp(Wb[sb_][:, :, hf * 256:(hf + 1) * 256], stage[hf].v.re("p k a c -> p k (a c)"),
                 eng=("pool" if hf else "dve"))
        for tc in range(L // 512):
            for k in range(4):
                b = it % 4
                it += 1
                mm_proj_fm(P, pp[b].v, C, Wb[sb_], k * 128, tc * 512, 512)
                if k < 3:
                    P.cp(raw[sb_][k][:, 1 + tc * 512: 1 + (tc + 1) * 512], pp[b].v, eng=("act" if (tc + k) % 2 else "dve"))
                else:
                    P.act(zs[:, tc * 512:(tc + 1) * 512], pp[b].v, AF.Silu)
        conv_row(P, cc[0], raw[sb_][0], wcol, bcol, cch[0] + i, 3)
        conv_row(P, cc[1], raw[sb_][1], wcol, bcol, cch[1] + i, 3)
        P.tt(gT[0][:, i % NG2, :], cc[0].v, cc[1].v, ALU.mult, eng="pool")
        conv_row(P, cc[0], raw[sb_][2], wcol, bcol, cch[2] + i, 3)
        P.tt(gT[1][:, i % NG2, :], cc[0].v, zs.v, ALU.mult, eng="pool")
        if i % NG2 == NG2 - 1:
            i0 = i - (NG2 - 1)
            for tt in range(NT):
                q = tt % 2
                for w_ in range(2):
                    for j in range(NG2):
                        P.tr(ptr[q][:, w_, j * 128:(j + 1) * 128], gT[w_][:, j, tt * 128:(tt + 1) * 128], C.ident.v)
                P.cp(so[q].v, ptr[q].v, eng=("act" if tt % 2 else "dve"))
                P.dma(S["gbf"][tt * 128:(tt + 1) * 128, i0 * 128:(i0 + NG2) * 128], so[q][:, 0, :])
                P.dma(S["gatebf"][tt * 128:(tt + 1) * 128, i0 * 128:(i0 + NG2) * 128], so[q][:, 1, :])
    P.end_phase()
    P.begin_phase()
    F1 = P.sb("F1", [128, 128], BF16)
    P.dma(F1.v, C.consts["hy_F1"].v)
    ntp = P.sb("ntp", [128, 64], F32)
    P.dma(ntp.v, C.consts["hy_ntpos"].v)
    dl = P.sb("dl", [128, DI], F32)
    row_bcast(P, dl.v, C.consts["hy_deltas"], slice(None))
    w1 = P.sb("fw1", [33, 64], F32)
    w2 = P.sb("fw2", [64, 64], F32)
    w3 = P.sb("fw3", [64, 64], F32)
    P.dma(w1.v, prm["filt_w1"].v)
    P.dma(w2.v, prm["filt_w2"].v)
    P.dma(w3.v, prm["filt_w3"].v)
    cols = P.sb("fcols", [64, 4], F32)
    for i, nme in enumerate(["filt_freq", "filt_b1", "filt_b2", "filt_b3"]):
        P.dma(cols[:, i:i + 1], prm[nme].v.us(1))
    fb = P.sb("ffb", [64, 3], F32)
    for i in range(3):
        P.tt(fb[:, i:i + 1], cols[:, 0:1], cols[:, i + 1:i + 2], ALU.mult)
    wo_f = P.sb("fwo_f", [64, 4096], F32)
    wo = P.sb("fwo", [64, 4096], BF16)
    P.dma(wo_f.v, prm["filt_w_out"].v)
    P.cp(wo.v, wo_f.v, eng="pool")
    h3T = P.sb("h3T", [64, NF], BF16)
    MW = 1024
    zc = [P.sb("zc%d" % i, [33, MW], F32) for i in range(2)]
    arg = [P.sb("farg%d" % i, [64, MW], F32) for i in range(2)]
    tmf = [P.sb("ftmf%d" % i, [64, MW], F32) for i in range(1)] * 2
    tmi = [P.sb("ftmi%d" % i, [64, MW], I32) for i in range(1)] * 2
    tm2 = [P.sb("ftm2%d" % i, [64, MW], F32) for i in range(1)] * 2
    hh = [P.sb("fhh%d" % i, [64, MW], F32) for i in range(2)]
    pm = [P.ps("fpm%d" % i, [64, MW], F32) for i in range(2)]
    it = 0
    for ch in range(NF // MW):
        s = ch % 2
        P.dma(zc[s].v, C.consts["hy_zT"][:, ch * MW:(ch + 1) * MW])
        src = zc[s].v
        for lyr, wl in enumerate([w1, w2, w3]):
            b = it % 2
            it += 1
            for q in range(MW // 512):
                P.mm(pm[b][:, q * 512:(q + 1) * 512], wl.v, src[:, q * 512:(q + 1) * 512])
            P.act(arg[b].v, pm[b].v, AF.Identity, scale=cols[:, 0:1], bias=fb[:, lyr:lyr + 1])
            dst = hh[lyr % 2].v if lyr < 2 else h3T[:, ch * MW:(ch + 1) * MW]
            sin_reduced(P, dst, arg[b].v, tmf[b].v, tmi[b].v, tm2[b].v)
            src = dst
    pk = [P.ps("fpk%d" % i, [128, 512], F32) for i in range(2)]
    pz = [P.ps("fpz%d" % i, [128, 512], F32) for i in range(2)]
    dec = [P.sb("fdec%d" % i, [128, DI], F32) for i in range(2)]
    ka = [P.sb("fka%d" % i, [128, 512], BF16) for i in range(2)]
    zt = [P.sb("fzt%d" % i, [128, DI], BF16) for i in range(2)]
    it = 0
    for a in range(64):
        sa = a % 2
        P.act(dec[sa].v, dl.v, AF.Exp, scale=ntp[:, a:a + 1])
        for c in range(4):
            s = it % 2
            it += 1
            P.mm(pk[s][0:64, :], h3T[:, a * 128: a * 128 + 64], wo[:, c * 512:(c + 1) * 512])
            P.mm(pk[s][64:128, :], h3T[:, a * 128 + 64: a * 128 + 128], wo[:, 2048 + c * 512: 2048 + (c + 1) * 512])
            P.tt(ka[s].v, pk[s].v, dec[sa][:, c * 512:(c + 1) * 512], ALU.mult)
            P.mm(pz[s].v, F1.v, ka[s].v)
            P.cp(zt[sa][:, c * 512:(c + 1) * 512], pz[s].v, eng="act")
        P.dma(S["Zs"][:, a, :], zt[sa].v)
    P.end_phase()
    P.begin_phase()
    TA = P.sb("TA", [128, NK2, 128], BF16)
    P.dma(TA.v, C.consts["hy_TA"].v)
    pz = [P.ps("fpz2_%d" % i, [128, 512], F32) for i in range(2)]
    zin = [P.sb("fzin%d" % i, [128, DI], BF16) for i in range(2)]
    kf = [P.sb("fkf%d" % i, [128, DI], BF16) for i in range(2)]
    for q in range(NK2):
        s = q % 2
        P.dma(zin[s][0:64, :], S["Zs"][q, :, :])
        if 1 <= q <= 63:
            P.dma(zin[s][64:128, :], S["Zs"][64 + q, :, :])
        else:
            P.memset(zin[s][64:128, :], 0.0)
        for c in range(4):
            b = (q * 4 + c) % 2
            P.mm(pz[b].v, TA[:, q, :], zin[s][:, c * 512:(c + 1) * 512])
            P.cp(kf[s][:, c * 512:(c + 1) * 512], pz[b].v, eng=("act" if c % 2 else "dve"))
        P.dma(S["Kf"][q, :, :], kf[s].v)
    P.end_phase()
    P.begin_phase()
    F1 = P.sb("F1c", [128, 128], BF16)
    P.dma(F1.v, C.consts["hy_F1"].v)
    gb_ = [P.sb("cgb%d" % i, [64, DI], BF16) for i in range(3)]
    zt = [P.sb("czt%d" % i, [128, DI], BF16) for i in range(2)]
    pz = [P.ps("cpz%d" % i, [128, 512], F32) for i in range(4)]
    gview = S["gbf"].v.re("(p a) c -> a p c", a=64)
    for a in range(64):
        s = a % 2
        P.dma(gb_[a % 3].v, gview[a])
        for c in range(4):
            b = c
            P.mm(pz[b].v, F1[0:64, :], gb_[a % 3][:, c * 512:(c + 1) * 512])
            P.cp(zt[s][:, c * 512:(c + 1) * 512], pz[b].v, eng=("act" if c % 2 else "dve"))
        P.dma(S["Zs"][:, a, :], zt[s].v)
    P.end_phase()
    P.begin_phase()
    TA = P.sb("dTA", [128, NK2, 128], BF16)
    TB = P.sb("dTB", [128, NK2, 128], BF16)
    TC = P.sb("dTC", [128, NK2, 128], BF16)
    P.dma(TA.v, C.consts["hy_TA"].v)
    P.dma(TB.v, C.consts["hy_TB"].v)
    P.dma(TC.v, C.consts["hy_TC"].v)
    zin = [P.sb("dzin%d" % i, [128, DI], BF16) for i in range(2)]
    KA = [P.sb("dKA%d" % i, [128, DI], BF16) for i in range(2)]
    KB = [P.sb("dKB%d" % i, [128, DI], BF16) for i in range(2)]
    t1 = [P.sb("dt1_%d" % i, [128, 512], F32) for i in range(2)]
    t2 = [P.sb("dt2_%d" % i, [128, 512], F32) for i in range(2)]
    yb = [P.sb("dyb%d" % i, [128, 512], BF16) for i in range(2)]
    vo = [P.sb("dvo%d" % i, [128, DI], BF16) for i in range(2)]
    pA = [P.ps("dpA%d" % i, [128, 512], F32) for i in range(2)]
    pB = [P.ps("dpB%d" % i, [128, 512], F32) for i in range(2)]
    pV = [P.ps("dpV%d" % i, [128, 512], F32) for i in range(2)]
    it = 0
    for q in range(NK2):
        s = q % 2
        P.dma(zin[s][0:64, :], S["Zs"][q, :, :])
        if 1 <= q <= 63:
            P.dma(zin[s][64:128, :], S["Zs"][64 + q, :, :])
        else:
            P.memset(zin[s][64:128, :], 0.0)
        P.dma(KA[s][0:64, :], S["Kf"][q, 0:64, :])
        P.dma(KA[s][64:128, :], S["Kf"][q, 0:64, :])
        P.dma(KB[s][0:64, :], S["Kf"][q, 64:128, :])
        P.dma(KB[s][64:128, :], S["Kf"][q, 64:128, :])
        for c in range(4):
            b = it % 2
            it += 1
            cs_ = slice(c * 512, (c + 1) * 512)
            P.mm(pA[b].v, TA[:, q, :], zin[s][:, cs_])
            P.mm(pB[b].v, TB[:, q, :], zin[s][:, cs_])
            P.tt(t1[b].v, pA[b].v, KA[s][:, cs_], ALU.mult)
            P.tt(t2[b].v, pB[b].v, KB[s][:, cs_], ALU.mult)
            P.tt(yb[b].v, t1[b].v, t2[b].v, ALU.add, eng=("pool" if c % 3 else "dve"))
            P.mm(pV[b].v, TC[:, q, :], yb[b].v)
            P.cp(vo[s][:, cs_], pV[b].v, eng="act")
        P.dma(S["Zv"][q, :, :], vo[s][0:64, :])
        if 1 <= q <= 63:
            P.dma(S["Zv"][64 + q, :, :], vo[s][64:128, :])
    P.end_phase()
    P.begin_phase()
    Fi = P.sb("eFi", [128, 64], BF16)
    P.dma(Fi.v, C.consts["hy_Finv"].v)
    skb = P.sb("eskb", [128, DI], F32)
    row_bcast(P, skb.v, prm["skip"], slice(None))
    vin = [P.sb("evin%d" % i, [128, 2, DI], BF16) for i in range(2)]
    gt = [P.sb("egt%d" % i, [128, DI], BF16) for i in range(2)]
    gs = [P.sb("egs%d" % i, [128, DI], F32) for i in range(2)]
    zg = [P.sb("ezg%d" % i, [128, DI], BF16) for i in range(2)]
    yo = [P.sb("eyo%d" % i, [128, DI], BF16) for i in range(2)]
    py = [P.ps("epy%d" % i, [128, 512], F32) for i in range(2)]
    gview = S["gbf"].v.re("(p a) c -> a p c", a=64)
    zview = S["gatebf"].v.re("(p a) c -> a p c", a=64)
    yview = S["y"].v.re("(p a) c -> a p c", a=64)
    it = 0
    for a2 in range(32):
        s = a2 % 2
        for h in range(2):
            a = a2 * 2 + h
            P.dma(vin[s][:, h, :], S["Zv"][:, a, :])
            P.dma(gt[s][h * 64:(h + 1) * 64, :], gview[a])
            P.dma(zg[s][h * 64:(h + 1) * 64, :], zview[a])
        P.tt(gs[s].v, gt[s].v, skb.v, ALU.mult)
        for c in range(4):
            b = it % 2
            it += 1
            cs_ = slice(c * 512, (c + 1) * 512)
            for h in range(2):
                P.mm(py[b][h * 64:(h + 1) * 64, :], Fi.v, vin[s][:, h, cs_])
            P.tt(gs[s][:, cs_], py[b].v, gs[s][:, cs_], ALU.add)
            P.tt(yo[s][:, cs_], gs[s][:, cs_], zg[s][:, cs_], ALU.mult, eng="pool")
        for h in range(2):
            P.dma(yview[a2 * 2 + h], yo[s][h * 64:(h + 1) * 64, :])
    P.end_phase()
    outproj_phase(P, C, prm["w_out"], prm["ln_g"], prm["ln_b"], last)


_PROG_CACHE = {}


def kernel(**inputs):
    layers = [0, 1, 2, 3]
    if "prog" not in _PROG_CACHE:
        _PROG_CACHE["prog"] = build_program(layers)
    P, C = _PROG_CACHE["prog"]
    inputs = {k: np.asarray(v) for k, v in inputs.items()}
    nb = inputs["x"].shape[0]
    in_maps = [in_map_for(inputs, b, layers) for b in range(nb)]
    res = run_bass_kernel_spmd(P.nc, in_maps, core_ids=list(range(nb)))
    out = np.stack([np.asarray(res.results[b]["out"]) for b in range(nb)], axis=0)
    return out.astype(np.float32)
```

```python
import math
from contextlib import ExitStack
import numpy as np
import concourse.bass as bass
import concourse.mybir as mybir
from concourse.bass_utils import run_bass_kernel_spmd

F32 = mybir.dt.float32
BF16 = mybir.dt.bfloat16
I32 = mybir.dt.int32
AF = mybir.ActivationFunctionType
ALU = mybir.AluOpType
AX = mybir.AxisListType

D = 1024
L = 4096
DI = 2048
DEPTH = 4
ALPHA = (2.0 * DEPTH) ** 0.25
LN_EPS = 1e-5
PAD = 2
NT = L // 128
SAME_ENG_SYNC = True
NDS = 48
SCHED = True
SCHED_WINDOW = 24
DEBUG_SCR = False
DBG_STOP = ""
DBG_LEVEL = 99
DBG_SKIPA = False
DBG_DIRS = 2
DBG_SUB = 99
DBG_NT = 2
DBG_PH = 99
_phc = [0]


class Buf:
    __slots__ = ("t", "w", "r", "name", "wn", "rn")

    def __init__(self, t, name=""):
        self.t = t
        self.w = None
        self.r = {}
        self.name = name
        self.wn = None
        self.rn = []

    def __getitem__(self, idx):
        return V(self, self.t[idx])

    @property
    def v(self):
        return V(self, self.t)


class V:
    __slots__ = ("b", "ap")

    def __init__(self, b, ap):
        self.b = b
        self.ap = ap

    def __getitem__(self, idx):
        return V(self.b, self.ap[idx])

    def re(self, s, **kw):
        return V(self.b, self.ap.rearrange(s, **kw))

    def bc(self, shape):
        return V(self.b, self.ap.broadcast_to(shape))

    def tb(self, shape):
        return V(self.b, self.ap.to_broadcast(shape))

    def pb(self, n):
        return V(self.b, self.ap.partition_broadcast(n))

    def cast(self, dt):
        return V(self.b, self.ap.bitcast(dt))

    def us(self, ax):
        return V(self.b, self.ap.unsqueeze(ax))


class StopBuild(Exception):
    pass


class Eng:
    def __init__(self, name, eng, sem):
        self.name = name
        self.eng = eng
        self.sem = sem
        self.n = 0
        self.seen = {}


class Prog:
    def __init__(self):
        nc = bass.Bass("TRN2", target_bir_lowering=False)
        self.nc = nc
        self.es = ExitStack()
        self.E = {}
        for name, e in [("pe", nc.tensor), ("dve", nc.vector), ("act", nc.scalar),
                        ("pool", nc.gpsimd), ("sp", nc.sync)]:
            sem = self.es.enter_context(nc.semaphore("s_" + name))
            self.E[name] = Eng(name, e, sem)
        self.dsem = []
        for i in range(NDS):
            self.dsem.append([self.es.enter_context(nc.semaphore("d%d" % i)), 0])
        self.dnext = 0
        self.phase = None
        self.uid = 0
        self.ninst = 0
        self.deferred = False
        self.nodes = []
        self.touched = []
        self.sim_time = 0.0
        self.synced = False
        self.phase_stats = []

    def begin_phase(self):
        self.phase = ExitStack()
        self.deferred = SCHED
        if SCHED and not self.synced:
            self.barrier()
            self.synced = True

    def end_phase(self):
        if self.deferred:
            self._flush()
            self.deferred = False
        self.barrier()
        self.phase.close()
        self.phase = None
        _phc[0] += 1
        if _phc[0] >= DBG_PH:
            raise StopBuild()

    def _nm(self, name):
        self.uid += 1
        return "%s_%d" % (name, self.uid)

    def sb(self, name, shape, dt, perm=False):
        st = self.es if perm else self.phase
        t = st.enter_context(self.nc.sbuf_tensor(self._nm(name), list(shape), dt))
        return Buf(t[:], name)

    def ps(self, name, shape, dt=F32, perm=False):
        st = self.es if perm else self.phase
        esz = 4 if dt == F32 else 2
        n = 1
        for x in shape[1:]:
            n *= x
        per_bank = 2048 // esz
        npad = ((n + per_bank - 1) // per_bank) * per_bank
        t = st.enter_context(self.nc.psum_tensor(self._nm(name), [shape[0], npad], dt))
        ap = t[:][:, 0:n]
        if len(shape) == 3:
            ap = ap.rearrange("p (a b) -> p a b", a=shape[1])
        return Buf(ap, name)

    def dram(self, name, shape, dt, kind="Internal"):
        if DEBUG_SCR and kind == "Internal" and name.startswith("s_"):
            kind = "ExternalOutput"
        t = self.nc.dram_tensor(name, list(shape), dt, kind=kind)
        return Buf(t.ap(), name)

    def _wait(self, E, toks):
        need = {}
        for (s, v) in toks:
            if s is E.sem and (E.name == "pe" or not SAME_ENG_SYNC):
                continue
            k = id(s)
            if k not in need or need[k][1] < v:
                need[k] = (s, v)
        for k, (s, v) in need.items():
            if E.seen.get(k, 0) >= v:
                continue
            E.eng.wait_ge(s, v)
            self.ninst += 1
            E.seen[k] = v

    @staticmethod
    def _deps(outs, ins):
        toks = []
        for v in ins:
            if v.b.w is not None:
                toks.append(v.b.w)
        for v in outs:
            if v.b.w is not None:
                toks.append(v.b.w)
            toks.extend(v.b.r.values())
        return toks

    @staticmethod
    def _mark(tok, outs, ins):
        k = id(tok[0])
        for v in ins:
            r = v.b.r
            if k not in r or r[k][1] < tok[1]:
                r[k] = tok
        for v in outs:
            v.b.w = tok
            v.b.r = {}

    def op(self, ename, fn, outs, ins, cost=0.3):
        if self.deferred:
            self._record(ename, "op", fn, outs, ins, cost, cost)
            return
        E = self.E[ename]
        self._wait(E, self._deps(outs, ins))
        E.n += 1
        fn(E.eng).then_inc(E.sem, 1)
        self.ninst += 1
        self._mark((E.sem, E.n), outs, ins)

    def dma(self, out, in_, q="sp", slow=False):
        if self.deferred:
            nbytes = 1
            for x in out.ap.shape:
                nbytes *= x
            nbytes *= (4 if out.ap.dtype in (F32, I32) else 2)
            self._record(q, "dma", (out, in_, slow), [out], [in_], 0.12, 2.0 + nbytes / 120e3)
            return
        self._emit_dma(self.E[q], out, in_, slow, self._deps([out], [in_]), True)

    def _emit_dma(self, E, out, in_, slow, toks, mark):
        i = self.dnext
        self.dnext = (i + 1) % NDS
        sem, cnt = self.dsem[i]
        if cnt > 0:
            toks = list(toks) + [(sem, cnt)]
        self._wait(E, toks)
        if slow:
            E.eng.dma_start(out=out.ap, in_=in_.ap, allow_slow_non_contiguous=True).then_inc(sem, 16)
        else:
            E.eng.dma_start(out=out.ap, in_=in_.ap).then_inc(sem, 16)
        self.ninst += 1
        self.dsem[i][1] = cnt + 16
        tok = (sem, cnt + 16)
        if mark:
            self._mark(tok, [out], [in_])
        return tok

    def _record(self, ename, kind, payload, outs, ins, busy, lat):
        nid = len(self.nodes)
        deps = set()
        for v in ins:
            b = v.b
            if b.wn is not None:
                deps.add(b.wn)
        for v in outs:
            b = v.b
            if b.wn is not None:
                deps.add(b.wn)
            deps.update(b.rn)
        for v in ins:
            b = v.b
            b.rn.append(nid)
            self.touched.append(b)
        for v in outs:
            b = v.b
            b.wn = nid
            b.rn = []
            self.touched.append(b)
        deps.discard(nid)
        self.nodes.append([ename, kind, payload, busy, lat, deps])

    def _flush(self):
        nodes = self.nodes
        n = len(nodes)
        if n == 0:
            return
        succ = [[] for _ in range(n)]
        ndep = [0] * n
        for i, nd in enumerate(nodes):
            ndep[i] = len(nd[5])
            for d in nd[5]:
                succ[d].append(i)
        ready = [0.0] * n
        fin = [0.0] * n
        queues = {}
        for i, nd in enumerate(nodes):
            queues.setdefault(nd[0], []).append(i)
        head = {e: 0 for e in queues}
        done = [False] * n
        T = {e: 0.0 for e in queues}
        order = {e: [] for e in queues}
        W = SCHED_WINDOW
        left = n
        while left:
            best = None
            for e, q in queues.items():
                h = head[e]
                while h < len(q) and done[q[h]]:
                    h += 1
                head[e] = h
                cnt = 0
                j = h
                te = T[e]
                while j < len(q) and cnt < W:
                    i = q[j]
                    j += 1
                    if done[i]:
                        continue
                    cnt += 1
                    if ndep[i]:
                        continue
                    st = ready[i] if ready[i] > te else te
                    if best is None or st < best[0] - 1e-9 or (st < best[0] + 1e-9 and i < best[1]):
                        best = (st, i, e)
                    if st <= te:
                        break
            st, i, e = best
            nd = nodes[i]
            T[e] = st + nd[3]
            f = st + nd[4]
            fin[i] = f
            done[i] = True
            left -= 1
            order[e].append(i)
            for sidx in succ[i]:
                ndep[sidx] -= 1
                lat = f + (0.12 if nodes[sidx][0] == e else 0.3)
                if lat > ready[sidx]:
                    ready[sidx] = lat
        tok = [None] * n
        for e, lst in order.items():
            E = self.E[e]
            k = E.n
            for i in lst:
                if nodes[i][1] == "op":
                    k += 1
                    tok[i] = (E.sem, k)
        dma_engs = [e for e in order if any(nodes[i][1] == "dma" for i in order[e])]
        pending = {e: list(order[e]) for e in order}
        ptr = {e: 0 for e in order}
        progress = True
        while progress:
            progress = False
            for e in list(pending.keys()):
                lst = pending[e]
                E = self.E[e]
                while ptr[e] < len(lst):
                    i = lst[ptr[e]]
                    nd = nodes[i]
                    if any(tok[d] is None for d in nd[5]):
                        break
                    toks = [tok[d] for d in nd[5]]
                    if nd[1] == "dma":
                        out, in_, slow = nd[2]
                        tok[i] = self._emit_dma(E, out, in_, slow, toks, False)
                    else:
                        self._wait(E, toks)
                        E.n += 1
                        assert tok[i] == (E.sem, E.n)
                        nd[2](E.eng).then_inc(E.sem, 1)
                        self.ninst += 1
                    ptr[e] += 1
                    progress = True
                if ptr[e] >= len(lst):
                    del pending[e]
        assert not pending, "emission deadlock"
        self.sim_time += max(T.values())
        busy = {}
        for nd in nodes:
            busy[nd[0]] = busy.get(nd[0], 0.0) + nd[3]
        self.phase_stats.append((max(T.values()), {k: round(v) for k, v in busy.items()}, n))
        self.nodes = []
        for b in self.touched:
            b.wn = None
            b.rn = []
        self.touched = []

    def barrier(self):
        toks = [(e.sem, e.n) for e in self.E.values() if e.n > 0]
        toks += [(s, c) for s, c in self.dsem if c > 0]
        for E in self.E.values():
            self._wait(E, [t for t in toks if t[0] is not E.sem])

    @staticmethod
    def _fs(v):
        n = 1
        for x in v.ap.shape[1:]:
            n *= x
        return n

    def _cost(self, eng, v, cast=False):
        n = self._fs(v)
        if eng == "act":
            return 0.2 + n / 1400.0
        if eng == "pool":
            return 0.3 + n * (0.0035 if cast else 0.0022)
        return 0.1 + n / 960.0

    def mm(self, o, lhsT, rhs, start=True, stop=True):
        n = self._fs(rhs) * (4 if lhsT.ap.dtype == F32 else 1)
        self.op("pe", lambda e: e.matmul(o.ap, lhsT=lhsT.ap, rhs=rhs.ap, start=start, stop=stop),
                [o], [lhsT, rhs], cost=0.07 + n / 2200.0)

    def tr(self, o, a, ident):
        self.op("pe", lambda e: e.transpose(o.ap, a.ap, ident.ap), [o], [a, ident], cost=0.15)

    def act(self, o, a, func, scale=1.0, bias=0.0, accum=None, eng="act"):
        ins = [a]
        outs = [o]
        kw = {}
        if isinstance(scale, V):
            ins.append(scale)
            kw["scale"] = scale.ap
        else:
            kw["scale"] = float(scale)
        if isinstance(bias, V):
            ins.append(bias)
            kw["bias"] = bias.ap
        else:
            kw["bias"] = float(bias)
        if accum is not None:
            outs.append(accum)
            kw["accum_out"] = accum.ap
        self.op("act", lambda e: e.activation(out=o.ap, in_=a.ap, func=func, **kw), outs, ins,
                cost=self._cost("act", a))

    def tt(self, o, a, b, op, eng="dve"):
        self.op(eng, lambda e: e.tensor_tensor(out=o.ap, in0=a.ap, in1=b.ap, op=op), [o], [a, b],
                cost=self._cost(eng, o))

    def ts(self, o, a, s1, op0, s2=None, op1=None, eng="dve", accum=None):
        ins = [a]
        outs = [o]
        a1 = s1.ap if isinstance(s1, V) else float(s1)
        if isinstance(s1, V):
            ins.append(s1)
        a2 = None
        if s2 is not None:
            a2 = s2.ap if isinstance(s2, V) else float(s2)
            if isinstance(s2, V):
                ins.append(s2)
        kw = {}
        if op1 is not None:
            kw["op1"] = op1
        if accum is not None:
            kw["accum_out"] = accum.ap
            outs.append(accum)
        self.op(eng, lambda e: e.tensor_scalar(out=o.ap, in0=a.ap, scalar1=a1, scalar2=a2, op0=op0, **kw),
                outs, ins, cost=self._cost(eng, o))

    def stt(self, o, a, s, b, op0, op1, eng="dve"):
        ins = [a, b]
        sv = s.ap if isinstance(s, V) else float(s)
        if isinstance(s, V):
            ins.append(s)
        self.op(eng, lambda e: e.scalar_tensor_tensor(out=o.ap, in0=a.ap, scalar=sv, in1=b.ap, op0=op0, op1=op1),
                [o], ins, cost=self._cost(eng, o))

    def cp(self, o, a, eng="dve"):
        if eng == "act":
            self.op("act", lambda e: e.copy(out=o.ap, in_=a.ap), [o], [a], cost=self._cost("act", o))
        else:
            self.op(eng, lambda e: e.tensor_copy(out=o.ap, in_=a.ap), [o], [a], cost=self._cost(eng, o, cast=True))

    def memset(self, o, val, eng="pool"):
        self.op(eng, lambda e: e.memset(o.ap, val), [o], [], cost=self._cost(eng, o))


def _ident_bf16():
    import ml_dtypes
    return np.eye(128, dtype=np.float32).astype(ml_dtypes.bfloat16)


class Ctx:
    pass


def setup_common(P, C):
    C.hT = P.sb("hT", [128, D // 128, L + 2 * PAD], BF16, perm=True)
    C.ident = P.sb("ident", [128, 128], BF16, perm=True)
    C.identf = P.sb("identf", [128, 128], F32, perm=True)
    C.ones_bf = P.sb("ones_bf", [128, 128], BF16, perm=True)
    C.ones_f = P.sb("ones_f", [128, 128], F32, perm=True)
    P.dma(C.ident.v, C.d_ident.v)
    P.dma(C.identf.v, C.d_identf.v)
    P.memset(C.ones_bf.v, 1.0)
    P.memset(C.ones_f.v, 1.0)
    P.memset(C.hT.v, 0.0)
    C.eps_col = P.sb("eps_col", [128, 1], F32, perm=True)
    P.memset(C.eps_col.v, LN_EPS)


def load_x_to_hT(P, C):
    P.begin_phase()
    xt = [P.sb("xt%d" % i, [128, D], F32) for i in range(2)]
    xb = [P.sb("xb%d" % i, [128, D], BF16) for i in range(2)]
    pt = [P.ps("ptr%d" % i, [128, D], BF16) for i in range(2)]
    for tt in range(NT):
        s = tt % 2
        P.dma(xt[s].v, C.x[tt * 128:(tt + 1) * 128, :])
        P.dma(C.h[tt * 128:(tt + 1) * 128, :], xt[s].v)
        P.cp(xb[s].v, xt[s].v, eng="act")
        for kc in range(D // 128):
            P.tr(pt[s][:, kc * 128:(kc + 1) * 128], xb[s][:, kc * 128:(kc + 1) * 128], C.ident.v)
        P.cp(C.hT[:, :, PAD + tt * 128: PAD + (tt + 1) * 128],
             pt[s].v.re("p (k t) -> p k t", k=D // 128))
    P.end_phase()


def row_bcast(P, dst, src_buf, sl):
    P.dma(dst, V(src_buf, src_buf.t[sl].partition_broadcast(128)))


def prep_w_chunk(P, Wt, w_d, c0, n, stage, convw_d=None, cc0=0, cwb=None, ntaps=0):
    nk = w_d.t.shape[0] // 128
    P.dma(stage[:, :nk, :n], w_d[:, c0:c0 + n].re("(k p) c -> p k c", p=128))
    if ntaps == 0:
        P.cp(Wt[0][:, :nk, :n], stage[:, :nk, :n], eng="pool")
        return
    row_bcast(P, cwb[:, :ntaps, :n], convw_d, (slice(None), slice(cc0, cc0 + n)))
    for j in range(ntaps):
        P.tt(Wt[j][:, :nk, :n], stage[:, :nk, :n],
             cwb[:, j:j + 1, :n].bc([128, nk, n]), ALU.mult, eng=("pool" if j % 2 else "dve"))


def mm_proj_tm(P, ps, C, Wts, shifts, tt, n, bias=None):
    nk = D // 128
    tot = len(Wts) * nk + (1 if bias is not None else 0)
    i = 0
    for Wt, sh in zip(Wts, shifts):
        for kc in range(nk):
            t0 = PAD + tt * 128 + sh
            P.mm(ps, C.hT[:, kc, t0:t0 + 128], Wt[:, kc, :n], start=(i == 0), stop=(i == tot - 1))
            i += 1
    if bias is not None:
        P.mm(ps, C.ones_bf[0:1, :], bias, start=False, stop=True)


class OutProj:
    def __init__(self, P, C, wout_d, lng_d, lnb_d, last=False):
        self.P, self.C, self.last = P, C, last
        self.w = P.sb("wout", [128, DI // 128, D], BF16)
        st = [P.sb("wost%d" % i, [128, 4, D], F32) for i in range(2)]
        for q in range(4):
            P.dma(st[q % 2].v, wout_d[q * 512:(q + 1) * 512, :].re("(k p) c -> p k c", p=128))
            P.cp(self.w[:, q * 4:(q + 1) * 4, :], st[q % 2].v, eng=("act" if q % 2 else "dve"))
        self.g = P.sb("lng", [128, D], F32)
        self.b = P.sb("lnb", [128, D], F32)
        row_bcast(P, self.g.v, lng_d, slice(None))
        row_bcast(P, self.b.v, lnb_d, slice(None))
        nb = 2
        self.yT = [P.sb("yT%d" % i, [128, DI // 128, 128], BF16) for i in range(nb)]
        self.ho = [P.sb("ho%d" % i, [128, D], F32) for i in range(nb)]
        self.r = [P.sb("r%d" % i, [128, D], F32) for i in range(nb)]
        self.hn = [P.sb("hn%d" % i, [128, D], F32) for i in range(nb)]
        self.hb = [P.sb("hb%d" % i, [128, D], BF16) for i in range(nb)]
        self.st6 = [P.sb("st6%d" % i, [128, 2, 6], F32) for i in range(nb)]
        self.mv = [P.sb("mv%d" % i, [128, 2], F32) for i in range(nb)]
        self.rstd = [P.sb("rstd%d" % i, [128, 1], F32) for i in range(nb)]
        self.pT = [P.ps("opT%d" % i, [128, DI], BF16) for i in range(2)]
        self.po = P.ps("opo", [128, D], F32)
        self.pH = P.ps("opH", [128, D], BF16)

    def tile(self, tt, y):
        P, C = self.P, self.C
        k = tt % 2
        yT, ho, r, hn, hb, st6, mv, rstd, pT = (self.yT[k], self.ho[k], self.r[k], self.hn[k], self.hb[k],
                                                  self.st6[k], self.mv[k], self.rstd[k], self.pT[k])
        for c in range(DI // 128):
            P.tr(pT[:, c * 128:(c + 1) * 128], y[:, c * 128:(c + 1) * 128], C.ident.v)
        h8 = DI // 256
        P.cp(yT[:, 0:h8, :], pT[:, 0:DI // 2].re("p (c t) -> p c t", c=h8), eng="act")
        P.cp(yT[:, h8:, :], pT[:, DI // 2:].re("p (c t) -> p c t", c=h8), eng="dve")
        P.dma(ho.v, C.h[tt * 128:(tt + 1) * 128, :])
        for dh in range(2):
            for c in range(DI // 128):
                P.mm(self.po[:, dh * 512:(dh + 1) * 512], yT[:, c, :],
                     self.w[:, c, dh * 512:(dh + 1) * 512], start=(c == 0), stop=(c == DI // 128 - 1))
        P.stt(r.v, ho.v, ALPHA, self.po.v, ALU.mult, ALU.add)
        for q in range(2):
            P.op("dve", lambda e, q=q: e.bn_stats(out=st6[:, q, :].ap, in_=r[:, q * 512:(q + 1) * 512].ap),
                 [st6.v], [r.v], cost=0.65)
        P.op("dve", lambda e: e.bn_aggr(out=mv.v.ap, in_=st6.v.re("p a b -> p (a b)").ap),
             [mv.v], [st6.v], cost=0.15)
        P.act(rstd.v, mv[:, 1:2], AF.Sqrt, scale=1.0, bias=C.eps_col[:, 0:1])
        P.op("dve", lambda e: e.reciprocal(out=rstd.v.ap, in_=rstd.v.ap), [rstd.v], [rstd.v], cost=0.12)
        P.ts(hn.v, r.v, mv[:, 0:1], ALU.subtract, rstd[:, 0:1], ALU.mult)
        P.tt(hn.v, hn.v, self.g.v, ALU.mult, eng="pool")
        P.tt(hn.v, hn.v, self.b.v, ALU.add)
        if self.last:
            P.dma(C.out[tt * 128:(tt + 1) * 128, :], hn.v)
            return
        P.dma(C.h[tt * 128:(tt + 1) * 128, :], hn.v)
        P.cp(hb.v, hn.v, eng="act")
        for kc in range(D // 128):
            P.tr(self.pH[:, kc * 128:(kc + 1) * 128], hb[:, kc * 128:(kc + 1) * 128], C.ident.v)
        P.cp(C.hT[:, :, PAD + tt * 128: PAD + (tt + 1) * 128],
             self.pH.v.re("p (k t) -> p k t", k=D // 128))


def mm_proj_fm(P, ps, C, Wt, c0, t0, n, shift=0, start=True, stop=True):
    nk = D // 128
    for kc in range(nk):
        a = PAD + t0 + shift
        P.mm(ps, Wt[:, kc, c0:c0 + 128], C.hT[:, kc, a:a + n],
             start=(start and kc == 0), stop=(stop and kc == nk - 1))


HG_H = 16


def hgrn2_consts():
    s = np.arange(128)[:, None]
    t = np.arange(128)[None, :]
    same = (s // 64) == (t // 64)
    mf = ((s <= t) & same).astype(np.float32)
    mb = ((s >= t) & same).astype(np.float32)
    rm = np.ones((128, 512), np.float32)
    rm[:, ::64] = 0.0
    return {"hg_mf": mf, "hg_mb": mb, "hg_rm": rm}


def hgrn2_layer(P, C, li, w_in, lb_raw, norm_g, w_out, ln_g, ln_b, last):
    S = C.scr
    qg = [S["qgf"], S["qgb"]]
    kg = [S["kgf"], S["kgb"]]
    P.begin_phase()
    lbr = P.sb("lbr", [32, 4, 128], F32)
    P.dma(lbr.v, lb_raw.v.re("l (g k) -> g l k", k=128))
    lbT = P.sb("lbT", [128, 4, 32], F32)
    pl = P.ps("pl", [128, 4, 32], F32)
    for l in range(4):
        P.tr(pl[:, l, :], lbr[:, l, :], C.identf[0:32, 0:32])
    P.act(lbT.v, pl.v, AF.Exp)
    den = P.sb("den", [128, 32], F32)
    num = P.sb("num", [128, 32], F32)
    P.tt(den.v, lbT[:, 0, :], lbT[:, 1, :], ALU.add)
    P.tt(den.v, den.v, lbT[:, 2, :], ALU.add)
    P.tt(den.v, den.v, lbT[:, 3, :], ALU.add)
    P.memset(num.v, 0.0, eng="dve")
    for j in range(1, li + 1):
        P.tt(num.v, num.v, lbT[:, j, :], ALU.add)
    P.op("dve", lambda e: e.reciprocal(out=den.v.ap, in_=den.v.ap), [den.v], [den.v])
    lb = P.sb("lb", [128, 32], F32)
    ln1mlb = P.sb("ln1mlb", [128, 32], F32)
    P.tt(lb.v, num.v, den.v, ALU.mult)
    P.ts(ln1mlb.v, lb.v, -1.0, ALU.mult, 1.0, ALU.add)
    P.act(ln1mlb.v, ln1mlb.v, AF.Ln)
    etot = P.sb("etot", [128, 2, HG_H, 64], F32)
    rm = P.sb("rm", [128, 512], F32)
    P.dma(rm.v, C.consts["hg_rm"].v)
    stage = P.sb("stA", [128, 8, 384], F32)
    Wt = [P.sb("WtA%d" % i, [128, 8, 384], BF16) for i in range(2)]
    pq = [P.ps("pq%d" % i, [128, 512], F32) for i in range(2)]
    pf = [[P.ps("pf%d_%d" % (d, i), [128, 512], F32) for i in range(2)] for d in range(2)]
    nb = 2
    e_ = [P.sb("e%d" % i, [128, 512], F32) for i in range(nb)]
    l1 = [P.sb("l1%d" % i, [128, 512], F32) for i in range(nb)]
    l2 = [P.sb("l2%d" % i, [128, 512], F32) for i in range(nb)]
    lf = [P.sb("lf%d" % i, [128, 512], F32) for i in range(nb)]
    pp = [P.sb("pp%d" % i, [128, 512], F32) for i in range(nb)]
    lk = [P.sb("lk%d" % i, [128, 512], F32) for i in range(nb)]
    eg = [P.sb("eg%d" % i, [128, 512], F32) for i in range(nb)]
    qo = [P.sb("qo%d" % i, [128, 512], BF16) for i in range(nb)]
    ko = [P.sb("ko%d" % i, [128, 512], BF16) for i in range(nb)]
    tot8 = [P.sb("tot8%d" % i, [128, 8], F32) for i in range(nb)]
    it = 0
    for hd in range(HG_H):
        W = Wt[hd % 2]
        for j, c0 in enumerate([hd * 128, 2048 + hd * 128, 4096 + hd * 128]):
            P.dma(stage[:, :, j * 128:(j + 1) * 128], w_in[:, c0:c0 + 128].re("(k p) c -> p k c", p=128))
        P.cp(W.v, stage.v, eng="pool")
        for tc in range(L // 512):
            t0 = tc * 512
            s = tc % 2
            mm_proj_fm(P, pq[s].v, C, W, 0, t0, 512)
            for d in range(2):
                mm_proj_fm(P, pf[d][s].v, C, W, 128 * (1 + d), t0, 512)
            for d in range(2):
                b = it % nb
                it += 1
                col = d * HG_H + hd
                ff = pf[d][s]
                P.act(e_[b].v, ff.v, AF.Exp, scale=-1.0)
                P.act(l1[b].v, e_[b].v, AF.Ln, scale=lb[:, col:col + 1], bias=1.0)
                P.act(l2[b].v, e_[b].v, AF.Ln, scale=1.0, bias=1.0)
                P.tt(lf[b].v, l1[b].v, l2[b].v, ALU.subtract)
                P.op("dve", lambda e, b=b: e.tensor_tensor_scan(out=pp[b].v.ap, data0=rm.v.ap, data1=lf[b].v.ap,
                                                               initial=0.0, op0=ALU.mult, op1=ALU.add),
                     [pp[b].v], [rm.v, lf[b].v])
                P.stt(lk[b].v, ff.v, -1.0, l2[b].v, ALU.mult, ALU.subtract)
                if d == 1:
                    P.tt(lf[b].v, lf[b].v, pp[b].v, ALU.subtract, eng="pool")
                    P.cp(tot8[b].v, pp[b].v.re("p (c t) -> p c t", t=64)[:, :, 63], eng="pool")
                    P.tt(pp[b].v.re("p (c t) -> p c t", t=64), lf[b].v.re("p (c t) -> p c t", t=64),
                         tot8[b].v.us(2).bc([128, 8, 64]), ALU.add)
                G = pp[b]
                P.act(eg[b].v, G.v, AF.Exp)
                ev = eg[b].v.re("p (c t) -> p c t", t=64)
                P.cp(etot[:, d, hd, tc * 8:(tc + 1) * 8], (ev[:, :, 63] if d == 0 else ev[:, :, 0]), eng="pool")
                P.tt(qo[b].v, pq[s].v, eg[b].v, ALU.mult)
                P.tt(lk[b].v, lk[b].v, G.v, ALU.subtract, eng="pool")
                P.act(ko[b].v, lk[b].v, AF.Exp, bias=ln1mlb[:, col:col + 1])
                P.dma(qg[d][hd, :, t0:t0 + 512], qo[b].v)
                P.dma(kg[d][hd, :, t0:t0 + 512], ko[b].v)
    P.dma(S["etot"].v, etot.v.re("p a h c -> p (a h c)"))
    P.end_phase()
    P.begin_phase()
    stage = P.sb("stB", [128, 8, 512], F32)
    WtB = [P.sb("WtB%d" % i, [128, 8, 512], BF16) for i in range(2)]
    pv = [P.ps("pv%d" % i, [128, 512], F32) for i in range(2)]
    vo = [P.sb("vo%d" % i, [128, 512], BF16) for i in range(2)]
    zo = [P.sb("zo%d" % i, [128, 512], F32) for i in range(2)]
    it = 0
    for isz in range(2):
        for c in range(4):
            W = WtB[c % 2]
            prep_w_chunk(P, [W], w_in, 6144 + isz * 2048 + c * 512, 512, stage)
            for tt in range(NT):
                s = it % 2
                it += 1
                mm_proj_tm(P, pv[s].v, C, [W], [0], tt, 512)
                if isz == 0:
                    P.cp(vo[s].v, pv[s].v, eng="act")
                    P.dma(S["vtm"][tt * 128:(tt + 1) * 128, c * 512:(c + 1) * 512], vo[s].v)
                else:
                    P.act(zo[s].v, pv[s].v, AF.Silu)
                    P.dma(S["zg"][tt * 128:(tt + 1) * 128, c * 512:(c + 1) * 512], zo[s].v)
    P.end_phase()
    P.begin_phase()
    etot = P.sb("etotC", [128, 2, HG_H, 64], F32)
    P.dma(etot.v.re("p a h c -> p (a h c)"), S["etot"].v)
    mask = [P.sb("mk%d" % i, [128, 128], F32) for i in range(2)]
    P.dma(mask[0].v, C.consts["hg_mf"].v)
    P.dma(mask[1].v, C.consts["hg_mb"].v)
    gb = P.sb("ngb", [128, DI], F32)
    row_bcast(P, gb.v, norm_g, slice(None))
    GH = 4
    Sf = [P.sb("Sf%d" % i, [128, GH, 128], F32) for i in range(HG_H // GH)]
    Sb = [P.sb("Sb%d" % i, [128, GH, 128], BF16) for i in range(HG_H // GH)]
    qT = [P.sb("qT%d" % i, [128, HG_H, 128], BF16) for i in range(2)]
    kT = [P.sb("kT%d" % i, [128, HG_H, 128], BF16) for i in range(2)]
    vt = [P.sb("vt%d" % i, [128, DI], BF16) for i in range(2)]
    ot = [P.sb("ot%d" % i, [128, DI], F32) for i in range(2)]
    of_t = [P.sb("oft%d" % i, [128, DI], F32) for i in range(2)]
    zt = [P.sb("zt%d" % i, [128, DI], F32) for i in range(2)]
    sq = P.sb("sq", [128, DI], F32)
    ss = P.sb("ss", [128, HG_H], F32)
    yb = [P.sb("yb%d" % i, [128, DI], BF16) for i in range(2)]
    pA = [P.ps("pA%d" % i, [128, GH, 128], F32) for i in range(2)]
    pT = [P.ps("pT%d" % i, [128, GH, 128], BF16) for i in range(2)]
    pO = [P.ps("pO%d" % i, [128, GH, 128], F32) for i in range(2)]
    pU = [P.ps("pU%d" % i, [128, GH, 128], F32) for i in range(2)]
    am = [P.sb("am%d" % i, [128, GH, 128], BF16) for i in range(2)]
    ktm = [P.sb("ktm%d" % i, [128, GH, 128], BF16) for i in range(2)]
    gi = 0
    for d in range(2):
        for g in range(HG_H // GH):
            P.memset(Sf[g].v, 0.0, eng="dve")
            P.memset(Sb[g].v, 0.0, eng="dve")
        order = list(range(NT)) if d == 0 else list(range(NT - 1, -1, -1))
        for n_, tt in enumerate(order):
            s = n_ % 2
            tsl = slice(tt * 128, (tt + 1) * 128)
            P.dma(qT[s].v, qg[d][:, :, tsl].re("h k t -> k h t"))
            P.dma(kT[s].v, kg[d][:, :, tsl].re("h k t -> k h t"))
            P.dma(vt[s].v, S["vtm"][tsl, :])
            if d == 1:
                P.dma(of_t[s].v, S["of"][tsl, :])
                P.dma(zt[s].v, S["zg"][tsl, :])
            chunks = [0, 1] if d == 0 else [1, 0]
            for g in range(HG_H // GH):
                p = gi % 2
                gi += 1
                hs = range(g * GH, (g + 1) * GH)
                for i, hd in enumerate(hs):
                    P.mm(pA[p][:, i, :], kT[s][:, hd, :], qT[s][:, hd, :])
                    P.tr(pT[p][:, i, :], kT[s][:, hd, :], C.ident.v)
                P.tt(am[p].v, pA[p].v, mask[d].v.us(1).bc([128, GH, 128]), ALU.mult)
                P.cp(ktm[p].v, pT[p].v, eng="act")
                for ci, ch in enumerate(chunks):
                    rs = slice(ch * 64, (ch + 1) * 64)
                    cidx = tt * 2 + ch
                    for i, hd in enumerate(hs):
                        vs = vt[s][:, hd * 128:(hd + 1) * 128]
                        P.mm(pO[p][rs, i, :], am[p][:, i, rs], vs, start=True, stop=False)
                        P.mm(pO[p][rs, i, :], qT[s][:, hd, rs], Sb[g][:, i, :], start=False, stop=True)
                        P.mm(pU[p][:, i, :], ktm[p][rs, i, :], vt[s][rs, hd * 128:(hd + 1) * 128])
                    e_bc = etot[:, d, g * GH:(g + 1) * GH, cidx].us(2).bc([128, GH, 128])
                    P.tt(Sf[g].v, pU[p].v, Sf[g].v, ALU.add)
                    P.tt(Sf[g].v, Sf[g].v, e_bc, ALU.mult)
                    P.cp(Sb[g].v, Sf[g].v, eng="act")
                osl = ot[s][:, g * GH * 128:(g + 1) * GH * 128]
                if d == 0:
                    P.cp(osl, pO[p].v.re("p g v -> p (g v)"), eng="act")
                else:
                    P.tt(osl, pO[p].v.re("p g v -> p (g v)"), of_t[s][:, g * GH * 128:(g + 1) * GH * 128], ALU.add)
            if d == 0:
                P.dma(S["of"][tsl, :], ot[s].v)
            else:
                o3 = ot[s].v.re("p (h v) -> p h v", v=128)
                P.tt(sq.v, ot[s].v, ot[s].v, ALU.mult, eng="pool")
                P.op("dve", lambda e: e.tensor_reduce(out=ss.v.ap, in_=sq.v.re("p (h v) -> p h v", v=128).ap,
                                                     axis=AX.X, op=ALU.add), [ss.v], [sq.v])
                P.act(ss.v, ss.v, AF.Sqrt, scale=1.0 / 128, bias=C.eps_col[:, 0:1])
                P.op("dve", lambda e: e.reciprocal(out=ss.v.ap, in_=ss.v.ap), [ss.v], [ss.v])
                P.tt(o3, o3, ss.v.us(2).bc([128, HG_H, 128]), ALU.mult)
                P.tt(zt[s].v, zt[s].v, gb.v, ALU.mult, eng="pool")
                P.tt(yb[s].v, ot[s].v, zt[s].v, ALU.mult)
                P.dma(S["y"][tsl, :], yb[s].v)
    P.end_phase()
    outproj_phase(P, C, w_out, ln_g, ln_b, last)


def outproj_phase(P, C, w_out, ln_g, ln_b, last):
    P.begin_phase()
    op = OutProj(P, C, w_out, ln_g, ln_b, last=last)
    yb = [P.sb("ypb%d" % i, [128, DI], BF16) for i in range(2)]
    for tt in range(NT):
        P.dma(yb[tt % 2].v, C.scr["y"][tt * 128:(tt + 1) * 128, :])
        op.tile(tt, yb[tt % 2].v)
    P.end_phase()


LAYER_KIND = ["hyena", "ssd", "hgrn2", "hyena"]
PARAMS = {
    "hyena": ["w_in", "conv_w", "conv_b", "filt_w1", "filt_b1", "filt_w2", "filt_b2", "filt_w3", "filt_b3",
              "filt_freq", "filt_w_out", "skip", "w_out", "ln_g", "ln_b"],
    "ssd": ["w_in", "conv_w", "conv_b", "dt_bias", "a_log", "d_skip", "norm_g", "w_out", "ln_g", "ln_b"],
    "hgrn2": ["w_in", "norm_g", "w_out", "ln_g", "ln_b"],
}
SHAPES = {
    "hyena": {"w_in": [D, 8192], "conv_w": [3, 6144], "conv_b": [6144], "filt_w1": [33, 64], "filt_b1": [64],
              "filt_w2": [64, 64], "filt_b2": [64], "filt_w3": [64, 64], "filt_b3": [64], "filt_freq": [64],
              "filt_w_out": [64, 4096], "skip": [DI], "w_out": [DI, D], "ln_g": [D], "ln_b": [D]},
    "ssd": {"w_in": [D, 5184], "conv_w": [5, 3072], "conv_b": [3072], "dt_bias": [2, 32], "a_log": [2, 32],
            "d_skip": [32], "norm_g": [DI], "w_out": [DI, D], "ln_g": [D], "ln_b": [D]},
    "hgrn2": {"w_in": [D, 10240], "norm_g": [DI], "w_out": [DI, D], "ln_g": [D], "ln_b": [D]},
}


def all_consts():
    c = {"ident": _ident_bf16(), "identf": np.eye(128, dtype=np.float32)}
    c.update(hgrn2_consts())
    c.update(ssd_consts())
    c.update(hyena_consts())
    return c


def build_program(layers):
    P = Prog()
    C = Ctx()
    C.x = P.dram("x", [L, D], F32, kind="ExternalInput")
    C.out = P.dram("out", [L, D], F32, kind="ExternalOutput")
    C.h = P.dram("h_scr", [L, D], F32)
    C.consts = {}
    cst = all_consts()
    for k, v in cst.items():
        dt = BF16 if v.dtype != np.float32 else F32
        C.consts[k] = P.dram("c_" + k, list(v.shape), dt, kind="ExternalInput")
    C.d_ident = C.consts["ident"]
    C.d_identf = C.consts["identf"]
    C.lbraw = P.dram("hgrn_lower_bounds", [4, 4096], F32, kind="ExternalInput")
    C.prm = {}
    for li in layers:
        kind = LAYER_KIND[li]
        for nme in PARAMS[kind]:
            key = "l%d_%s" % (li, nme)
            C.prm[key] = P.dram(key, SHAPES[kind][nme], F32, kind="ExternalInput")
    S = {}
    for nme in ["qgf", "qgb", "kgf", "kgb"]:
        S[nme] = P.dram("s_" + nme, [HG_H, 128, L], BF16)
    S["vtm"] = P.dram("s_vtm", [L, DI], BF16)
    S["zg"] = P.dram("s_zg", [L, DI], F32)
    S["of"] = P.dram("s_of", [L, DI], F32)
    S["y"] = P.dram("s_y", [L, DI], BF16)
    S["xs"] = P.dram("s_xs", [L, DI], F32)
    S["xsb"] = P.dram("s_xsb", [L, DI], BF16)
    S["btm"] = P.dram("s_btm", [L, 512], BF16)
    S["bT"] = P.dram("s_bT", [4, 128, L], BF16)
    S["cT"] = P.dram("s_cT", [4, 128, L], BF16)
    S["dt"] = P.dram("s_dt", [L, 64], F32)
    S["Zs"] = P.dram("s_Zs", [128, 64, DI], BF16)
    S["Zv"] = P.dram("s_Zv", [128, 64, DI], BF16)
    S["Kf"] = P.dram("s_Kf", [NK2, 128, DI], BF16)
    S["gbf"] = P.dram("s_gbf", [L, DI], BF16)
    S["etot"] = P.dram("s_etot", [128, 2 * HG_H * 64], F32)
    S["gatebf"] = P.dram("s_gatebf", [L, DI], BF16)
    S["adt"] = P.dram("s_adt", [L, 64], F32)
    C.scr = S
    setup_common(P, C)
    load_x_to_hT(P, C)
    for n_, li in enumerate(layers):
      try:
          kind = LAYER_KIND[li]
          last = (n_ == len(layers) - 1)
          pr = lambda nme: C.prm["l%d_%s" % (li, nme)]
          if kind == "hgrn2":
              hgrn2_layer(P, C, li, pr("w_in"), C.lbraw, pr("norm_g"), pr("w_out"), pr("ln_g"), pr("ln_b"), last)
          elif kind == "ssd":
              ssd_layer(P, C, li, pr("w_in"), pr("conv_w"), pr("conv_b"), pr("dt_bias"), pr("a_log"), pr("d_skip"),
                        pr("norm_g"), pr("w_out"), pr("ln_g"), pr("ln_b"), last)
          else:
              hyena_layer(P, C, li, {nme: pr(nme) for nme in PARAMS["hyena"]}, last)
      except StopBuild:
        break
    P.barrier()
    return P, C


def in_map_for(inputs, b, layers):
    m = {"x": np.ascontiguousarray(inputs["x"][b]), "hgrn_lower_bounds": inputs["hgrn_lower_bounds"]}
    for k, v in all_consts().items():
        m["c_" + k] = v
    for li in layers:
        for nme in PARAMS[LAYER_KIND[li]]:
            key = "l%d_%s" % (li, nme)
            m[key] = inputs[key]
    return m


def pipeline(n, stages, offs=None):
    ns = len(stages)
    if offs is None:
        offs = list(range(ns))
    for step in range(n + max(offs)):
        for k in range(ns - 1, -1, -1):
            i = step - offs[k]
            if 0 <= i < n:
                stages[k](i)


def ssd_scan_phase(P, C, norm_g, d_skip):
    S = C.scr
    P.begin_phase()
    M1 = [P.sb("M1%d" % i, [128, 128], F32) for i in range(2)]
    P.dma(M1[0].v, C.consts["sd_mf"].v)
    P.dma(M1[1].v, C.consts["sd_mb"].v)
    gb = P.sb("sgb", [128, DI], F32)
    row_bcast(P, gb.v, norm_g, slice(None))
    dsk = P.sb("dsk", [128, 32], F32)
    row_bcast(P, dsk.v, d_skip, slice(None))
    Sp = [P.sb("Sp%d" % i, [128, 512], F32) for i in range(4)]
    Spb = [P.sb("Spb%d" % i, [128, 512], BF16) for i in range(4)]
    NL = 2
    xs = [P.sb("xs%d" % i, [128, DI], BF16) for i in range(NL)]
    dtt = [P.sb("dtt%d" % i, [128, 64], F32) for i in range(NL)]
    adt = [P.sb("adt%d" % i, [128, 64], F32) for i in range(NL)]
    btm = [P.sb("btm%d" % i, [128, 512], BF16) for i in range(NL)]
    bT = [P.sb("bT%d" % i, [128, 4, 128], BF16) for i in range(NL)]
    cT = [P.sb("cT%d" % i, [128, 4, 128], BF16) for i in range(NL)]
    yft = [P.sb("yft%d" % i, [128, DI], F32) for i in range(2)]
    zt = [P.sb("szt%d" % i, [128, DI], F32) for i in range(2)]
    ct = [P.sb("ct%d" % i, [128, 64], F32) for i in range(2)]
    ncs = [P.sb("ncs%d" % i, [128, 32], F32) for i in range(2)]
    ecs = [P.sb("ecs%d" % i, [128, 32], F32) for i in range(2)]
    dst_ = [P.sb("dst%d" % i, [128, 32], F32) for i in range(2)]
    etot = [P.sb("setot%d" % i, [128, 32], F32) for i in range(2)]
    xdt = [P.sb("xdt%d" % i, [128, DI], BF16) for i in range(2)]
    xdtd = [P.sb("xdtd%d" % i, [128, DI], BF16) for i in range(2)]
    CBm = [P.sb("CBm%d" % i, [128, 128], F32) for i in range(1)] * 2
    X = [P.sb("X%d" % i, [128, 8, 128], F32) for i in range(1)] * 2
    X2 = [P.sb("X2%d" % i, [128, 8, 128], F32) for i in range(1)] * 2
    Mh = [P.sb("Mh%d" % i, [128, 8, 128], BF16) for i in range(2)]
    tmp = [P.sb("stmp%d" % i, [128, 512], F32) for i in range(1)] * 2
    yt = [P.sb("syt%d" % i, [128, DI], F32) for i in range(2)]
    ss = P.sb("sss", [128, 4], F32)
    yb = [P.sb("syb%d" % i, [128, DI], BF16) for i in range(1)] * 2
    p_ct = P.ps("p_ct", [128, 64], F32)
    p_cb = [P.ps("p_cb%d" % i, [128, 128], F32) for i in range(2)]
    p_row = P.ps("p_row", [128, 8, 128], F32)
    p_y = P.ps("p_y", [128, 512], F32)
    p_st = P.ps("p_st", [128, 512], F32)
    p_yo = P.ps("p_yo", [128, 512], F32)
    NTD = NT if DBG_LEVEL >= 99 else DBG_NT
    tiles = []
    for d in range(2):
        order = list(range(NT)) if d == 0 else list(range(NT - 1, -1, -1))
        tiles += [(d, tt) for tt in order[:NTD]]
    NG = len(tiles) * 4

    def info(i):
        Ti = i // 4
        d, tt = tiles[Ti]
        return Ti, d, tt, i % 4

    def st_load(i):
        Ti, d, tt, g = info(i)
        if g != 0:
            return
        s = Ti % NL
        tsl = slice(tt * 128, (tt + 1) * 128)
        P.dma(xs[s].v, S["xsb"][tsl, :])
        P.dma(dtt[s].v, S["dt"][tsl, :])
        P.dma(adt[s].v, S["adt"][tsl, :])
        P.dma(btm[s].v, S["btm"][tsl, :])
        P.dma(bT[s].v, S["bT"][:, :, tsl].re("g n t -> n g t"))
        P.dma(cT[s].v, S["cT"][:, :, tsl].re("g n t -> n g t"))
        if d == 1:
            P.dma(zt[Ti % 2].v, S["zg"][tsl, :])

    def st_pro(i):
        Ti, d, tt, g = info(i)
        if g != 0:
            return
        s = Ti % NL
        u = Ti % 2
        hc = slice(d * 32, (d + 1) * 32)
        P.mm(p_ct[:, 0:32], M1[d].v, adt[s][:, hc])
        P.mm(p_ct[:, 32:64], C.ones_f.v, adt[s][:, hc])
        P.cp(ct[u].v, p_ct.v)
        P.ts(ncs[u].v, ct[u][:, 0:32], -1.0, ALU.mult)
        P.act(ecs[u].v, ct[u][:, 0:32], AF.Exp)
        P.act(etot[u].v, ct[u][:, 32:64], AF.Exp)
        P.tt(dst_[u].v, ct[u][:, 32:64], ct[u][:, 0:32], ALU.subtract)
        P.act(dst_[u].v, dst_[u].v, AF.Exp)
        P.tt(dst_[u].v, dst_[u].v, dtt[s][:, hc], ALU.mult)
        x3 = xs[s].v.re("p (h q) -> p h q", q=64)
        P.tt(xdt[u].v.re("p (h q) -> p h q", q=64), x3, dtt[s][:, hc].us(2).bc([128, 32, 64]), ALU.mult)
        P.tt(xdtd[u].v.re("p (h q) -> p h q", q=64), x3, dst_[u].v.us(2).bc([128, 32, 64]), ALU.mult, eng="pool")

    def st1(i):
        Ti, d, tt, g = info(i)
        s = Ti % NL
        k = i % 2
        hg = slice(d * 32 + g * 8, d * 32 + (g + 1) * 8)
        P.mm(p_cb[k].v, bT[s][:, g, :], cT[s][:, g, :])
        P.tt(X[k].v, M1[d].v.us(1).bc([128, 8, 128]), adt[s][:, hg].us(2).bc([128, 8, 128]), ALU.mult, eng="pool")
        for q in range(2):
            P.mm(p_row[:, q * 4:(q + 1) * 4, :], C.ones_f.v, X[k][:, q * 4:(q + 1) * 4, :])

    def st2(i):
        Ti, d, tt, g = info(i)
        u = Ti % 2
        k = i % 2
        hl = slice(g * 8, (g + 1) * 8)
        P.tt(CBm[k].v, p_cb[k].v, M1[d].v, ALU.mult)
        P.tt(X2[k].v, p_row.v, ncs[u][:, hl].us(2).bc([128, 8, 128]), ALU.add)
        P.act(X2[k].v, X2[k].v, AF.Relu, scale=-1.0)
        P.act(X2[k].v, X2[k].v, AF.Exp, scale=-1.0)
        P.tt(Mh[k].v, X2[k].v, CBm[k].v.us(1).bc([128, 8, 128]), ALU.mult)

    def st3(i):
        Ti, d, tt, g = info(i)
        s = Ti % NL
        u = Ti % 2
        k = i % 2
        hl = slice(g * 8, (g + 1) * 8)
        tsl = slice(tt * 128, (tt + 1) * 128)
        if tt == (0 if d == 0 else NT - 1):
            P.memset(Sp[g].v, 0.0, eng="dve")
            P.memset(Spb[g].v, 0.0, eng="dve")
        if g == 0 and d == 1:
            P.dma(yft[u].v, S["of"][tsl, :])
        for h in range(8):
            hh = g * 8 + h
            P.mm(p_y[:, h * 64:(h + 1) * 64], Mh[k][:, h, :], xdt[u][:, hh * 64:(hh + 1) * 64])
        P.mm(p_yo.v, cT[s][:, g, :], Spb[g].v)
        P.mm(p_st.v, btm[s][:, g * 128:(g + 1) * 128], xdtd[u][:, g * 512:(g + 1) * 512])
        P.tt(tmp[k].v.re("p (h q) -> p h q", q=64), p_yo.v.re("p (h q) -> p h q", q=64),
             ecs[u][:, hl].us(2).bc([128, 8, 64]), ALU.mult)
        P.tt(yt[u][:, g * 512:(g + 1) * 512], p_y.v, tmp[k].v, ALU.add)
        sg = Sp[g].v
        P.tt(sg.re("p (h q) -> p h q", q=64), sg.re("p (h q) -> p h q", q=64),
             etot[u][:, hl].us(2).bc([128, 8, 64]), ALU.mult, eng="pool")
        P.tt(sg, sg, p_st.v, ALU.add)
        P.cp(Spb[g].v, sg, eng="act")
        if g != 3:
            return
        if d == 0:
            P.dma(S["of"][tsl, :], yt[u].v)
            return
        x3 = xs[s].v.re("p (h q) -> p h q", q=64)
        sq = yft[u]
        P.tt(yt[u].v, yt[u].v, yft[u].v, ALU.add)
        P.tt(sq.v.re("p (h q) -> p h q", q=64), x3, dsk.v.us(2).bc([128, 32, 64]), ALU.mult, eng="pool")
        P.tt(yt[u].v, yt[u].v, sq.v, ALU.add)
        P.tt(yt[u].v, yt[u].v, zt[u].v, ALU.mult)
        for q in range(4):
            P.act(sq[:, q * 512:(q + 1) * 512], yt[u][:, q * 512:(q + 1) * 512], AF.Square, accum=ss[:, q:q + 1])
        P.act(ss.v, ss.v, AF.Sqrt, scale=1.0 / 512, bias=C.eps_col[:, 0:1])
        P.op("dve", lambda e: e.reciprocal(out=ss.v.ap, in_=ss.v.ap), [ss.v], [ss.v], cost=0.12)
        for q in range(4):
            P.act(yt[u][:, q * 512:(q + 1) * 512], yt[u][:, q * 512:(q + 1) * 512], AF.Copy, scale=ss[:, q:q + 1])
        P.tt(yb[u].v, yt[u].v, gb.v, ALU.mult)
        P.dma(S["y"][tsl, :], yb[u].v)

    pipeline(NG, [st_load, st_pro, st1, st2, st3], [2, 4, 5, 6, 7])
    P.end_phase()


def ssd_consts():
    r = np.arange(128)[:, None]
    t = np.arange(128)[None, :]
    return {"sd_mf": (r <= t).astype(np.float32), "sd_mb": (r >= t).astype(np.float32)}


def ssd_layer(P, C, li, w_in, conv_w, conv_b, dt_bias, a_log, d_skip, norm_g, w_out, ln_g, ln_b, last):
    S = C.scr
    NTAP = 5
    shifts = [-2, -1, 0, 1, 2]
    P.begin_phase()
    stage = P.sb("sstA", [128, 8, 512], F32)
    Wz = [P.sb("Wz%d" % i, [128, 8, 512], BF16) for i in range(2)]
    pz = [P.ps("pz%d" % i, [128, 512], F32) for i in range(2)]
    zo = [P.sb("szo%d" % i, [128, 512], F32) for i in range(2)]
    bo = [P.sb("sbo%d" % i, [128, 512], BF16) for i in range(2)]
    it = 0
    for c in range(0 if DBG_SKIPA else 4):
        W = Wz[c % 2]
        prep_w_chunk(P, [W], w_in, c * 512, 512, stage)
        for tt in range(NT):
            s = it % 2
            it += 1
            mm_proj_tm(P, pz[s].v, C, [W], [0], tt, 512)
            P.act(zo[s].v, pz[s].v, AF.Silu)
            P.dma(S["zg"][tt * 128:(tt + 1) * 128, c * 512:(c + 1) * 512], zo[s].v)
    if not DBG_SKIPA:
        wcol, bcol = conv_cols(P, C, conv_w, conv_b, NTAP, 24)
        stg = P.sb("sstg", [128, 8, 128], F32)
        Wb = [P.sb("sWb%d" % i, [128, 8, 128], BF16) for i in range(2)]
        raw = [P.sb("sraw%d" % i, [128, L + 4], BF16) for i in range(2)]
        for i in range(2):
            P.memset(raw[i][:, 0:2], 0.0, eng="dve")
            P.memset(raw[i][:, L + 2:L + 4], 0.0, eng="dve")
        crow = [P.sb("scrow%d" % i, [128, L], BF16) for i in range(2)]
        xT2 = [P.sb("sxT%d" % i, [128, 2, L], BF16) for i in range(2)]
        pp = [P.ps("spp%d" % i, [128, 512], F32) for i in range(3)]
        ptr = [P.ps("sptr%d" % i, [128, 256], BF16) for i in range(2)]
        so = [P.sb("sso%d" % i, [128, 256], BF16) for i in range(2)]
        it2 = 0
        for i in range(24):
            sb_ = i % 2
            P.dma(stg.v, w_in[:, 2048 + i * 128: 2048 + (i + 1) * 128].re("(k p) c -> p k c", p=128))
            P.cp(Wb[sb_].v, stg.v)
            for tc in range(L // 512):
                b = it2 % 3
                it2 += 1
                mm_proj_fm(P, pp[b].v, C, Wb[sb_], 0, tc * 512, 512)
                P.cp(raw[sb_][:, 2 + tc * 512: 2 + (tc + 1) * 512], pp[b].v, eng=("act" if tc % 2 else "dve"))
            if i < 20:
                grp = (i // 2) % 2
                dst = xT2[grp][:, i % 2, :]
            else:
                dst = crow[i % 2].v
            conv_row(P, dst, raw[sb_], wcol, bcol, i, NTAP, func=AF.Silu)
            if 16 <= i < 20:
                P.dma(S["bT"][i - 16, :, :], dst)
            if i >= 20:
                P.dma(S["cT"][i - 20, :, :], dst)
            if i < 20 and i % 2 == 1:
                i0 = i - 1
                for tt in range(NT):
                    q = tt % 2
                    for j in range(2):
                        P.tr(ptr[q][:, j * 128:(j + 1) * 128], xT2[grp][:, j, tt * 128:(tt + 1) * 128], C.ident.v)
                    P.cp(so[q].v, ptr[q].v, eng=("act" if tt % 2 else "dve"))
                    if i0 < 16:
                        P.dma(S["xsb"][tt * 128:(tt + 1) * 128, i0 * 128:(i0 + 2) * 128], so[q].v)
                    else:
                        P.dma(S["btm"][tt * 128:(tt + 1) * 128, (i0 - 16) * 128:(i0 - 14) * 128], so[q].v)
    Wd = Wz[0]
    prep_w_chunk(P, [Wd], w_in, 5120, 64, stage)
    dtb = P.sb("dtb", [128, 64], F32)
    arow = P.sb("arow", [128, 64], F32)
    P.dma(dtb.v, V(dt_bias, dt_bias.t.rearrange("a b -> (a b)").partition_broadcast(128)))
    P.dma(arow.v, V(a_log, a_log.t.rearrange("a b -> (a b)").partition_broadcast(128)))
    P.act(arow.v, arow.v, AF.Exp)
    dto = [P.sb("dto%d" % i, [128, 64], F32) for i in range(2)]
    ado = [P.sb("ado%d" % i, [128, 64], F32) for i in range(2)]
    for tt in range(NT):
        s = tt % 2
        mm_proj_tm(P, pz[s][:, 0:64], C, [Wd], [0], tt, 64)
        P.tt(dto[s].v, pz[s][:, 0:64], dtb.v, ALU.add)
        P.act(dto[s].v, dto[s].v, AF.Exp)
        P.act(dto[s].v, dto[s].v, AF.Ln, bias=1.0)
        P.stt(ado[s].v, dto[s].v, -1.0, arow.v, ALU.mult, ALU.mult)
        P.dma(S["dt"][tt * 128:(tt + 1) * 128, :], dto[s].v)
        P.dma(S["adt"][tt * 128:(tt + 1) * 128, :], ado[s].v)
    P.end_phase()
    ssd_scan_phase(P, C, norm_g, d_skip)
    outproj_phase(P, C, w_out, ln_g, ln_b, last)


NF = 8192
NK2 = 65


def hyena_consts():
    import ml_dtypes
    bf = ml_dtypes.bfloat16
    n = NF
    tp = 2.0 * np.pi
    p = np.arange(128)
    k2 = np.arange(65)
    k2s = np.arange(1, 64)
    F1 = np.zeros((128, 128))
    F1[:, :65] = np.cos(tp * ((p[:, None] * k2[None, :]) % 128) / 128)
    F1[:, 65:] = -np.sin(tp * ((p[:, None] * k2s[None, :]) % 128) / 128)
    pp = np.arange(64)
    m = np.full(65, 2.0)
    m[0] = 1.0
    m[64] = 1.0
    Finv = np.zeros((128, 64))
    Finv[:65, :] = (m[:, None] / n) * np.cos(tp * ((k2[:, None] * pp[None, :]) % 128) / 128)
    Finv[65:, :] = -(2.0 / n) * np.sin(tp * ((k2s[:, None] * pp[None, :]) % 128) / 128)
    a = np.arange(64)
    k1 = np.arange(64)
    TA = np.zeros((65, 128, 128))
    TB = np.zeros((65, 128, 128))
    TC = np.zeros((65, 128, 128))
    for q in range(65):
        th = tp * ((a[:, None] * (q + 128 * k1[None, :])) % n) / n
        Tr = np.cos(th)
        Ti = -np.sin(th)
        TA[q, :64, :64] = Tr
        TA[q, :64, 64:] = Ti
        TA[q, 64:, :64] = -Ti
        TA[q, 64:, 64:] = Tr
        TB[q, :64, :64] = -Ti
        TB[q, :64, 64:] = Tr
        TB[q, 64:, :64] = -Tr
        TB[q, 64:, 64:] = -Ti
        Ur = np.cos(th).T
        Ui = np.sin(th).T
        TC[q, :64, :64] = Ur
        TC[q, 64:, :64] = -Ui
        TC[q, :64, 64:] = Ui
        TC[q, 64:, 64:] = Ur
    j = (64 * p[None, :] + a[:, None]).reshape(-1)
    pos = np.where(j < L, j, n - j)
    pos[j == L] = 0
    tl = np.linspace(0.0, 1.0, L)
    t = tl[pos]
    w = tp * pos / L
    f = np.linspace(1e-4, 15.0, 16)
    z = np.concatenate([t[:, None], np.cos(f[None, :] * w[:, None]), -np.sin(f[None, :] * w[:, None])], axis=1)
    tdec = t.copy()
    tdec[j == L] = 1e4
    ntpos = -(tdec.reshape(64, 128).T)
    min_d = math.log(1e-2) / 0.3
    max_d = math.log(1e-2) / 1.5
    deltas = np.abs(np.linspace(min_d, max_d, DI))
    return {
        "hy_F1": F1.astype(np.float32).astype(bf),
        "hy_Finv": Finv.astype(np.float32).astype(bf),
        "hy_TA": np.ascontiguousarray(TA.transpose(1, 0, 2)).astype(np.float32).astype(bf),
        "hy_TB": np.ascontiguousarray(TB.transpose(1, 0, 2)).astype(np.float32).astype(bf),
        "hy_TC": np.ascontiguousarray(TC.transpose(1, 0, 2)).astype(np.float32).astype(bf),
        "hy_zT": np.ascontiguousarray(z.T).astype(np.float32),
        "hy_ntpos": ntpos.astype(np.float32),
        "hy_deltas": deltas.astype(np.float32),
    }


def sin_reduced(P, o, arg, tmpf, tmpi, tmp2):
    tp = 2.0 * math.pi
    P.ts(tmpf, arg, 1.0 / tp, ALU.mult, 64.0, ALU.add)
    P.cp(tmpi, tmpf)
    P.cp(tmpf, tmpi)
    P.ts(tmpf, tmpf, -64.0, ALU.add, -tp, ALU.mult)
    P.tt(tmpf, tmpf, arg, ALU.add)
    P.ts(tmp2, tmpf, math.pi, ALU.is_gt, -tp, ALU.mult)
    P.tt(tmpf, tmpf, tmp2, ALU.add)
    P.ts(tmpf, tmpf, 3.14159, ALU.min, -3.14159, ALU.max)
    P.act(o, tmpf, AF.Sin)


def conv_cols(P, C, conv_w, conv_b, ntaps, nch):
    wrow = P.sb("cw_row", [nch, ntaps, 128], F32)
    brow = P.sb("cb_row", [nch, 128], F32)
    P.dma(wrow.v, conv_w.v.re("j (k p) -> k j p", p=128))
    P.dma(brow.v, conv_b.v.re("(k p) -> k p", p=128))
    wcol = P.sb("cw_col", [128, ntaps, nch], F32)
    bcol = P.sb("cb_col", [128, nch], F32)
    pt = P.ps("cw_ps", [128, ntaps + 1, nch], F32)
    for j in range(ntaps):
        P.tr(pt[:, j, :], wrow[:, j, :], C.identf[0:nch, 0:nch])
    P.tr(pt[:, ntaps, :], brow.v, C.identf[0:nch, 0:nch])
    P.cp(wcol.v, pt[:, 0:ntaps, :])
    P.cp(bcol.v, pt[:, ntaps, :])
    return wcol, bcol


def conv_row(P, out, raw, wcol, bcol, k, ntaps, func=None):
    c = ntaps // 2
    H = L // 2
    for hf in range(2):
        o = out[:, hf * H:(hf + 1) * H]
        P.act(o, raw[:, c + hf * H: c + (hf + 1) * H], AF.Identity, scale=wcol[:, c, k:k + 1], bias=bcol[:, k:k + 1])
        js = [j for j in range(ntaps) if j != c]
        for n_, j in enumerate(js):
            P.stt(o, raw[:, j + hf * H: j + (hf + 1) * H], wcol[:, j, k:k + 1], o, ALU.mult, ALU.add)
        if func is not None:
            P.act(o, o, func)


def hyena_layer(P, C, li, prm, last):
    S = C.scr
    w_in, conv_w, conv_b = prm["w_in"], prm["conv_w"], prm["conv_b"]
    shifts = [-1, 0, 1]
    P.begin_phase()
    wcol, bcol = conv_cols(P, C, conv_w, conv_b, 3, 48)
    NG2 = 2
    stage = [P.sb("hstA%d" % i, [128, 8, 2, 128], F32) for i in range(1)] * 2
    Wb = [P.sb("hWb%d" % i, [128, 8, 512], BF16) for i in range(2)]
    raw = [[P.sb("hraw%d_%d" % (i, k), [128, L + 2], BF16) for k in range(3)] for i in range(2)]
    for i in range(2):
        for k in range(3):
            P.memset(raw[i][k][:, 0:1], 0.0, eng="dve")
            P.memset(raw[i][k][:, L + 1:L + 2], 0.0, eng="dve")
    cc = [P.sb("hcc%d" % k, [128, L], BF16) for k in range(2)]
    zs2 = [P.sb("hzs%d" % i, [128, L], BF16) for i in range(2)]
    gT = [P.sb("hgT%d" % i, [128, NG2, L], BF16) for i in range(2)]
    pp = [P.ps("hpp%d" % i, [128, 512], F32) for i in range(4)]
    ptr = [P.ps("hptr%d" % i, [128, 2, NG2 * 128], BF16) for i in range(2)]
    so = [P.sb("hso%d" % i, [128, 2, NG2 * 128], BF16) for i in range(2)]
    col0 = [2048, 4096, 0, 6144]
    cch = [16, 32, 0]
    it = 0
    for i in range(16):
        sb_ = i % 2
        zs = zs2[sb_]
        for hf in range(2):
            for k2_ in range(2):
                k = hf * 2 + k2_
                P.dma(stage[hf][:, :, k2_, :],
                      w_in[:, col0[k] + i * 128: col0[k] + (i + 1) * 128].re("(k p) c -> p k c", p=128))
            P.cp(Wb[sb_][:, :, hf * 256:(hf + 1) * 256], stage[hf].v.re("p k a c -> p k (a c)"),
                 eng=("pool" if hf else "dve"))
        for tc in range(L // 512):
            for k in range(4):
                b = it % 4
                it += 1
                mm_proj_fm(P, pp[b].v, C, Wb[sb_], k * 128, tc * 512, 512)
                if k < 3:
                    P.cp(raw[sb_][k][:, 1 + tc * 512: 1 + (tc + 1) * 512], pp[b].v, eng=("act" if (tc + k) % 2 else "dve"))
                else:
                    P.act(zs[:, tc * 512:(tc + 1) * 512], pp[b].v, AF.Silu)
        conv_row(P, cc[0], raw[sb_][0], wcol, bcol, cch[0] + i, 3)
        conv_row(P, cc[1], raw[sb_][1], wcol, bcol, cch[1] + i, 3)
        P.tt(gT[0][:, i % NG2, :], cc[0].v, cc[1].v, ALU.mult, eng="pool")
        conv_row(P, cc[0], raw[sb_][2], wcol, bcol, cch[2] + i, 3)
        P.tt(gT[1][:, i % NG2, :], cc[0].v, zs.v, ALU.mult, eng="pool")
        if i % NG2 == NG2 - 1:
            i0 = i - (NG2 - 1)
            for tt in range(NT):
                q = tt % 2
                for w_ in range(2):
                    for j in range(NG2):
                        P.tr(ptr[q][:, w_, j * 128:(j + 1) * 128], gT[w_][:, j, tt * 128:(tt + 1) * 128], C.ident.v)
                P.cp(so[q].v, ptr[q].v, eng=("act" if tt % 2 else "dve"))
                P.dma(S["gbf"][tt * 128:(tt + 1) * 128, i0 * 128:(i0 + NG2) * 128], so[q][:, 0, :])
                P.dma(S["gatebf"][tt * 128:(tt + 1) * 128, i0 * 128:(i0 + NG2) * 128], so[q][:, 1, :])
    P.end_phase()
    P.begin_phase()
    F1 = P.sb("F1", [128, 128], BF16)
    P.dma(F1.v, C.consts["hy_F1"].v)
    ntp = P.sb("ntp", [128, 64], F32)
    P.dma(ntp.v, C.consts["hy_ntpos"].v)
    dl = P.sb("dl", [128, DI], F32)
    row_bcast(P, dl.v, C.consts["hy_deltas"], slice(None))
    w1 = P.sb("fw1", [33, 64], F32)
    w2 = P.sb("fw2", [64, 64], F32)
    w3 = P.sb("fw3", [64, 64], F32)
    P.dma(w1.v, prm["filt_w1"].v)
    P.dma(w2.v, prm["filt_w2"].v)
    P.dma(w3.v, prm["filt_w3"].v)
    cols = P.sb("fcols", [64, 4], F32)
    for i, nme in enumerate(["filt_freq", "filt_b1", "filt_b2", "filt_b3"]):
        P.dma(cols[:, i:i + 1], prm[nme].v.us(1))
    fb = P.sb("ffb", [64, 3], F32)
    for i in range(3):
        P.tt(fb[:, i:i + 1], cols[:, 0:1], cols[:, i + 1:i + 2], ALU.mult)
    wo_f = P.sb("fwo_f", [64, 4096], F32)
    wo = P.sb("fwo", [64, 4096], BF16)
    P.dma(wo_f.v, prm["filt_w_out"].v)
    P.cp(wo.v, wo_f.v, eng="pool")
    h3T = P.sb("h3T", [64, NF], BF16)
    MW = 1024
    zc = [P.sb("zc%d" % i, [33, MW], F32) for i in range(2)]
    arg = [P.sb("farg%d" % i, [64, MW], F32) for i in range(2)]
    tmf = [P.sb("ftmf%d" % i, [64, MW], F32) for i in range(1)] * 2
    tmi = [P.sb("ftmi%d" % i, [64, MW], I32) for i in range(1)] * 2
    tm2 = [P.sb("ftm2%d" % i, [64, MW], F32) for i in range(1)] * 2
    hh = [P.sb("fhh%d" % i, [64, MW], F32) for i in range(2)]
    pm = [P.ps("fpm%d" % i, [64, MW], F32) for i in range(2)]
    it = 0
    for ch in range(NF // MW):
        s = ch % 2
        P.dma(zc[s].v, C.consts["hy_zT"][:, ch * MW:(ch + 1) * MW])
        src = zc[s].v
        for lyr, wl in enumerate([w1, w2, w3]):
            b = it % 2
            it += 1
            for q in range(MW // 512):
                P.mm(pm[b][:, q * 512:(q + 1) * 512], wl.v, src[:, q * 512:(q + 1) * 512])
            P.act(arg[b].v, pm[b].v, AF.Identity, scale=cols[:, 0:1], bias=fb[:, lyr:lyr + 1])
            dst = hh[lyr % 2].v if lyr < 2 else h3T[:, ch * MW:(ch + 1) * MW]
            sin_reduced(P, dst, arg[b].v, tmf[b].v, tmi[b].v, tm2[b].v)
            src = dst
    pk = [P.ps("fpk%d" % i, [128, 512], F32) for i in range(2)]
    pz = [P.ps("fpz%d" % i, [128, 512], F32) for i in range(2)]
    dec = [P.sb("fdec%d" % i, [128, DI], F32) for i in range(2)]
    ka = [P.sb("fka%d" % i, [128, 512], BF16) for i in range(2)]
    zt = [P.sb("fzt%d" % i, [128, DI], BF16) for i in range(2)]
    it = 0
    for a in range(64):
        sa = a % 2
        P.act(dec[sa].v, dl.v, AF.Exp, scale=ntp[:, a:a + 1])
        for c in range(4):
            s = it % 2
            it += 1
            P.mm(pk[s][0:64, :], h3T[:, a * 128: a * 128 + 64], wo[:, c * 512:(c + 1) * 512])
            P.mm(pk[s][64:128, :], h3T[:, a * 128 + 64: a * 128 + 128], wo[:, 2048 + c * 512: 2048 + (c + 1) * 512])
            P.tt(ka[s].v, pk[s].v, dec[sa][:, c * 512:(c + 1) * 512], ALU.mult)
            P.mm(pz[s].v, F1.v, ka[s].v)
            P.cp(zt[sa][:, c * 512:(c + 1) * 512], pz[s].v, eng="act")
        P.dma(S["Zs"][:, a, :], zt[sa].v)
    P.end_phase()
    P.begin_phase()
    TA = P.sb("TA", [128, NK2, 128], BF16)
    P.dma(TA.v, C.consts["hy_TA"].v)
    pz = [P.ps("fpz2_%d" % i, [128, 512], F32) for i in range(2)]
    zin = [P.sb("fzin%d" % i, [128, DI], BF16) for i in range(2)]
    kf = [P.sb("fkf%d" % i, [128, DI], BF16) for i in range(2)]
    for q in range(NK2):
        s = q % 2
        P.dma(zin[s][0:64, :], S["Zs"][q, :, :])
        if 1 <= q <= 63:
            P.dma(zin[s][64:128, :], S["Zs"][64 + q, :, :])
        else:
            P.memset(zin[s][64:128, :], 0.0)
        for c in range(4):
            b = (q * 4 + c) % 2
            P.mm(pz[b].v, TA[:, q, :], zin[s][:, c * 512:(c + 1) * 512])
            P.cp(kf[s][:, c * 512:(c + 1) * 512], pz[b].v, eng=("act" if c % 2 else "dve"))
        P.dma(S["Kf"][q, :, :], kf[s].v)
    P.end_phase()
    P.begin_phase()
    F1 = P.sb("F1c", [128, 128], BF16)
    P.dma(F1.v, C.consts["hy_F1"].v)
    gb_ = [P.sb("cgb%d" % i, [64, DI], BF16) for i in range(3)]
    zt = [P.sb("czt%d" % i, [128, DI], BF16) for i in range(2)]
    pz = [P.ps("cpz%d" % i, [128, 512], F32) for i in range(4)]
    gview = S["gbf"].v.re("(p a) c -> a p c", a=64)
    for a in range(64):
        s = a % 2
        P.dma(gb_[a % 3].v, gview[a])
        for c in range(4):
            b = c
            P.mm(pz[b].v, F1[0:64, :], gb_[a % 3][:, c * 512:(c + 1) * 512])
            P.cp(zt[s][:, c * 512:(c + 1) * 512], pz[b].v, eng=("act" if c % 2 else "dve"))
        P.dma(S["Zs"][:, a, :], zt[s].v)
    P.end_phase()
    P.begin_phase()
    TA = P.sb("dTA", [128, NK2, 128], BF16)
    TB = P.sb("dTB", [128, NK2, 128], BF16)
    TC = P.sb("dTC", [128, NK2, 128], BF16)
    P.dma(TA.v, C.consts["hy_TA"].v)
    P.dma(TB.v, C.consts["hy_TB"].v)
    P.dma(TC.v, C.consts["hy_TC"].v)
    zin = [P.sb("dzin%d" % i, [128, DI], BF16) for i in range(2)]
    KA = [P.sb("dKA%d" % i, [128, DI], BF16) for i in range(2)]
    KB = [P.sb("dKB%d" % i, [128, DI], BF16) for i in range(2)]
    t1 = [P.sb("dt1_%d" % i, [128, 512], F32) for i in range(2)]
    t2 = [P.sb("dt2_%d" % i, [128, 512], F32) for i in range(2)]
    yb = [P.sb("dyb%d" % i, [128, 512], BF16) for i in range(2)]
    vo = [P.sb("dvo%d" % i, [128, DI], BF16) for i in range(2)]
    pA = [P.ps("dpA%d" % i, [128, 512], F32) for i in range(2)]
    pB = [P.ps("dpB%d" % i, [128, 512], F32) for i in range(2)]
    pV = [P.ps("dpV%d" % i, [128, 512], F32) for i in range(2)]
    it = 0
    for q in range(NK2):
        s = q % 2
        P.dma(zin[s][0:64, :], S["Zs"][q, :, :])
        if 1 <= q <= 63:
            P.dma(zin[s][64:128, :], S["Zs"][64 + q, :, :])
        else:
            P.memset(zin[s][64:128, :], 0.0)
        P.dma(KA[s][0:64, :], S["Kf"][q, 0:64, :])
        P.dma(KA[s][64:128, :], S["Kf"][q, 0:64, :])
        P.dma(KB[s][0:64, :], S["Kf"][q, 64:128, :])
        P.dma(KB[s][64:128, :], S["Kf"][q, 64:128, :])
        for c in range(4):
            b = it % 2
            it += 1
            cs_ = slice(c * 512, (c + 1) * 512)
            P.mm(pA[b].v, TA[:, q, :], zin[s][:, cs_])
            P.mm(pB[b].v, TB[:, q, :], zin[s][:, cs_])
            P.tt(t1[b].v, pA[b].v, KA[s][:, cs_], ALU.mult)
            P.tt(t2[b].v, pB[b].v, KB[s][:, cs_], ALU.mult)
            P.tt(yb[b].v, t1[b].v, t2[b].v, ALU.add, eng=("pool" if c % 3 else "dve"))
            P.mm(pV[b].v, TC[:, q, :], yb[b].v)
            P.cp(vo[s][:, cs_], pV[b].v, eng="act")
        P.dma(S["Zv"][q, :, :], vo[s][0:64, :])
        if 1 <= q <= 63:
            P.dma(S["Zv"][64 + q, :, :], vo[s][64:128, :])
    P.end_phase()
    P.begin_phase()
    Fi = P.sb("eFi", [128, 64], BF16)
    P.dma(Fi.v, C.consts["hy_Finv"].v)
    skb = P.sb("eskb", [128, DI], F32)
    row_bcast(P, skb.v, prm["skip"], slice(None))
    vin = [P.sb("evin%d" % i, [128, 2, DI], BF16) for i in range(2)]
    gt = [P.sb("egt%d" % i, [128, DI], BF16) for i in range(2)]
    gs = [P.sb("egs%d" % i, [128, DI], F32) for i in range(2)]
    zg = [P.sb("ezg%d" % i, [128, DI], BF16) for i in range(2)]
    yo = [P.sb("eyo%d" % i, [128, DI], BF16) for i in range(2)]
    py = [P.ps("epy%d" % i, [128, 512], F32) for i in range(2)]
    gview = S["gbf"].v.re("(p a) c -> a p c", a=64)
    zview = S["gatebf"].v.re("(p a) c -> a p c", a=64)
    yview = S["y"].v.re("(p a) c -> a p c", a=64)
    it = 0
    for a2 in range(32):
        s = a2 % 2
        for h in range(2):
            a = a2 * 2 + h
            P.dma(vin[s][:, h, :], S["Zv"][:, a, :])
            P.dma(gt[s][h * 64:(h + 1) * 64, :], gview[a])
            P.dma(zg[s][h * 64:(h + 1) * 64, :], zview[a])
        P.tt(gs[s].v, gt[s].v, skb.v, ALU.mult)
        for c in range(4):
            b = it % 2
            it += 1
            cs_ = slice(c * 512, (c + 1) * 512)
            for h in range(2):
                P.mm(py[b][h * 64:(h + 1) * 64, :], Fi.v, vin[s][:, h, cs_])
            P.tt(gs[s][:, cs_], py[b].v, gs[s][:, cs_], ALU.add)
            P.tt(yo[s][:, cs_], gs[s][:, cs_], zg[s][:, cs_], ALU.mult, eng="pool")
        for h in range(2):
            P.dma(yview[a2 * 2 + h], yo[s][h * 64:(h + 1) * 64, :])
    P.end_phase()
    outproj_phase(P, C, prm["w_out"], prm["ln_g"], prm["ln_b"], last)


_PROG_CACHE = {}


def kernel(**inputs):
    layers = [0, 1, 2, 3]
    if "prog" not in _PROG_CACHE:
        _PROG_CACHE["prog"] = build_program(layers)
    P, C = _PROG_CACHE["prog"]
    inputs = {k: np.asarray(v) for k, v in inputs.items()}
    nb = inputs["x"].shape[0]
    in_maps = [in_map_for(inputs, b, layers) for b in range(nb)]
    res = run_bass_kernel_spmd(P.nc, in_maps, core_ids=list(range(nb)))
    out = np.stack([np.asarray(res.results[b]["out"]) for b in range(nb)], axis=0)
    return out.astype(np.float32)
```

```python
import math
from contextlib import ExitStack
import numpy as np
import concourse.bass as bass
import concourse.mybir as mybir
from concourse.bass_utils import run_bass_kernel_spmd

F32 = mybir.dt.float32
BF16 = mybir.dt.bfloat16
I32 = mybir.dt.int32
AF = mybir.ActivationFunctionType
ALU = mybir.AluOpType
AX = mybir.AxisListType

D = 1024
L = 4096
DI = 2048
DEPTH = 4
ALPHA = (2.0 * DEPTH) ** 0.25
LN_EPS = 1e-5
PAD = 2
NT = L // 128
SAME_ENG_SYNC = True
NDS = 48
SCHED = True
SCHED_WINDOW = 1024
DEBUG_SCR = False
DBG_STOP = ""
DBG_LEVEL = 99
DBG_SKIPA = False
DBG_DIRS = 2
DBG_SUB = 99
DBG_NT = 2
DBG_PH = 99
_phc = [0]


class Buf:
    __slots__ = ("t", "w", "r", "name", "wn", "rn")

    def __init__(self, t, name=""):
        self.t = t
        self.w = None
        self.r = {}
        self.name = name
        self.wn = None
        self.rn = []

    def __getitem__(self, idx):
        return V(self, self.t[idx])

    @property
    def v(self):
        return V(self, self.t)


class V:
    __slots__ = ("b", "ap")

    def __init__(self, b, ap):
        self.b = b
        self.ap = ap

    def __getitem__(self, idx):
        return V(self.b, self.ap[idx])

    def re(self, s, **kw):
        return V(self.b, self.ap.rearrange(s, **kw))

    def bc(self, shape):
        return V(self.b, self.ap.broadcast_to(shape))

    def tb(self, shape):
        return V(self.b, self.ap.to_broadcast(shape))

    def pb(self, n):
        return V(self.b, self.ap.partition_broadcast(n))

    def cast(self, dt):
        return V(self.b, self.ap.bitcast(dt))

    def us(self, ax):
        return V(self.b, self.ap.unsqueeze(ax))


class StopBuild(Exception):
    pass


class Eng:
    def __init__(self, name, eng, sem):
        self.name = name
        self.eng = eng
        self.sem = sem
        self.n = 0
        self.seen = {}


class Prog:
    def __init__(self):
        nc = bass.Bass("TRN2", target_bir_lowering=False)
        self.nc = nc
        self.es = ExitStack()
        self.E = {}
        for name, e in [("pe", nc.tensor), ("dve", nc.vector), ("act", nc.scalar),
                        ("pool", nc.gpsimd), ("sp", nc.sync)]:
            sem = self.es.enter_context(nc.semaphore("s_" + name))
            self.E[name] = Eng(name, e, sem)
        self.dsem = []
        for i in range(NDS):
            self.dsem.append([self.es.enter_context(nc.semaphore("d%d" % i)), 0])
        self.dnext = 0
        self.phase = None
        self.uid = 0
        self.ninst = 0
        self.deferred = False
        self.nodes = []
        self.touched = []
        self.sim_time = 0.0
        self.synced = False
        self.phase_stats = []

    def begin_phase(self):
        self.phase = ExitStack()
        self.deferred = SCHED
        if SCHED and not self.synced:
            self.barrier()
            self.synced = True

    def end_phase(self):
        if self.deferred:
            self._flush()
            self.deferred = False
        self.barrier()
        self.phase.close()
        self.phase = None
        _phc[0] += 1
        if _phc[0] >= DBG_PH:
            raise StopBuild()

    def _nm(self, name):
        self.uid += 1
        return "%s_%d" % (name, self.uid)

    def sb(self, name, shape, dt, perm=False):
        st = self.es if perm else self.phase
        t = st.enter_context(self.nc.sbuf_tensor(self._nm(name), list(shape), dt))
        return Buf(t[:], name)

    def ps(self, name, shape, dt=F32, perm=False):
        st = self.es if perm else self.phase
        esz = 4 if dt == F32 else 2
        n = 1
        for x in shape[1:]:
            n *= x
        per_bank = 2048 // esz
        npad = ((n + per_bank - 1) // per_bank) * per_bank
        t = st.enter_context(self.nc.psum_tensor(self._nm(name), [shape[0], npad], dt))
        ap = t[:][:, 0:n]
        if len(shape) == 3:
            ap = ap.rearrange("p (a b) -> p a b", a=shape[1])
        return Buf(ap, name)

    def dram(self, name, shape, dt, kind="Internal"):
        if DEBUG_SCR and kind == "Internal" and name.startswith("s_"):
            kind = "ExternalOutput"
        t = self.nc.dram_tensor(name, list(shape), dt, kind=kind)
        return Buf(t.ap(), name)

    def _wait(self, E, toks):
        need = {}
        for (s, v) in toks:
            if s is E.sem and (E.name == "pe" or not SAME_ENG_SYNC):
                continue
            k = id(s)
            if k not in need or need[k][1] < v:
                need[k] = (s, v)
        for k, (s, v) in need.items():
            if E.seen.get(k, 0) >= v:
                continue
            E.eng.wait_ge(s, v)
            self.ninst += 1
            E.seen[k] = v

    @staticmethod
    def _deps(outs, ins):
        toks = []
        for v in ins:
            if v.b.w is not None:
                toks.append(v.b.w)
        for v in outs:
            if v.b.w is not None:
                toks.append(v.b.w)
            toks.extend(v.b.r.values())
        return toks

    @staticmethod
    def _mark(tok, outs, ins):
        k = id(tok[0])
        for v in ins:
            r = v.b.r
            if k not in r or r[k][1] < tok[1]:
                r[k] = tok
        for v in outs:
            v.b.w = tok
            v.b.r = {}

    def op(self, ename, fn, outs, ins, cost=0.3):
        if self.deferred:
            self._record(ename, "op", fn, outs, ins, cost, cost)
            return
        E = self.E[ename]
        self._wait(E, self._deps(outs, ins))
        E.n += 1
        fn(E.eng).then_inc(E.sem, 1)
        self.ninst += 1
        self._mark((E.sem, E.n), outs, ins)

    def dma(self, out, in_, q="sp", slow=False):
        if self.deferred:
            nbytes = 1
            for x in out.ap.shape:
                nbytes *= x
            nbytes *= (4 if out.ap.dtype in (F32, I32) else 2)
            self._record(q, "dma", (out, in_, slow), [out], [in_], 0.12, 2.0 + nbytes / 120e3)
            return
        self._emit_dma(self.E[q], out, in_, slow, self._deps([out], [in_]), True)

    def _emit_dma(self, E, out, in_, slow, toks, mark):
        i = self.dnext
        self.dnext = (i + 1) % NDS
        sem, cnt = self.dsem[i]
        if cnt > 0:
            toks = list(toks) + [(sem, cnt)]
        self._wait(E, toks)
        if slow:
            E.eng.dma_start(out=out.ap, in_=in_.ap, allow_slow_non_contiguous=True).then_inc(sem, 16)
        else:
            E.eng.dma_start(out=out.ap, in_=in_.ap).then_inc(sem, 16)
        self.ninst += 1
        self.dsem[i][1] = cnt + 16
        tok = (sem, cnt + 16)
        if mark:
            self._mark(tok, [out], [in_])
        return tok

    def _record(self, ename, kind, payload, outs, ins, busy, lat):
        nid = len(self.nodes)
        deps = set()
        for v in ins:
            b = v.b
            if b.wn is not None:
                deps.add(b.wn)
        for v in outs:
            b = v.b
            if b.wn is not None:
                deps.add(b.wn)
            deps.update(b.rn)
        for v in ins:
            b = v.b
            b.rn.append(nid)
            self.touched.append(b)
        for v in outs:
            b = v.b
            b.wn = nid
            b.rn = []
            self.touched.append(b)
        deps.discard(nid)
        self.nodes.append([ename, kind, payload, busy, lat, deps])

    def _flush(self):
        nodes = self.nodes
        n = len(nodes)
        if n == 0:
            return
        succ = [[] for _ in range(n)]
        ndep = [0] * n
        for i, nd in enumerate(nodes):
            ndep[i] = len(nd[5])
            for d in nd[5]:
                succ[d].append(i)
        ready = [0.0] * n
        fin = [0.0] * n
        queues = {}
        for i, nd in enumerate(nodes):
            queues.setdefault(nd[0], []).append(i)
        head = {e: 0 for e in queues}
        done = [False] * n
        T = {e: 0.0 for e in queues}
        order = {e: [] for e in queues}
        W = SCHED_WINDOW
        left = n
        while left:
            best = None
            for e, q in queues.items():
                h = head[e]
                while h < len(q) and done[q[h]]:
                    h += 1
                head[e] = h
                cnt = 0
                j = h
                te = T[e]
                while j < len(q) and cnt < W:
                    i = q[j]
                    j += 1
                    if done[i]:
                        continue
                    cnt += 1
                    if ndep[i]:
                        continue
                    st = ready[i] if ready[i] > te else te
                    if best is None or st < best[0] - 1e-9 or (st < best[0] + 1e-9 and i < best[1]):
                        best = (st, i, e)
                    if st <= te:
                        break
            st, i, e = best
            nd = nodes[i]
            T[e] = st + nd[3]
            f = st + nd[4]
            fin[i] = f
            done[i] = True
            left -= 1
            order[e].append(i)
            for sidx in succ[i]:
                ndep[sidx] -= 1
                lat = f + (0.12 if nodes[sidx][0] == e else 0.3)
                if lat > ready[sidx]:
                    ready[sidx] = lat
        tok = [None] * n
        for e, lst in order.items():
            E = self.E[e]
            k = E.n
            for i in lst:
                if nodes[i][1] == "op":
                    k += 1
                    tok[i] = (E.sem, k)
        dma_engs = [e for e in order if any(nodes[i][1] == "dma" for i in order[e])]
        pending = {e: list(order[e]) for e in order}
        ptr = {e: 0 for e in order}
        progress = True
        while progress:
            progress = False
            for e in list(pending.keys()):
                lst = pending[e]
                E = self.E[e]
                while ptr[e] < len(lst):
                    i = lst[ptr[e]]
                    nd = nodes[i]
                    if any(tok[d] is None for d in nd[5]):
                        break
                    toks = [tok[d] for d in nd[5]]
                    if nd[1] == "dma":
                        out, in_, slow = nd[2]
                        tok[i] = self._emit_dma(E, out, in_, slow, toks, False)
                    else:
                        self._wait(E, toks)
                        E.n += 1
                        assert tok[i] == (E.sem, E.n)
                        nd[2](E.eng).then_inc(E.sem, 1)
                        self.ninst += 1
                    ptr[e] += 1
                    progress = True
                if ptr[e] >= len(lst):
                    del pending[e]
        assert not pending, "emission deadlock"
        self.sim_time += max(T.values())
        busy = {}
        for nd in nodes:
            busy[nd[0]] = busy.get(nd[0], 0.0) + nd[3]
        self.phase_stats.append((max(T.values()), {k: round(v) for k, v in busy.items()}, n))
        self.nodes = []
        for b in self.touched:
            b.wn = None
            b.rn = []
        self.touched = []

    def barrier(self):
        toks = [(e.sem, e.n) for e in self.E.values() if e.n > 0]
        toks += [(s, c) for s, c in self.dsem if c > 0]
        for E in self.E.values():
            self._wait(E, [t for t in toks if t[0] is not E.sem])

    @staticmethod
    def _fs(v):
        n = 1
        for x in v.ap.shape[1:]:
            n *= x
        return n

    def _cost(self, eng, v, cast=False):
        n = self._fs(v)
        if eng == "act":
            return 0.2 + n / 1400.0
        if eng == "pool":
            return 0.3 + n * (0.0035 if cast else 0.0022)
        return 0.1 + n / 960.0

    def mm(self, o, lhsT, rhs, start=True, stop=True):
        n = self._fs(rhs) * (4 if lhsT.ap.dtype == F32 else 1)
        self.op("pe", lambda e: e.matmul(o.ap, lhsT=lhsT.ap, rhs=rhs.ap, start=start, stop=stop),
                [o], [lhsT, rhs], cost=0.07 + n / 2200.0)

    def tr(self, o, a, ident):
        self.op("pe", lambda e: e.transpose(o.ap, a.ap, ident.ap), [o], [a, ident], cost=0.15)

    def act(self, o, a, func, scale=1.0, bias=0.0, accum=None, eng="act"):
        ins = [a]
        outs = [o]
        kw = {}
        if isinstance(scale, V):
            ins.append(scale)
            kw["scale"] = scale.ap
        else:
            kw["scale"] = float(scale)
        if isinstance(bias, V):
            ins.append(bias)
            kw["bias"] = bias.ap
        else:
            kw["bias"] = float(bias)
        if accum is not None:
            outs.append(accum)
            kw["accum_out"] = accum.ap
        self.op("act", lambda e: e.activation(out=o.ap, in_=a.ap, func=func, **kw), outs, ins,
                cost=self._cost("act", a))

    def tt(self, o, a, b, op, eng="dve"):
        self.op(eng, lambda e: e.tensor_tensor(out=o.ap, in0=a.ap, in1=b.ap, op=op), [o], [a, b],
                cost=self._cost(eng, o))

    def ts(self, o, a, s1, op0, s2=None, op1=None, eng="dve", accum=None):
        ins = [a]
        outs = [o]
        a1 = s1.ap if isinstance(s1, V) else float(s1)
        if isinstance(s1, V):
            ins.append(s1)
        a2 = None
        if s2 is not None:
            a2 = s2.ap if isinstance(s2, V) else float(s2)
            if isinstance(s2, V):
                ins.append(s2)
        kw = {}
        if op1 is not None:
            kw["op1"] = op1
        if accum is not None:
            kw["accum_out"] = accum.ap
            outs.append(accum)
        self.op(eng, lambda e: e.tensor_scalar(out=o.ap, in0=a.ap, scalar1=a1, scalar2=a2, op0=op0, **kw),
                outs, ins, cost=self._cost(eng, o))

    def stt(self, o, a, s, b, op0, op1, eng="dve"):
        ins = [a, b]
        sv = s.ap if isinstance(s, V) else float(s)
        if isinstance(s, V):
            ins.append(s)
        self.op(eng, lambda e: e.scalar_tensor_tensor(out=o.ap, in0=a.ap, scalar=sv, in1=b.ap, op0=op0, op1=op1),
                [o], ins, cost=self._cost(eng, o))

    def cp(self, o, a, eng="dve"):
        if eng == "act":
            self.op("act", lambda e: e.copy(out=o.ap, in_=a.ap), [o], [a], cost=self._cost("act", o))
        else:
            self.op(eng, lambda e: e.tensor_copy(out=o.ap, in_=a.ap), [o], [a], cost=self._cost(eng, o, cast=True))

    def memset(self, o, val, eng="pool"):
        self.op(eng, lambda e: e.memset(o.ap, val), [o], [], cost=self._cost(eng, o))


def _ident_bf16():
    import ml_dtypes
    return np.eye(128, dtype=np.float32).astype(ml_dtypes.bfloat16)


class Ctx:
    pass


def setup_common(P, C):
    C.hT = P.sb("hT", [128, D // 128, L + 2 * PAD], BF16, perm=True)
    C.ident = P.sb("ident", [128, 128], BF16, perm=True)
    C.identf = P.sb("identf", [128, 128], F32, perm=True)
    C.ones_bf = P.sb("ones_bf", [128, 128], BF16, perm=True)
    C.ones_f = P.sb("ones_f", [128, 128], F32, perm=True)
    P.dma(C.ident.v, C.d_ident.v)
    P.dma(C.identf.v, C.d_identf.v)
    P.memset(C.ones_bf.v, 1.0)
    P.memset(C.ones_f.v, 1.0)
    P.memset(C.hT.v, 0.0)
    C.eps_col = P.sb("eps_col", [128, 1], F32, perm=True)
    P.memset(C.eps_col.v, LN_EPS)


def load_x_to_hT(P, C):
    P.begin_phase()
    xt = [P.sb("xt%d" % i, [128, D], F32) for i in range(2)]
    xb = [P.sb("xb%d" % i, [128, D], BF16) for i in range(2)]
    pt = [P.ps("ptr%d" % i, [128, D], BF16) for i in range(2)]
    for tt in range(NT):
        s = tt % 2
        P.dma(xt[s].v, C.x[tt * 128:(tt + 1) * 128, :])
        P.dma(C.h[tt * 128:(tt + 1) * 128, :], xt[s].v)
        P.cp(xb[s].v, xt[s].v, eng="act")
        for kc in range(D // 128):
            P.tr(pt[s][:, kc * 128:(kc + 1) * 128], xb[s][:, kc * 128:(kc + 1) * 128], C.ident.v)
        P.cp(C.hT[:, :, PAD + tt * 128: PAD + (tt + 1) * 128],
             pt[s].v.re("p (k t) -> p k t", k=D // 128))
    P.end_phase()


def row_bcast(P, dst, src_buf, sl):
    P.dma(dst, V(src_buf, src_buf.t[sl].partition_broadcast(128)))


def prep_w_chunk(P, Wt, w_d, c0, n, stage, convw_d=None, cc0=0, cwb=None, ntaps=0):
    nk = w_d.t.shape[0] // 128
    P.dma(stage[:, :nk, :n], w_d[:, c0:c0 + n].re("(k p) c -> p k c", p=128))
    if ntaps == 0:
        P.cp(Wt[0][:, :nk, :n], stage[:, :nk, :n], eng="pool")
        return
    row_bcast(P, cwb[:, :ntaps, :n], convw_d, (slice(None), slice(cc0, cc0 + n)))
    for j in range(ntaps):
        P.tt(Wt[j][:, :nk, :n], stage[:, :nk, :n],
             cwb[:, j:j + 1, :n].bc([128, nk, n]), ALU.mult, eng=("pool" if j % 2 else "dve"))


def mm_proj_tm(P, ps, C, Wts, shifts, tt, n, bias=None):
    nk = D // 128
    tot = len(Wts) * nk + (1 if bias is not None else 0)
    i = 0
    for Wt, sh in zip(Wts, shifts):
        for kc in range(nk):
            t0 = PAD + tt * 128 + sh
            P.mm(ps, C.hT[:, kc, t0:t0 + 128], Wt[:, kc, :n], start=(i == 0), stop=(i == tot - 1))
            i += 1
    if bias is not None:
        P.mm(ps, C.ones_bf[0:1, :], bias, start=False, stop=True)


class OutProj:
    def __init__(self, P, C, wout_d, lng_d, lnb_d, last=False):
        self.P, self.C, self.last = P, C, last
        self.w = P.sb("wout", [128, DI // 128, D], BF16)
        st = [P.sb("wost%d" % i, [128, 4, D], F32) for i in range(2)]
        for q in range(4):
            P.dma(st[q % 2].v, wout_d[q * 512:(q + 1) * 512, :].re("(k p) c -> p k c", p=128))
            P.cp(self.w[:, q * 4:(q + 1) * 4, :], st[q % 2].v, eng=("act" if q % 2 else "dve"))
        self.g = P.sb("lng", [128, D], F32)
        self.b = P.sb("lnb", [128, D], F32)
        row_bcast(P, self.g.v, lng_d, slice(None))
        row_bcast(P, self.b.v, lnb_d, slice(None))
        nb = 2
        self.yT = [P.sb("yT%d" % i, [128, DI // 128, 128], BF16) for i in range(nb)]
        self.ho = [P.sb("ho%d" % i, [128, D], F32) for i in range(nb)]
        self.r = [P.sb("r%d" % i, [128, D], F32) for i in range(nb)]
        self.hn = [P.sb("hn%d" % i, [128, D], F32) for i in range(nb)]
        self.hb = [P.sb("hb%d" % i, [128, D], BF16) for i in range(nb)]
        self.st6 = [P.sb("st6%d" % i, [128, 2, 6], F32) for i in range(nb)]
        self.mv = [P.sb("mv%d" % i, [128, 2], F32) for i in range(nb)]
        self.rstd = [P.sb("rstd%d" % i, [128, 1], F32) for i in range(nb)]
        self.pT = [P.ps("opT%d" % i, [128, DI], BF16) for i in range(2)]
        self.po = P.ps("opo", [128, D], F32)
        self.pH = P.ps("opH", [128, D], BF16)

    def tile(self, tt, y):
        P, C = self.P, self.C
        k = tt % 2
        yT, ho, r, hn, hb, st6, mv, rstd, pT = (self.yT[k], self.ho[k], self.r[k], self.hn[k], self.hb[k],
                                                  self.st6[k], self.mv[k], self.rstd[k], self.pT[k])
        for c in range(DI // 128):
            P.tr(pT[:, c * 128:(c + 1) * 128], y[:, c * 128:(c + 1) * 128], C.ident.v)
        h8 = DI // 256
        P.cp(yT[:, 0:h8, :], pT[:, 0:DI // 2].re("p (c t) -> p c t", c=h8), eng="act")
        P.cp(yT[:, h8:, :], pT[:, DI // 2:].re("p (c t) -> p c t", c=h8), eng="dve")
        P.dma(ho.v, C.h[tt * 128:(tt + 1) * 128, :])
        for dh in range(2):
            for c in range(DI // 128):
                P.mm(self.po[:, dh * 512:(dh + 1) * 512], yT[:, c, :],
                     self.w[:, c, dh * 512:(dh + 1) * 512], start=(c == 0), stop=(c == DI // 128 - 1))
        P.stt(r.v, ho.v, ALPHA, self.po.v, ALU.mult, ALU.add)
        for q in range(2):
            P.op("dve", lambda e, q=q: e.bn_stats(out=st6[:, q, :].ap, in_=r[:, q * 512:(q + 1) * 512].ap),
                 [st6.v], [r.v], cost=0.65)
        P.op("dve", lambda e: e.bn_aggr(out=mv.v.ap, in_=st6.v.re("p a b -> p (a b)").ap),
             [mv.v], [st6.v], cost=0.15)
        P.act(rstd.v, mv[:, 1:2], AF.Sqrt, scale=1.0, bias=C.eps_col[:, 0:1])
        P.op("dve", lambda e: e.reciprocal(out=rstd.v.ap, in_=rstd.v.ap), [rstd.v], [rstd.v], cost=0.12)
        P.ts(hn.v, r.v, mv[:, 0:1], ALU.subtract, rstd[:, 0:1], ALU.mult)
        P.tt(hn.v, hn.v, self.g.v, ALU.mult, eng="pool")
        P.tt(hn.v, hn.v, self.b.v, ALU.add)
        if self.last:
            P.dma(C.out[tt * 128:(tt + 1) * 128, :], hn.v)
            return
        P.dma(C.h[tt * 128:(tt + 1) * 128, :], hn.v)
        P.cp(hb.v, hn.v, eng="act")
        for kc in range(D // 128):
            P.tr(self.pH[:, kc * 128:(kc + 1) * 128], hb[:, kc * 128:(kc + 1) * 128], C.ident.v)
        P.cp(C.hT[:, :, PAD + tt * 128: PAD + (tt + 1) * 128],
             self.pH.v.re("p (k t) -> p k t", k=D // 128))


def mm_proj_fm(P, ps, C, Wt, c0, t0, n, shift=0, start=True, stop=True):
    nk = D // 128
    for kc in range(nk):
        a = PAD + t0 + shift
        P.mm(ps, Wt[:, kc, c0:c0 + 128], C.hT[:, kc, a:a + n],
             start=(start and kc == 0), stop=(stop and kc == nk - 1))


HG_H = 16


def hgrn2_consts():
    s = np.arange(128)[:, None]
    t = np.arange(128)[None, :]
    same = (s // 64) == (t // 64)
    mf = ((s <= t) & same).astype(np.float32)
    mb = ((s >= t) & same).astype(np.float32)
    rm = np.ones((128, 512), np.float32)
    rm[:, ::64] = 0.0
    return {"hg_mf": mf, "hg_mb": mb, "hg_rm": rm}


def hgrn2_layer(P, C, li, w_in, lb_raw, norm_g, w_out, ln_g, ln_b, last):
    S = C.scr
    qg = [S["qgf"], S["qgb"]]
    kg = [S["kgf"], S["kgb"]]
    P.begin_phase()
    lbr = P.sb("lbr", [32, 4, 128], F32)
    P.dma(lbr.v, lb_raw.v.re("l (g k) -> g l k", k=128))
    lbT = P.sb("lbT", [128, 4, 32], F32)
    pl = P.ps("pl", [128, 4, 32], F32)
    for l in range(4):
        P.tr(pl[:, l, :], lbr[:, l, :], C.identf[0:32, 0:32])
    P.act(lbT.v, pl.v, AF.Exp)
    den = P.sb("den", [128, 32], F32)
    num = P.sb("num", [128, 32], F32)
    P.tt(den.v, lbT[:, 0, :], lbT[:, 1, :], ALU.add)
    P.tt(den.v, den.v, lbT[:, 2, :], ALU.add)
    P.tt(den.v, den.v, lbT[:, 3, :], ALU.add)
    P.memset(num.v, 0.0, eng="dve")
    for j in range(1, li + 1):
        P.tt(num.v, num.v, lbT[:, j, :], ALU.add)
    P.op("dve", lambda e: e.reciprocal(out=den.v.ap, in_=den.v.ap), [den.v], [den.v])
    lb = P.sb("lb", [128, 32], F32)
    ln1mlb = P.sb("ln1mlb", [128, 32], F32)
    P.tt(lb.v, num.v, den.v, ALU.mult)
    P.ts(ln1mlb.v, lb.v, -1.0, ALU.mult, 1.0, ALU.add)
    P.act(ln1mlb.v, ln1mlb.v, AF.Ln)
    etot = P.sb("etot", [128, 2, HG_H, 64], F32)
    rm = P.sb("rm", [128, 512], F32)
    P.dma(rm.v, C.consts["hg_rm"].v)
    stage = P.sb("stA", [128, 8, 384], F32)
    Wt = [P.sb("WtA%d" % i, [128, 8, 384], BF16) for i in range(2)]
    pq = [P.ps("pq%d" % i, [128, 512], F32) for i in range(2)]
    pf = [[P.ps("pf%d_%d" % (d, i), [128, 512], F32) for i in range(2)] for d in range(2)]
    nb = 2
    e_ = [P.sb("e%d" % i, [128, 512], F32) for i in range(nb)]
    l1 = [P.sb("l1%d" % i, [128, 512], F32) for i in range(nb)]
    l2 = [P.sb("l2%d" % i, [128, 512], F32) for i in range(nb)]
    lf = [P.sb("lf%d" % i, [128, 512], F32) for i in range(nb)]
    pp = [P.sb("pp%d" % i, [128, 512], F32) for i in range(nb)]
    lk = [P.sb("lk%d" % i, [128, 512], F32) for i in range(nb)]
    eg = [P.sb("eg%d" % i, [128, 512], F32) for i in range(nb)]
    qo = [P.sb("qo%d" % i, [128, 512], BF16) for i in range(nb)]
    ko = [P.sb("ko%d" % i, [128, 512], BF16) for i in range(nb)]
    tot8 = [P.sb("tot8%d" % i, [128, 8], F32) for i in range(nb)]
    it = 0
    for hd in range(HG_H):
        W = Wt[hd % 2]
        for j, c0 in enumerate([hd * 128, 2048 + hd * 128, 4096 + hd * 128]):
            P.dma(stage[:, :, j * 128:(j + 1) * 128], w_in[:, c0:c0 + 128].re("(k p) c -> p k c", p=128))
        P.cp(W.v, stage.v, eng="pool")
        for tc in range(L // 512):
            t0 = tc * 512
            s = tc % 2
            mm_proj_fm(P, pq[s].v, C, W, 0, t0, 512)
            for d in range(2):
                mm_proj_fm(P, pf[d][s].v, C, W, 128 * (1 + d), t0, 512)
            for d in range(2):
                b = it % nb
                it += 1
                col = d * HG_H + hd
                ff = pf[d][s]
                P.act(e_[b].v, ff.v, AF.Exp, scale=-1.0)
                P.act(l1[b].v, e_[b].v, AF.Ln, scale=lb[:, col:col + 1], bias=1.0)
                P.act(l2[b].v, e_[b].v, AF.Ln, scale=1.0, bias=1.0)
                P.tt(lf[b].v, l1[b].v, l2[b].v, ALU.subtract)
                P.op("dve", lambda e, b=b: e.tensor_tensor_scan(out=pp[b].v.ap, data0=rm.v.ap, data1=lf[b].v.ap,
                                                               initial=0.0, op0=ALU.mult, op1=ALU.add),
                     [pp[b].v], [rm.v, lf[b].v])
                P.stt(lk[b].v, ff.v, -1.0, l2[b].v, ALU.mult, ALU.subtract)
                if d == 1:
                    P.tt(lf[b].v, lf[b].v, pp[b].v, ALU.subtract, eng="pool")
                    P.cp(tot8[b].v, pp[b].v.re("p (c t) -> p c t", t=64)[:, :, 63], eng="pool")
                    P.tt(pp[b].v.re("p (c t) -> p c t", t=64), lf[b].v.re("p (c t) -> p c t", t=64),
                         tot8[b].v.us(2).bc([128, 8, 64]), ALU.add)
                G = pp[b]
                P.act(eg[b].v, G.v, AF.Exp)
                ev = eg[b].v.re("p (c t) -> p c t", t=64)
                P.cp(etot[:, d, hd, tc * 8:(tc + 1) * 8], (ev[:, :, 63] if d == 0 else ev[:, :, 0]), eng="pool")
                P.tt(qo[b].v, pq[s].v, eg[b].v, ALU.mult)
                P.tt(lk[b].v, lk[b].v, G.v, ALU.subtract, eng="pool")
                P.act(ko[b].v, lk[b].v, AF.Exp, bias=ln1mlb[:, col:col + 1])
                P.dma(qg[d][hd, :, t0:t0 + 512], qo[b].v)
                P.dma(kg[d][hd, :, t0:t0 + 512], ko[b].v)
    P.dma(S["etot"].v, etot.v.re("p a h c -> p (a h c)"))
    P.end_phase()
    P.begin_phase()
    stage = P.sb("stB", [128, 8, 512], F32)
    WtB = [P.sb("WtB%d" % i, [128, 8, 512], BF16) for i in range(2)]
    pv = [P.ps("pv%d" % i, [128, 512], F32) for i in range(2)]
    vo = [P.sb("vo%d" % i, [128, 512], BF16) for i in range(2)]
    zo = [P.sb("zo%d" % i, [128, 512], F32) for i in range(2)]
    it = 0
    for isz in range(2):
        for c in range(4):
            W = WtB[c % 2]
            prep_w_chunk(P, [W], w_in, 6144 + isz * 2048 + c * 512, 512, stage)
            for tt in range(NT):
                s = it % 2
                it += 1
                mm_proj_tm(P, pv[s].v, C, [W], [0], tt, 512)
                if isz == 0:
                    P.cp(vo[s].v, pv[s].v, eng="act")
                    P.dma(S["vtm"][tt * 128:(tt + 1) * 128, c * 512:(c + 1) * 512], vo[s].v)
                else:
                    P.act(zo[s].v, pv[s].v, AF.Silu)
                    P.dma(S["zg"][tt * 128:(tt + 1) * 128, c * 512:(c + 1) * 512], zo[s].v)
    P.end_phase()
    P.begin_phase()
    etot = P.sb("etotC", [128, 2, HG_H, 64], F32)
    P.dma(etot.v.re("p a h c -> p (a h c)"), S["etot"].v)
    mask = [P.sb("mk%d" % i, [128, 128], F32) for i in range(2)]
    P.dma(mask[0].v, C.consts["hg_mf"].v)
    P.dma(mask[1].v, C.consts["hg_mb"].v)
    gb = P.sb("ngb", [128, DI], F32)
    row_bcast(P, gb.v, norm_g, slice(None))
    GH = 4
    Sf = [P.sb("Sf%d" % i, [128, GH, 128], F32) for i in range(HG_H // GH)]
    Sb = [P.sb("Sb%d" % i, [128, GH, 128], BF16) for i in range(HG_H // GH)]
    qT = [P.sb("qT%d" % i, [128, HG_H, 128], BF16) for i in range(2)]
    kT = [P.sb("kT%d" % i, [128, HG_H, 128], BF16) for i in range(2)]
    vt = [P.sb("vt%d" % i, [128, DI], BF16) for i in range(2)]
    ot = [P.sb("ot%d" % i, [128, DI], F32) for i in range(2)]
    of_t = [P.sb("oft%d" % i, [128, DI], F32) for i in range(2)]
    zt = [P.sb("zt%d" % i, [128, DI], F32) for i in range(2)]
    sq = P.sb("sq", [128, DI], F32)
    ss = P.sb("ss", [128, HG_H], F32)
    yb = [P.sb("yb%d" % i, [128, DI], BF16) for i in range(2)]
    pA = [P.ps("pA%d" % i, [128, GH, 128], F32) for i in range(2)]
    pT = [P.ps("pT%d" % i, [128, GH, 128], BF16) for i in range(2)]
    pO = [P.ps("pO%d" % i, [128, GH, 128], F32) for i in range(2)]
    pU = [P.ps("pU%d" % i, [128, GH, 128], F32) for i in range(2)]
    am = [P.sb("am%d" % i, [128, GH, 128], BF16) for i in range(2)]
    ktm = [P.sb("ktm%d" % i, [128, GH, 128], BF16) for i in range(2)]
    gi = 0
    for d in range(2):
        for g in range(HG_H // GH):
            P.memset(Sf[g].v, 0.0, eng="dve")
            P.memset(Sb[g].v, 0.0, eng="dve")
        order = list(range(NT)) if d == 0 else list(range(NT - 1, -1, -1))
        for n_, tt in enumerate(order):
            s = n_ % 2
            tsl = slice(tt * 128, (tt + 1) * 128)
            P.dma(qT[s].v, qg[d][:, :, tsl].re("h k t -> k h t"))
            P.dma(kT[s].v, kg[d][:, :, tsl].re("h k t -> k h t"))
            P.dma(vt[s].v, S["vtm"][tsl, :])
            if d == 1:
                P.dma(of_t[s].v, S["of"][tsl, :])
                P.dma(zt[s].v, S["zg"][tsl, :])
            chunks = [0, 1] if d == 0 else [1, 0]
            for g in range(HG_H // GH):
                p = gi % 2
                gi += 1
                hs = range(g * GH, (g + 1) * GH)
                for i, hd in enumerate(hs):
                    P.mm(pA[p][:, i, :], kT[s][:, hd, :], qT[s][:, hd, :])
                    P.tr(pT[p][:, i, :], kT[s][:, hd, :], C.ident.v)
                P.tt(am[p].v, pA[p].v, mask[d].v.us(1).bc([128, GH, 128]), ALU.mult)
                P.cp(ktm[p].v, pT[p].v, eng="act")
                for ci, ch in enumerate(chunks):
                    rs = slice(ch * 64, (ch + 1) * 64)
                    cidx = tt * 2 + ch
                    for i, hd in enumerate(hs):
                        vs = vt[s][:, hd * 128:(hd + 1) * 128]
                        P.mm(pO[p][rs, i, :], am[p][:, i, rs], vs, start=True, stop=False)
                        P.mm(pO[p][rs, i, :], qT[s][:, hd, rs], Sb[g][:, i, :], start=False, stop=True)
                        P.mm(pU[p][:, i, :], ktm[p][rs, i, :], vt[s][rs, hd * 128:(hd + 1) * 128])
                    e_bc = etot[:, d, g * GH:(g + 1) * GH, cidx].us(2).bc([128, GH, 128])
                    P.tt(Sf[g].v, pU[p].v, Sf[g].v, ALU.add)
                    P.tt(Sf[g].v, Sf[g].v, e_bc, ALU.mult)
                    P.cp(Sb[g].v, Sf[g].v, eng="act")
                osl = ot[s][:, g * GH * 128:(g + 1) * GH * 128]
                if d == 0:
                    P.cp(osl, pO[p].v.re("p g v -> p (g v)"), eng="act")
                else:
                    P.tt(osl, pO[p].v.re("p g v -> p (g v)"), of_t[s][:, g * GH * 128:(g + 1) * GH * 128], ALU.add)
            if d == 0:
                P.dma(S["of"][tsl, :], ot[s].v)
            else:
                o3 = ot[s].v.re("p (h v) -> p h v", v=128)
                P.tt(sq.v, ot[s].v, ot[s].v, ALU.mult, eng="pool")
                P.op("dve", lambda e: e.tensor_reduce(out=ss.v.ap, in_=sq.v.re("p (h v) -> p h v", v=128).ap,
                                                     axis=AX.X, op=ALU.add), [ss.v], [sq.v])
                P.act(ss.v, ss.v, AF.Sqrt, scale=1.0 / 128, bias=C.eps_col[:, 0:1])
                P.op("dve", lambda e: e.reciprocal(out=ss.v.ap, in_=ss.v.ap), [ss.v], [ss.v])
                P.tt(o3, o3, ss.v.us(2).bc([128, HG_H, 128]), ALU.mult)
                P.tt(zt[s].v, zt[s].v, gb.v, ALU.mult, eng="pool")
                P.tt(yb[s].v, ot[s].v, zt[s].v, ALU.mult)
                P.dma(S["y"][tsl, :], yb[s].v)
    P.end_phase()
    outproj_phase(P, C, w_out, ln_g, ln_b, last)


def outproj_phase(P, C, w_out, ln_g, ln_b, last):
    P.begin_phase()
    op = OutProj(P, C, w_out, ln_g, ln_b, last=last)
    yb = [P.sb("ypb%d" % i, [128, DI], BF16) for i in range(2)]
    for tt in range(NT):
        P.dma(yb[tt % 2].v, C.scr["y"][tt * 128:(tt + 1) * 128, :])
        op.tile(tt, yb[tt % 2].v)
    P.end_phase()


LAYER_KIND = ["hyena", "ssd", "hgrn2", "hyena"]
PARAMS = {
    "hyena": ["w_in", "conv_w", "conv_b", "filt_w1", "filt_b1", "filt_w2", "filt_b2", "filt_w3", "filt_b3",
              "filt_freq", "filt_w_out", "skip", "w_out", "ln_g", "ln_b"],
    "ssd": ["w_in", "conv_w", "conv_b", "dt_bias", "a_log", "d_skip", "norm_g", "w_out", "ln_g", "ln_b"],
    "hgrn2": ["w_in", "norm_g", "w_out", "ln_g", "ln_b"],
}
SHAPES = {
    "hyena": {"w_in": [D, 8192], "conv_w": [3, 6144], "conv_b": [6144], "filt_w1": [33, 64], "filt_b1": [64],
              "filt_w2": [64, 64], "filt_b2": [64], "filt_w3": [64, 64], "filt_b3": [64], "filt_freq": [64],
              "filt_w_out": [64, 4096], "skip": [DI], "w_out": [DI, D], "ln_g": [D], "ln_b": [D]},
    "ssd": {"w_in": [D, 5184], "conv_w": [5, 3072], "conv_b": [3072], "dt_bias": [2, 32], "a_log": [2, 32],
            "d_skip": [32], "norm_g": [DI], "w_out": [DI, D], "ln_g": [D], "ln_b": [D]},
    "hgrn2": {"w_in": [D, 10240], "norm_g": [DI], "w_out": [DI, D], "ln_g": [D], "ln_b": [D]},
}


def all_consts():
    c = {"ident": _ident_bf16(), "identf": np.eye(128, dtype=np.float32)}
    c.update(hgrn2_consts())
    c.update(ssd_consts())
    c.update(hyena_consts())
    return c


def build_program(layers):
    P = Prog()
    C = Ctx()
    C.x = P.dram("x", [L, D], F32, kind="ExternalInput")
    C.out = P.dram("out", [L, D], F32, kind="ExternalOutput")
    C.h = P.dram("h_scr", [L, D], F32)
    C.consts = {}
    cst = all_consts()
    for k, v in cst.items():
        dt = BF16 if v.dtype != np.float32 else F32
        C.consts[k] = P.dram("c_" + k, list(v.shape), dt, kind="ExternalInput")
    C.d_ident = C.consts["ident"]
    C.d_identf = C.consts["identf"]
    C.lbraw = P.dram("hgrn_lower_bounds", [4, 4096], F32, kind="ExternalInput")
    C.prm = {}
    for li in layers:
        kind = LAYER_KIND[li]
        for nme in PARAMS[kind]:
            key = "l%d_%s" % (li, nme)
            C.prm[key] = P.dram(key, SHAPES[kind][nme], F32, kind="ExternalInput")
    S = {}
    for nme in ["qgf", "qgb", "kgf", "kgb"]:
        S[nme] = P.dram("s_" + nme, [HG_H, 128, L], BF16)
    S["vtm"] = P.dram("s_vtm", [L, DI], BF16)
    S["zg"] = P.dram("s_zg", [L, DI], F32)
    S["of"] = P.dram("s_of", [L, DI], F32)
    S["y"] = P.dram("s_y", [L, DI], BF16)
    S["xs"] = P.dram("s_xs", [L, DI], F32)
    S["xsb"] = P.dram("s_xsb", [L, DI], BF16)
    S["btm"] = P.dram("s_btm", [L, 512], BF16)
    S["bT"] = P.dram("s_bT", [4, 128, L], BF16)
    S["cT"] = P.dram("s_cT", [4, 128, L], BF16)
    S["dt"] = P.dram("s_dt", [L, 64], F32)
    S["Zs"] = P.dram("s_Zs", [128, 64, DI], BF16)
    S["Zv"] = P.dram("s_Zv", [128, 64, DI], BF16)
    S["Zk"] = P.dram("s_Zk", [128, 64, DI], BF16)
    S["gbf"] = P.dram("s_gbf", [L, DI], BF16)
    S["etot"] = P.dram("s_etot", [128, 2 * HG_H * 64], F32)
    S["gatebf"] = P.dram("s_gatebf", [L, DI], BF16)
    S["adt"] = P.dram("s_adt", [L, 64], F32)
    C.scr = S
    setup_common(P, C)
    load_x_to_hT(P, C)
    for n_, li in enumerate(layers):
      try:
          kind = LAYER_KIND[li]
          last = (n_ == len(layers) - 1)
          pr = lambda nme: C.prm["l%d_%s" % (li, nme)]
          if kind == "hgrn2":
              hgrn2_layer(P, C, li, pr("w_in"), C.lbraw, pr("norm_g"), pr("w_out"), pr("ln_g"), pr("ln_b"), last)
          elif kind == "ssd":
              ssd_layer(P, C, li, pr("w_in"), pr("conv_w"), pr("conv_b"), pr("dt_bias"), pr("a_log"), pr("d_skip"),
                        pr("norm_g"), pr("w_out"), pr("ln_g"), pr("ln_b"), last)
          else:
              hyena_layer(P, C, li, {nme: pr(nme) for nme in PARAMS["hyena"]}, last)
      except StopBuild:
        break
    P.barrier()
    return P, C


def in_map_for(inputs, b, layers):
    m = {"x": np.ascontiguousarray(inputs["x"][b]), "hgrn_lower_bounds": inputs["hgrn_lower_bounds"]}
    for k, v in all_consts().items():
        m["c_" + k] = v
    for li in layers:
        for nme in PARAMS[LAYER_KIND[li]]:
            key = "l%d_%s" % (li, nme)
            m[key] = inputs[key]
    return m


def pipeline(n, stages, offs=None):
    ns = len(stages)
    if offs is None:
        offs = list(range(ns))
    for step in range(n + max(offs)):
        for k in range(ns - 1, -1, -1):
            i = step - offs[k]
            if 0 <= i < n:
                stages[k](i)


def ssd_scan_phase(P, C, norm_g, d_skip):
    S = C.scr
    P.begin_phase()
    M1 = [P.sb("M1%d" % i, [128, 128], F32) for i in range(2)]
    P.dma(M1[0].v, C.consts["sd_mf"].v)
    P.dma(M1[1].v, C.consts["sd_mb"].v)
    gb = P.sb("sgb", [128, DI], F32)
    row_bcast(P, gb.v, norm_g, slice(None))
    dsk = P.sb("dsk", [128, 32], F32)
    row_bcast(P, dsk.v, d_skip, slice(None))
    Sp = [P.sb("Sp%d" % i, [128, 512], F32) for i in range(4)]
    Spb = [P.sb("Spb%d" % i, [128, 512], BF16) for i in range(4)]
    NL = 2
    xs = [P.sb("xs%d" % i, [128, DI], BF16) for i in range(NL)]
    dtt = [P.sb("dtt%d" % i, [128, 64], F32) for i in range(NL)]
    adt = [P.sb("adt%d" % i, [128, 64], F32) for i in range(NL)]
    btm = [P.sb("btm%d" % i, [128, 512], BF16) for i in range(NL)]
    bT = [P.sb("bT%d" % i, [128, 4, 128], BF16) for i in range(NL)]
    cT = [P.sb("cT%d" % i, [128, 4, 128], BF16) for i in range(NL)]
    yft = [P.sb("yft%d" % i, [128, DI], F32) for i in range(2)]
    zt = [P.sb("szt%d" % i, [128, DI], F32) for i in range(2)]
    ct = [P.sb("ct%d" % i, [128, 64], F32) for i in range(2)]
    ncs = [P.sb("ncs%d" % i, [128, 32], F32) for i in range(2)]
    ecs = [P.sb("ecs%d" % i, [128, 32], F32) for i in range(2)]
    dst_ = [P.sb("dst%d" % i, [128, 32], F32) for i in range(2)]
    etot = [P.sb("setot%d" % i, [128, 32], F32) for i in range(2)]
    xdt = [P.sb("xdt%d" % i, [128, DI], BF16) for i in range(2)]
    xdtd = [P.sb("xdtd%d" % i, [128, DI], BF16) for i in range(2)]
    CBm = [P.sb("CBm%d" % i, [128, 128], F32) for i in range(1)] * 2
    X = [P.sb("X%d" % i, [128, 8, 128], F32) for i in range(1)] * 2
    X2 = [P.sb("X2%d" % i, [128, 8, 128], F32) for i in range(1)] * 2
    Mh = [P.sb("Mh%d" % i, [128, 8, 128], BF16) for i in range(2)]
    tmp = [P.sb("stmp%d" % i, [128, 512], F32) for i in range(1)] * 2
    yt = [P.sb("syt%d" % i, [128, DI], F32) for i in range(2)]
    ss = P.sb("sss", [128, 4], F32)
    yb = [P.sb("syb%d" % i, [128, DI], BF16) for i in range(1)] * 2
    p_ct = P.ps("p_ct", [128, 64], F32)
    p_cb = [P.ps("p_cb%d" % i, [128, 128], F32) for i in range(2)]
    p_row = P.ps("p_row", [128, 8, 128], F32)
    p_y = P.ps("p_y", [128, 512], F32)
    p_st = P.ps("p_st", [128, 512], F32)
    p_yo = P.ps("p_yo", [128, 512], F32)
    NTD = NT if DBG_LEVEL >= 99 else DBG_NT
    tiles = []
    for d in range(2):
        order = list(range(NT)) if d == 0 else list(range(NT - 1, -1, -1))
        tiles += [(d, tt) for tt in order[:NTD]]
    NG = len(tiles) * 4

    def info(i):
        Ti = i // 4
        d, tt = tiles[Ti]
        return Ti, d, tt, i % 4

    def st_load(i):
        Ti, d, tt, g = info(i)
        if g != 0:
            return
        s = Ti % NL
        tsl = slice(tt * 128, (tt + 1) * 128)
        P.dma(xs[s].v, S["xsb"][tsl, :])
        P.dma(dtt[s].v, S["dt"][tsl, :])
        P.dma(adt[s].v, S["adt"][tsl, :])
        P.dma(btm[s].v, S["btm"][tsl, :])
        P.dma(bT[s].v, S["bT"][:, :, tsl].re("g n t -> n g t"))
        P.dma(cT[s].v, S["cT"][:, :, tsl].re("g n t -> n g t"))
        if d == 1:
            P.dma(zt[Ti % 2].v, S["zg"][tsl, :])

    def st_pro(i):
        Ti, d, tt, g = info(i)
        if g != 0:
            return
        s = Ti % NL
        u = Ti % 2
        hc = slice(d * 32, (d + 1) * 32)
        P.mm(p_ct[:, 0:32], M1[d].v, adt[s][:, hc])
        P.mm(p_ct[:, 32:64], C.ones_f.v, adt[s][:, hc])
        P.cp(ct[u].v, p_ct.v)
        P.ts(ncs[u].v, ct[u][:, 0:32], -1.0, ALU.mult)
        P.act(ecs[u].v, ct[u][:, 0:32], AF.Exp)
        P.act(etot[u].v, ct[u][:, 32:64], AF.Exp)
        P.tt(dst_[u].v, ct[u][:, 32:64], ct[u][:, 0:32], ALU.subtract)
        P.act(dst_[u].v, dst_[u].v, AF.Exp)
        P.tt(dst_[u].v, dst_[u].v, dtt[s][:, hc], ALU.mult)
        x3 = xs[s].v.re("p (h q) -> p h q", q=64)
        P.tt(xdt[u].v.re("p (h q) -> p h q", q=64), x3, dtt[s][:, hc].us(2).bc([128, 32, 64]), ALU.mult)
        P.tt(xdtd[u].v.re("p (h q) -> p h q", q=64), x3, dst_[u].v.us(2).bc([128, 32, 64]), ALU.mult, eng="pool")

    def st1(i):
        Ti, d, tt, g = info(i)
        s = Ti % NL
        k = i % 2
        hg = slice(d * 32 + g * 8, d * 32 + (g + 1) * 8)
        P.mm(p_cb[k].v, bT[s][:, g, :], cT[s][:, g, :])
        P.tt(X[k].v, M1[d].v.us(1).bc([128, 8, 128]), adt[s][:, hg].us(2).bc([128, 8, 128]), ALU.mult, eng="pool")
        for q in range(2):
            P.mm(p_row[:, q * 4:(q + 1) * 4, :], C.ones_f.v, X[k][:, q * 4:(q + 1) * 4, :])

    def st2(i):
        Ti, d, tt, g = info(i)
        u = Ti % 2
        k = i % 2
        hl = slice(g * 8, (g + 1) * 8)
        P.tt(CBm[k].v, p_cb[k].v, M1[d].v, ALU.mult)
        P.tt(X2[k].v, p_row.v, ncs[u][:, hl].us(2).bc([128, 8, 128]), ALU.add)
        P.act(X2[k].v, X2[k].v, AF.Relu, scale=-1.0)
        P.act(X2[k].v, X2[k].v, AF.Exp, scale=-1.0)
        P.tt(Mh[k].v, X2[k].v, CBm[k].v.us(1).bc([128, 8, 128]), ALU.mult)

    def st3(i):
        Ti, d, tt, g = info(i)
        s = Ti % NL
        u = Ti % 2
        k = i % 2
        hl = slice(g * 8, (g + 1) * 8)
        tsl = slice(tt * 128, (tt + 1) * 128)
        if tt == (0 if d == 0 else NT - 1):
            P.memset(Sp[g].v, 0.0, eng="dve")
            P.memset(Spb[g].v, 0.0, eng="dve")
        if g == 0 and d == 1:
            P.dma(yft[u].v, S["of"][tsl, :])
        for h in range(8):
            hh = g * 8 + h
            P.mm(p_y[:, h * 64:(h + 1) * 64], Mh[k][:, h, :], xdt[u][:, hh * 64:(hh + 1) * 64])
        P.mm(p_yo.v, cT[s][:, g, :], Spb[g].v)
        P.mm(p_st.v, btm[s][:, g * 128:(g + 1) * 128], xdtd[u][:, g * 512:(g + 1) * 512])
        P.tt(tmp[k].v.re("p (h q) -> p h q", q=64), p_yo.v.re("p (h q) -> p h q", q=64),
             ecs[u][:, hl].us(2).bc([128, 8, 64]), ALU.mult)
        P.tt(yt[u][:, g * 512:(g + 1) * 512], p_y.v, tmp[k].v, ALU.add)
        sg = Sp[g].v
        P.tt(sg.re("p (h q) -> p h q", q=64), sg.re("p (h q) -> p h q", q=64),
             etot[u][:, hl].us(2).bc([128, 8, 64]), ALU.mult, eng="pool")
        P.tt(sg, sg, p_st.v, ALU.add)
        P.cp(Spb[g].v, sg, eng="act")
        if g != 3:
            return
        if d == 0:
            P.dma(S["of"][tsl, :], yt[u].v)
            return
        x3 = xs[s].v.re("p (h q) -> p h q", q=64)
        sq = yft[u]
        P.tt(yt[u].v, yt[u].v, yft[u].v, ALU.add)
        P.tt(sq.v.re("p (h q) -> p h q", q=64), x3, dsk.v.us(2).bc([128, 32, 64]), ALU.mult, eng="pool")
        P.tt(yt[u].v, yt[u].v, sq.v, ALU.add)
        P.tt(yt[u].v, yt[u].v, zt[u].v, ALU.mult)
        for q in range(4):
            P.act(sq[:, q * 512:(q + 1) * 512], yt[u][:, q * 512:(q + 1) * 512], AF.Square, accum=ss[:, q:q + 1])
        P.act(ss.v, ss.v, AF.Sqrt, scale=1.0 / 512, bias=C.eps_col[:, 0:1])
        P.op("dve", lambda e: e.reciprocal(out=ss.v.ap, in_=ss.v.ap), [ss.v], [ss.v], cost=0.12)
        for q in range(4):
            P.act(yt[u][:, q * 512:(q + 1) * 512], yt[u][:, q * 512:(q + 1) * 512], AF.Copy, scale=ss[:, q:q + 1])
        P.tt(yb[u].v, yt[u].v, gb.v, ALU.mult)
        P.dma(S["y"][tsl, :], yb[u].v)

    pipeline(NG, [st_load, st_pro, st1, st2, st3], [2, 4, 5, 6, 7])
    P.end_phase()


def ssd_consts():
    r = np.arange(128)[:, None]
    t = np.arange(128)[None, :]
    return {"sd_mf": (r <= t).astype(np.float32), "sd_mb": (r >= t).astype(np.float32)}


def ssd_layer(P, C, li, w_in, conv_w, conv_b, dt_bias, a_log, d_skip, norm_g, w_out, ln_g, ln_b, last):
    S = C.scr
    NTAP = 5
    shifts = [-2, -1, 0, 1, 2]
    P.begin_phase()
    stage = P.sb("sstA", [128, 8, 512], F32)
    Wz = [P.sb("Wz%d" % i, [128, 8, 512], BF16) for i in range(2)]
    pz = [P.ps("pz%d" % i, [128, 512], F32) for i in range(2)]
    zo = [P.sb("szo%d" % i, [128, 512], F32) for i in range(2)]
    bo = [P.sb("sbo%d" % i, [128, 512], BF16) for i in range(2)]
    it = 0
    for c in range(0 if DBG_SKIPA else 4):
        W = Wz[c % 2]
        prep_w_chunk(P, [W], w_in, c * 512, 512, stage)
        for tt in range(NT):
            s = it % 2
            it += 1
            mm_proj_tm(P, pz[s].v, C, [W], [0], tt, 512)
            P.act(zo[s].v, pz[s].v, AF.Silu)
            P.dma(S["zg"][tt * 128:(tt + 1) * 128, c * 512:(c + 1) * 512], zo[s].v)
    if not DBG_SKIPA:
        wcol, bcol = conv_cols(P, C, conv_w, conv_b, NTAP, 24)
        stg = P.sb("sstg", [128, 8, 128], F32)
        Wb = [P.sb("sWb%d" % i, [128, 8, 128], BF16) for i in range(2)]
        raw = [P.sb("sraw%d" % i, [128, L + 4], BF16) for i in range(2)]
        for i in range(2):
            P.memset(raw[i][:, 0:2], 0.0, eng="dve")
            P.memset(raw[i][:, L + 2:L + 4], 0.0, eng="dve")
        crow = [P.sb("scrow%d" % i, [128, L], BF16) for i in range(2)]
        xT2 = [P.sb("sxT%d" % i, [128, 2, L], BF16) for i in range(2)]
        pp = [P.ps("spp%d" % i, [128, 512], F32) for i in range(3)]
        ptr = [P.ps("sptr%d" % i, [128, 256], BF16) for i in range(2)]
        so = [P.sb("sso%d" % i, [128, 256], BF16) for i in range(2)]
        it2 = 0
        for i in range(24):
            sb_ = i % 2
            P.dma(stg.v, w_in[:, 2048 + i * 128: 2048 + (i + 1) * 128].re("(k p) c -> p k c", p=128))
            P.cp(Wb[sb_].v, stg.v)
            for tc in range(L // 512):
                b = it2 % 3
                it2 += 1
                mm_proj_fm(P, pp[b].v, C, Wb[sb_], 0, tc * 512, 512)
                P.cp(raw[sb_][:, 2 + tc * 512: 2 + (tc + 1) * 512], pp[b].v, eng=("act" if tc % 2 else "dve"))
            if i < 20:
                grp = (i // 2) % 2
                dst = xT2[grp][:, i % 2, :]
            else:
                dst = crow[i % 2].v
            conv_row(P, dst, raw[sb_], wcol, bcol, i, NTAP, func=AF.Silu)
            if 16 <= i < 20:
                P.dma(S["bT"][i - 16, :, :], dst)
            if i >= 20:
                P.dma(S["cT"][i - 20, :, :], dst)
            if i < 20 and i % 2 == 1:
                i0 = i - 1
                for tt in range(NT):
                    q = tt % 2
                    for j in range(2):
                        P.tr(ptr[q][:, j * 128:(j + 1) * 128], xT2[grp][:, j, tt * 128:(tt + 1) * 128], C.ident.v)
                    P.cp(so[q].v, ptr[q].v, eng=("act" if tt % 2 else "dve"))
                    if i0 < 16:
                        P.dma(S["xsb"][tt * 128:(tt + 1) * 128, i0 * 128:(i0 + 2) * 128], so[q].v)
                    else:
                        P.dma(S["btm"][tt * 128:(tt + 1) * 128, (i0 - 16) * 128:(i0 - 14) * 128], so[q].v)
    Wd = Wz[0]
    prep_w_chunk(P, [Wd], w_in, 5120, 64, stage)
    dtb = P.sb("dtb", [128, 64], F32)
    arow = P.sb("arow", [128, 64], F32)
    P.dma(dtb.v, V(dt_bias, dt_bias.t.rearrange("a b -> (a b)").partition_broadcast(128)))
    P.dma(arow.v, V(a_log, a_log.t.rearrange("a b -> (a b)").partition_broadcast(128)))
    P.act(arow.v, arow.v, AF.Exp)
    dto = [P.sb("dto%d" % i, [128, 64], F32) for i in range(2)]
    ado = [P.sb("ado%d" % i, [128, 64], F32) for i in range(2)]
    for tt in range(NT):
        s = tt % 2
        mm_proj_tm(P, pz[s][:, 0:64], C, [Wd], [0], tt, 64)
        P.tt(dto[s].v, pz[s][:, 0:64], dtb.v, ALU.add)
        P.act(dto[s].v, dto[s].v, AF.Exp)
        P.act(dto[s].v, dto[s].v, AF.Ln, bias=1.0)
        P.stt(ado[s].v, dto[s].v, -1.0, arow.v, ALU.mult, ALU.mult)
        P.dma(S["dt"][tt * 128:(tt + 1) * 128, :], dto[s].v)
        P.dma(S["adt"][tt * 128:(tt + 1) * 128, :], ado[s].v)
    P.end_phase()
    ssd_scan_phase(P, C, norm_g, d_skip)
    outproj_phase(P, C, w_out, ln_g, ln_b, last)


NF = 8192
NK2 = 65


def hyena_consts():
    import ml_dtypes
    bf = ml_dtypes.bfloat16
    n = NF
    tp = 2.0 * np.pi
    p = np.arange(128)
    k2 = np.arange(65)
    k2s = np.arange(1, 64)
    F1 = np.zeros((128, 128))
    F1[:, :65] = np.cos(tp * ((p[:, None] * k2[None, :]) % 128) / 128)
    F1[:, 65:] = -np.sin(tp * ((p[:, None] * k2s[None, :]) % 128) / 128)
    pp = np.arange(64)
    m = np.full(65, 2.0)
    m[0] = 1.0
    m[64] = 1.0
    Finv = np.zeros((128, 64))
    Finv[:65, :] = (m[:, None] / n) * np.cos(tp * ((k2[:, None] * pp[None, :]) % 128) / 128)
    Finv[65:, :] = -(2.0 / n) * np.sin(tp * ((k2s[:, None] * pp[None, :]) % 128) / 128)
    a = np.arange(64)
    k1 = np.arange(64)
    TA = np.zeros((65, 128, 128))
    TB = np.zeros((65, 128, 128))
    TC = np.zeros((65, 128, 128))
    TKA = np.zeros((65, 128, 128))
    TKB = np.zeros((65, 128, 128))
    for q in range(65):
        th = tp * ((a[:, None] * (q + 128 * k1[None, :])) % n) / n
        Tr = np.cos(th)
        Ti = -np.sin(th)
        TA[q, :64, :64] = Tr
        TA[q, :64, 64:] = Ti
        TA[q, 64:, :64] = -Ti
        TA[q, 64:, 64:] = Tr
        TB[q, :64, :64] = -Ti
        TB[q, :64, 64:] = Tr
        TB[q, 64:, :64] = -Tr
        TB[q, 64:, 64:] = -Ti
        Ur = np.cos(th).T
        Ui = np.sin(th).T
        TKA[q, :64, :64] = Tr
        TKA[q, :64, 64:] = Tr
        TKA[q, 64:, :64] = -Ti
        TKA[q, 64:, 64:] = -Ti
        TKB[q, :64, :64] = Ti
        TKB[q, :64, 64:] = Ti
        TKB[q, 64:, :64] = Tr
        TKB[q, 64:, 64:] = Tr
        TC[q, :64, :64] = Ur
        TC[q, 64:, :64] = -Ui
        TC[q, :64, 64:] = Ui
        TC[q, 64:, 64:] = Ur
    j = (64 * p[None, :] + a[:, None]).reshape(-1)
    pos = np.where(j < L, j, n - j)
    pos[j == L] = 0
    tl = np.linspace(0.0, 1.0, L)
    t = tl[pos]
    w = tp * pos / L
    f = np.linspace(1e-4, 15.0, 16)
    z = np.concatenate([t[:, None], np.cos(f[None, :] * w[:, None]), -np.sin(f[None, :] * w[:, None])], axis=1)
    tdec = t.copy()
    tdec[j == L] = 1e4
    ntpos = -(tdec.reshape(64, 128).T)
    min_d = math.log(1e-2) / 0.3
    max_d = math.log(1e-2) / 1.5
    deltas = np.abs(np.linspace(min_d, max_d, DI))
    return {
        "hy_F1": F1.astype(np.float32).astype(bf),
        "hy_Finv": Finv.astype(np.float32).astype(bf),
        "hy_TA": np.ascontiguousarray(TA.transpose(1, 0, 2)).astype(np.float32).astype(bf),
        "hy_TB": np.ascontiguousarray(TB.transpose(1, 0, 2)).astype(np.float32).astype(bf),
        "hy_TC": np.ascontiguousarray(TC.transpose(1, 0, 2)).astype(np.float32).astype(bf),
        "hy_TKA": np.ascontiguousarray(TKA.transpose(1, 0, 2)).astype(np.float32).astype(bf),
        "hy_TKB": np.ascontiguousarray(TKB.transpose(1, 0, 2)).astype(np.float32).astype(bf),
        "hy_zT": np.ascontiguousarray(z.T).astype(np.float32),
        "hy_ntpos": ntpos.astype(np.float32),
        "hy_deltas": deltas.astype(np.float32),
    }


def sin_reduced(P, o, arg, tmpf, tmpi, tmp2):
    tp = 2.0 * math.pi
    P.ts(tmpf, arg, 1.0 / tp, ALU.mult, 64.0, ALU.add)
    P.cp(tmpi, tmpf)
    P.cp(tmpf, tmpi)
    P.ts(tmpf, tmpf, -64.0, ALU.add, -tp, ALU.mult)
    P.tt(tmpf, tmpf, arg, ALU.add)
    P.ts(tmp2, tmpf, math.pi, ALU.is_gt, -tp, ALU.mult)
    P.tt(tmpf, tmpf, tmp2, ALU.add)
    P.ts(tmpf, tmpf, 3.14159, ALU.min, -3.14159, ALU.max)
    P.act(o, tmpf, AF.Sin)


def conv_cols(P, C, conv_w, conv_b, ntaps, nch):
    wrow = P.sb("cw_row", [nch, ntaps, 128], F32)
    brow = P.sb("cb_row", [nch, 128], F32)
    P.dma(wrow.v, conv_w.v.re("j (k p) -> k j p", p=128))
    P.dma(brow.v, conv_b.v.re("(k p) -> k p", p=128))
    wcol = P.sb("cw_col", [128, ntaps, nch], F32)
    bcol = P.sb("cb_col", [128, nch], F32)
    pt = P.ps("cw_ps", [128, ntaps + 1, nch], F32)
    for j in range(ntaps):
        P.tr(pt[:, j, :], wrow[:, j, :], C.identf[0:nch, 0:nch])
    P.tr(pt[:, ntaps, :], brow.v, C.identf[0:nch, 0:nch])
    P.cp(wcol.v, pt[:, 0:ntaps, :])
    P.cp(bcol.v, pt[:, ntaps, :])
    return wcol, bcol


def conv_row(P, out, raw, wcol, bcol, k, ntaps, func=None):
    c = ntaps // 2
    H = L // 2
    for hf in range(2):
        o = out[:, hf * H:(hf + 1) * H]
        P.act(o, raw[:, c + hf * H: c + (hf + 1) * H], AF.Identity, scale=wcol[:, c, k:k + 1], bias=bcol[:, k:k + 1])
        js = [j for j in range(ntaps) if j != c]
        for n_, j in enumerate(js):
            P.stt(o, raw[:, j + hf * H: j + (hf + 1) * H], wcol[:, j, k:k + 1], o, ALU.mult, ALU.add)
        if func is not None:
            P.act(o, o, func)


def hyena_layer(P, C, li, prm, last):
    S = C.scr
    w_in, conv_w, conv_b = prm["w_in"], prm["conv_w"], prm["conv_b"]
    shifts = [-1, 0, 1]
    P.begin_phase()
    wcol, bcol = conv_cols(P, C, conv_w, conv_b, 3, 48)
    NG2 = 2
    stage = [P.sb("hstA%d" % i, [128, 8, 2, 128], F32) for i in range(1)] * 2
    Wb = [P.sb("hWb%d" % i, [128, 8, 512], BF16) for i in range(2)]
    raw = [[P.sb("hraw%d_%d" % (i, k), [128, L + 2], BF16) for k in range(3)] for i in range(2)]
    for i in range(2):
        for k in range(3):
            P.memset(raw[i][k][:, 0:1], 0.0, eng="dve")
            P.memset(raw[i][k][:, L + 1:L + 2], 0.0, eng="dve")
    cc = [P.sb("hcc%d" % k, [128, L], BF16) for k in range(2)]
    zs2 = [P.sb("hzs%d" % i, [128, L], BF16) for i in range(2)]
    gT = [P.sb("hgT%d" % i, [128, NG2, L], BF16) for i in range(2)]
    pp = [P.ps("hpp%d" % i, [128, 512], F32) for i in range(4)]
    ptr = [P.ps("hptr%d" % i, [128, 2, NG2 * 128], BF16) for i in range(2)]
    so = [P.sb("hso%d" % i, [128, 2, NG2 * 128], BF16) for i in range(2)]
    col0 = [2048, 4096, 0, 6144]
    cch = [16, 32, 0]
    it = 0
    for i in range(16):
        sb_ = i % 2
        zs = zs2[sb_]
        for hf in range(2):
            for k2_ in range(2):
                k = hf * 2 + k2_
                P.dma(stage[hf][:, :, k2_, :],
                      w_in[:, col0[k] + i * 128: col0[k] + (i + 1) * 128].re("(k p) c -> p k c", p=128))
            P.cp(Wb[sb_][:, :, hf * 256:(hf + 1) * 256], stage[hf].v.re("p k a c -> p k (a c)"),
                 eng=("pool" if hf else "dve"))
        for tc in range(L // 512):
            for k in range(4):
                b = it % 4
                it += 1
                mm_proj_fm(P, pp[b].v, C, Wb[sb_], k * 128, tc * 512, 512)
                if k < 3:
                    P.cp(raw[sb_][k][:, 1 + tc * 512: 1 + (tc + 1) * 512], pp[b].v, eng=("act" if (tc + k) % 2 else "dve"))
                else:
                    P.act(zs[:, tc * 512:(tc + 1) * 512], pp[b].v, AF.Silu)
        conv_row(P, cc[0], raw[sb_][0], wcol, bcol, cch[0] + i, 3)
        conv_row(P, cc[1], raw[sb_][1], wcol, bcol, cch[1] + i, 3)
        P.tt(gT[0][:, i % NG2, :], cc[0].v, cc[1].v, ALU.mult, eng="pool")
        conv_row(P, cc[0], raw[sb_][2], wcol, bcol, cch[2] + i, 3)
        P.tt(gT[1][:, i % NG2, :], cc[0].v, zs.v, ALU.mult, eng="pool")
        if i % NG2 == NG2 - 1:
            i0 = i - (NG2 - 1)
            for tt in range(NT):
                q = tt % 2
                for w_ in range(2):
                    for j in range(NG2):
                        P.tr(ptr[q][:, w_, j * 128:(j + 1) * 128], gT[w_][:, j, tt * 128:(tt + 1) * 128], C.ident.v)
                P.cp(so[q].v, ptr[q].v, eng=("act" if tt % 2 else "dve"))
                P.dma(S["gbf"][tt * 128:(tt + 1) * 128, i0 * 128:(i0 + NG2) * 128], so[q][:, 0, :])
                P.dma(S["gatebf"][tt * 128:(tt + 1) * 128, i0 * 128:(i0 + NG2) * 128], so[q][:, 1, :])
    P.end_phase()
    P.begin_phase()
    F1 = P.sb("F1", [128, 128], BF16)
    P.dma(F1.v, C.consts["hy_F1"].v)
    ntp = P.sb("ntp", [128, 64], F32)
    P.dma(ntp.v, C.consts["hy_ntpos"].v)
    dl = P.sb("dl", [128, DI], F32)
    row_bcast(P, dl.v, C.consts["hy_deltas"], slice(None))
    w1 = P.sb("fw1", [33, 64], F32)
    w2 = P.sb("fw2", [64, 64], F32)
    w3 = P.sb("fw3", [64, 64], F32)
    P.dma(w1.v, prm["filt_w1"].v)
    P.dma(w2.v, prm["filt_w2"].v)
    P.dma(w3.v, prm["filt_w3"].v)
    cols = P.sb("fcols", [64, 4], F32)
    for i, nme in enumerate(["filt_freq", "filt_b1", "filt_b2", "filt_b3"]):
        P.dma(cols[:, i:i + 1], prm[nme].v.us(1))
    fb = P.sb("ffb", [64, 3], F32)
    for i in range(3):
        P.tt(fb[:, i:i + 1], cols[:, 0:1], cols[:, i + 1:i + 2], ALU.mult)
    wo_f = P.sb("fwo_f", [64, 4096], F32)
    wo = P.sb("fwo", [64, 4096], BF16)
    P.dma(wo_f.v, prm["filt_w_out"].v)
    P.cp(wo.v, wo_f.v, eng="pool")
    h3T = P.sb("h3T", [64, NF], BF16)
    MW = 1024
    zc = [P.sb("zc%d" % i, [33, MW], F32) for i in range(2)]
    arg = [P.sb("farg%d" % i, [64, MW], F32) for i in range(2)]
    tmf = [P.sb("ftmf%d" % i, [64, MW], F32) for i in range(2)]
    tmi = [P.sb("ftmi%d" % i, [64, MW], I32) for i in range(2)]
    tm2 = [P.sb("ftm2%d" % i, [64, MW], F32) for i in range(2)]
    hh = [P.sb("fhh%d" % i, [64, MW], F32) for i in range(2)]
    pm = [P.ps("fpm%d" % i, [64, MW], F32) for i in range(2)]
    it = 0
    for ch in range(NF // MW):
        s = ch % 2
        P.dma(zc[s].v, C.consts["hy_zT"][:, ch * MW:(ch + 1) * MW])
        src = zc[s].v
        for lyr, wl in enumerate([w1, w2, w3]):
            b = it % 2
            it += 1
            for q in range(MW // 512):
                P.mm(pm[b][:, q * 512:(q + 1) * 512], wl.v, src[:, q * 512:(q + 1) * 512])
            P.act(arg[b].v, pm[b].v, AF.Identity, scale=cols[:, 0:1], bias=fb[:, lyr:lyr + 1])
            dst = hh[lyr % 2].v if lyr < 2 else h3T[:, ch * MW:(ch + 1) * MW]
            sin_reduced(P, dst, arg[b].v, tmf[b].v, tmi[b].v, tm2[b].v)
            src = dst
    pk = [P.ps("fpk%d" % i, [128, 512], F32) for i in range(2)]
    pz = [P.ps("fpz%d" % i, [128, 512], F32) for i in range(2)]
    dec = [P.sb("fdec%d" % i, [128, DI], F32) for i in range(2)]
    ka = [P.sb("fka%d" % i, [128, 512], BF16) for i in range(2)]
    zt = [P.sb("fzt%d" % i, [128, DI], BF16) for i in range(2)]
    it = 0
    for a in range(64):
        sa = a % 2
        P.act(dec[sa].v, dl.v, AF.Exp, scale=ntp[:, a:a + 1])
        for c in range(4):
            s = it % 2
            it += 1
            P.mm(pk[s][0:64, :], h3T[:, a * 128: a * 128 + 64], wo[:, c * 512:(c + 1) * 512])
            P.mm(pk[s][64:128, :], h3T[:, a * 128 + 64: a * 128 + 128], wo[:, 2048 + c * 512: 2048 + (c + 1) * 512])
            P.tt(ka[s].v, pk[s].v, dec[sa][:, c * 512:(c + 1) * 512], ALU.mult)
            P.mm(pz[s].v, F1.v, ka[s].v)
            P.cp(zt[sa][:, c * 512:(c + 1) * 512], pz[s].v, eng="act")
        P.dma(S["Zk"][:, a, :], zt[sa].v)
    P.end_phase()
    P.begin_phase()
    F1 = P.sb("F1c", [128, 128], BF16)
    P.dma(F1.v, C.consts["hy_F1"].v)
    gb_ = [P.sb("cgb%d" % i, [64, DI], BF16) for i in range(3)]
    zt = [P.sb("czt%d" % i, [128, DI], BF16) for i in range(2)]
    pz = [P.ps("cpz%d" % i, [128, 512], F32) for i in range(4)]
    gview = S["gbf"].v.re("(p a) c -> a p c", a=64)
    for a in range(64):
        s = a % 2
        P.dma(gb_[a % 3].v, gview[a])
        for c in range(4):
            b = c
            P.mm(pz[b].v, F1[0:64, :], gb_[a % 3][:, c * 512:(c + 1) * 512])
            P.cp(zt[s][:, c * 512:(c + 1) * 512], pz[b].v, eng=("act" if c % 2 else "dve"))
        P.dma(S["Zs"][:, a, :], zt[s].v)
    P.end_phase()
    P.begin_phase()
    TA = P.sb("dTA", [128, NK2, 128], BF16)
    TB = P.sb("dTB", [128, NK2, 128], BF16)
    TC = P.sb("dTC", [128, NK2, 128], BF16)
    P.dma(TA.v, C.consts["hy_TA"].v)
    P.dma(TB.v, C.consts["hy_TB"].v)
    P.dma(TC.v, C.consts["hy_TC"].v)
    TKA = P.sb("dTKA", [128, NK2, 128], BF16)
    TKB = P.sb("dTKB", [128, NK2, 128], BF16)
    P.dma(TKA.v, C.consts["hy_TKA"].v)
    P.dma(TKB.v, C.consts["hy_TKB"].v)
    zin = [P.sb("dzin%d" % i, [128, DI], BF16) for i in range(2)]
    kin = [P.sb("dkin%d" % i, [128, DI], BF16) for i in range(2)]
    KA = [P.sb("dKA%d" % i, [128, 512], F32) for i in range(2)]
    KB = [P.sb("dKB%d" % i, [128, 512], F32) for i in range(2)]
    t1 = [P.sb("dt1_%d" % i, [128, 512], F32) for i in range(2)]
    t2 = [P.sb("dt2_%d" % i, [128, 512], F32) for i in range(2)]
    yb = [P.sb("dyb%d" % i, [128, 512], BF16) for i in range(2)]
    vo = [P.sb("dvo%d" % i, [128, DI], BF16) for i in range(2)]
    pA = [P.ps("dpA%d" % i, [128, 512], F32) for i in range(1)] * 2
    pB = [P.ps("dpB%d" % i, [128, 512], F32) for i in range(1)] * 2
    pKA = [P.ps("dpKA%d" % i, [128, 512], F32) for i in range(2)]
    pKB = [P.ps("dpKB%d" % i, [128, 512], F32) for i in range(2)]
    pV = [P.ps("dpV%d" % i, [128, 512], F32) for i in range(2)]
    it = 0
    for q in range(NK2):
        s = q % 2
        P.dma(zin[s][0:64, :], S["Zs"][q, :, :])
        P.dma(kin[s][0:64, :], S["Zk"][q, :, :])
        if 1 <= q <= 63:
            P.dma(zin[s][64:128, :], S["Zs"][64 + q, :, :])
            P.dma(kin[s][64:128, :], S["Zk"][64 + q, :, :])
        else:
            P.memset(zin[s][64:128, :], 0.0)
            P.memset(kin[s][64:128, :], 0.0)
        for c in range(4):
            b = it % 2
            it += 1
            cs_ = slice(c * 512, (c + 1) * 512)
            P.mm(pKA[b].v, TKA[:, q, :], kin[s][:, cs_])
            P.mm(pKB[b].v, TKB[:, q, :], kin[s][:, cs_])
            P.cp(KA[b].v, pKA[b].v, eng="act")
            P.cp(KB[b].v, pKB[b].v, eng="act")
            P.mm(pA[b].v, TA[:, q, :], zin[s][:, cs_])
            P.mm(pB[b].v, TB[:, q, :], zin[s][:, cs_])
            P.tt(t1[b].v, pA[b].v, KA[b].v, ALU.mult)
            P.tt(t2[b].v, pB[b].v, KB[b].v, ALU.mult)
            P.tt(yb[b].v, t1[b].v, t2[b].v, ALU.add, eng="pool")
            P.mm(pV[b].v, TC[:, q, :], yb[b].v)
            P.cp(vo[s][:, cs_], pV[b].v, eng=("act" if c % 2 else "dve"))
        P.dma(S["Zv"][q, :, :], vo[s][0:64, :])
        if 1 <= q <= 63:
            P.dma(S["Zv"][64 + q, :, :], vo[s][64:128, :])
    P.end_phase()
    P.begin_phase()
    Fi = P.sb("eFi", [128, 64], BF16)
    P.dma(Fi.v, C.consts["hy_Finv"].v)
    skb = P.sb("eskb", [128, DI], F32)
    row_bcast(P, skb.v, prm["skip"], slice(None))
    vin = [P.sb("evin%d" % i, [128, 2, DI], BF16) for i in range(2)]
    gt = [P.sb("egt%d" % i, [128, DI], BF16) for i in range(2)]
    gs = [P.sb("egs%d" % i, [128, DI], F32) for i in range(2)]
    zg = [P.sb("ezg%d" % i, [128, DI], BF16) for i in range(2)]
    yo = [P.sb("eyo%d" % i, [128, DI], BF16) for i in range(2)]
    py = [P.ps("epy%d" % i, [128, 512], F32) for i in range(2)]
    gview = S["gbf"].v.re("(p a) c -> a p c", a=64)
    zview = S["gatebf"].v.re("(p a) c -> a p c", a=64)
    yview = S["y"].v.re("(p a) c -> a p c", a=64)
    it = 0
    for a2 in range(32):
        s = a2 % 2
        for h in range(2):
            a = a2 * 2 + h
            P.dma(vin[s][:, h, :], S["Zv"][:, a, :])
            P.dma(gt[s][h * 64:(h + 1) * 64, :], gview[a])
            P.dma(zg[s][h * 64:(h + 1) * 64, :], zview[a])
        P.tt(gs[s].v, gt[s].v, skb.v, ALU.mult)
        for c in range(4):
            b = it % 2
            it += 1
            cs_ = slice(c * 512, (c + 1) * 512)
            for h in range(2):
                P.mm(py[b][h * 64:(h + 1) * 64, :], Fi.v, vin[s][:, h, cs_])
            P.tt(gs[s][:, cs_], py[b].v, gs[s][:, cs_], ALU.add)
            P.tt(yo[s][:, cs_], gs[s][:, cs_], zg[s][:, cs_], ALU.mult, eng="pool")
        for h in range(2):
            P.dma(yview[a2 * 2 + h], yo[s][h * 64:(h + 1) * 64, :])
    P.end_phase()
    outproj_phase(P, C, prm["w_out"], prm["ln_g"], prm["ln_b"], last)


_PROG_CACHE = {}


def kernel(**inputs):
    layers = [0, 1, 2, 3]
    if "prog" not in _PROG_CACHE:
        _PROG_CACHE["prog"] = build_program(layers)
    P, C = _PROG_CACHE["prog"]
    inputs = {k: np.asarray(v) for k, v in inputs.items()}
    nb = inputs["x"].shape[0]
    in_maps = [in_map_for(inputs, b, layers) for b in range(nb)]
    res = run_bass_kernel_spmd(P.nc, in_maps, core_ids=list(range(nb)))
    out = np.stack([np.asarray(res.results[b]["out"]) for b in range(nb)], axis=0)
    return out.astype(np.float32)
```

```python
import math
from contextlib import ExitStack
import numpy as np
import concourse.bass as bass
import concourse.mybir as mybir
from concourse.bass_utils import run_bass_kernel_spmd

F32 = mybir.dt.float32
BF16 = mybir.dt.bfloat16
I32 = mybir.dt.int32
AF = mybir.ActivationFunctionType
ALU = mybir.AluOpType
AX = mybir.AxisListType

D = 1024
L = 4096
DI = 2048
DEPTH = 4
ALPHA = (2.0 * DEPTH) ** 0.25
LN_EPS = 1e-5
PAD = 2
NT = L // 128
SAME_ENG_SYNC = True
NDS = 48
SCHED = True
SCHED_WINDOW = 1024
DEBUG_SCR = False
DBG_STOP = ""
DBG_LEVEL = 99
DBG_SKIPA = False
DBG_DIRS = 2
DBG_SUB = 99
DBG_NT = 2
DBG_PH = 99
_phc = [0]


class Buf:
    __slots__ = ("t", "w", "r", "name", "wn", "rn")

    def __init__(self, t, name=""):
        self.t = t
        self.w = None
        self.r = {}
        self.name = name
        self.wn = None
        self.rn = []

    def __getitem__(self, idx):
        return V(self, self.t[idx])

    @property
    def v(self):
        return V(self, self.t)


class V:
    __slots__ = ("b", "ap")

    def __init__(self, b, ap):
        self.b = b
        self.ap = ap

    def __getitem__(self, idx):
        return V(self.b, self.ap[idx])

    def re(self, s, **kw):
        return V(self.b, self.ap.rearrange(s, **kw))

    def bc(self, shape):
        return V(self.b, self.ap.broadcast_to(shape))

    def tb(self, shape):
        return V(self.b, self.ap.to_broadcast(shape))

    def pb(self, n):
        return V(self.b, self.ap.partition_broadcast(n))

    def cast(self, dt):
        return V(self.b, self.ap.bitcast(dt))

    def us(self, ax):
        return V(self.b, self.ap.unsqueeze(ax))


class StopBuild(Exception):
    pass


class Eng:
    def __init__(self, name, eng, sem):
        self.name = name
        self.eng = eng
        self.sem = sem
        self.n = 0
        self.seen = {}


class Prog:
    def __init__(self):
        nc = bass.Bass("TRN2", target_bir_lowering=False)
        self.nc = nc
        self.es = ExitStack()
        self.E = {}
        for name, e in [("pe", nc.tensor), ("dve", nc.vector), ("act", nc.scalar),
                        ("pool", nc.gpsimd), ("sp", nc.sync)]:
            sem = self.es.enter_context(nc.semaphore("s_" + name))
            self.E[name] = Eng(name, e, sem)
        self.dsem = []
        for i in range(NDS):
            self.dsem.append([self.es.enter_context(nc.semaphore("d%d" % i)), 0])
        self.dnext = 0
        self.phase = None
        self.uid = 0
        self.ninst = 0
        self.deferred = False
        self.nodes = []
        self.touched = []
        self.sim_time = 0.0
        self.synced = False
        self.phase_stats = []

    def begin_phase(self):
        self.phase = ExitStack()
        self.deferred = SCHED
        if SCHED and not self.synced:
            self.barrier()
            self.synced = True

    def end_phase(self):
        if self.deferred:
            self._flush()
            self.deferred = False
        self.barrier()
        self.phase.close()
        self.phase = None
        _phc[0] += 1
        if _phc[0] >= DBG_PH:
            raise StopBuild()

    def _nm(self, name):
        self.uid += 1
        return "%s_%d" % (name, self.uid)

    def sb(self, name, shape, dt, perm=False):
        st = self.es if perm else self.phase
        t = st.enter_context(self.nc.sbuf_tensor(self._nm(name), list(shape), dt))
        return Buf(t[:], name)

    def ps(self, name, shape, dt=F32, perm=False):
        st = self.es if perm else self.phase
        esz = 4 if dt == F32 else 2
        n = 1
        for x in shape[1:]:
            n *= x
        per_bank = 2048 // esz
        npad = ((n + per_bank - 1) // per_bank) * per_bank
        t = st.enter_context(self.nc.psum_tensor(self._nm(name), [shape[0], npad], dt))
        ap = t[:][:, 0:n]
        if len(shape) == 3:
            ap = ap.rearrange("p (a b) -> p a b", a=shape[1])
        return Buf(ap, name)

    def dram(self, name, shape, dt, kind="Internal"):
        if DEBUG_SCR and kind == "Internal" and name.startswith("s_"):
            kind = "ExternalOutput"
        t = self.nc.dram_tensor(name, list(shape), dt, kind=kind)
        return Buf(t.ap(), name)

    def _wait(self, E, toks):
        need = {}
        for (s, v) in toks:
            if s is E.sem and (E.name == "pe" or not SAME_ENG_SYNC):
                continue
            k = id(s)
            if k not in need or need[k][1] < v:
                need[k] = (s, v)
        for k, (s, v) in need.items():
            if E.seen.get(k, 0) >= v:
                continue
            E.eng.wait_ge(s, v)
            self.ninst += 1
            E.seen[k] = v

    @staticmethod
    def _deps(outs, ins):
        toks = []
        for v in ins:
            if v.b.w is not None:
                toks.append(v.b.w)
        for v in outs:
            if v.b.w is not None:
                toks.append(v.b.w)
            toks.extend(v.b.r.values())
        return toks

    @staticmethod
    def _mark(tok, outs, ins):
        k = id(tok[0])
        for v in ins:
            r = v.b.r
            if k not in r or r[k][1] < tok[1]:
                r[k] = tok
        for v in outs:
            v.b.w = tok
            v.b.r = {}

    def op(self, ename, fn, outs, ins, cost=0.3):
        if self.deferred:
            self._record(ename, "op", fn, outs, ins, cost, cost)
            return
        E = self.E[ename]
        self._wait(E, self._deps(outs, ins))
        E.n += 1
        fn(E.eng).then_inc(E.sem, 1)
        self.ninst += 1
        self._mark((E.sem, E.n), outs, ins)

    def dma(self, out, in_, q="sp", slow=False):
        if self.deferred:
            nbytes = 1
            for x in out.ap.shape:
                nbytes *= x
            nbytes *= (4 if out.ap.dtype in (F32, I32) else 2)
            self._record(q, "dma", (out, in_, slow), [out], [in_], 0.12, 2.0 + nbytes / 120e3)
            return
        self._emit_dma(self.E[q], out, in_, slow, self._deps([out], [in_]), True)

    def _emit_dma(self, E, out, in_, slow, toks, mark):
        i = self.dnext
        self.dnext = (i + 1) % NDS
        sem, cnt = self.dsem[i]
        if cnt > 0:
            toks = list(toks) + [(sem, cnt)]
        self._wait(E, toks)
        if slow:
            E.eng.dma_start(out=out.ap, in_=in_.ap, allow_slow_non_contiguous=True).then_inc(sem, 16)
        else:
            E.eng.dma_start(out=out.ap, in_=in_.ap).then_inc(sem, 16)
        self.ninst += 1
        self.dsem[i][1] = cnt + 16
        tok = (sem, cnt + 16)
        if mark:
            self._mark(tok, [out], [in_])
        return tok

    def _record(self, ename, kind, payload, outs, ins, busy, lat):
        nid = len(self.nodes)
        deps = set()
        for v in ins:
            b = v.b
            if b.wn is not None:
                deps.add(b.wn)
        for v in outs:
            b = v.b
            if b.wn is not None:
                deps.add(b.wn)
            deps.update(b.rn)
        for v in ins:
            b = v.b
            b.rn.append(nid)
            self.touched.append(b)
        for v in outs:
            b = v.b
            b.wn = nid
            b.rn = []
            self.touched.append(b)
        deps.discard(nid)
        self.nodes.append([ename, kind, payload, busy, lat, deps])

    def _flush(self):
        nodes = self.nodes
        n = len(nodes)
        if n == 0:
            return
        succ = [[] for _ in range(n)]
        ndep = [0] * n
        for i, nd in enumerate(nodes):
            ndep[i] = len(nd[5])
            for d in nd[5]:
                succ[d].append(i)
        ready = [0.0] * n
        fin = [0.0] * n
        queues = {}
        for i, nd in enumerate(nodes):
            queues.setdefault(nd[0], []).append(i)
        head = {e: 0 for e in queues}
        done = [False] * n
        T = {e: 0.0 for e in queues}
        order = {e: [] for e in queues}
        W = SCHED_WINDOW
        left = n
        while left:
            best = None
            for e, q in queues.items():
                h = head[e]
                while h < len(q) and done[q[h]]:
                    h += 1
                head[e] = h
                cnt = 0
                j = h
                te = T[e]
                while j < len(q) and cnt < W:
                    i = q[j]
                    j += 1
                    if done[i]:
                        continue
                    cnt += 1
                    if ndep[i]:
                        continue
                    st = ready[i] if ready[i] > te else te
                    if best is None or st < best[0] - 1e-9 or (st < best[0] + 1e-9 and i < best[1]):
                        best = (st, i, e)
                    if st <= te:
                        break
            st, i, e = best
            nd = nodes[i]
            T[e] = st + nd[3]
            f = st + nd[4]
            fin[i] = f
            done[i] = True
            left -= 1
            order[e].append(i)
            for sidx in succ[i]:
                ndep[sidx] -= 1
                lat = f + (0.12 if nodes[sidx][0] == e else 0.3)
                if lat > ready[sidx]:
                    ready[sidx] = lat
        tok = [None] * n
        for e, lst in order.items():
            E = self.E[e]
            k = E.n
            for i in lst:
                if nodes[i][1] == "op":
                    k += 1
                    tok[i] = (E.sem, k)
        dma_engs = [e for e in order if any(nodes[i][1] == "dma" for i in order[e])]
        pending = {e: list(order[e]) for e in order}
        ptr = {e: 0 for e in order}
        progress = True
        while progress:
            progress = False
            for e in list(pending.keys()):
                lst = pending[e]
                E = self.E[e]
                while ptr[e] < len(lst):
                    i = lst[ptr[e]]
                    nd = nodes[i]
                    if any(tok[d] is None for d in nd[5]):
                        break
                    toks = [tok[d] for d in nd[5]]
                    if nd[1] == "dma":
                        out, in_, slow = nd[2]
                        tok[i] = self._emit_dma(E, out, in_, slow, toks, False)
                    else:
                        self._wait(E, toks)
                        E.n += 1
                        assert tok[i] == (E.sem, E.n)
                        nd[2](E.eng).then_inc(E.sem, 1)
                        self.ninst += 1
                    ptr[e] += 1
                    progress = True
                if ptr[e] >= len(lst):
                    del pending[e]
        assert not pending, "emission deadlock"
        self.sim_time += max(T.values())
        busy = {}
        for nd in nodes:
            busy[nd[0]] = busy.get(nd[0], 0.0) + nd[3]
        self.phase_stats.append((max(T.values()), {k: round(v) for k, v in busy.items()}, n))
        self.nodes = []
        for b in self.touched:
            b.wn = None
            b.rn = []
        self.touched = []

    def barrier(self):
        toks = [(e.sem, e.n) for e in self.E.values() if e.n > 0]
        toks += [(s, c) for s, c in self.dsem if c > 0]
        for E in self.E.values():
            self._wait(E, [t for t in toks if t[0] is not E.sem])

    @staticmethod
    def _fs(v):
        n = 1
        for x in v.ap.shape[1:]:
            n *= x
        return n

    def _cost(self, eng, v, cast=False):
        n = self._fs(v)
        if eng == "act":
            return 0.2 + n / 1400.0
        if eng == "pool":
            return 0.3 + n * (0.0035 if cast else 0.0022)
        return 0.1 + n / 960.0

    def mm(self, o, lhsT, rhs, start=True, stop=True):
        n = self._fs(rhs) * (4 if lhsT.ap.dtype == F32 else 1)
        self.op("pe", lambda e: e.matmul(o.ap, lhsT=lhsT.ap, rhs=rhs.ap, start=start, stop=stop),
                [o], [lhsT, rhs], cost=0.07 + n / 2200.0)

    def tr(self, o, a, ident):
        self.op("pe", lambda e: e.transpose(o.ap, a.ap, ident.ap), [o], [a, ident], cost=0.15)

    def act(self, o, a, func, scale=1.0, bias=0.0, accum=None, eng="act"):
        ins = [a]
        outs = [o]
        kw = {}
        if isinstance(scale, V):
            ins.append(scale)
            kw["scale"] = scale.ap
        else:
            kw["scale"] = float(scale)
        if isinstance(bias, V):
            ins.append(bias)
            kw["bias"] = bias.ap
        else:
            kw["bias"] = float(bias)
        if accum is not None:
            outs.append(accum)
            kw["accum_out"] = accum.ap
        self.op("act", lambda e: e.activation(out=o.ap, in_=a.ap, func=func, **kw), outs, ins,
                cost=self._cost("act", a))

    def tt(self, o, a, b, op, eng="dve"):
        self.op(eng, lambda e: e.tensor_tensor(out=o.ap, in0=a.ap, in1=b.ap, op=op), [o], [a, b],
                cost=self._cost(eng, o))

    def ts(self, o, a, s1, op0, s2=None, op1=None, eng="dve", accum=None):
        ins = [a]
        outs = [o]
        a1 = s1.ap if isinstance(s1, V) else float(s1)
        if isinstance(s1, V):
            ins.append(s1)
        a2 = None
        if s2 is not None:
            a2 = s2.ap if isinstance(s2, V) else float(s2)
            if isinstance(s2, V):
                ins.append(s2)
        kw = {}
        if op1 is not None:
            kw["op1"] = op1
        if accum is not None:
            kw["accum_out"] = accum.ap
            outs.append(accum)
        self.op(eng, lambda e: e.tensor_scalar(out=o.ap, in0=a.ap, scalar1=a1, scalar2=a2, op0=op0, **kw),
                outs, ins, cost=self._cost(eng, o))

    def stt(self, o, a, s, b, op0, op1, eng="dve"):
        ins = [a, b]
        sv = s.ap if isinstance(s, V) else float(s)
        if isinstance(s, V):
            ins.append(s)
        self.op(eng, lambda e: e.scalar_tensor_tensor(out=o.ap, in0=a.ap, scalar=sv, in1=b.ap, op0=op0, op1=op1),
                [o], ins, cost=self._cost(eng, o))

    def cp(self, o, a, eng="dve"):
        if eng == "act":
            self.op("act", lambda e: e.copy(out=o.ap, in_=a.ap), [o], [a], cost=self._cost("act", o))
        else:
            self.op(eng, lambda e: e.tensor_copy(out=o.ap, in_=a.ap), [o], [a], cost=self._cost(eng, o, cast=True))

    def memset(self, o, val, eng="pool"):
        self.op(eng, lambda e: e.memset(o.ap, val), [o], [], cost=self._cost(eng, o))


def _ident_bf16():
    import ml_dtypes
    return np.eye(128, dtype=np.float32).astype(ml_dtypes.bfloat16)


class Ctx:
    pass


def setup_common(P, C):
    C.hT = P.sb("hT", [128, D // 128, L + 2 * PAD], BF16, perm=True)
    C.ident = P.sb("ident", [128, 128], BF16, perm=True)
    C.identf = P.sb("identf", [128, 128], F32, perm=True)
    C.ones_bf = P.sb("ones_bf", [128, 128], BF16, perm=True)
    C.ones_f = P.sb("ones_f", [128, 128], F32, perm=True)
    P.dma(C.ident.v, C.d_ident.v)
    P.dma(C.identf.v, C.d_identf.v)
    P.memset(C.ones_bf.v, 1.0)
    P.memset(C.ones_f.v, 1.0)
    P.memset(C.hT.v, 0.0)
    C.eps_col = P.sb("eps_col", [128, 1], F32, perm=True)
    P.memset(C.eps_col.v, LN_EPS)


def load_x_to_hT(P, C):
    P.begin_phase()
    xt = [P.sb("xt%d" % i, [128, D], F32) for i in range(2)]
    xb = [P.sb("xb%d" % i, [128, D], BF16) for i in range(2)]
    pt = [P.ps("ptr%d" % i, [128, D], BF16) for i in range(2)]
    for tt in range(NT):
        s = tt % 2
        P.dma(xt[s].v, C.x[tt * 128:(tt + 1) * 128, :])
        P.dma(C.h_tiles[tt].v, xt[s].v)
        P.cp(xb[s].v, xt[s].v, eng="act")
        for kc in range(D // 128):
            P.tr(pt[s][:, kc * 128:(kc + 1) * 128], xb[s][:, kc * 128:(kc + 1) * 128], C.ident.v)
        P.cp(C.hT[:, :, PAD + tt * 128: PAD + (tt + 1) * 128],
             pt[s].v.re("p (k t) -> p k t", k=D // 128))
    P.end_phase()


def row_bcast(P, dst, src_buf, sl):
    P.dma(dst, V(src_buf, src_buf.t[sl].partition_broadcast(128)))


def prep_w_chunk(P, Wt, w_d, c0, n, stage, convw_d=None, cc0=0, cwb=None, ntaps=0):
    nk = w_d.t.shape[0] // 128
    P.dma(stage[:, :nk, :n], w_d[:, c0:c0 + n].re("(k p) c -> p k c", p=128))
    if ntaps == 0:
        h = nk // 2
        P.cp(Wt[0][:, :h, :n], stage[:, :h, :n], eng="dve")
        P.cp(Wt[0][:, h:nk, :n], stage[:, h:nk, :n], eng="act")
        return
    row_bcast(P, cwb[:, :ntaps, :n], convw_d, (slice(None), slice(cc0, cc0 + n)))
    for j in range(ntaps):
        P.tt(Wt[j][:, :nk, :n], stage[:, :nk, :n],
             cwb[:, j:j + 1, :n].bc([128, nk, n]), ALU.mult, eng=("pool" if j % 2 else "dve"))


def mm_proj_tm(P, ps, C, Wts, shifts, tt, n, bias=None):
    nk = D // 128
    tot = len(Wts) * nk + (1 if bias is not None else 0)
    i = 0
    for Wt, sh in zip(Wts, shifts):
        for kc in range(nk):
            t0 = PAD + tt * 128 + sh
            P.mm(ps, C.hT[:, kc, t0:t0 + 128], Wt[:, kc, :n], start=(i == 0), stop=(i == tot - 1))
            i += 1
    if bias is not None:
        P.mm(ps, C.ones_bf[0:1, :], bias, start=False, stop=True)


class OutProj:
    def __init__(self, P, C, wout_d, lng_d, lnb_d, last=False):
        self.P, self.C, self.last = P, C, last
        self.w = P.sb("wout", [128, DI // 128, D], BF16)
        st = [P.sb("wost%d" % i, [128, 4, D], F32) for i in range(2)]
        for q in range(4):
            P.dma(st[q % 2].v, wout_d[q * 512:(q + 1) * 512, :].re("(k p) c -> p k c", p=128))
            P.cp(self.w[:, q * 4:(q + 1) * 4, :], st[q % 2].v, eng=("act" if q % 2 else "dve"))
        self.g = P.sb("lng", [128, D], F32)
        self.b = P.sb("lnb", [128, D], F32)
        row_bcast(P, self.g.v, lng_d, slice(None))
        row_bcast(P, self.b.v, lnb_d, slice(None))
        nb = 2
        self.yT = [P.sb("yT%d" % i, [128, DI // 128, 128], BF16) for i in range(nb)]
        self.ho = [P.sb("ho%d" % i, [128, D], F32) for i in range(nb)]
        self.r = [P.sb("r%d" % i, [128, D], F32) for i in range(nb)]
        self.hn = [P.sb("hn%d" % i, [128, D], F32) for i in range(nb)]
        self.hb = [P.sb("hb%d" % i, [128, D], BF16) for i in range(nb)]
        self.st6 = [P.sb("st6%d" % i, [128, 2, 6], F32) for i in range(nb)]
        self.mv = [P.sb("mv%d" % i, [128, 2], F32) for i in range(nb)]
        self.rstd = [P.sb("rstd%d" % i, [128, 1], F32) for i in range(nb)]
        self.pT = [P.ps("opT%d" % i, [128, DI // 2], BF16) for i in range(2)]
        self.po = [P.ps("opo%d" % i, [128, D], F32) for i in range(2)]
        self.pH = [P.ps("opH%d" % i, [128, D], BF16) for i in range(2)]

    def tile(self, tt, y):
        P, C = self.P, self.C
        k = tt % 2
        yT, ho, r, hn, hb, st6, mv, rstd = (self.yT[k], self.ho[k], self.r[k], self.hn[k], self.hb[k],
                                              self.st6[k], self.mv[k], self.rstd[k])
        po = self.po[k]
        pH = self.pH[k]
        h8 = DI // 256
        for hf in range(2):
            pT = self.pT[hf]
            for c in range(h8):
                cg = hf * h8 + c
                P.tr(pT[:, c * 128:(c + 1) * 128], y[:, cg * 128:(cg + 1) * 128], C.ident.v)
            P.cp(yT[:, hf * h8:(hf + 1) * h8, :], pT.v.re("p (c t) -> p c t", c=h8), eng=("act" if hf else "dve"))
        P.dma(ho.v, C.h_tiles[tt].v)
        for dh in range(2):
            for c in range(DI // 128):
                P.mm(po[:, dh * 512:(dh + 1) * 512], yT[:, c, :],
                     self.w[:, c, dh * 512:(dh + 1) * 512], start=(c == 0), stop=(c == DI // 128 - 1))
        P.stt(r.v, ho.v, ALPHA, po.v, ALU.mult, ALU.add)
        for q in range(2):
            P.op("dve", lambda e, q=q: e.bn_stats(out=st6[:, q, :].ap, in_=r[:, q * 512:(q + 1) * 512].ap),
                 [st6.v], [r.v], cost=0.65)
        P.op("dve", lambda e: e.bn_aggr(out=mv.v.ap, in_=st6.v.re("p a b -> p (a b)").ap),
             [mv.v], [st6.v], cost=0.15)
        P.act(rstd.v, mv[:, 1:2], AF.Sqrt, scale=1.0, bias=C.eps_col[:, 0:1])
        P.op("dve", lambda e: e.reciprocal(out=rstd.v.ap, in_=rstd.v.ap), [rstd.v], [rstd.v], cost=0.12)
        P.ts(hn.v, r.v, mv[:, 0:1], ALU.subtract, rstd[:, 0:1], ALU.mult)
        P.tt(hn.v, hn.v, self.g.v, ALU.mult, eng="pool")
        P.tt(hn.v, hn.v, self.b.v, ALU.add)
        if self.last:
            P.dma(C.out[tt * 128:(tt + 1) * 128, :], hn.v)
            return
        P.dma(C.h_tiles[tt].v, hn.v)
        P.cp(hb.v, hn.v, eng="act")
        for kc in range(D // 128):
            P.tr(pH[:, kc * 128:(kc + 1) * 128], hb[:, kc * 128:(kc + 1) * 128], C.ident.v)
        P.cp(C.hT[:, :, PAD + tt * 128: PAD + (tt + 1) * 128],
             pH.v.re("p (k t) -> p k t", k=D // 128))


def mm_proj_fm(P, ps, C, Wt, c0, t0, n, shift=0, start=True, stop=True):
    nk = D // 128
    for kc in range(nk):
        a = PAD + t0 + shift
        P.mm(ps, Wt[:, kc, c0:c0 + 128], C.hT[:, kc, a:a + n],
             start=(start and kc == 0), stop=(stop and kc == nk - 1))


HG_H = 16


def hgrn2_consts():
    s = np.arange(128)[:, None]
    t = np.arange(128)[None, :]
    same = (s // 64) == (t // 64)
    mf = ((s <= t) & same).astype(np.float32)
    mb = ((s >= t) & same).astype(np.float32)
    rm = np.ones((128, 512), np.float32)
    rm[:, ::64] = 0.0
    return {"hg_mf": mf, "hg_mb": mb, "hg_rm": rm}


def hgrn2_layer(P, C, li, w_in, lb_raw, norm_g, w_out, ln_g, ln_b, last):
    S = C.scr
    qg = [S["qgf"], S["qgb"]]
    kg = [S["kgf"], S["kgb"]]
    P.begin_phase()
    lbr = P.sb("lbr", [32, 4, 128], F32)
    P.dma(lbr.v, lb_raw.v.re("l (g k) -> g l k", k=128))
    lbT = P.sb("lbT", [128, 4, 32], F32)
    pl = P.ps("pl", [128, 4, 32], F32)
    for l in range(4):
        P.tr(pl[:, l, :], lbr[:, l, :], C.identf[0:32, 0:32])
    P.act(lbT.v, pl.v, AF.Exp)
    den = P.sb("den", [128, 32], F32)
    num = P.sb("num", [128, 32], F32)
    P.tt(den.v, lbT[:, 0, :], lbT[:, 1, :], ALU.add)
    P.tt(den.v, den.v, lbT[:, 2, :], ALU.add)
    P.tt(den.v, den.v, lbT[:, 3, :], ALU.add)
    P.memset(num.v, 0.0, eng="dve")
    for j in range(1, li + 1):
        P.tt(num.v, num.v, lbT[:, j, :], ALU.add)
    P.op("dve", lambda e: e.reciprocal(out=den.v.ap, in_=den.v.ap), [den.v], [den.v])
    lb = P.sb("lb", [128, 32], F32)
    ln1mlb = P.sb("ln1mlb", [128, 32], F32)
    P.tt(lb.v, num.v, den.v, ALU.mult)
    P.ts(ln1mlb.v, lb.v, -1.0, ALU.mult, 1.0, ALU.add)
    P.act(ln1mlb.v, ln1mlb.v, AF.Ln)
    etot = P.sb("etot", [128, 2, HG_H, 64], F32)
    rm = P.sb("rm", [128, 512], F32)
    P.dma(rm.v, C.consts["hg_rm"].v)
    stage = P.sb("stA", [128, 8, 384], F32)
    Wt = [P.sb("WtA%d" % i, [128, 8, 384], BF16) for i in range(2)]
    pq = [P.ps("pq%d" % i, [128, 512], F32) for i in range(2)]
    pf = [[P.ps("pf%d_%d" % (d, i), [128, 512], F32) for i in range(2)] for d in range(2)]
    nb = 2
    e_ = [P.sb("e%d" % i, [128, 512], F32) for i in range(nb)]
    l1 = [P.sb("l1%d" % i, [128, 512], F32) for i in range(nb)]
    l2 = [P.sb("l2%d" % i, [128, 512], F32) for i in range(nb)]
    lf = [P.sb("lf%d" % i, [128, 512], F32) for i in range(nb)]
    pp = [P.sb("pp%d" % i, [128, 512], F32) for i in range(nb)]
    lk = [P.sb("lk%d" % i, [128, 512], F32) for i in range(nb)]
    eg = [P.sb("eg%d" % i, [128, 512], F32) for i in range(nb)]
    qo = [P.sb("qo%d" % i, [128, 512], BF16) for i in range(nb)]
    ko = [P.sb("ko%d" % i, [128, 512], BF16) for i in range(nb)]
    tot8 = [P.sb("tot8%d" % i, [128, 8], F32) for i in range(nb)]
    it = 0
    for hd in range(HG_H):
        W = Wt[hd % 2]
        for j, c0 in enumerate([hd * 128, 2048 + hd * 128, 4096 + hd * 128]):
            P.dma(stage[:, :, j * 128:(j + 1) * 128], w_in[:, c0:c0 + 128].re("(k p) c -> p k c", p=128))
        P.cp(W.v, stage.v, eng="pool")
        for tc in range(L // 512):
            t0 = tc * 512
            s = tc % 2
            mm_proj_fm(P, pq[s].v, C, W, 0, t0, 512)
            for d in range(2):
                mm_proj_fm(P, pf[d][s].v, C, W, 128 * (1 + d), t0, 512)
            for d in range(2):
                b = it % nb
                it += 1
                col = d * HG_H + hd
                ff = pf[d][s]
                P.act(e_[b].v, ff.v, AF.Exp, scale=-1.0)
                P.act(l1[b].v, e_[b].v, AF.Ln, scale=lb[:, col:col + 1], bias=1.0)
                P.act(l2[b].v, e_[b].v, AF.Ln, scale=1.0, bias=1.0)
                P.tt(lf[b].v, l1[b].v, l2[b].v, ALU.subtract)
                P.op("dve", lambda e, b=b: e.tensor_tensor_scan(out=pp[b].v.ap, data0=rm.v.ap, data1=lf[b].v.ap,
                                                               initial=0.0, op0=ALU.mult, op1=ALU.add),
                     [pp[b].v], [rm.v, lf[b].v])
                P.stt(lk[b].v, ff.v, -1.0, l2[b].v, ALU.mult, ALU.subtract)
                if d == 1:
                    P.tt(lf[b].v, lf[b].v, pp[b].v, ALU.subtract, eng="pool")
                    P.cp(tot8[b].v, pp[b].v.re("p (c t) -> p c t", t=64)[:, :, 63], eng="pool")
                    P.tt(pp[b].v.re("p (c t) -> p c t", t=64), lf[b].v.re("p (c t) -> p c t", t=64),
                         tot8[b].v.us(2).bc([128, 8, 64]), ALU.add)
                G = pp[b]
                P.act(eg[b].v, G.v, AF.Exp)
                ev = eg[b].v.re("p (c t) -> p c t", t=64)
                P.cp(etot[:, d, hd, tc * 8:(tc + 1) * 8], (ev[:, :, 63] if d == 0 else ev[:, :, 0]), eng="pool")
                P.tt(qo[b].v, pq[s].v, eg[b].v, ALU.mult)
                P.tt(lk[b].v, lk[b].v, G.v, ALU.subtract, eng="pool")
                P.act(ko[b].v, lk[b].v, AF.Exp, bias=ln1mlb[:, col:col + 1])
                P.dma(qg[d][hd, :, t0:t0 + 512], qo[b].v)
                P.dma(kg[d][hd, :, t0:t0 + 512], ko[b].v)
    P.dma(S["etot"].v, etot.v.re("p a h c -> p (a h c)"))
    P.end_phase()
    P.begin_phase()
    stageB = [P.sb("stB%d" % i, [128, 8, 512], F32) for i in range(2)]
    WtB = [P.sb("WtB%d" % i, [128, 8, 512], BF16) for i in range(2)]
    pv = [P.ps("pv%d" % i, [128, 512], F32) for i in range(4)]
    vo = [P.sb("vo%d" % i, [128, 512], BF16) for i in range(4)]
    zo = [P.sb("zo%d" % i, [128, 512], F32) for i in range(4)]
    it = 0
    for isz in range(2):
        for c in range(4):
            W = WtB[c % 2]
            prep_w_chunk(P, [W], w_in, 6144 + isz * 2048 + c * 512, 512, stageB[c % 2])
            for tt in range(NT):
                s = it % 4
                it += 1
                mm_proj_tm(P, pv[s].v, C, [W], [0], tt, 512)
                if isz == 0:
                    P.cp(vo[s].v, pv[s].v, eng=("act" if it % 2 else "dve"))
                    P.dma(S["vtm"][tt * 128:(tt + 1) * 128, c * 512:(c + 1) * 512], vo[s].v)
                else:
                    P.act(zo[s].v, pv[s].v, AF.Silu)
                    P.dma(S["zg"][tt * 128:(tt + 1) * 128, c * 512:(c + 1) * 512], zo[s].v)
    P.end_phase()
    P.begin_phase()
    etot = P.sb("etotC", [128, 2, HG_H, 64], F32)
    P.dma(etot.v.re("p a h c -> p (a h c)"), S["etot"].v)
    mask = [P.sb("mk%d" % i, [128, 128], F32) for i in range(2)]
    P.dma(mask[0].v, C.consts["hg_mf"].v)
    P.dma(mask[1].v, C.consts["hg_mb"].v)
    gb = P.sb("ngb", [128, DI], F32)
    row_bcast(P, gb.v, norm_g, slice(None))
    GH = 4
    Sf = [P.sb("Sf%d" % i, [128, GH, 128], F32) for i in range(HG_H // GH)]
    Sb = [P.sb("Sb%d" % i, [128, GH, 128], BF16) for i in range(HG_H // GH)]
    qT = [P.sb("qT%d" % i, [128, HG_H, 128], BF16) for i in range(2)]
    kT = [P.sb("kT%d" % i, [128, HG_H, 128], BF16) for i in range(2)]
    vt = [P.sb("vt%d" % i, [128, DI], BF16) for i in range(2)]
    ot = [P.sb("ot%d" % i, [128, DI], F32) for i in range(2)]
    of_t = [P.sb("oft%d" % i, [128, DI], F32) for i in range(2)]
    zt = [P.sb("zt%d" % i, [128, DI], F32) for i in range(2)]
    sq = P.sb("sq", [128, DI], F32)
    ss = P.sb("ss", [128, HG_H], F32)
    yb = [P.sb("yb%d" % i, [128, DI], BF16) for i in range(2)]
    pA = [P.ps("pA%d" % i, [128, GH, 128], F32) for i in range(2)]
    pT = [P.ps("pT%d" % i, [128, GH, 128], BF16) for i in range(2)]
    pO = [P.ps("pO%d" % i, [128, GH, 128], F32) for i in range(2)]
    pU = [P.ps("pU%d" % i, [128, GH, 128], F32) for i in range(2)]
    am = [P.sb("am%d" % i, [128, GH, 128], BF16) for i in range(2)]
    ktm = [P.sb("ktm%d" % i, [128, GH, 128], BF16) for i in range(2)]
    gi = 0
    for d in range(2):
        for g in range(HG_H // GH):
            P.memset(Sf[g].v, 0.0, eng="dve")
            P.memset(Sb[g].v, 0.0, eng="dve")
        order = list(range(NT)) if d == 0 else list(range(NT - 1, -1, -1))
        for n_, tt in enumerate(order):
            s = n_ % 2
            tsl = slice(tt * 128, (tt + 1) * 128)
            P.dma(qT[s].v, qg[d][:, :, tsl].re("h k t -> k h t"))
            P.dma(kT[s].v, kg[d][:, :, tsl].re("h k t -> k h t"))
            P.dma(vt[s].v, S["vtm"][tsl, :])
            if d == 1:
                P.dma(of_t[s].v, S["of"][tsl, :])
                P.dma(zt[s].v, S["zg"][tsl, :])
            chunks = [0, 1] if d == 0 else [1, 0]
            for g in range(HG_H // GH):
                p = gi % 2
                gi += 1
                hs = range(g * GH, (g + 1) * GH)
                for i, hd in enumerate(hs):
                    P.mm(pA[p][:, i, :], kT[s][:, hd, :], qT[s][:, hd, :])
                    P.tr(pT[p][:, i, :], kT[s][:, hd, :], C.ident.v)
                P.tt(am[p].v, pA[p].v, mask[d].v.us(1).bc([128, GH, 128]), ALU.mult)
                P.cp(ktm[p].v, pT[p].v, eng="act")
                for ci, ch in enumerate(chunks):
                    rs = slice(ch * 64, (ch + 1) * 64)
                    cidx = tt * 2 + ch
                    for i, hd in enumerate(hs):
                        vs = vt[s][:, hd * 128:(hd + 1) * 128]
                        P.mm(pO[p][rs, i, :], am[p][:, i, rs], vs, start=True, stop=False)
                        P.mm(pO[p][rs, i, :], qT[s][:, hd, rs], Sb[g][:, i, :], start=False, stop=True)
                        P.mm(pU[p][:, i, :], ktm[p][rs, i, :], vt[s][rs, hd * 128:(hd + 1) * 128])
                    e_bc = etot[:, d, g * GH:(g + 1) * GH, cidx].us(2).bc([128, GH, 128])
                    P.tt(Sf[g].v, pU[p].v, Sf[g].v, ALU.add)
                    P.tt(Sf[g].v, Sf[g].v, e_bc, ALU.mult)
                    P.cp(Sb[g].v, Sf[g].v, eng="act")
                osl = ot[s][:, g * GH * 128:(g + 1) * GH * 128]
                if d == 0:
                    P.cp(osl, pO[p].v.re("p g v -> p (g v)"), eng="act")
                else:
                    P.tt(osl, pO[p].v.re("p g v -> p (g v)"), of_t[s][:, g * GH * 128:(g + 1) * GH * 128], ALU.add)
            if d == 0:
                P.dma(S["of"][tsl, :], ot[s].v)
            else:
                o3 = ot[s].v.re("p (h v) -> p h v", v=128)
                P.tt(sq.v, ot[s].v, ot[s].v, ALU.mult, eng="pool")
                P.op("dve", lambda e: e.tensor_reduce(out=ss.v.ap, in_=sq.v.re("p (h v) -> p h v", v=128).ap,
                                                     axis=AX.X, op=ALU.add), [ss.v], [sq.v])
                P.act(ss.v, ss.v, AF.Sqrt, scale=1.0 / 128, bias=C.eps_col[:, 0:1])
                P.op("dve", lambda e: e.reciprocal(out=ss.v.ap, in_=ss.v.ap), [ss.v], [ss.v])
                P.tt(o3, o3, ss.v.us(2).bc([128, HG_H, 128]), ALU.mult)
                P.tt(zt[s].v, zt[s].v, gb.v, ALU.mult, eng="pool")
                P.tt(yb[s].v, ot[s].v, zt[s].v, ALU.mult)
                P.dma(S["y"][tsl, :], yb[s].v)
    P.end_phase()
    outproj_phase(P, C, w_out, ln_g, ln_b, last)


def outproj_phase(P, C, w_out, ln_g, ln_b, last):
    P.begin_phase()
    op = OutProj(P, C, w_out, ln_g, ln_b, last=last)
    yb = [P.sb("ypb%d" % i, [128, DI], BF16) for i in range(2)]
    for tt in range(NT):
        P.dma(yb[tt % 2].v, C.scr["y"][tt * 128:(tt + 1) * 128, :])
        op.tile(tt, yb[tt % 2].v)
    P.end_phase()


LAYER_KIND = ["hyena", "ssd", "hgrn2", "hyena"]
PARAMS = {
    "hyena": ["w_in", "conv_w", "conv_b", "filt_w1", "filt_b1", "filt_w2", "filt_b2", "filt_w3", "filt_b3",
              "filt_freq", "filt_w_out", "skip", "w_out", "ln_g", "ln_b"],
    "ssd": ["w_in", "conv_w", "conv_b", "dt_bias", "a_log", "d_skip", "norm_g", "w_out", "ln_g", "ln_b"],
    "hgrn2": ["w_in", "norm_g", "w_out", "ln_g", "ln_b"],
}
SHAPES = {
    "hyena": {"w_in": [D, 8192], "conv_w": [3, 6144], "conv_b": [6144], "filt_w1": [33, 64], "filt_b1": [64],
              "filt_w2": [64, 64], "filt_b2": [64], "filt_w3": [64, 64], "filt_b3": [64], "filt_freq": [64],
              "filt_w_out": [64, 4096], "skip": [DI], "w_out": [DI, D], "ln_g": [D], "ln_b": [D]},
    "ssd": {"w_in": [D, 5184], "conv_w": [5, 3072], "conv_b": [3072], "dt_bias": [2, 32], "a_log": [2, 32],
            "d_skip": [32], "norm_g": [DI], "w_out": [DI, D], "ln_g": [D], "ln_b": [D]},
    "hgrn2": {"w_in": [D, 10240], "norm_g": [DI], "w_out": [DI, D], "ln_g": [D], "ln_b": [D]},
}


def all_consts():
    c = {"ident": _ident_bf16(), "identf": np.eye(128, dtype=np.float32)}
    c.update(hgrn2_consts())
    c.update(ssd_consts())
    c.update(hyena_consts())
    return c


def build_program(layers):
    P = Prog()
    C = Ctx()
    C.x = P.dram("x", [L, D], F32, kind="ExternalInput")
    C.out = P.dram("out", [L, D], F32, kind="ExternalOutput")
    C.h = P.dram("h_scr", [L, D], F32)
    C.h_tiles = [Buf(C.h.t[tt * 128:(tt + 1) * 128, :], "h%d" % tt) for tt in range(NT)]
    C.consts = {}
    cst = all_consts()
    for k, v in cst.items():
        dt = BF16 if v.dtype != np.float32 else F32
        C.consts[k] = P.dram("c_" + k, list(v.shape), dt, kind="ExternalInput")
    C.d_ident = C.consts["ident"]
    C.d_identf = C.consts["identf"]
    C.lbraw = P.dram("hgrn_lower_bounds", [4, 4096], F32, kind="ExternalInput")
    C.prm = {}
    for li in layers:
        kind = LAYER_KIND[li]
        for nme in PARAMS[kind]:
            key = "l%d_%s" % (li, nme)
            C.prm[key] = P.dram(key, SHAPES[kind][nme], F32, kind="ExternalInput")
    S = {}
    for nme in ["qgf", "qgb", "kgf", "kgb"]:
        S[nme] = P.dram("s_" + nme, [HG_H, 128, L], BF16)
    S["vtm"] = P.dram("s_vtm", [L, DI], BF16)
    S["zg"] = P.dram("s_zg", [L, DI], F32)
    S["of"] = P.dram("s_of", [L, DI], F32)
    S["y"] = P.dram("s_y", [L, DI], BF16)
    S["xs"] = P.dram("s_xs", [L, DI], F32)
    S["xsb"] = P.dram("s_xsb", [L, DI], BF16)
    S["btm"] = P.dram("s_btm", [L, 512], BF16)
    S["bT"] = P.dram("s_bT", [4, 128, L], BF16)
    S["cT"] = P.dram("s_cT", [4, 128, L], BF16)
    S["dt"] = P.dram("s_dt", [L, 64], F32)
    S["Zs"] = P.dram("s_Zs", [128, 64, DI], BF16)
    S["Zv"] = P.dram("s_Zv", [128, 64, DI], BF16)
    S["Zk"] = P.dram("s_Zk", [128, 64, DI], BF16)
    S["gbf"] = P.dram("s_gbf", [L, DI], BF16)
    S["etot"] = P.dram("s_etot", [128, 2 * HG_H * 64], F32)
    S["gatebf"] = P.dram("s_gatebf", [L, DI], BF16)
    S["adt"] = P.dram("s_adt", [L, 64], F32)
    C.scr = S
    setup_common(P, C)
    load_x_to_hT(P, C)
    for n_, li in enumerate(layers):
      try:
          kind = LAYER_KIND[li]
          last = (n_ == len(layers) - 1)
          pr = lambda nme: C.prm["l%d_%s" % (li, nme)]
          if kind == "hgrn2":
              hgrn2_layer(P, C, li, pr("w_in"), C.lbraw, pr("norm_g"), pr("w_out"), pr("ln_g"), pr("ln_b"), last)
          elif kind == "ssd":
              ssd_layer(P, C, li, pr("w_in"), pr("conv_w"), pr("conv_b"), pr("dt_bias"), pr("a_log"), pr("d_skip"),
                        pr("norm_g"), pr("w_out"), pr("ln_g"), pr("ln_b"), last)
          else:
              hyena_layer(P, C, li, {nme: pr(nme) for nme in PARAMS["hyena"]}, last)
      except StopBuild:
        break
    P.barrier()
    return P, C


def in_map_for(inputs, b, layers):
    m = {"x": np.ascontiguousarray(inputs["x"][b]), "hgrn_lower_bounds": inputs["hgrn_lower_bounds"]}
    for k, v in all_consts().items():
        m["c_" + k] = v
    for li in layers:
        for nme in PARAMS[LAYER_KIND[li]]:
            key = "l%d_%s" % (li, nme)
            m[key] = inputs[key]
    return m


def pipeline(n, stages, offs=None):
    ns = len(stages)
    if offs is None:
        offs = list(range(ns))
    for step in range(n + max(offs)):
        for k in range(ns - 1, -1, -1):
            i = step - offs[k]
            if 0 <= i < n:
                stages[k](i)


def ssd_scan_phase(P, C, norm_g, d_skip):
    S = C.scr
    P.begin_phase()
    M1 = [P.sb("M1%d" % i, [128, 128], F32) for i in range(2)]
    P.dma(M1[0].v, C.consts["sd_mf"].v)
    P.dma(M1[1].v, C.consts["sd_mb"].v)
    gb = P.sb("sgb", [128, DI], F32)
    row_bcast(P, gb.v, norm_g, slice(None))
    dsk = P.sb("dsk", [128, 32], F32)
    row_bcast(P, dsk.v, d_skip, slice(None))
    Sp = [P.sb("Sp%d" % i, [128, 512], F32) for i in range(4)]
    Spb = [P.sb("Spb%d" % i, [128, 512], BF16) for i in range(4)]
    NL = 2
    xs = [P.sb("xs%d" % i, [128, DI], BF16) for i in range(NL)]
    dtt = [P.sb("dtt%d" % i, [128, 64], F32) for i in range(NL)]
    adt = [P.sb("adt%d" % i, [128, 64], F32) for i in range(NL)]
    btm = [P.sb("btm%d" % i, [128, 512], BF16) for i in range(NL)]
    bT = [P.sb("bT%d" % i, [128, 4, 128], BF16) for i in range(NL)]
    cT = [P.sb("cT%d" % i, [128, 4, 128], BF16) for i in range(NL)]
    yft = [P.sb("yft%d" % i, [128, DI], F32) for i in range(2)]
    zt = [P.sb("szt%d" % i, [128, DI], F32) for i in range(2)]
    ct = [P.sb("ct%d" % i, [128, 64], F32) for i in range(2)]
    ncs = [P.sb("ncs%d" % i, [128, 32], F32) for i in range(2)]
    ecs = [P.sb("ecs%d" % i, [128, 32], F32) for i in range(2)]
    dst_ = [P.sb("dst%d" % i, [128, 32], F32) for i in range(2)]
    etot = [P.sb("setot%d" % i, [128, 32], F32) for i in range(2)]
    xdt = [P.sb("xdt%d" % i, [128, DI], BF16) for i in range(2)]
    xdtd = [P.sb("xdtd%d" % i, [128, DI], BF16) for i in range(2)]
    CBm = [P.sb("CBm%d" % i, [128, 128], F32) for i in range(1)] * 2
    X = [P.sb("X%d" % i, [128, 8, 128], F32) for i in range(1)] * 2
    X2 = [P.sb("X2%d" % i, [128, 8, 128], F32) for i in range(1)] * 2
    Mh = [P.sb("Mh%d" % i, [128, 8, 128], BF16) for i in range(2)]
    tmp = [P.sb("stmp%d" % i, [128, 512], F32) for i in range(1)] * 2
    yt = [P.sb("syt%d" % i, [128, DI], F32) for i in range(2)]
    ss = P.sb("sss", [128, 4], F32)
    yb = [P.sb("syb%d" % i, [128, DI], BF16) for i in range(1)] * 2
    p_ct = P.ps("p_ct", [128, 64], F32)
    p_cb = [P.ps("p_cb%d" % i, [128, 128], F32) for i in range(2)]
    p_row = P.ps("p_row", [128, 8, 128], F32)
    p_y = P.ps("p_y", [128, 512], F32)
    p_st = P.ps("p_st", [128, 512], F32)
    p_yo = P.ps("p_yo", [128, 512], F32)
    NTD = NT if DBG_LEVEL >= 99 else DBG_NT
    tiles = []
    for d in range(2):
        order = list(range(NT)) if d == 0 else list(range(NT - 1, -1, -1))
        tiles += [(d, tt) for tt in order[:NTD]]
    NG = len(tiles) * 4

    def info(i):
        Ti = i // 4
        d, tt = tiles[Ti]
        return Ti, d, tt, i % 4

    def st_load(i):
        Ti, d, tt, g = info(i)
        if g != 0:
            return
        s = Ti % NL
        tsl = slice(tt * 128, (tt + 1) * 128)
        P.dma(xs[s].v, S["xsb"][tsl, :])
        P.dma(dtt[s].v, S["dt"][tsl, :])
        P.dma(adt[s].v, S["adt"][tsl, :])
        P.dma(btm[s].v, S["btm"][tsl, :])
        P.dma(bT[s].v, S["bT"][:, :, tsl].re("g n t -> n g t"))
        P.dma(cT[s].v, S["cT"][:, :, tsl].re("g n t -> n g t"))
        if d == 1:
            P.dma(zt[Ti % 2].v, S["zg"][tsl, :])

    def st_pro(i):
        Ti, d, tt, g = info(i)
        if g != 0:
            return
        s = Ti % NL
        u = Ti % 2
        hc = slice(d * 32, (d + 1) * 32)
        P.mm(p_ct[:, 0:32], M1[d].v, adt[s][:, hc])
        P.mm(p_ct[:, 32:64], C.ones_f.v, adt[s][:, hc])
        P.cp(ct[u].v, p_ct.v)
        P.ts(ncs[u].v, ct[u][:, 0:32], -1.0, ALU.mult)
        P.act(ecs[u].v, ct[u][:, 0:32], AF.Exp)
        P.act(etot[u].v, ct[u][:, 32:64], AF.Exp)
        P.tt(dst_[u].v, ct[u][:, 32:64], ct[u][:, 0:32], ALU.subtract)
        P.act(dst_[u].v, dst_[u].v, AF.Exp)
        P.tt(dst_[u].v, dst_[u].v, dtt[s][:, hc], ALU.mult)
        x3 = xs[s].v.re("p (h q) -> p h q", q=64)
        P.tt(xdt[u].v.re("p (h q) -> p h q", q=64), x3, dtt[s][:, hc].us(2).bc([128, 32, 64]), ALU.mult)
        P.tt(xdtd[u].v.re("p (h q) -> p h q", q=64), x3, dst_[u].v.us(2).bc([128, 32, 64]), ALU.mult, eng="pool")

    def st1(i):
        Ti, d, tt, g = info(i)
        s = Ti % NL
        k = i % 2
        hg = slice(d * 32 + g * 8, d * 32 + (g + 1) * 8)
        P.mm(p_cb[k].v, bT[s][:, g, :], cT[s][:, g, :])
        P.tt(X[k].v, M1[d].v.us(1).bc([128, 8, 128]), adt[s][:, hg].us(2).bc([128, 8, 128]), ALU.mult, eng="pool")
        for q in range(2):
            P.mm(p_row[:, q * 4:(q + 1) * 4, :], C.ones_f.v, X[k][:, q * 4:(q + 1) * 4, :])

    def st2(i):
        Ti, d, tt, g = info(i)
        u = Ti % 2
        k = i % 2
        hl = slice(g * 8, (g + 1) * 8)
        P.tt(CBm[k].v, p_cb[k].v, M1[d].v, ALU.mult)
        P.tt(X2[k].v, p_row.v, ncs[u][:, hl].us(2).bc([128, 8, 128]), ALU.add)
        P.act(X2[k].v, X2[k].v, AF.Relu, scale=-1.0)
        P.act(X2[k].v, X2[k].v, AF.Exp, scale=-1.0)
        P.tt(Mh[k].v, X2[k].v, CBm[k].v.us(1).bc([128, 8, 128]), ALU.mult)

    def st3(i):
        Ti, d, tt, g = info(i)
        s = Ti % NL
        u = Ti % 2
        k = i % 2
        hl = slice(g * 8, (g + 1) * 8)
        tsl = slice(tt * 128, (tt + 1) * 128)
        if tt == (0 if d == 0 else NT - 1):
            P.memset(Sp[g].v, 0.0, eng="dve")
            P.memset(Spb[g].v, 0.0, eng="dve")
        if g == 0 and d == 1:
            P.dma(yft[u].v, S["of"][tsl, :])
        for h in range(8):
            hh = g * 8 + h
            P.mm(p_y[:, h * 64:(h + 1) * 64], Mh[k][:, h, :], xdt[u][:, hh * 64:(hh + 1) * 64])
        P.mm(p_yo.v, cT[s][:, g, :], Spb[g].v)
        P.mm(p_st.v, btm[s][:, g * 128:(g + 1) * 128], xdtd[u][:, g * 512:(g + 1) * 512])
        P.tt(tmp[k].v.re("p (h q) -> p h q", q=64), p_yo.v.re("p (h q) -> p h q", q=64),
             ecs[u][:, hl].us(2).bc([128, 8, 64]), ALU.mult)
        P.tt(yt[u][:, g * 512:(g + 1) * 512], p_y.v, tmp[k].v, ALU.add)
        sg = Sp[g].v
        P.tt(sg.re("p (h q) -> p h q", q=64), sg.re("p (h q) -> p h q", q=64),
             etot[u][:, hl].us(2).bc([128, 8, 64]), ALU.mult, eng="pool")
        P.tt(sg, sg, p_st.v, ALU.add)
        P.cp(Spb[g].v, sg, eng="act")
        if g != 3:
            return
        if d == 0:
            P.dma(S["of"][tsl, :], yt[u].v)
            return
        x3 = xs[s].v.re("p (h q) -> p h q", q=64)
        sq = yft[u]
        P.tt(yt[u].v, yt[u].v, yft[u].v, ALU.add)
        P.tt(sq.v.re("p (h q) -> p h q", q=64), x3, dsk.v.us(2).bc([128, 32, 64]), ALU.mult, eng="pool")
        P.tt(yt[u].v, yt[u].v, sq.v, ALU.add)
        P.tt(yt[u].v, yt[u].v, zt[u].v, ALU.mult)
        for q in range(4):
            P.act(sq[:, q * 512:(q + 1) * 512], yt[u][:, q * 512:(q + 1) * 512], AF.Square, accum=ss[:, q:q + 1])
        P.act(ss.v, ss.v, AF.Sqrt, scale=1.0 / 512, bias=C.eps_col[:, 0:1])
        P.op("dve", lambda e: e.reciprocal(out=ss.v.ap, in_=ss.v.ap), [ss.v], [ss.v], cost=0.12)
        for q in range(4):
            P.act(yt[u][:, q * 512:(q + 1) * 512], yt[u][:, q * 512:(q + 1) * 512], AF.Copy, scale=ss[:, q:q + 1])
        P.tt(yb[u].v, yt[u].v, gb.v, ALU.mult)
        P.dma(S["y"][tsl, :], yb[u].v)

    pipeline(NG, [st_load, st_pro, st1, st2, st3], [2, 4, 5, 6, 7])
    P.end_phase()


def ssd_consts():
    r = np.arange(128)[:, None]
    t = np.arange(128)[None, :]
    return {"sd_mf": (r <= t).astype(np.float32), "sd_mb": (r >= t).astype(np.float32)}


def ssd_layer(P, C, li, w_in, conv_w, conv_b, dt_bias, a_log, d_skip, norm_g, w_out, ln_g, ln_b, last):
    S = C.scr
    NTAP = 5
    shifts = [-2, -1, 0, 1, 2]
    P.begin_phase()
    stage = P.sb("sstA", [128, 8, 512], F32)
    Wz = [P.sb("Wz%d" % i, [128, 8, 512], BF16) for i in range(2)]
    pz = [P.ps("pz%d" % i, [128, 512], F32) for i in range(2)]
    zo = [P.sb("szo%d" % i, [128, 512], F32) for i in range(2)]
    bo = [P.sb("sbo%d" % i, [128, 512], BF16) for i in range(2)]
    it = 0
    for c in range(0 if DBG_SKIPA else 4):
        W = Wz[c % 2]
        prep_w_chunk(P, [W], w_in, c * 512, 512, stage)
        for tt in range(NT):
            s = it % 2
            it += 1
            mm_proj_tm(P, pz[s].v, C, [W], [0], tt, 512)
            P.act(zo[s].v, pz[s].v, AF.Silu)
            P.dma(S["zg"][tt * 128:(tt + 1) * 128, c * 512:(c + 1) * 512], zo[s].v)
    if not DBG_SKIPA:
        wcol, bcol = conv_cols(P, C, conv_w, conv_b, NTAP, 24)
        stg = P.sb("sstg", [128, 8, 128], F32)
        Wb = [P.sb("sWb%d" % i, [128, 8, 128], BF16) for i in range(2)]
        raw = [P.sb("sraw%d" % i, [128, L + 4], BF16) for i in range(2)]
        for i in range(2):
            P.memset(raw[i][:, 0:2], 0.0, eng="dve")
            P.memset(raw[i][:, L + 2:L + 4], 0.0, eng="dve")
        crow = [P.sb("scrow%d" % i, [128, L], BF16) for i in range(2)]
        xT2 = [P.sb("sxT%d" % i, [128, 2, L], BF16) for i in range(2)]
        pp = [P.ps("spp%d" % i, [128, 512], F32) for i in range(3)]
        ptr = [P.ps("sptr%d" % i, [128, 256], BF16) for i in range(2)]
        so = [P.sb("sso%d" % i, [128, 256], BF16) for i in range(2)]
        it2 = 0
        for i in range(24):
            sb_ = i % 2
            P.dma(stg.v, w_in[:, 2048 + i * 128: 2048 + (i + 1) * 128].re("(k p) c -> p k c", p=128))
            P.cp(Wb[sb_].v, stg.v)
            for tc in range(L // 512):
                b = it2 % 3
                it2 += 1
                mm_proj_fm(P, pp[b].v, C, Wb[sb_], 0, tc * 512, 512)
                P.cp(raw[sb_][:, 2 + tc * 512: 2 + (tc + 1) * 512], pp[b].v, eng=("act" if tc % 2 else "dve"))
            if i < 20:
                grp = (i // 2) % 2
                dst = xT2[grp][:, i % 2, :]
            else:
                dst = crow[i % 2].v
            conv_row(P, dst, raw[sb_], wcol, bcol, i, NTAP, func=AF.Silu)
            if 16 <= i < 20:
                P.dma(S["bT"][i - 16, :, :], dst)
            if i >= 20:
                P.dma(S["cT"][i - 20, :, :], dst)
            if i < 20 and i % 2 == 1:
                i0 = i - 1
                for tt in range(NT):
                    q = tt % 2
                    for j in range(2):
                        P.tr(ptr[q][:, j * 128:(j + 1) * 128], xT2[grp][:, j, tt * 128:(tt + 1) * 128], C.ident.v)
                    P.cp(so[q].v, ptr[q].v, eng=("act" if tt % 2 else "dve"))
                    if i0 < 16:
                        P.dma(S["xsb"][tt * 128:(tt + 1) * 128, i0 * 128:(i0 + 2) * 128], so[q].v)
                    else:
                        P.dma(S["btm"][tt * 128:(tt + 1) * 128, (i0 - 16) * 128:(i0 - 14) * 128], so[q].v)
    Wd = Wz[0]
    prep_w_chunk(P, [Wd], w_in, 5120, 64, stage)
    dtb = P.sb("dtb", [128, 64], F32)
    arow = P.sb("arow", [128, 64], F32)
    P.dma(dtb.v, V(dt_bias, dt_bias.t.rearrange("a b -> (a b)").partition_broadcast(128)))
    P.dma(arow.v, V(a_log, a_log.t.rearrange("a b -> (a b)").partition_broadcast(128)))
    P.act(arow.v, arow.v, AF.Exp)
    dto = [P.sb("dto%d" % i, [128, 64], F32) for i in range(2)]
    ado = [P.sb("ado%d" % i, [128, 64], F32) for i in range(2)]
    for tt in range(NT):
        s = tt % 2
        mm_proj_tm(P, pz[s][:, 0:64], C, [Wd], [0], tt, 64)
        P.tt(dto[s].v, pz[s][:, 0:64], dtb.v, ALU.add)
        P.act(dto[s].v, dto[s].v, AF.Exp)
        P.act(dto[s].v, dto[s].v, AF.Ln, bias=1.0)
        P.stt(ado[s].v, dto[s].v, -1.0, arow.v, ALU.mult, ALU.mult)
        P.dma(S["dt"][tt * 128:(tt + 1) * 128, :], dto[s].v)
        P.dma(S["adt"][tt * 128:(tt + 1) * 128, :], ado[s].v)
    P.end_phase()
    ssd_scan_phase(P, C, norm_g, d_skip)
    outproj_phase(P, C, w_out, ln_g, ln_b, last)


NF = 8192
NK2 = 65


def hyena_consts():
    import ml_dtypes
    bf = ml_dtypes.bfloat16
    n = NF
    tp = 2.0 * np.pi
    p = np.arange(128)
    k2 = np.arange(65)
    k2s = np.arange(1, 64)
    F1 = np.zeros((128, 128))
    F1[:, :65] = np.cos(tp * ((p[:, None] * k2[None, :]) % 128) / 128)
    F1[:, 65:] = -np.sin(tp * ((p[:, None] * k2s[None, :]) % 128) / 128)
    pp = np.arange(64)
    m = np.full(65, 2.0)
    m[0] = 1.0
    m[64] = 1.0
    Finv = np.zeros((128, 64))
    Finv[:65, :] = (m[:, None] / n) * np.cos(tp * ((k2[:, None] * pp[None, :]) % 128) / 128)
    Finv[65:, :] = -(2.0 / n) * np.sin(tp * ((k2s[:, None] * pp[None, :]) % 128) / 128)
    a = np.arange(64)
    k1 = np.arange(64)
    TA = np.zeros((65, 128, 128))
    TB = np.zeros((65, 128, 128))
    TC = np.zeros((65, 128, 128))
    TKA = np.zeros((65, 128, 128))
    TKB = np.zeros((65, 128, 128))
    for q in range(65):
        th = tp * ((a[:, None] * (q + 128 * k1[None, :])) % n) / n
        Tr = np.cos(th)
        Ti = -np.sin(th)
        TA[q, :64, :64] = Tr
        TA[q, :64, 64:] = Ti
        TA[q, 64:, :64] = -Ti
        TA[q, 64:, 64:] = Tr
        TB[q, :64, :64] = -Ti
        TB[q, :64, 64:] = Tr
        TB[q, 64:, :64] = -Tr
        TB[q, 64:, 64:] = -Ti
        Ur = np.cos(th).T
        Ui = np.sin(th).T
        TKA[q, :64, :64] = Tr
        TKA[q, :64, 64:] = Tr
        TKA[q, 64:, :64] = -Ti
        TKA[q, 64:, 64:] = -Ti
        TKB[q, :64, :64] = Ti
        TKB[q, :64, 64:] = Ti
        TKB[q, 64:, :64] = Tr
        TKB[q, 64:, 64:] = Tr
        TC[q, :64, :64] = Ur
        TC[q, 64:, :64] = -Ui
        TC[q, :64, 64:] = Ui
        TC[q, 64:, 64:] = Ur
    j = (64 * p[None, :] + a[:, None]).reshape(-1)
    pos = np.where(j < L, j, n - j)
    pos[j == L] = 0
    tl = np.linspace(0.0, 1.0, L)
    t = tl[pos]
    w = tp * pos / L
    f = np.linspace(1e-4, 15.0, 16)
    z = np.concatenate([t[:, None], np.cos(f[None, :] * w[:, None]), -np.sin(f[None, :] * w[:, None])], axis=1)
    tdec = t.copy()
    tdec[j == L] = 1e4
    ntpos = -(tdec.reshape(64, 128).T)
    min_d = math.log(1e-2) / 0.3
    max_d = math.log(1e-2) / 1.5
    deltas = np.abs(np.linspace(min_d, max_d, DI))
    return {
        "hy_F1": F1.astype(np.float32).astype(bf),
        "hy_Finv": Finv.astype(np.float32).astype(bf),
        "hy_TA": np.ascontiguousarray(TA.transpose(1, 0, 2)).astype(np.float32).astype(bf),
        "hy_TB": np.ascontiguousarray(TB.transpose(1, 0, 2)).astype(np.float32).astype(bf),
        "hy_TC": np.ascontiguousarray(TC.transpose(1, 0, 2)).astype(np.float32).astype(bf),
        "hy_TKA": np.ascontiguousarray(TKA.transpose(1, 0, 2)).astype(np.float32).astype(bf),
        "hy_TKB": np.ascontiguousarray(TKB.transpose(1, 0, 2)).astype(np.float32).astype(bf),
        "hy_zT": np.ascontiguousarray(z.T).astype(np.float32),
        "hy_ntpos": ntpos.astype(np.float32),
        "hy_deltas": deltas.astype(np.float32),
    }


def sin_reduced(P, o, arg, tmpf, tmpi, tmp2):
    tp = 2.0 * math.pi
    P.ts(tmpf, arg, 1.0 / tp, ALU.mult, 64.0, ALU.add)
    P.cp(tmpi, tmpf)
    P.cp(tmpf, tmpi)
    P.ts(tmpf, tmpf, -64.0, ALU.add, -tp, ALU.mult)
    P.tt(tmpf, tmpf, arg, ALU.add)
    P.ts(tmp2, tmpf, math.pi, ALU.is_gt, -tp, ALU.mult)
    P.tt(tmpf, tmpf, tmp2, ALU.add)
    P.ts(tmpf, tmpf, 3.14159, ALU.min, -3.14159, ALU.max)
    P.act(o, tmpf, AF.Sin)


def conv_cols(P, C, conv_w, conv_b, ntaps, nch):
    wrow = P.sb("cw_row", [nch, ntaps, 128], F32)
    brow = P.sb("cb_row", [nch, 128], F32)
    P.dma(wrow.v, conv_w.v.re("j (k p) -> k j p", p=128))
    P.dma(brow.v, conv_b.v.re("(k p) -> k p", p=128))
    wcol = P.sb("cw_col", [128, ntaps, nch], F32)
    bcol = P.sb("cb_col", [128, nch], F32)
    pt = P.ps("cw_ps", [128, ntaps + 1, nch], F32)
    for j in range(ntaps):
        P.tr(pt[:, j, :], wrow[:, j, :], C.identf[0:nch, 0:nch])
    P.tr(pt[:, ntaps, :], brow.v, C.identf[0:nch, 0:nch])
    P.cp(wcol.v, pt[:, 0:ntaps, :])
    P.cp(bcol.v, pt[:, ntaps, :])
    return wcol, bcol


def conv_row(P, out, raw, wcol, bcol, k, ntaps, func=None):
    c = ntaps // 2
    H = L // 2
    for hf in range(2):
        o = out[:, hf * H:(hf + 1) * H]
        P.act(o, raw[:, c + hf * H: c + (hf + 1) * H], AF.Identity, scale=wcol[:, c, k:k + 1], bias=bcol[:, k:k + 1])
        js = [j for j in range(ntaps) if j != c]
        for n_, j in enumerate(js):
            P.stt(o, raw[:, j + hf * H: j + (hf + 1) * H], wcol[:, j, k:k + 1], o, ALU.mult, ALU.add)
        if func is not None:
            P.act(o, o, func)


def hyena_layer(P, C, li, prm, last):
    S = C.scr
    w_in, conv_w, conv_b = prm["w_in"], prm["conv_w"], prm["conv_b"]
    shifts = [-1, 0, 1]
    P.begin_phase()
    wcol, bcol = conv_cols(P, C, conv_w, conv_b, 3, 48)
    NG2 = 2
    stage = [P.sb("hstA%d" % i, [128, 8, 2, 128], F32) for i in range(1)] * 2
    Wb = [P.sb("hWb%d" % i, [128, 8, 512], BF16) for i in range(2)]
    raw = [[P.sb("hraw%d_%d" % (i, k), [128, L + 2], BF16) for k in range(3)] for i in range(2)]
    for i in range(2):
        for k in range(3):
            P.memset(raw[i][k][:, 0:1], 0.0, eng="dve")
            P.memset(raw[i][k][:, L + 1:L + 2], 0.0, eng="dve")
    cc = [P.sb("hcc%d" % k, [128, L], BF16) for k in range(2)]
    zs2 = [P.sb("hzs%d" % i, [128, L], BF16) for i in range(2)]
    gT = [P.sb("hgT%d" % i, [128, NG2, L], BF16) for i in range(2)]
    pp = [P.ps("hpp%d" % i, [128, 512], F32) for i in range(4)]
    ptr = [P.ps("hptr%d" % i, [128, 2, NG2 * 128], BF16) for i in range(2)]
    so = [P.sb("hso%d" % i, [128, 2, NG2 * 128], BF16) for i in range(2)]
    col0 = [2048, 4096, 0, 6144]
    cch = [16, 32, 0]
    it = 0
    for i in range(16):
        sb_ = i % 2
        zs = zs2[sb_]
        for hf in range(2):
            for k2_ in range(2):
                k = hf * 2 + k2_
                P.dma(stage[hf][:, :, k2_, :],
                      w_in[:, col0[k] + i * 128: col0[k] + (i + 1) * 128].re("(k p) c -> p k c", p=128))
            P.cp(Wb[sb_][:, :, hf * 256:(hf + 1) * 256], stage[hf].v.re("p k a c -> p k (a c)"),
                 eng=("pool" if hf else "dve"))
        for tc in range(L // 512):
            for k in range(4):
                b = it % 4
                it += 1
                mm_proj_fm(P, pp[b].v, C, Wb[sb_], k * 128, tc * 512, 512)
                if k < 3:
                    P.cp(raw[sb_][k][:, 1 + tc * 512: 1 + (tc + 1) * 512], pp[b].v, eng=("act" if (tc + k) % 2 else "dve"))
                else:
                    P.act(zs[:, tc * 512:(tc + 1) * 512], pp[b].v, AF.Silu)
        conv_row(P, cc[0], raw[sb_][0], wcol, bcol, cch[0] + i, 3)
        conv_row(P, cc[1], raw[sb_][1], wcol, bcol, cch[1] + i, 3)
        P.tt(gT[0][:, i % NG2, :], cc[0].v, cc[1].v, ALU.mult, eng="pool")
        conv_row(P, cc[0], raw[sb_][2], wcol, bcol, cch[2] + i, 3)
        P.tt(gT[1][:, i % NG2, :], cc[0].v, zs.v, ALU.mult, eng="pool")
        if i % NG2 == NG2 - 1:
            i0 = i - (NG2 - 1)
            for tt in range(NT):
                q = tt % 2
                for w_ in range(2):
                    for j in range(NG2):
                        P.tr(ptr[q][:, w_, j * 128:(j + 1) * 128], gT[w_][:, j, tt * 128:(tt + 1) * 128], C.ident.v)
                P.cp(so[q].v, ptr[q].v, eng=("act" if tt % 2 else "dve"))
                P.dma(S["gbf"][tt * 128:(tt + 1) * 128, i0 * 128:(i0 + NG2) * 128], so[q][:, 0, :])
                P.dma(S["gatebf"][tt * 128:(tt + 1) * 128, i0 * 128:(i0 + NG2) * 128], so[q][:, 1, :])
    P.end_phase()
    P.begin_phase()
    F1 = P.sb("F1", [128, 128], BF16)
    P.dma(F1.v, C.consts["hy_F1"].v)
    ntp = P.sb("ntp", [128, 64], F32)
    P.dma(ntp.v, C.consts["hy_ntpos"].v)
    dl = P.sb("dl", [128, DI], F32)
    row_bcast(P, dl.v, C.consts["hy_deltas"], slice(None))
    w1 = P.sb("fw1", [33, 64], F32)
    w2 = P.sb("fw2", [64, 64], F32)
    w3 = P.sb("fw3", [64, 64], F32)
    P.dma(w1.v, prm["filt_w1"].v)
    P.dma(w2.v, prm["filt_w2"].v)
    P.dma(w3.v, prm["filt_w3"].v)
    cols = P.sb("fcols", [64, 4], F32)
    for i, nme in enumerate(["filt_freq", "filt_b1", "filt_b2", "filt_b3"]):
        P.dma(cols[:, i:i + 1], prm[nme].v.us(1))
    fb = P.sb("ffb", [64, 3], F32)
    for i in range(3):
        P.tt(fb[:, i:i + 1], cols[:, 0:1], cols[:, i + 1:i + 2], ALU.mult)
    wo_f = P.sb("fwo_f", [64, 4096], F32)
    wo = P.sb("fwo", [64, 4096], BF16)
    P.dma(wo_f.v, prm["filt_w_out"].v)
    P.cp(wo.v, wo_f.v, eng="pool")
    h3T = P.sb("h3T", [64, NF], BF16)
    MW = 1024
    zc = [P.sb("zc%d" % i, [33, MW], F32) for i in range(2)]
    arg = [P.sb("farg%d" % i, [64, MW], F32) for i in range(2)]
    tmf = [P.sb("ftmf%d" % i, [64, MW], F32) for i in range(2)]
    tmi = [P.sb("ftmi%d" % i, [64, MW], I32) for i in range(2)]
    tm2 = [P.sb("ftm2%d" % i, [64, MW], F32) for i in range(2)]
    hh = [P.sb("fhh%d" % i, [64, MW], F32) for i in range(2)]
    pm = [P.ps("fpm%d" % i, [64, MW], F32) for i in range(2)]
    it = 0
    for ch in range(NF // MW):
        s = ch % 2
        P.dma(zc[s].v, C.consts["hy_zT"][:, ch * MW:(ch + 1) * MW])
        src = zc[s].v
        for lyr, wl in enumerate([w1, w2, w3]):
            b = it % 2
            it += 1
            for q in range(MW // 512):
                P.mm(pm[b][:, q * 512:(q + 1) * 512], wl.v, src[:, q * 512:(q + 1) * 512])
            P.act(arg[b].v, pm[b].v, AF.Identity, scale=cols[:, 0:1], bias=fb[:, lyr:lyr + 1])
            dst = hh[lyr % 2].v if lyr < 2 else h3T[:, ch * MW:(ch + 1) * MW]
            sin_reduced(P, dst, arg[b].v, tmf[b].v, tmi[b].v, tm2[b].v)
            src = dst
    pk = [P.ps("fpk%d" % i, [128, 512], F32) for i in range(2)]
    pz = [P.ps("fpz%d" % i, [128, 512], F32) for i in range(2)]
    dec = [P.sb("fdec%d" % i, [128, DI], F32) for i in range(2)]
    ka = [P.sb("fka%d" % i, [128, 512], BF16) for i in range(2)]
    zt = [P.sb("fzt%d" % i, [128, DI], BF16) for i in range(2)]
    it = 0
    for a in range(64):
        sa = a % 2
        P.act(dec[sa].v, dl.v, AF.Exp, scale=ntp[:, a:a + 1])
        for c in range(4):
            s = it % 2
            it += 1
            P.mm(pk[s][0:64, :], h3T[:, a * 128: a * 128 + 64], wo[:, c * 512:(c + 1) * 512])
            P.mm(pk[s][64:128, :], h3T[:, a * 128 + 64: a * 128 + 128], wo[:, 2048 + c * 512: 2048 + (c + 1) * 512])
            P.tt(ka[s].v, pk[s].v, dec[sa][:, c * 512:(c + 1) * 512], ALU.mult)
            P.mm(pz[s].v, F1.v, ka[s].v)
            P.cp(zt[sa][:, c * 512:(c + 1) * 512], pz[s].v, eng="act")
        P.dma(S["Zk"][:, a, :], zt[sa].v)
    P.end_phase()
    P.begin_phase()
    F1 = P.sb("F1c", [128, 128], BF16)
    P.dma(F1.v, C.consts["hy_F1"].v)
    gb_ = [P.sb("cgb%d" % i, [64, DI], BF16) for i in range(3)]
    zt = [P.sb("czt%d" % i, [128, DI], BF16) for i in range(2)]
    pz = [P.ps("cpz%d" % i, [128, 512], F32) for i in range(4)]
    gview = S["gbf"].v.re("(p a) c -> a p c", a=64)
    for a in range(64):
        s = a % 2
        P.dma(gb_[a % 3].v, gview[a])
        for c in range(4):
            b = c
            P.mm(pz[b].v, F1[0:64, :], gb_[a % 3][:, c * 512:(c + 1) * 512])
            P.cp(zt[s][:, c * 512:(c + 1) * 512], pz[b].v, eng=("act" if c % 2 else "dve"))
        P.dma(S["Zs"][:, a, :], zt[s].v)
    P.end_phase()
    P.begin_phase()
    TA = P.sb("dTA", [128, NK2, 128], BF16)
    TB = P.sb("dTB", [128, NK2, 128], BF16)
    TC = P.sb("dTC", [128, NK2, 128], BF16)
    P.dma(TA.v, C.consts["hy_TA"].v)
    P.dma(TB.v, C.consts["hy_TB"].v)
    P.dma(TC.v, C.consts["hy_TC"].v)
    TKA = P.sb("dTKA", [128, NK2, 128], BF16)
    TKB = P.sb("dTKB", [128, NK2, 128], BF16)
    P.dma(TKA.v, C.consts["hy_TKA"].v)
    P.dma(TKB.v, C.consts["hy_TKB"].v)
    zin = [P.sb("dzin%d" % i, [128, DI], BF16) for i in range(2)]
    kin = [P.sb("dkin%d" % i, [128, DI], BF16) for i in range(2)]
    KA = [P.sb("dKA%d" % i, [128, 512], F32) for i in range(2)]
    KB = [P.sb("dKB%d" % i, [128, 512], F32) for i in range(2)]
    t1 = [P.sb("dt1_%d" % i, [128, 512], F32) for i in range(2)]
    t2 = [P.sb("dt2_%d" % i, [128, 512], F32) for i in range(2)]
    yb = [P.sb("dyb%d" % i, [128, 512], BF16) for i in range(2)]
    vo = [P.sb("dvo%d" % i, [128, DI], BF16) for i in range(2)]
    pA = [P.ps("dpA%d" % i, [128, 512], F32) for i in range(1)] * 2
    pB = [P.ps("dpB%d" % i, [128, 512], F32) for i in range(1)] * 2
    pKA = [P.ps("dpKA%d" % i, [128, 512], F32) for i in range(2)]
    pKB = [P.ps("dpKB%d" % i, [128, 512], F32) for i in range(2)]
    pV = [P.ps("dpV%d" % i, [128, 512], F32) for i in range(2)]
    it = 0
    for q in range(NK2):
        s = q % 2
        P.dma(zin[s][0:64, :], S["Zs"][q, :, :])
        P.dma(kin[s][0:64, :], S["Zk"][q, :, :])
        if 1 <= q <= 63:
            P.dma(zin[s][64:128, :], S["Zs"][64 + q, :, :])
            P.dma(kin[s][64:128, :], S["Zk"][64 + q, :, :])
        else:
            P.memset(zin[s][64:128, :], 0.0)
            P.memset(kin[s][64:128, :], 0.0)
        for c in range(4):
            b = it % 2
            it += 1
            cs_ = slice(c * 512, (c + 1) * 512)
            P.mm(pKA[b].v, TKA[:, q, :], kin[s][:, cs_])
            P.mm(pKB[b].v, TKB[:, q, :], kin[s][:, cs_])
            P.cp(KA[b].v, pKA[b].v, eng="act")
            P.cp(KB[b].v, pKB[b].v, eng="act")
            P.mm(pA[b].v, TA[:, q, :], zin[s][:, cs_])
            P.mm(pB[b].v, TB[:, q, :], zin[s][:, cs_])
            P.tt(t1[b].v, pA[b].v, KA[b].v, ALU.mult)
            P.tt(t2[b].v, pB[b].v, KB[b].v, ALU.mult)
            P.tt(yb[b].v, t1[b].v, t2[b].v, ALU.add, eng="pool")
            P.mm(pV[b].v, TC[:, q, :], yb[b].v)
            P.cp(vo[s][:, cs_], pV[b].v, eng=("act" if c % 2 else "dve"))
        P.dma(S["Zv"][q, :, :], vo[s][0:64, :])
        if 1 <= q <= 63:
            P.dma(S["Zv"][64 + q, :, :], vo[s][64:128, :])
    P.end_phase()
    P.begin_phase()
    Fi = P.sb("eFi", [128, 64], BF16)
    P.dma(Fi.v, C.consts["hy_Finv"].v)
    skb = P.sb("eskb", [128, DI], F32)
    row_bcast(P, skb.v, prm["skip"], slice(None))
    vin = [P.sb("evin%d" % i, [128, 2, DI], BF16) for i in range(2)]
    gt = [P.sb("egt%d" % i, [128, DI], BF16) for i in range(2)]
    gs = [P.sb("egs%d" % i, [128, DI], F32) for i in range(2)]
    zg = [P.sb("ezg%d" % i, [128, DI], BF16) for i in range(2)]
    yo = [P.sb("eyo%d" % i, [128, DI], BF16) for i in range(2)]
    py = [P.ps("epy%d" % i, [128, 512], F32) for i in range(2)]
    gview = S["gbf"].v.re("(p a) c -> a p c", a=64)
    zview = S["gatebf"].v.re("(p a) c -> a p c", a=64)
    yview = S["y"].v.re("(p a) c -> a p c", a=64)
    it = 0
    for a2 in range(32):
        s = a2 % 2
        for h in range(2):
            a = a2 * 2 + h
            P.dma(vin[s][:, h, :], S["Zv"][:, a, :])
            P.dma(gt[s][h * 64:(h + 1) * 64, :], gview[a])
            P.dma(zg[s][h * 64:(h + 1) * 64, :], zview[a])
        P.tt(gs[s].v, gt[s].v, skb.v, ALU.mult)
        for c in range(4):
            b = it % 2
            it += 1
            cs_ = slice(c * 512, (c + 1) * 512)
            for h in range(2):
                P.mm(py[b][h * 64:(h + 1) * 64, :], Fi.v, vin[s][:, h, cs_])
            P.tt(gs[s][:, cs_], py[b].v, gs[s][:, cs_], ALU.add)
            P.tt(yo[s][:, cs_], gs[s][:, cs_], zg[s][:, cs_], ALU.mult, eng="pool")
        for h in range(2):
            P.dma(yview[a2 * 2 + h], yo[s][h * 64:(h + 1) * 64, :])
    P.end_phase()
    outproj_phase(P, C, prm["w_out"], prm["ln_g"], prm["ln_b"], last)


_PROG_CACHE = {}


def kernel(**inputs):
    layers = [0, 1, 2, 3]
    if "prog" not in _PROG_CACHE:
        _PROG_CACHE["prog"] = build_program(layers)
    P, C = _PROG_CACHE["prog"]
    inputs = {k: np.asarray(v) for k, v in inputs.items()}
    nb = inputs["x"].shape[0]
    in_maps = [in_map_for(inputs, b, layers) for b in range(nb)]
    res = run_bass_kernel_spmd(P.nc, in_maps, core_ids=list(range(nb)))
    out = np.stack([np.asarray(res.results[b]["out"]) for b in range(nb)], axis=0)
    return out.astype(np.float32)
```

```python
import math
from contextlib import ExitStack
import numpy as np
import concourse.bass as bass
import concourse.mybir as mybir
from concourse.bass_utils import run_bass_kernel_spmd

F32 = mybir.dt.float32
BF16 = mybir.dt.bfloat16
I32 = mybir.dt.int32
AF = mybir.ActivationFunctionType
ALU = mybir.AluOpType
AX = mybir.AxisListType

D = 1024
L = 4096
DI = 2048
DEPTH = 4
ALPHA = (2.0 * DEPTH) ** 0.25
LN_EPS = 1e-5
PAD = 2
NT = L // 128
SAME_ENG_SYNC = True
NDS = 48
SCHED = True
SCHED_WINDOW = 1024
DEBUG_SCR = False
DBG_STOP = ""
DBG_LEVEL = 99
DBG_SKIPA = False
DBG_DIRS = 2
DBG_SUB = 99
DBG_NT = 2
DBG_PH = 99
_phc = [0]


class Buf:
    __slots__ = ("t", "w", "r", "name", "wn", "rn")

    def __init__(self, t, name=""):
        self.t = t
        self.w = None
        self.r = {}
        self.name = name
        self.wn = None
        self.rn = []

    def __getitem__(self, idx):
        return V(self, self.t[idx])

    @property
    def v(self):
        return V(self, self.t)


class V:
    __slots__ = ("b", "ap")

    def __init__(self, b, ap):
        self.b = b
        self.ap = ap

    def __getitem__(self, idx):
        return V(self.b, self.ap[idx])

    def re(self, s, **kw):
        return V(self.b, self.ap.rearrange(s, **kw))

    def bc(self, shape):
        return V(self.b, self.ap.broadcast_to(shape))

    def tb(self, shape):
        return V(self.b, self.ap.to_broadcast(shape))

    def pb(self, n):
        return V(self.b, self.ap.partition_broadcast(n))

    def cast(self, dt):
        return V(self.b, self.ap.bitcast(dt))

    def us(self, ax):
        return V(self.b, self.ap.unsqueeze(ax))


class StopBuild(Exception):
    pass


class Eng:
    def __init__(self, name, eng, sem):
        self.name = name
        self.eng = eng
        self.sem = sem
        self.n = 0
        self.seen = {}


class Prog:
    def __init__(self):
        nc = bass.Bass("TRN2", target_bir_lowering=False)
        self.nc = nc
        self.es = ExitStack()
        self.E = {}
        for name, e in [("pe", nc.tensor), ("dve", nc.vector), ("act", nc.scalar),
                        ("pool", nc.gpsimd), ("sp", nc.sync)]:
            sem = self.es.enter_context(nc.semaphore("s_" + name))
            self.E[name] = Eng(name, e, sem)
        self.dsem = []
        for i in range(NDS):
            self.dsem.append([self.es.enter_context(nc.semaphore("d%d" % i)), 0])
        self.dnext = 0
        self.phase = None
        self.uid = 0
        self.ninst = 0
        self.deferred = False
        self.nodes = []
        self.touched = []
        self.sim_time = 0.0
        self.synced = False
        self.phase_stats = []

    def begin_phase(self):
        self.phase = ExitStack()
        self.deferred = SCHED
        if SCHED and not self.synced:
            self.barrier()
            self.synced = True

    def end_phase(self):
        if self.deferred:
            self._flush()
            self.deferred = False
        self.barrier()
        self.phase.close()
        self.phase = None
        _phc[0] += 1
        if _phc[0] >= DBG_PH:
            raise StopBuild()

    def _nm(self, name):
        self.uid += 1
        return "%s_%d" % (name, self.uid)

    def sb(self, name, shape, dt, perm=False):
        st = self.es if perm else self.phase
        t = st.enter_context(self.nc.sbuf_tensor(self._nm(name), list(shape), dt))
        return Buf(t[:], name)

    def ps(self, name, shape, dt=F32, perm=False):
        st = self.es if perm else self.phase
        esz = 4 if dt == F32 else 2
        n = 1
        for x in shape[1:]:
            n *= x
        per_bank = 2048 // esz
        npad = ((n + per_bank - 1) // per_bank) * per_bank
        t = st.enter_context(self.nc.psum_tensor(self._nm(name), [shape[0], npad], dt))
        ap = t[:][:, 0:n]
        if len(shape) == 3:
            ap = ap.rearrange("p (a b) -> p a b", a=shape[1])
        return Buf(ap, name)

    def dram(self, name, shape, dt, kind="Internal"):
        if DEBUG_SCR and kind == "Internal" and name.startswith("s_"):
            kind = "ExternalOutput"
        t = self.nc.dram_tensor(name, list(shape), dt, kind=kind)
        return Buf(t.ap(), name)

    def _wait(self, E, toks):
        need = {}
        for (s, v) in toks:
            if s is E.sem and (E.name == "pe" or not SAME_ENG_SYNC):
                continue
            k = id(s)
            if k not in need or need[k][1] < v:
                need[k] = (s, v)
        for k, (s, v) in need.items():
            if E.seen.get(k, 0) >= v:
                continue
            E.eng.wait_ge(s, v)
            self.ninst += 1
            E.seen[k] = v

    @staticmethod
    def _deps(outs, ins):
        toks = []
        for v in ins:
            if v.b.w is not None:
                toks.append(v.b.w)
        for v in outs:
            if v.b.w is not None:
                toks.append(v.b.w)
            toks.extend(v.b.r.values())
        return toks

    @staticmethod
    def _mark(tok, outs, ins):
        k = id(tok[0])
        for v in ins:
            r = v.b.r
            if k not in r or r[k][1] < tok[1]:
                r[k] = tok
        for v in outs:
            v.b.w = tok
            v.b.r = {}

    def op(self, ename, fn, outs, ins, cost=0.3):
        if self.deferred:
            self._record(ename, "op", fn, outs, ins, cost, cost)
            return
        E = self.E[ename]
        self._wait(E, self._deps(outs, ins))
        E.n += 1
        fn(E.eng).then_inc(E.sem, 1)
        self.ninst += 1
        self._mark((E.sem, E.n), outs, ins)

    def dma(self, out, in_, q="sp", slow=False):
        if self.deferred:
            nbytes = 1
            for x in out.ap.shape:
                nbytes *= x
            nbytes *= (4 if out.ap.dtype in (F32, I32) else 2)
            self._record(q, "dma", (out, in_, slow), [out], [in_], 0.12, 2.0 + nbytes / 120e3)
            return
        self._emit_dma(self.E[q], out, in_, slow, self._deps([out], [in_]), True)

    def _emit_dma(self, E, out, in_, slow, toks, mark):
        i = self.dnext
        self.dnext = (i + 1) % NDS
        sem, cnt = self.dsem[i]
        if cnt > 0:
            toks = list(toks) + [(sem, cnt)]
        self._wait(E, toks)
        if slow:
            E.eng.dma_start(out=out.ap, in_=in_.ap, allow_slow_non_contiguous=True).then_inc(sem, 16)
        else:
            E.eng.dma_start(out=out.ap, in_=in_.ap).then_inc(sem, 16)
        self.ninst += 1
        self.dsem[i][1] = cnt + 16
        tok = (sem, cnt + 16)
        if mark:
            self._mark(tok, [out], [in_])
        return tok

    def _record(self, ename, kind, payload, outs, ins, busy, lat):
        nid = len(self.nodes)
        deps = set()
        for v in ins:
            b = v.b
            if b.wn is not None:
                deps.add(b.wn)
        for v in outs:
            b = v.b
            if b.wn is not None:
                deps.add(b.wn)
            deps.update(b.rn)
        for v in ins:
            b = v.b
            b.rn.append(nid)
            self.touched.append(b)
        for v in outs:
            b = v.b
            b.wn = nid
            b.rn = []
            self.touched.append(b)
        deps.discard(nid)
        self.nodes.append([ename, kind, payload, busy, lat, deps])

    def _flush(self):
        nodes = self.nodes
        n = len(nodes)
        if n == 0:
            return
        succ = [[] for _ in range(n)]
        ndep = [0] * n
        for i, nd in enumerate(nodes):
            ndep[i] = len(nd[5])
            for d in nd[5]:
                succ[d].append(i)
        ready = [0.0] * n
        fin = [0.0] * n
        queues = {}
        for i, nd in enumerate(nodes):
            queues.setdefault(nd[0], []).append(i)
        head = {e: 0 for e in queues}
        done = [False] * n
        T = {e: 0.0 for e in queues}
        order = {e: [] for e in queues}
        W = SCHED_WINDOW
        left = n
        while left:
            best = None
            for e, q in queues.items():
                h = head[e]
                while h < len(q) and done[q[h]]:
                    h += 1
                head[e] = h
                cnt = 0
                j = h
                te = T[e]
                while j < len(q) and cnt < W:
                    i = q[j]
                    j += 1
                    if done[i]:
                        continue
                    cnt += 1
                    if ndep[i]:
                        continue
                    st = ready[i] if ready[i] > te else te
                    if best is None or st < best[0] - 1e-9 or (st < best[0] + 1e-9 and i < best[1]):
                        best = (st, i, e)
                    if st <= te:
                        break
            st, i, e = best
            nd = nodes[i]
            T[e] = st + nd[3]
            f = st + nd[4]
            fin[i] = f
            done[i] = True
            left -= 1
            order[e].append(i)
            for sidx in succ[i]:
                ndep[sidx] -= 1
                lat = f + (0.12 if nodes[sidx][0] == e else 0.3)
                if lat > ready[sidx]:
                    ready[sidx] = lat
        tok = [None] * n
        for e, lst in order.items():
            E = self.E[e]
            k = E.n
            for i in lst:
                if nodes[i][1] == "op":
                    k += 1
                    tok[i] = (E.sem, k)
        dma_engs = [e for e in order if any(nodes[i][1] == "dma" for i in order[e])]
        pending = {e: list(order[e]) for e in order}
        ptr = {e: 0 for e in order}
        progress = True
        while progress:
            progress = False
            for e in list(pending.keys()):
                lst = pending[e]
                E = self.E[e]
                while ptr[e] < len(lst):
                    i = lst[ptr[e]]
                    nd = nodes[i]
                    if any(tok[d] is None for d in nd[5]):
                        break
                    toks = [tok[d] for d in nd[5]]
                    if nd[1] == "dma":
                        out, in_, slow = nd[2]
                        tok[i] = self._emit_dma(E, out, in_, slow, toks, False)
                    else:
                        self._wait(E, toks)
                        E.n += 1
                        assert tok[i] == (E.sem, E.n)
                        nd[2](E.eng).then_inc(E.sem, 1)
                        self.ninst += 1
                    ptr[e] += 1
                    progress = True
                if ptr[e] >= len(lst):
                    del pending[e]
        assert not pending, "emission deadlock"
        self.sim_time += max(T.values())
        busy = {}
        for nd in nodes:
            busy[nd[0]] = busy.get(nd[0], 0.0) + nd[3]
        self.phase_stats.append((max(T.values()), {k: round(v) for k, v in busy.items()}, n))
        self.nodes = []
        for b in self.touched:
            b.wn = None
            b.rn = []
        self.touched = []

    def barrier(self):
        toks = [(e.sem, e.n) for e in self.E.values() if e.n > 0]
        toks += [(s, c) for s, c in self.dsem if c > 0]
        for E in self.E.values():
            self._wait(E, [t for t in toks if t[0] is not E.sem])

    @staticmethod
    def _fs(v):
        n = 1
        for x in v.ap.shape[1:]:
            n *= x
        return n

    def _cost(self, eng, v, cast=False):
        n = self._fs(v)
        if eng == "act":
            return 0.2 + n / 1400.0
        if eng == "pool":
            return 0.3 + n * (0.0035 if cast else 0.0022)
        return 0.1 + n / 960.0

    def mm(self, o, lhsT, rhs, start=True, stop=True):
        n = self._fs(rhs) * (4 if lhsT.ap.dtype == F32 else 1)
        self.op("pe", lambda e: e.matmul(o.ap, lhsT=lhsT.ap, rhs=rhs.ap, start=start, stop=stop),
                [o], [lhsT, rhs], cost=0.07 + n / 2200.0)

    def tr(self, o, a, ident):
        self.op("pe", lambda e: e.transpose(o.ap, a.ap, ident.ap), [o], [a, ident], cost=0.15)

    def act(self, o, a, func, scale=1.0, bias=0.0, accum=None, eng="act"):
        ins = [a]
        outs = [o]
        kw = {}
        if isinstance(scale, V):
            ins.append(scale)
            kw["scale"] = scale.ap
        else:
            kw["scale"] = float(scale)
        if isinstance(bias, V):
            ins.append(bias)
            kw["bias"] = bias.ap
        else:
            kw["bias"] = float(bias)
        if accum is not None:
            outs.append(accum)
            kw["accum_out"] = accum.ap
        self.op("act", lambda e: e.activation(out=o.ap, in_=a.ap, func=func, **kw), outs, ins,
                cost=self._cost("act", a))

    def tt(self, o, a, b, op, eng="dve"):
        self.op(eng, lambda e: e.tensor_tensor(out=o.ap, in0=a.ap, in1=b.ap, op=op), [o], [a, b],
                cost=self._cost(eng, o))

    def ts(self, o, a, s1, op0, s2=None, op1=None, eng="dve", accum=None):
        ins = [a]
        outs = [o]
        a1 = s1.ap if isinstance(s1, V) else float(s1)
        if isinstance(s1, V):
            ins.append(s1)
        a2 = None
        if s2 is not None:
            a2 = s2.ap if isinstance(s2, V) else float(s2)
            if isinstance(s2, V):
                ins.append(s2)
        kw = {}
        if op1 is not None:
            kw["op1"] = op1
        if accum is not None:
            kw["accum_out"] = accum.ap
            outs.append(accum)
        self.op(eng, lambda e: e.tensor_scalar(out=o.ap, in0=a.ap, scalar1=a1, scalar2=a2, op0=op0, **kw),
                outs, ins, cost=self._cost(eng, o))

    def stt(self, o, a, s, b, op0, op1, eng="dve"):
        ins = [a, b]
        sv = s.ap if isinstance(s, V) else float(s)
        if isinstance(s, V):
            ins.append(s)
        self.op(eng, lambda e: e.scalar_tensor_tensor(out=o.ap, in0=a.ap, scalar=sv, in1=b.ap, op0=op0, op1=op1),
                [o], ins, cost=self._cost(eng, o))

    def cp(self, o, a, eng="dve"):
        if eng == "act":
            self.op("act", lambda e: e.copy(out=o.ap, in_=a.ap), [o], [a], cost=self._cost("act", o))
        else:
            self.op(eng, lambda e: e.tensor_copy(out=o.ap, in_=a.ap), [o], [a], cost=self._cost(eng, o, cast=True))

    def memset(self, o, val, eng="pool"):
        self.op(eng, lambda e: e.memset(o.ap, val), [o], [], cost=self._cost(eng, o))


def _ident_bf16():
    import ml_dtypes
    return np.eye(128, dtype=np.float32).astype(ml_dtypes.bfloat16)


class Ctx:
    pass


def setup_common(P, C):
    C.hT = P.sb("hT", [128, D // 128, L + 2 * PAD], BF16, perm=True)
    C.ident = P.sb("ident", [128, 128], BF16, perm=True)
    C.identf = P.sb("identf", [128, 128], F32, perm=True)
    C.ones_bf = P.sb("ones_bf", [128, 128], BF16, perm=True)
    C.ones_f = P.sb("ones_f", [128, 128], F32, perm=True)
    P.dma(C.ident.v, C.d_ident.v)
    P.dma(C.identf.v, C.d_identf.v)
    P.memset(C.ones_bf.v, 1.0)
    P.memset(C.ones_f.v, 1.0)
    P.memset(C.hT.v, 0.0)
    C.eps_col = P.sb("eps_col", [128, 1], F32, perm=True)
    P.memset(C.eps_col.v, LN_EPS)


def load_x_to_hT(P, C):
    P.begin_phase()
    xt = [P.sb("xt%d" % i, [128, D], F32) for i in range(2)]
    xb = [P.sb("xb%d" % i, [128, D], BF16) for i in range(2)]
    pt = [P.ps("ptr%d" % i, [128, D], BF16) for i in range(2)]
    for tt in range(NT):
        s = tt % 2
        P.dma(xt[s].v, C.x[tt * 128:(tt + 1) * 128, :])
        P.dma(C.h_tiles[tt].v, xt[s].v)
        P.cp(xb[s].v, xt[s].v, eng="act")
        for kc in range(D // 128):
            P.tr(pt[s][:, kc * 128:(kc + 1) * 128], xb[s][:, kc * 128:(kc + 1) * 128], C.ident.v)
        P.cp(C.hT[:, :, PAD + tt * 128: PAD + (tt + 1) * 128],
             pt[s].v.re("p (k t) -> p k t", k=D // 128))
    P.end_phase()


def row_bcast(P, dst, src_buf, sl):
    P.dma(dst, V(src_buf, src_buf.t[sl].partition_broadcast(128)))


def prep_w_chunk(P, Wt, w_d, c0, n, stage, convw_d=None, cc0=0, cwb=None, ntaps=0):
    nk = w_d.t.shape[0] // 128
    P.dma(stage[:, :nk, :n], w_d[:, c0:c0 + n].re("(k p) c -> p k c", p=128))
    if ntaps == 0:
        h = nk // 2
        P.cp(Wt[0][:, :h, :n], stage[:, :h, :n], eng="dve")
        P.cp(Wt[0][:, h:nk, :n], stage[:, h:nk, :n], eng="act")
        return
    row_bcast(P, cwb[:, :ntaps, :n], convw_d, (slice(None), slice(cc0, cc0 + n)))
    for j in range(ntaps):
        P.tt(Wt[j][:, :nk, :n], stage[:, :nk, :n],
             cwb[:, j:j + 1, :n].bc([128, nk, n]), ALU.mult, eng=("pool" if j % 2 else "dve"))


def mm_proj_tm(P, ps, C, Wts, shifts, tt, n, bias=None):
    nk = D // 128
    tot = len(Wts) * nk + (1 if bias is not None else 0)
    i = 0
    for Wt, sh in zip(Wts, shifts):
        for kc in range(nk):
            t0 = PAD + tt * 128 + sh
            P.mm(ps, C.hT[:, kc, t0:t0 + 128], Wt[:, kc, :n], start=(i == 0), stop=(i == tot - 1))
            i += 1
    if bias is not None:
        P.mm(ps, C.ones_bf[0:1, :], bias, start=False, stop=True)


class OutProj:
    def __init__(self, P, C, wout_d, lng_d, lnb_d, last=False):
        self.P, self.C, self.last = P, C, last
        self.w = P.sb("wout", [128, DI // 128, D], BF16)
        st = [P.sb("wost%d" % i, [128, 4, D], F32) for i in range(2)]
        for q in range(4):
            P.dma(st[q % 2].v, wout_d[q * 512:(q + 1) * 512, :].re("(k p) c -> p k c", p=128))
            P.cp(self.w[:, q * 4:(q + 1) * 4, :], st[q % 2].v, eng=("act" if q % 2 else "dve"))
        self.g = P.sb("lng", [128, D], F32)
        self.b = P.sb("lnb", [128, D], F32)
        row_bcast(P, self.g.v, lng_d, slice(None))
        row_bcast(P, self.b.v, lnb_d, slice(None))
        nb = 2
        self.yT = [P.sb("yT%d" % i, [128, DI // 128, 128], BF16) for i in range(nb)]
        self.ho = [P.sb("ho%d" % i, [128, D], F32) for i in range(nb)]
        self.r = [P.sb("r%d" % i, [128, D], F32) for i in range(nb)]
        self.hn = [P.sb("hn%d" % i, [128, D], F32) for i in range(nb)]
        self.hb = [P.sb("hb%d" % i, [128, D], BF16) for i in range(nb)]
        self.st6 = [P.sb("st6%d" % i, [128, 2, 6], F32) for i in range(nb)]
        self.mv = [P.sb("mv%d" % i, [128, 2], F32) for i in range(nb)]
        self.rstd = [P.sb("rstd%d" % i, [128, 1], F32) for i in range(nb)]
        self.pT = [P.ps("opT%d" % i, [128, DI // 2], BF16) for i in range(2)]
        self.po = [P.ps("opo%d" % i, [128, D], F32) for i in range(2)]
        self.pH = [P.ps("opH%d" % i, [128, D], BF16) for i in range(2)]

    def tile(self, tt, y):
        P, C = self.P, self.C
        k = tt % 2
        yT, ho, r, hn, hb, st6, mv, rstd = (self.yT[k], self.ho[k], self.r[k], self.hn[k], self.hb[k],
                                              self.st6[k], self.mv[k], self.rstd[k])
        po = self.po[k]
        pH = self.pH[k]
        h8 = DI // 256
        for hf in range(2):
            pT = self.pT[hf]
            for c in range(h8):
                cg = hf * h8 + c
                P.tr(pT[:, c * 128:(c + 1) * 128], y[:, cg * 128:(cg + 1) * 128], C.ident.v)
            P.cp(yT[:, hf * h8:(hf + 1) * h8, :], pT.v.re("p (c t) -> p c t", c=h8), eng=("act" if hf else "dve"))
        P.dma(ho.v, C.h_tiles[tt].v)
        for dh in range(2):
            for c in range(DI // 128):
                P.mm(po[:, dh * 512:(dh + 1) * 512], yT[:, c, :],
                     self.w[:, c, dh * 512:(dh + 1) * 512], start=(c == 0), stop=(c == DI // 128 - 1))
        P.stt(r.v, ho.v, ALPHA, po.v, ALU.mult, ALU.add)
        for q in range(2):
            P.op("dve", lambda e, q=q: e.bn_stats(out=st6[:, q, :].ap, in_=r[:, q * 512:(q + 1) * 512].ap),
                 [st6.v], [r.v], cost=0.65)
        P.op("dve", lambda e: e.bn_aggr(out=mv.v.ap, in_=st6.v.re("p a b -> p (a b)").ap),
             [mv.v], [st6.v], cost=0.15)
        P.act(rstd.v, mv[:, 1:2], AF.Sqrt, scale=1.0, bias=C.eps_col[:, 0:1])
        P.op("dve", lambda e: e.reciprocal(out=rstd.v.ap, in_=rstd.v.ap), [rstd.v], [rstd.v], cost=0.12)
        P.ts(hn.v, r.v, mv[:, 0:1], ALU.subtract, rstd[:, 0:1], ALU.mult)
        P.tt(hn.v, hn.v, self.g.v, ALU.mult, eng="pool")
        P.tt(hn.v, hn.v, self.b.v, ALU.add)
        if self.last:
            P.dma(C.out[tt * 128:(tt + 1) * 128, :], hn.v)
            return
        P.dma(C.h_tiles[tt].v, hn.v)
        P.cp(hb.v, hn.v, eng="act")
        for kc in range(D // 128):
            P.tr(pH[:, kc * 128:(kc + 1) * 128], hb[:, kc * 128:(kc + 1) * 128], C.ident.v)
        P.cp(C.hT[:, :, PAD + tt * 128: PAD + (tt + 1) * 128],
             pH.v.re("p (k t) -> p k t", k=D // 128))


def mm_proj_fm(P, ps, C, Wt, c0, t0, n, shift=0, start=True, stop=True):
    nk = D // 128
    for kc in range(nk):
        a = PAD + t0 + shift
        P.mm(ps, Wt[:, kc, c0:c0 + 128], C.hT[:, kc, a:a + n],
             start=(start and kc == 0), stop=(stop and kc == nk - 1))


HG_H = 16


def hgrn2_consts():
    s = np.arange(128)[:, None]
    t = np.arange(128)[None, :]
    same = (s // 64) == (t // 64)
    mf = ((s <= t) & same).astype(np.float32)
    mb = ((s >= t) & same).astype(np.float32)
    rm = np.ones((128, 512), np.float32)
    rm[:, ::64] = 0.0
    return {"hg_mf": mf, "hg_mb": mb, "hg_rm": rm}


def hgrn2_layer(P, C, li, w_in, lb_raw, norm_g, w_out, ln_g, ln_b, last):
    S = C.scr
    qg = [S["qgf"], S["qgb"]]
    kg = [S["kgf"], S["kgb"]]
    P.begin_phase()
    lbr = P.sb("lbr", [32, 4, 128], F32)
    P.dma(lbr.v, lb_raw.v.re("l (g k) -> g l k", k=128))
    lbT = P.sb("lbT", [128, 4, 32], F32)
    pl = P.ps("pl", [128, 4, 32], F32)
    for l in range(4):
        P.tr(pl[:, l, :], lbr[:, l, :], C.identf[0:32, 0:32])
    P.act(lbT.v, pl.v, AF.Exp)
    den = P.sb("den", [128, 32], F32)
    num = P.sb("num", [128, 32], F32)
    P.tt(den.v, lbT[:, 0, :], lbT[:, 1, :], ALU.add)
    P.tt(den.v, den.v, lbT[:, 2, :], ALU.add)
    P.tt(den.v, den.v, lbT[:, 3, :], ALU.add)
    P.memset(num.v, 0.0, eng="dve")
    for j in range(1, li + 1):
        P.tt(num.v, num.v, lbT[:, j, :], ALU.add)
    P.op("dve", lambda e: e.reciprocal(out=den.v.ap, in_=den.v.ap), [den.v], [den.v])
    lb = P.sb("lb", [128, 32], F32)
    ln1mlb = P.sb("ln1mlb", [128, 32], F32)
    P.tt(lb.v, num.v, den.v, ALU.mult)
    P.ts(ln1mlb.v, lb.v, -1.0, ALU.mult, 1.0, ALU.add)
    P.act(ln1mlb.v, ln1mlb.v, AF.Ln)
    etot = P.sb("etot", [128, 2, HG_H, 64], F32)
    rm = P.sb("rm", [128, 512], F32)
    P.dma(rm.v, C.consts["hg_rm"].v)
    stage = P.sb("stA", [128, 8, 384], F32)
    Wt = [P.sb("WtA%d" % i, [128, 8, 384], BF16) for i in range(2)]
    pq = [P.ps("pq%d" % i, [128, 512], F32) for i in range(2)]
    pf = [[P.ps("pf%d_%d" % (d, i), [128, 512], F32) for i in range(2)] for d in range(2)]
    nb = 2
    e_ = [P.sb("e%d" % i, [128, 512], F32) for i in range(nb)]
    l1 = [P.sb("l1%d" % i, [128, 512], F32) for i in range(nb)]
    l2 = [P.sb("l2%d" % i, [128, 512], F32) for i in range(nb)]
    lf = [P.sb("lf%d" % i, [128, 512], F32) for i in range(nb)]
    pp = [P.sb("pp%d" % i, [128, 512], F32) for i in range(nb)]
    lk = [P.sb("lk%d" % i, [128, 512], F32) for i in range(nb)]
    eg = [P.sb("eg%d" % i, [128, 512], F32) for i in range(nb)]
    qo = [P.sb("qo%d" % i, [128, 512], BF16) for i in range(nb)]
    ko = [P.sb("ko%d" % i, [128, 512], BF16) for i in range(nb)]
    tot8 = [P.sb("tot8%d" % i, [128, 8], F32) for i in range(nb)]
    it = 0
    for hd in range(HG_H):
        W = Wt[hd % 2]
        for j, c0 in enumerate([hd * 128, 2048 + hd * 128, 4096 + hd * 128]):
            P.dma(stage[:, :, j * 128:(j + 1) * 128], w_in[:, c0:c0 + 128].re("(k p) c -> p k c", p=128))
        P.cp(W.v, stage.v, eng="pool")
        for tc in range(L // 512):
            t0 = tc * 512
            s = tc % 2
            mm_proj_fm(P, pq[s].v, C, W, 0, t0, 512)
            for d in range(2):
                mm_proj_fm(P, pf[d][s].v, C, W, 128 * (1 + d), t0, 512)
            for d in range(2):
                b = it % nb
                it += 1
                col = d * HG_H + hd
                ff = pf[d][s]
                P.act(e_[b].v, ff.v, AF.Exp, scale=-1.0)
                P.act(l1[b].v, e_[b].v, AF.Ln, scale=lb[:, col:col + 1], bias=1.0)
                P.act(l2[b].v, e_[b].v, AF.Ln, scale=1.0, bias=1.0)
                P.tt(lf[b].v, l1[b].v, l2[b].v, ALU.subtract)
                P.op("dve", lambda e, b=b: e.tensor_tensor_scan(out=pp[b].v.ap, data0=rm.v.ap, data1=lf[b].v.ap,
                                                               initial=0.0, op0=ALU.mult, op1=ALU.add),
                     [pp[b].v], [rm.v, lf[b].v])
                P.stt(lk[b].v, ff.v, -1.0, l2[b].v, ALU.mult, ALU.subtract)
                if d == 1:
                    P.tt(lf[b].v, lf[b].v, pp[b].v, ALU.subtract, eng="pool")
                    P.cp(tot8[b].v, pp[b].v.re("p (c t) -> p c t", t=64)[:, :, 63], eng="pool")
                    P.tt(pp[b].v.re("p (c t) -> p c t", t=64), lf[b].v.re("p (c t) -> p c t", t=64),
                         tot8[b].v.us(2).bc([128, 8, 64]), ALU.add)
                G = pp[b]
                P.act(eg[b].v, G.v, AF.Exp)
                ev = eg[b].v.re("p (c t) -> p c t", t=64)
                P.cp(etot[:, d, hd, tc * 8:(tc + 1) * 8], (ev[:, :, 63] if d == 0 else ev[:, :, 0]), eng="pool")
                P.tt(qo[b].v, pq[s].v, eg[b].v, ALU.mult)
                P.tt(lk[b].v, lk[b].v, G.v, ALU.subtract, eng="pool")
                P.act(ko[b].v, lk[b].v, AF.Exp, bias=ln1mlb[:, col:col + 1])
                P.dma(qg[d][hd, :, t0:t0 + 512], qo[b].v)
                P.dma(kg[d][hd, :, t0:t0 + 512], ko[b].v)
    P.dma(S["etot"].v, etot.v.re("p a h c -> p (a h c)"))
    P.end_phase()
    P.begin_phase()
    stageB = [P.sb("stB%d" % i, [128, 8, 512], F32) for i in range(2)]
    WtB = [P.sb("WtB%d" % i, [128, 8, 512], BF16) for i in range(2)]
    pv = [P.ps("pv%d" % i, [128, 512], F32) for i in range(4)]
    vo = [P.sb("vo%d" % i, [128, 512], BF16) for i in range(4)]
    zo = [P.sb("zo%d" % i, [128, 512], F32) for i in range(4)]
    it = 0
    for isz in range(2):
        for c in range(4):
            W = WtB[c % 2]
            prep_w_chunk(P, [W], w_in, 6144 + isz * 2048 + c * 512, 512, stageB[c % 2])
            for tt in range(NT):
                s = it % 4
                it += 1
                mm_proj_tm(P, pv[s].v, C, [W], [0], tt, 512)
                if isz == 0:
                    P.cp(vo[s].v, pv[s].v, eng=("act" if it % 2 else "dve"))
                    P.dma(S["vtm"][tt * 128:(tt + 1) * 128, c * 512:(c + 1) * 512], vo[s].v)
                else:
                    P.act(zo[s].v, pv[s].v, AF.Silu)
                    P.dma(S["zg"][tt * 128:(tt + 1) * 128, c * 512:(c + 1) * 512], zo[s].v)
    P.end_phase()
    P.begin_phase()
    etot = P.sb("etotC", [128, 2, HG_H, 64], F32)
    P.dma(etot.v.re("p a h c -> p (a h c)"), S["etot"].v)
    mask = [P.sb("mk%d" % i, [128, 128], F32) for i in range(2)]
    P.dma(mask[0].v, C.consts["hg_mf"].v)
    P.dma(mask[1].v, C.consts["hg_mb"].v)
    gb = P.sb("ngb", [128, DI], F32)
    row_bcast(P, gb.v, norm_g, slice(None))
    GH = 4
    Sf = [P.sb("Sf%d" % i, [128, GH, 128], F32) for i in range(HG_H // GH)]
    Sb = [P.sb("Sb%d" % i, [128, GH, 128], BF16) for i in range(HG_H // GH)]
    qT = [P.sb("qT%d" % i, [128, HG_H, 128], BF16) for i in range(2)]
    kT = [P.sb("kT%d" % i, [128, HG_H, 128], BF16) for i in range(2)]
    vt = [P.sb("vt%d" % i, [128, DI], BF16) for i in range(2)]
    ot = [P.sb("ot%d" % i, [128, DI], F32) for i in range(2)]
    of_t = [P.sb("oft%d" % i, [128, DI], F32) for i in range(2)]
    zt = [P.sb("zt%d" % i, [128, DI], F32) for i in range(2)]
    sq = P.sb("sq", [128, DI], F32)
    ss = P.sb("ss", [128, HG_H], F32)
    yb = [P.sb("yb%d" % i, [128, DI], BF16) for i in range(2)]
    pA = [P.ps("pA%d" % i, [128, GH, 128], F32) for i in range(2)]
    pT = [P.ps("pT%d" % i, [128, GH, 128], BF16) for i in range(2)]
    pO = [P.ps("pO%d" % i, [128, GH, 128], F32) for i in range(2)]
    pU = [P.ps("pU%d" % i, [128, GH, 128], F32) for i in range(2)]
    am = [P.sb("am%d" % i, [128, GH, 128], BF16) for i in range(2)]
    ktm = [P.sb("ktm%d" % i, [128, GH, 128], BF16) for i in range(2)]
    gi = 0
    for d in range(2):
        for g in range(HG_H // GH):
            P.memset(Sf[g].v, 0.0, eng="dve")
            P.memset(Sb[g].v, 0.0, eng="dve")
        order = list(range(NT)) if d == 0 else list(range(NT - 1, -1, -1))
        for n_, tt in enumerate(order):
            s = n_ % 2
            tsl = slice(tt * 128, (tt + 1) * 128)
            P.dma(qT[s].v, qg[d][:, :, tsl].re("h k t -> k h t"))
            P.dma(kT[s].v, kg[d][:, :, tsl].re("h k t -> k h t"))
            P.dma(vt[s].v, S["vtm"][tsl, :])
            if d == 1:
                P.dma(of_t[s].v, S["of"][tsl, :])
                P.dma(zt[s].v, S["zg"][tsl, :])
            chunks = [0, 1] if d == 0 else [1, 0]
            for g in range(HG_H // GH):
                p = gi % 2
                gi += 1
                hs = range(g * GH, (g + 1) * GH)
                for i, hd in enumerate(hs):
                    P.mm(pA[p][:, i, :], kT[s][:, hd, :], qT[s][:, hd, :])
                    P.tr(pT[p][:, i, :], kT[s][:, hd, :], C.ident.v)
                P.tt(am[p].v, pA[p].v, mask[d].v.us(1).bc([128, GH, 128]), ALU.mult)
                P.cp(ktm[p].v, pT[p].v, eng="act")
                for ci, ch in enumerate(chunks):
                    rs = slice(ch * 64, (ch + 1) * 64)
                    cidx = tt * 2 + ch
                    for i, hd in enumerate(hs):
                        vs = vt[s][:, hd * 128:(hd + 1) * 128]
                        P.mm(pO[p][rs, i, :], am[p][:, i, rs], vs, start=True, stop=False)
                        P.mm(pO[p][rs, i, :], qT[s][:, hd, rs], Sb[g][:, i, :], start=False, stop=True)
                        P.mm(pU[p][:, i, :], ktm[p][rs, i, :], vt[s][rs, hd * 128:(hd + 1) * 128])
                    e_bc = etot[:, d, g * GH:(g + 1) * GH, cidx].us(2).bc([128, GH, 128])
                    P.tt(Sf[g].v, pU[p].v, Sf[g].v, ALU.add)
                    P.tt(Sf[g].v, Sf[g].v, e_bc, ALU.mult)
                    P.cp(Sb[g].v, Sf[g].v, eng="act")
                osl = ot[s][:, g * GH * 128:(g + 1) * GH * 128]
                if d == 0:
                    P.cp(osl, pO[p].v.re("p g v -> p (g v)"), eng="act")
                else:
                    P.tt(osl, pO[p].v.re("p g v -> p (g v)"), of_t[s][:, g * GH * 128:(g + 1) * GH * 128], ALU.add)
            if d == 0:
                P.dma(S["of"][tsl, :], ot[s].v)
            else:
                o3 = ot[s].v.re("p (h v) -> p h v", v=128)
                P.tt(sq.v, ot[s].v, ot[s].v, ALU.mult, eng="pool")
                P.op("dve", lambda e: e.tensor_reduce(out=ss.v.ap, in_=sq.v.re("p (h v) -> p h v", v=128).ap,
                                                     axis=AX.X, op=ALU.add), [ss.v], [sq.v])
                P.act(ss.v, ss.v, AF.Sqrt, scale=1.0 / 128, bias=C.eps_col[:, 0:1])
                P.op("dve", lambda e: e.reciprocal(out=ss.v.ap, in_=ss.v.ap), [ss.v], [ss.v])
                P.tt(o3, o3, ss.v.us(2).bc([128, HG_H, 128]), ALU.mult)
                P.tt(zt[s].v, zt[s].v, gb.v, ALU.mult, eng="pool")
                P.tt(yb[s].v, ot[s].v, zt[s].v, ALU.mult)
                P.dma(S["y"][tsl, :], yb[s].v)
    P.end_phase()
    outproj_phase(P, C, w_out, ln_g, ln_b, last)


def outproj_phase(P, C, w_out, ln_g, ln_b, last):
    P.begin_phase()
    op = OutProj(P, C, w_out, ln_g, ln_b, last=last)
    yb = [P.sb("ypb%d" % i, [128, DI], BF16) for i in range(2)]
    for tt in range(NT):
        P.dma(yb[tt % 2].v, C.scr["y"][tt * 128:(tt + 1) * 128, :])
        op.tile(tt, yb[tt % 2].v)
    P.end_phase()


LAYER_KIND = ["hyena", "ssd", "hgrn2", "hyena"]
PARAMS = {
    "hyena": ["w_in", "conv_w", "conv_b", "filt_w1", "filt_b1", "filt_w2", "filt_b2", "filt_w3", "filt_b3",
              "filt_freq", "filt_w_out", "skip", "w_out", "ln_g", "ln_b"],
    "ssd": ["w_in", "conv_w", "conv_b", "dt_bias", "a_log", "d_skip", "norm_g", "w_out", "ln_g", "ln_b"],
    "hgrn2": ["w_in", "norm_g", "w_out", "ln_g", "ln_b"],
}
SHAPES = {
    "hyena": {"w_in": [D, 8192], "conv_w": [3, 6144], "conv_b": [6144], "filt_w1": [33, 64], "filt_b1": [64],
              "filt_w2": [64, 64], "filt_b2": [64], "filt_w3": [64, 64], "filt_b3": [64], "filt_freq": [64],
              "filt_w_out": [64, 4096], "skip": [DI], "w_out": [DI, D], "ln_g": [D], "ln_b": [D]},
    "ssd": {"w_in": [D, 5184], "conv_w": [5, 3072], "conv_b": [3072], "dt_bias": [2, 32], "a_log": [2, 32],
            "d_skip": [32], "norm_g": [DI], "w_out": [DI, D], "ln_g": [D], "ln_b": [D]},
    "hgrn2": {"w_in": [D, 10240], "norm_g": [DI], "w_out": [DI, D], "ln_g": [D], "ln_b": [D]},
}


def all_consts():
    c = {"ident": _ident_bf16(), "identf": np.eye(128, dtype=np.float32)}
    c.update(hgrn2_consts())
    c.update(ssd_consts())
    c.update(hyena_consts())
    return c


def build_program(layers):
    P = Prog()
    C = Ctx()
    C.x = P.dram("x", [L, D], F32, kind="ExternalInput")
    C.out = P.dram("out", [L, D], F32, kind="ExternalOutput")
    C.h = P.dram("h_scr", [L, D], F32)
    C.h_tiles = [Buf(C.h.t[tt * 128:(tt + 1) * 128, :], "h%d" % tt) for tt in range(NT)]
    C.consts = {}
    cst = all_consts()
    for k, v in cst.items():
        dt = BF16 if v.dtype != np.float32 else F32
        C.consts[k] = P.dram("c_" + k, list(v.shape), dt, kind="ExternalInput")
    C.d_ident = C.consts["ident"]
    C.d_identf = C.consts["identf"]
    C.lbraw = P.dram("hgrn_lower_bounds", [4, 4096], F32, kind="ExternalInput")
    C.prm = {}
    for li in layers:
        kind = LAYER_KIND[li]
        for nme in PARAMS[kind]:
            key = "l%d_%s" % (li, nme)
            C.prm[key] = P.dram(key, SHAPES[kind][nme], F32, kind="ExternalInput")
    S = {}
    for nme in ["qgf", "qgb", "kgf", "kgb"]:
        S[nme] = P.dram("s_" + nme, [HG_H, 128, L], BF16)
    S["vtm"] = P.dram("s_vtm", [L, DI], BF16)
    S["zg"] = P.dram("s_zg", [L, DI], F32)
    S["of"] = P.dram("s_of", [L, DI], F32)
    S["y"] = P.dram("s_y", [L, DI], BF16)
    S["xs"] = P.dram("s_xs", [L, DI], F32)
    S["xsb"] = P.dram("s_xsb", [L, DI], BF16)
    S["btm"] = P.dram("s_btm", [L, 512], BF16)
    S["bT"] = P.dram("s_bT", [4, 128, L], BF16)
    S["cT"] = P.dram("s_cT", [4, 128, L], BF16)
    S["dt"] = P.dram("s_dt", [L, 64], F32)
    S["Zs"] = P.dram("s_Zs", [128, 64, DI], BF16)
    S["Zv"] = P.dram("s_Zv", [128, 64, DI], BF16)
    S["Zk"] = P.dram("s_Zk", [128, 64, DI], BF16)
    S["gbf"] = P.dram("s_gbf", [L, DI], BF16)
    S["etot"] = P.dram("s_etot", [128, 2 * HG_H * 64], F32)
    S["gatebf"] = P.dram("s_gatebf", [L, DI], BF16)
    S["adt"] = P.dram("s_adt", [L, 64], F32)
    C.scr = S
    setup_common(P, C)
    load_x_to_hT(P, C)
    for n_, li in enumerate(layers):
      try:
          kind = LAYER_KIND[li]
          last = (n_ == len(layers) - 1)
          pr = lambda nme: C.prm["l%d_%s" % (li, nme)]
          if kind == "hgrn2":
              hgrn2_layer(P, C, li, pr("w_in"), C.lbraw, pr("norm_g"), pr("w_out"), pr("ln_g"), pr("ln_b"), last)
          elif kind == "ssd":
              ssd_layer(P, C, li, pr("w_in"), pr("conv_w"), pr("conv_b"), pr("dt_bias"), pr("a_log"), pr("d_skip"),
                        pr("norm_g"), pr("w_out"), pr("ln_g"), pr("ln_b"), last)
          else:
              hyena_layer(P, C, li, {nme: pr(nme) for nme in PARAMS["hyena"]}, last)
      except StopBuild:
        break
    P.barrier()
    return P, C


def in_map_for(inputs, b, layers):
    m = {"x": np.ascontiguousarray(inputs["x"][b]), "hgrn_lower_bounds": inputs["hgrn_lower_bounds"]}
    for k, v in all_consts().items():
        m["c_" + k] = v
    for li in layers:
        for nme in PARAMS[LAYER_KIND[li]]:
            key = "l%d_%s" % (li, nme)
            m[key] = inputs[key]
    return m


def pipeline(n, stages, offs=None):
    ns = len(stages)
    if offs is None:
        offs = list(range(ns))
    for step in range(n + max(offs)):
        for k in range(ns - 1, -1, -1):
            i = step - offs[k]
            if 0 <= i < n:
                stages[k](i)


def ssd_scan_phase(P, C, norm_g, d_skip):
    S = C.scr
    P.begin_phase()
    M1 = [P.sb("M1%d" % i, [128, 128], F32) for i in range(2)]
    P.dma(M1[0].v, C.consts["sd_mf"].v)
    P.dma(M1[1].v, C.consts["sd_mb"].v)
    gb = P.sb("sgb", [128, DI], F32)
    row_bcast(P, gb.v, norm_g, slice(None))
    dsk = P.sb("dsk", [128, 32], F32)
    row_bcast(P, dsk.v, d_skip, slice(None))
    Sp = [P.sb("Sp%d" % i, [128, 512], F32) for i in range(4)]
    Spb = [P.sb("Spb%d" % i, [128, 512], BF16) for i in range(4)]
    NL = 2
    xs = [P.sb("xs%d" % i, [128, DI], BF16) for i in range(NL)]
    dtt = [P.sb("dtt%d" % i, [128, 64], F32) for i in range(NL)]
    adt = [P.sb("adt%d" % i, [128, 64], F32) for i in range(NL)]
    btm = [P.sb("btm%d" % i, [128, 512], BF16) for i in range(NL)]
    bT = [P.sb("bT%d" % i, [128, 4, 128], BF16) for i in range(NL)]
    cT = [P.sb("cT%d" % i, [128, 4, 128], BF16) for i in range(NL)]
    yft = [P.sb("yft%d" % i, [128, DI], F32) for i in range(2)]
    zt = [P.sb("szt%d" % i, [128, DI], F32) for i in range(2)]
    ct = [P.sb("ct%d" % i, [128, 64], F32) for i in range(2)]
    ncs = [P.sb("ncs%d" % i, [128, 32], F32) for i in range(2)]
    ecs = [P.sb("ecs%d" % i, [128, 32], F32) for i in range(2)]
    dst_ = [P.sb("dst%d" % i, [128, 32], F32) for i in range(2)]
    etot = [P.sb("setot%d" % i, [128, 32], F32) for i in range(2)]
    xdt = [P.sb("xdt%d" % i, [128, DI], BF16) for i in range(2)]
    xdtd = [P.sb("xdtd%d" % i, [128, DI], BF16) for i in range(2)]
    CBm = [P.sb("CBm%d" % i, [128, 128], F32) for i in range(1)] * 2
    X = [P.sb("X%d" % i, [128, 8, 128], F32) for i in range(1)] * 2
    X2 = [P.sb("X2%d" % i, [128, 8, 128], F32) for i in range(1)] * 2
    Mh = [P.sb("Mh%d" % i, [128, 8, 128], BF16) for i in range(2)]
    tmp = [P.sb("stmp%d" % i, [128, 512], F32) for i in range(1)] * 2
    yt = [P.sb("syt%d" % i, [128, DI], F32) for i in range(2)]
    ss = P.sb("sss", [128, 4], F32)
    yb = [P.sb("syb%d" % i, [128, DI], BF16) for i in range(1)] * 2
    p_ct = P.ps("p_ct", [128, 64], F32)
    p_cb = [P.ps("p_cb%d" % i, [128, 128], F32) for i in range(2)]
    p_row = P.ps("p_row", [128, 8, 128], F32)
    p_y = P.ps("p_y", [128, 512], F32)
    p_st = P.ps("p_st", [128, 512], F32)
    p_yo = P.ps("p_yo", [128, 512], F32)
    NTD = NT if DBG_LEVEL >= 99 else DBG_NT
    tiles = []
    for d in range(2):
        order = list(range(NT)) if d == 0 else list(range(NT - 1, -1, -1))
        tiles += [(d, tt) for tt in order[:NTD]]
    NG = len(tiles) * 4

    def info(i):
        Ti = i // 4
        d, tt = tiles[Ti]
        return Ti, d, tt, i % 4

    def st_load(i):
        Ti, d, tt, g = info(i)
        if g != 0:
            return
        s = Ti % NL
        tsl = slice(tt * 128, (tt + 1) * 128)
        P.dma(xs[s].v, S["xsb"][tsl, :])
        P.dma(dtt[s].v, S["dt"][tsl, :])
        P.dma(adt[s].v, S["adt"][tsl, :])
        P.dma(btm[s].v, S["btm"][tsl, :])
        P.dma(bT[s].v, S["bT"][:, :, tsl].re("g n t -> n g t"))
        P.dma(cT[s].v, S["cT"][:, :, tsl].re("g n t -> n g t"))
        if d == 1:
            P.dma(zt[Ti % 2].v, S["zg"][tsl, :])

    def st_pro(i):
        Ti, d, tt, g = info(i)
        if g != 0:
            return
        s = Ti % NL
        u = Ti % 2
        hc = slice(d * 32, (d + 1) * 32)
        P.mm(p_ct[:, 0:32], M1[d].v, adt[s][:, hc])
        P.mm(p_ct[:, 32:64], C.ones_f.v, adt[s][:, hc])
        P.cp(ct[u].v, p_ct.v)
        P.ts(ncs[u].v, ct[u][:, 0:32], -1.0, ALU.mult)
        P.act(ecs[u].v, ct[u][:, 0:32], AF.Exp)
        P.act(etot[u].v, ct[u][:, 32:64], AF.Exp)
        P.tt(dst_[u].v, ct[u][:, 32:64], ct[u][:, 0:32], ALU.subtract)
        P.act(dst_[u].v, dst_[u].v, AF.Exp)
        P.tt(dst_[u].v, dst_[u].v, dtt[s][:, hc], ALU.mult)
        x3 = xs[s].v.re("p (h q) -> p h q", q=64)
        P.tt(xdt[u].v.re("p (h q) -> p h q", q=64), x3, dtt[s][:, hc].us(2).bc([128, 32, 64]), ALU.mult)
        P.tt(xdtd[u].v.re("p (h q) -> p h q", q=64), x3, dst_[u].v.us(2).bc([128, 32, 64]), ALU.mult, eng="pool")

    def st1(i):
        Ti, d, tt, g = info(i)
        s = Ti % NL
        k = i % 2
        hg = slice(d * 32 + g * 8, d * 32 + (g + 1) * 8)
        P.mm(p_cb[k].v, bT[s][:, g, :], cT[s][:, g, :])
        P.tt(X[k].v, M1[d].v.us(1).bc([128, 8, 128]), adt[s][:, hg].us(2).bc([128, 8, 128]), ALU.mult, eng="pool")
        for q in range(2):
            P.mm(p_row[:, q * 4:(q + 1) * 4, :], C.ones_f.v, X[k][:, q * 4:(q + 1) * 4, :])

    def st2(i):
        Ti, d, tt, g = info(i)
        u = Ti % 2
        k = i % 2
        hl = slice(g * 8, (g + 1) * 8)
        P.tt(CBm[k].v, p_cb[k].v, M1[d].v, ALU.mult)
        P.tt(X2[k].v, p_row.v, ncs[u][:, hl].us(2).bc([128, 8, 128]), ALU.add)
        P.act(X2[k].v, X2[k].v, AF.Relu, scale=-1.0)
        P.act(X2[k].v, X2[k].v, AF.Exp, scale=-1.0)
        P.tt(Mh[k].v, X2[k].v, CBm[k].v.us(1).bc([128, 8, 128]), ALU.mult)

    def st3(i):
        Ti, d, tt, g = info(i)
        s = Ti % NL
        u = Ti % 2
        k = i % 2
        hl = slice(g * 8, (g + 1) * 8)
        tsl = slice(tt * 128, (tt + 1) * 128)
        if tt == (0 if d == 0 else NT - 1):
            P.memset(Sp[g].v, 0.0, eng="dve")
            P.memset(Spb[g].v, 0.0, eng="dve")
        if g == 0 and d == 1:
            P.dma(yft[u].v, S["of"][tsl, :])
        for h in range(8):
            hh = g * 8 + h
            P.mm(p_y[:, h * 64:(h + 1) * 64], Mh[k][:, h, :], xdt[u][:, hh * 64:(hh + 1) * 64])
        P.mm(p_yo.v, cT[s][:, g, :], Spb[g].v)
        P.mm(p_st.v, btm[s][:, g * 128:(g + 1) * 128], xdtd[u][:, g * 512:(g + 1) * 512])
        P.tt(tmp[k].v.re("p (h q) -> p h q", q=64), p_yo.v.re("p (h q) -> p h q", q=64),
             ecs[u][:, hl].us(2).bc([128, 8, 64]), ALU.mult)
        P.tt(yt[u][:, g * 512:(g + 1) * 512], p_y.v, tmp[k].v, ALU.add)
        sg = Sp[g].v
        P.tt(sg.re("p (h q) -> p h q", q=64), sg.re("p (h q) -> p h q", q=64),
             etot[u][:, hl].us(2).bc([128, 8, 64]), ALU.mult, eng="pool")
        P.tt(sg, sg, p_st.v, ALU.add)
        P.cp(Spb[g].v, sg, eng="act")
        if g != 3:
            return
        if d == 0:
            P.dma(S["of"][tsl, :], yt[u].v)
            return
        x3 = xs[s].v.re("p (h q) -> p h q", q=64)
        sq = yft[u]
        P.tt(yt[u].v, yt[u].v, yft[u].v, ALU.add)
        P.tt(sq.v.re("p (h q) -> p h q", q=64), x3, dsk.v.us(2).bc([128, 32, 64]), ALU.mult, eng="pool")
        P.tt(yt[u].v, yt[u].v, sq.v, ALU.add)
        P.tt(yt[u].v, yt[u].v, zt[u].v, ALU.mult)
        for q in range(4):
            P.act(sq[:, q * 512:(q + 1) * 512], yt[u][:, q * 512:(q + 1) * 512], AF.Square, accum=ss[:, q:q + 1])
        P.act(ss.v, ss.v, AF.Sqrt, scale=1.0 / 512, bias=C.eps_col[:, 0:1])
        P.op("dve", lambda e: e.reciprocal(out=ss.v.ap, in_=ss.v.ap), [ss.v], [ss.v], cost=0.12)
        for q in range(4):
            P.act(yt[u][:, q * 512:(q + 1) * 512], yt[u][:, q * 512:(q + 1) * 512], AF.Copy, scale=ss[:, q:q + 1])
        P.tt(yb[u].v, yt[u].v, gb.v, ALU.mult)
        P.dma(S["y"][tsl, :], yb[u].v)

    pipeline(NG, [st_load, st_pro, st1, st2, st3], [2, 4, 5, 6, 7])
    P.end_phase()


def ssd_consts():
    r = np.arange(128)[:, None]
    t = np.arange(128)[None, :]
    return {"sd_mf": (r <= t).astype(np.float32), "sd_mb": (r >= t).astype(np.float32)}


def ssd_layer(P, C, li, w_in, conv_w, conv_b, dt_bias, a_log, d_skip, norm_g, w_out, ln_g, ln_b, last):
    S = C.scr
    NTAP = 5
    shifts = [-2, -1, 0, 1, 2]
    P.begin_phase()
    stage = P.sb("sstA", [128, 8, 512], F32)
    Wz = [P.sb("Wz%d" % i, [128, 8, 512], BF16) for i in range(2)]
    pz = [P.ps("pz%d" % i, [128, 512], F32) for i in range(2)]
    zo = [P.sb("szo%d" % i, [128, 512], F32) for i in range(2)]
    bo = [P.sb("sbo%d" % i, [128, 512], BF16) for i in range(2)]
    it = 0
    for c in range(0 if DBG_SKIPA else 4):
        W = Wz[c % 2]
        prep_w_chunk(P, [W], w_in, c * 512, 512, stage)
        for tt in range(NT):
            s = it % 2
            it += 1
            mm_proj_tm(P, pz[s].v, C, [W], [0], tt, 512)
            P.act(zo[s].v, pz[s].v, AF.Silu)
            P.dma(S["zg"][tt * 128:(tt + 1) * 128, c * 512:(c + 1) * 512], zo[s].v)
    if not DBG_SKIPA:
        wcol, bcol = conv_cols(P, C, conv_w, conv_b, NTAP, 24)
        stg = P.sb("sstg", [128, 8, 128], F32)
        Wb = [P.sb("sWb%d" % i, [128, 8, 128], BF16) for i in range(2)]
        raw = [P.sb("sraw%d" % i, [128, L + 4], BF16) for i in range(2)]
        for i in range(2):
            P.memset(raw[i][:, 0:2], 0.0, eng="dve")
            P.memset(raw[i][:, L + 2:L + 4], 0.0, eng="dve")
        crow = [P.sb("scrow%d" % i, [128, L], BF16) for i in range(2)]
        xT2 = [P.sb("sxT%d" % i, [128, 2, L], BF16) for i in range(2)]
        pp = [P.ps("spp%d" % i, [128, 512], F32) for i in range(3)]
        ptr = [P.ps("sptr%d" % i, [128, 256], BF16) for i in range(2)]
        so = [P.sb("sso%d" % i, [128, 256], BF16) for i in range(2)]
        it2 = 0
        for i in range(24):
            sb_ = i % 2
            P.dma(stg.v, w_in[:, 2048 + i * 128: 2048 + (i + 1) * 128].re("(k p) c -> p k c", p=128))
            P.cp(Wb[sb_].v, stg.v)
            for tc in range(L // 512):
                b = it2 % 3
                it2 += 1
                mm_proj_fm(P, pp[b].v, C, Wb[sb_], 0, tc * 512, 512)
                P.cp(raw[sb_][:, 2 + tc * 512: 2 + (tc + 1) * 512], pp[b].v, eng=("act" if tc % 2 else "dve"))
            if i < 20:
                grp = (i // 2) % 2
                dst = xT2[grp][:, i % 2, :]
            else:
                dst = crow[i % 2].v
            conv_row(P, dst, raw[sb_], wcol, bcol, i, NTAP, func=AF.Silu)
            if 16 <= i < 20:
                P.dma(S["bT"][i - 16, :, :], dst)
            if i >= 20:
                P.dma(S["cT"][i - 20, :, :], dst)
            if i < 20 and i % 2 == 1:
                i0 = i - 1
                for tt in range(NT):
                    q = tt % 2
                    for j in range(2):
                        P.tr(ptr[q][:, j * 128:(j + 1) * 128], xT2[grp][:, j, tt * 128:(tt + 1) * 128], C.ident.v)
                    P.cp(so[q].v, ptr[q].v, eng=("act" if tt % 2 else "dve"))
                    if i0 < 16:
                        P.dma(S["xsb"][tt * 128:(tt + 1) * 128, i0 * 128:(i0 + 2) * 128], so[q].v)
                    else:
                        P.dma(S["btm"][tt * 128:(tt + 1) * 128, (i0 - 16) * 128:(i0 - 14) * 128], so[q].v)
    Wd = Wz[0]
    prep_w_chunk(P, [Wd], w_in, 5120, 64, stage)
    dtb = P.sb("dtb", [128, 64], F32)
    arow = P.sb("arow", [128, 64], F32)
    P.dma(dtb.v, V(dt_bias, dt_bias.t.rearrange("a b -> (a b)").partition_broadcast(128)))
    P.dma(arow.v, V(a_log, a_log.t.rearrange("a b -> (a b)").partition_broadcast(128)))
    P.act(arow.v, arow.v, AF.Exp)
    dto = [P.sb("dto%d" % i, [128, 64], F32) for i in range(2)]
    ado = [P.sb("ado%d" % i, [128, 64], F32) for i in range(2)]
    for tt in range(NT):
        s = tt % 2
        mm_proj_tm(P, pz[s][:, 0:64], C, [Wd], [0], tt, 64)
        P.tt(dto[s].v, pz[s][:, 0:64], dtb.v, ALU.add)
        P.act(dto[s].v, dto[s].v, AF.Exp)
        P.act(dto[s].v, dto[s].v, AF.Ln, bias=1.0)
        P.stt(ado[s].v, dto[s].v, -1.0, arow.v, ALU.mult, ALU.mult)
        P.dma(S["dt"][tt * 128:(tt + 1) * 128, :], dto[s].v)
        P.dma(S["adt"][tt * 128:(tt + 1) * 128, :], ado[s].v)
    P.end_phase()
    ssd_scan_phase(P, C, norm_g, d_skip)
    outproj_phase(P, C, w_out, ln_g, ln_b, last)


NF = 8192
NK2 = 65


def hyena_consts():
    import ml_dtypes
    bf = ml_dtypes.bfloat16
    n = NF
    tp = 2.0 * np.pi
    p = np.arange(128)
    k2 = np.arange(65)
    k2s = np.arange(1, 64)
    F1 = np.zeros((128, 128))
    F1[:, :65] = np.cos(tp * ((p[:, None] * k2[None, :]) % 128) / 128)
    F1[:, 65:] = -np.sin(tp * ((p[:, None] * k2s[None, :]) % 128) / 128)
    pp = np.arange(64)
    m = np.full(65, 2.0)
    m[0] = 1.0
    m[64] = 1.0
    Finv = np.zeros((128, 64))
    Finv[:65, :] = (m[:, None] / n) * np.cos(tp * ((k2[:, None] * pp[None, :]) % 128) / 128)
    Finv[65:, :] = -(2.0 / n) * np.sin(tp * ((k2s[:, None] * pp[None, :]) % 128) / 128)
    a = np.arange(64)
    k1 = np.arange(64)
    TA = np.zeros((65, 128, 128))
    TB = np.zeros((65, 128, 128))
    TC = np.zeros((65, 128, 128))
    TKA = np.zeros((65, 128, 128))
    TKB = np.zeros((65, 128, 128))
    for q in range(65):
        th = tp * ((a[:, None] * (q + 128 * k1[None, :])) % n) / n
        Tr = np.cos(th)
        Ti = -np.sin(th)
        TA[q, :64, :64] = Tr
        TA[q, :64, 64:] = Ti
        TA[q, 64:, :64] = -Ti
        TA[q, 64:, 64:] = Tr
        TB[q, :64, :64] = -Ti
        TB[q, :64, 64:] = Tr
        TB[q, 64:, :64] = -Tr
        TB[q, 64:, 64:] = -Ti
        Ur = np.cos(th).T
        Ui = np.sin(th).T
        TKA[q, :64, :64] = Tr
        TKA[q, :64, 64:] = Tr
        TKA[q, 64:, :64] = -Ti
        TKA[q, 64:, 64:] = -Ti
        TKB[q, :64, :64] = Ti
        TKB[q, :64, 64:] = Ti
        TKB[q, 64:, :64] = Tr
        TKB[q, 64:, 64:] = Tr
        TC[q, :64, :64] = Ur
        TC[q, 64:, :64] = -Ui
        TC[q, :64, 64:] = Ui
        TC[q, 64:, 64:] = Ur
    j = (64 * p[None, :] + a[:, None]).reshape(-1)
    pos = np.where(j < L, j, n - j)
    pos[j == L] = 0
    tl = np.linspace(0.0, 1.0, L)
    t = tl[pos]
    w = tp * pos / L
    f = np.linspace(1e-4, 15.0, 16)
    z = np.concatenate([t[:, None], np.cos(f[None, :] * w[:, None]), -np.sin(f[None, :] * w[:, None])], axis=1)
    tdec = t.copy()
    tdec[j == L] = 1e4
    ntpos = -(tdec.reshape(64, 128).T)
    min_d = math.log(1e-2) / 0.3
    max_d = math.log(1e-2) / 1.5
    deltas = np.abs(np.linspace(min_d, max_d, DI))
    return {
        "hy_F1": F1.astype(np.float32).astype(bf),
        "hy_Finv": Finv.astype(np.float32).astype(bf),
        "hy_TA": np.ascontiguousarray(TA.transpose(1, 0, 2)).astype(np.float32).astype(bf),
        "hy_TB": np.ascontiguousarray(TB.transpose(1, 0, 2)).astype(np.float32).astype(bf),
        "hy_TC": np.ascontiguousarray(TC.transpose(1, 0, 2)).astype(np.float32).astype(bf),
        "hy_TKA": np.ascontiguousarray(TKA.transpose(1, 0, 2)).astype(np.float32).astype(bf),
        "hy_TKB": np.ascontiguousarray(TKB.transpose(1, 0, 2)).astype(np.float32).astype(bf),
        "hy_zT": np.ascontiguousarray(z.T).astype(np.float32),
        "hy_ntpos": ntpos.astype(np.float32),
        "hy_deltas": deltas.astype(np.float32),
    }


def sin_reduced(P, o, arg, tmpf, tmpi, tmp2):
    tp = 2.0 * math.pi
    P.ts(tmpf, arg, 1.0 / tp, ALU.mult, 64.0, ALU.add)
    P.cp(tmpi, tmpf)
    P.cp(tmpf, tmpi)
    P.ts(tmpf, tmpf, -64.0, ALU.add, -tp, ALU.mult)
    P.tt(tmpf, tmpf, arg, ALU.add)
    P.ts(tmp2, tmpf, math.pi, ALU.is_gt, -tp, ALU.mult)
    P.tt(tmpf, tmpf, tmp2, ALU.add)
    P.ts(tmpf, tmpf, 3.14159, ALU.min, -3.14159, ALU.max)
    P.act(o, tmpf, AF.Sin)


def conv_cols(P, C, conv_w, conv_b, ntaps, nch):
    wrow = P.sb("cw_row", [nch, ntaps, 128], F32)
    brow = P.sb("cb_row", [nch, 128], F32)
    P.dma(wrow.v, conv_w.v.re("j (k p) -> k j p", p=128))
    P.dma(brow.v, conv_b.v.re("(k p) -> k p", p=128))
    wcol = P.sb("cw_col", [128, ntaps, nch], F32)
    bcol = P.sb("cb_col", [128, nch], F32)
    pt = P.ps("cw_ps", [128, ntaps + 1, nch], F32)
    for j in range(ntaps):
        P.tr(pt[:, j, :], wrow[:, j, :], C.identf[0:nch, 0:nch])
    P.tr(pt[:, ntaps, :], brow.v, C.identf[0:nch, 0:nch])
    P.cp(wcol.v, pt[:, 0:ntaps, :])
    P.cp(bcol.v, pt[:, ntaps, :])
    return wcol, bcol


def conv_row(P, out, raw, wcol, bcol, k, ntaps, func=None):
    c = ntaps // 2
    H = L // 2
    for hf in range(2):
        o = out[:, hf * H:(hf + 1) * H]
        P.act(o, raw[:, c + hf * H: c + (hf + 1) * H], AF.Identity, scale=wcol[:, c, k:k + 1], bias=bcol[:, k:k + 1])
        js = [j for j in range(ntaps) if j != c]
        for n_, j in enumerate(js):
            P.stt(o, raw[:, j + hf * H: j + (hf + 1) * H], wcol[:, j, k:k + 1], o, ALU.mult, ALU.add)
        if func is not None:
            P.act(o, o, func)


def hyena_layer(P, C, li, prm, last):
    S = C.scr
    w_in, conv_w, conv_b = prm["w_in"], prm["conv_w"], prm["conv_b"]
    shifts = [-1, 0, 1]
    P.begin_phase()
    wcol, bcol = conv_cols(P, C, conv_w, conv_b, 3, 48)
    NG2 = 2
    stage = [P.sb("hstA%d" % i, [128, 8, 2, 128], F32) for i in range(1)] * 2
    Wb = [P.sb("hWb%d" % i, [128, 8, 512], BF16) for i in range(2)]
    raw = [[P.sb("hraw%d_%d" % (i, k), [128, L + 2], BF16) for k in range(3)] for i in range(2)]
    for i in range(2):
        for k in range(3):
            P.memset(raw[i][k][:, 0:1], 0.0, eng="dve")
            P.memset(raw[i][k][:, L + 1:L + 2], 0.0, eng="dve")
    cc = [P.sb("hcc%d" % k, [128, L], BF16) for k in range(2)]
    zs2 = [P.sb("hzs%d" % i, [128, L], BF16) for i in range(2)]
    gT = [P.sb("hgT%d" % i, [128, NG2, L], BF16) for i in range(2)]
    pp = [P.ps("hpp%d" % i, [128, 512], F32) for i in range(4)]
    ptr = [P.ps("hptr%d" % i, [128, 2, NG2 * 128], BF16) for i in range(2)]
    so = [P.sb("hso%d" % i, [128, 2, NG2 * 128], BF16) for i in range(2)]
    col0 = [2048, 4096, 0, 6144]
    cch = [16, 32, 0]
    it = 0
    for i in range(16):
        sb_ = i % 2
        zs = zs2[sb_]
        for hf in range(2):
            for k2_ in range(2):
                k = hf * 2 + k2_
                P.dma(stage[hf][:, :, k2_, :],
                      w_in[:, col0[k] + i * 128: col0[k] + (i + 1) * 128].re("(k p) c -> p k c", p=128))
            P.cp(Wb[sb_][:, :, hf * 256:(hf + 1) * 256], stage[hf].v.re("p k a c -> p k (a c)"),
                 eng=("pool" if hf else "dve"))
        for tc in range(L // 512):
            for k in range(4):
                b = it % 4
                it += 1
                mm_proj_fm(P, pp[b].v, C, Wb[sb_], k * 128, tc * 512, 512)
                if k < 3:
                    P.cp(raw[sb_][k][:, 1 + tc * 512: 1 + (tc + 1) * 512], pp[b].v, eng=("act" if (tc + k) % 2 else "dve"))
                else:
                    P.act(zs[:, tc * 512:(tc + 1) * 512], pp[b].v, AF.Silu)
        conv_row(P, cc[0], raw[sb_][0], wcol, bcol, cch[0] + i, 3)
        conv_row(P, cc[1], raw[sb_][1], wcol, bcol, cch[1] + i, 3)
        P.tt(gT[0][:, i % NG2, :], cc[0].v, cc[1].v, ALU.mult, eng="pool")
        conv_row(P, cc[0], raw[sb_][2], wcol, bcol, cch[2] + i, 3)
        P.tt(gT[1][:, i % NG2, :], cc[0].v, zs.v, ALU.mult, eng="pool")
        if i % NG2 == NG2 - 1:
            i0 = i - (NG2 - 1)
            for tt in range(NT):
                q = tt % 2
                for w_ in range(2):
                    for j in range(NG2):
                        P.tr(ptr[q][:, w_, j * 128:(j + 1) * 128], gT[w_][:, j, tt * 128:(tt + 1) * 128], C.ident.v)
                P.cp(so[q].v, ptr[q].v, eng=("act" if tt % 2 else "dve"))
                P.dma(S["gbf"][tt * 128:(tt + 1) * 128, i0 * 128:(i0 + NG2) * 128], so[q][:, 0, :])
                P.dma(S["gatebf"][tt * 128:(tt + 1) * 128, i0 * 128:(i0 + NG2) * 128], so[q][:, 1, :])
    P.end_phase()
    P.begin_phase()
    F1 = P.sb("F1", [128, 128], BF16)
    P.dma(F1.v, C.consts["hy_F1"].v)
    ntp = P.sb("ntp", [128, 64], F32)
    P.dma(ntp.v, C.consts["hy_ntpos"].v)
    dl = P.sb("dl", [128, DI], F32)
    row_bcast(P, dl.v, C.consts["hy_deltas"], slice(None))
    w1 = P.sb("fw1", [33, 64], F32)
    w2 = P.sb("fw2", [64, 64], F32)
    w3 = P.sb("fw3", [64, 64], F32)
    P.dma(w1.v, prm["filt_w1"].v)
    P.dma(w2.v, prm["filt_w2"].v)
    P.dma(w3.v, prm["filt_w3"].v)
    cols = P.sb("fcols", [64, 4], F32)
    for i, nme in enumerate(["filt_freq", "filt_b1", "filt_b2", "filt_b3"]):
        P.dma(cols[:, i:i + 1], prm[nme].v.us(1))
    fb = P.sb("ffb", [64, 3], F32)
    for i in range(3):
        P.tt(fb[:, i:i + 1], cols[:, 0:1], cols[:, i + 1:i + 2], ALU.mult)
    wo_f = P.sb("fwo_f", [64, 4096], F32)
    wo = P.sb("fwo", [64, 4096], BF16)
    P.dma(wo_f.v, prm["filt_w_out"].v)
    P.cp(wo.v, wo_f.v, eng="pool")
    h3Tc = [P.sb("h3T%d" % i, [64, 1024], BF16) for i in range(NF // 1024)]
    MW = 1024
    zc = [P.sb("zc%d" % i, [33, MW], F32) for i in range(2)]
    arg = [P.sb("farg%d" % i, [64, MW], F32) for i in range(2)]
    tmf = [P.sb("ftmf%d" % i, [64, MW], F32) for i in range(2)]
    tmi = [P.sb("ftmi%d" % i, [64, MW], I32) for i in range(2)]
    tm2 = [P.sb("ftm2%d" % i, [64, MW], F32) for i in range(2)]
    hh = [P.sb("fhh%d" % i, [64, MW], F32) for i in range(2)]
    pm = [P.ps("fpm%d" % i, [64, MW], F32) for i in range(2)]
    it = 0
    for ch in range(NF // MW):
        s = ch % 2
        P.dma(zc[s].v, C.consts["hy_zT"][:, ch * MW:(ch + 1) * MW])
        src = zc[s].v
        for lyr, wl in enumerate([w1, w2, w3]):
            b = it % 2
            it += 1
            for q in range(MW // 512):
                P.mm(pm[b][:, q * 512:(q + 1) * 512], wl.v, src[:, q * 512:(q + 1) * 512])
            P.act(arg[b].v, pm[b].v, AF.Identity, scale=cols[:, 0:1], bias=fb[:, lyr:lyr + 1])
            dst = hh[lyr % 2].v if lyr < 2 else h3Tc[ch].v
            sin_reduced(P, dst, arg[b].v, tmf[b].v, tmi[b].v, tm2[b].v)
            src = dst
    pk = [P.ps("fpk%d" % i, [128, 512], F32) for i in range(2)]
    pz = [P.ps("fpz%d" % i, [128, 512], F32) for i in range(2)]
    dec = [P.sb("fdec%d" % i, [128, DI], F32) for i in range(2)]
    ka = [P.sb("fka%d" % i, [128, 512], BF16) for i in range(2)]
    zt = [P.sb("fzt%d" % i, [128, DI], BF16) for i in range(2)]
    it = 0
    for a in range(64):
        sa = a % 2
        P.act(dec[sa].v, dl.v, AF.Exp, scale=ntp[:, a:a + 1])
        for c in range(4):
            s = it % 2
            it += 1
            hb_ = h3Tc[a // 8]
            ao = (a % 8) * 128
            P.mm(pk[s][0:64, :], hb_[:, ao: ao + 64], wo[:, c * 512:(c + 1) * 512])
            P.mm(pk[s][64:128, :], hb_[:, ao + 64: ao + 128], wo[:, 2048 + c * 512: 2048 + (c + 1) * 512])
            P.tt(ka[s].v, pk[s].v, dec[sa][:, c * 512:(c + 1) * 512], ALU.mult)
            P.mm(pz[s].v, F1.v, ka[s].v)
            P.cp(zt[sa][:, c * 512:(c + 1) * 512], pz[s].v, eng=("act" if c % 2 else "dve"))
        P.dma(S["Zk"][:, a, :], zt[sa].v)
    P.end_phase()
    P.begin_phase()
    F1 = P.sb("F1c", [128, 128], BF16)
    P.dma(F1.v, C.consts["hy_F1"].v)
    gb_ = [P.sb("cgb%d" % i, [64, DI], BF16) for i in range(3)]
    zt = [P.sb("czt%d" % i, [128, DI], BF16) for i in range(2)]
    pz = [P.ps("cpz%d" % i, [128, 512], F32) for i in range(4)]
    gview = S["gbf"].v.re("(p a) c -> a p c", a=64)
    for a in range(64):
        s = a % 2
        P.dma(gb_[a % 3].v, gview[a])
        for c in range(4):
            b = c
            P.mm(pz[b].v, F1[0:64, :], gb_[a % 3][:, c * 512:(c + 1) * 512])
            P.cp(zt[s][:, c * 512:(c + 1) * 512], pz[b].v, eng=("act" if c % 2 else "dve"))
        P.dma(S["Zs"][:, a, :], zt[s].v)
    P.end_phase()
    P.begin_phase()
    TA = P.sb("dTA", [128, NK2, 128], BF16)
    TB = P.sb("dTB", [128, NK2, 128], BF16)
    TC = P.sb("dTC", [128, NK2, 128], BF16)
    P.dma(TA.v, C.consts["hy_TA"].v)
    P.dma(TB.v, C.consts["hy_TB"].v)
    P.dma(TC.v, C.consts["hy_TC"].v)
    TKA = P.sb("dTKA", [128, NK2, 128], BF16)
    TKB = P.sb("dTKB", [128, NK2, 128], BF16)
    P.dma(TKA.v, C.consts["hy_TKA"].v)
    P.dma(TKB.v, C.consts["hy_TKB"].v)
    zin = [P.sb("dzin%d" % i, [128, DI], BF16) for i in range(2)]
    kin = [P.sb("dkin%d" % i, [128, DI], BF16) for i in range(2)]
    KA = [P.sb("dKA%d" % i, [128, 512], F32) for i in range(2)]
    KB = [P.sb("dKB%d" % i, [128, 512], F32) for i in range(2)]
    t1 = [P.sb("dt1_%d" % i, [128, 512], F32) for i in range(2)]
    t2 = [P.sb("dt2_%d" % i, [128, 512], F32) for i in range(2)]
    yb = [P.sb("dyb%d" % i, [128, 512], BF16) for i in range(2)]
    vo = [P.sb("dvo%d" % i, [128, DI], BF16) for i in range(2)]
    pA = [P.ps("dpA%d" % i, [128, 512], F32) for i in range(1)] * 2
    pB = [P.ps("dpB%d" % i, [128, 512], F32) for i in range(1)] * 2
    pKA = [P.ps("dpKA%d" % i, [128, 512], F32) for i in range(2)]
    pKB = [P.ps("dpKB%d" % i, [128, 512], F32) for i in range(2)]
    pV = [P.ps("dpV%d" % i, [128, 512], F32) for i in range(2)]
    it = 0
    for q in range(NK2):
        s = q % 2
        P.dma(zin[s][0:64, :], S["Zs"][q, :, :])
        P.dma(kin[s][0:64, :], S["Zk"][q, :, :])
        if 1 <= q <= 63:
            P.dma(zin[s][64:128, :], S["Zs"][64 + q, :, :])
            P.dma(kin[s][64:128, :], S["Zk"][64 + q, :, :])
        else:
            P.memset(zin[s][64:128, :], 0.0)
            P.memset(kin[s][64:128, :], 0.0)
        for c in range(4):
            b = it % 2
            it += 1
            cs_ = slice(c * 512, (c + 1) * 512)
            P.mm(pKA[b].v, TKA[:, q, :], kin[s][:, cs_])
            P.mm(pKB[b].v, TKB[:, q, :], kin[s][:, cs_])
            P.cp(KA[b].v, pKA[b].v, eng="act")
            P.cp(KB[b].v, pKB[b].v, eng="act")
            P.mm(pA[b].v, TA[:, q, :], zin[s][:, cs_])
            P.mm(pB[b].v, TB[:, q, :], zin[s][:, cs_])
            P.tt(t1[b].v, pA[b].v, KA[b].v, ALU.mult)
            P.tt(t2[b].v, pB[b].v, KB[b].v, ALU.mult)
            P.tt(yb[b].v, t1[b].v, t2[b].v, ALU.add, eng="pool")
            P.mm(pV[b].v, TC[:, q, :], yb[b].v)
            P.cp(vo[s][:, cs_], pV[b].v, eng=("act" if c % 2 else "dve"))
        P.dma(S["Zv"][q, :, :], vo[s][0:64, :])
        if 1 <= q <= 63:
            P.dma(S["Zv"][64 + q, :, :], vo[s][64:128, :])
    P.end_phase()
    P.begin_phase()
    Fi = P.sb("eFi", [128, 64], BF16)
    P.dma(Fi.v, C.consts["hy_Finv"].v)
    skb = P.sb("eskb", [128, DI], F32)
    row_bcast(P, skb.v, prm["skip"], slice(None))
    vin = [P.sb("evin%d" % i, [128, 2, DI], BF16) for i in range(2)]
    gt = [P.sb("egt%d" % i, [128, DI], BF16) for i in range(2)]
    gs = [P.sb("egs%d" % i, [128, DI], F32) for i in range(2)]
    zg = [P.sb("ezg%d" % i, [128, DI], BF16) for i in range(2)]
    yo = [P.sb("eyo%d" % i, [128, DI], BF16) for i in range(2)]
    py = [P.ps("epy%d" % i, [128, 512], F32) for i in range(2)]
    gview = S["gbf"].v.re("(p a) c -> a p c", a=64)
    zview = S["gatebf"].v.re("(p a) c -> a p c", a=64)
    yview = S["y"].v.re("(p a) c -> a p c", a=64)
    it = 0
    for a2 in range(32):
        s = a2 % 2
        for h in range(2):
            a = a2 * 2 + h
            P.dma(vin[s][:, h, :], S["Zv"][:, a, :])
            P.dma(gt[s][h * 64:(h + 1) * 64, :], gview[a])
            P.dma(zg[s][h * 64:(h + 1) * 64, :], zview[a])
        P.tt(gs[s].v, gt[s].v, skb.v, ALU.mult)
        for c in range(4):
            b = it % 2
            it += 1
            cs_ = slice(c * 512, (c + 1) * 512)
            for h in range(2):
                P.mm(py[b][h * 64:(h + 1) * 64, :], Fi.v, vin[s][:, h, cs_])
            P.tt(gs[s][:, cs_], py[b].v, gs[s][:, cs_], ALU.add)
            P.tt(yo[s][:, cs_], gs[s][:, cs_], zg[s][:, cs_], ALU.mult, eng="pool")
        for h in range(2):
            P.dma(yview[a2 * 2 + h], yo[s][h * 64:(h + 1) * 64, :])
    P.end_phase()
    outproj_phase(P, C, prm["w_out"], prm["ln_g"], prm["ln_b"], last)


_ALL_INPUT_NAMES = (
    "x",
    "hgrn_lower_bounds",
    "l0_w_in",
    "l0_conv_w",
    "l0_conv_b",
    "l0_filt_w1",
    "l0_filt_b1",
    "l0_filt_w2",
    "l0_filt_b2",
    "l0_filt_w3",
    "l0_filt_b3",
    "l0_filt_freq",
    "l0_filt_w_out",
    "l0_skip",
    "l0_w_out",
    "l0_ln_g",
    "l0_ln_b",
    "l1_w_in",
    "l1_conv_w",
    "l1_conv_b",
    "l1_dt_bias",
    "l1_a_log",
    "l1_d_skip",
    "l1_norm_g",
    "l1_w_out",
    "l1_ln_g",
    "l1_ln_b",
    "l2_w_in",
    "l2_norm_g",
    "l2_w_out",
    "l2_ln_g",
    "l2_ln_b",
    "l3_w_in",
    "l3_conv_w",
    "l3_conv_b",
    "l3_filt_w1",
    "l3_filt_b1",
    "l3_filt_w2",
    "l3_filt_b2",
    "l3_filt_w3",
    "l3_filt_b3",
    "l3_filt_freq",
    "l3_filt_w_out",
    "l3_skip",
    "l3_w_out",
    "l3_ln_g",
    "l3_ln_b",
)
_PROG_CACHE = {}


def kernel(**inputs):
    layers = [0, 1, 2, 3]
    if "prog" not in _PROG_CACHE:
        _PROG_CACHE["prog"] = build_program(layers)
    P, C = _PROG_CACHE["prog"]
    inputs = {k: np.asarray(inputs[k]) for k in _ALL_INPUT_NAMES}
    nb = inputs["x"].shape[0]
    in_maps = [in_map_for(inputs, b, layers) for b in range(nb)]
    res = run_bass_kernel_spmd(P.nc, in_maps, core_ids=list(range(nb)))
    out = np.stack([np.asarray(res.results[b]["out"]) for b in range(nb)], axis=0)
    return out.astype(np.float32)
```
